# Optimizing a Trainium2 kernel written in Bass

```python
import math
import jax, jax.numpy as jnp
from jax import lax
import numpy as np

D_MODEL = 1024
BATCH = 4
SEQ = 4096
DEPTH = 4
DEC_BATCH = 8
DEC_SEQ = 16
PAST_LEN = 4096

CHUNK = 64
N_META = 16
Q_BLOCK = 128
N_EVEN = (DEPTH + 1) // 2
N_ODD = DEPTH // 2
S5_WIDTH = 512
S5_CH = 16
S5_GROUPS = S5_WIDTH // S5_CH
S5_STATE = 64
MLA_HEADS = 8
MLA_Q_RANK = 256
MLA_KV_RANK = 128
MLA_NOPE = 64
MLA_ROPE = 32
MLA_V = 64
FOX_HEADS = 8
FOX_DIM = 64
FOX_W = FOX_HEADS * FOX_DIM
RET_HEADS = 4
RET_DK = 64
RET_DV = 128
RET_QK = RET_HEADS * RET_DK
RET_VW = RET_HEADS * RET_DV
D_FF = 2816
ROPE_BASE = 10000.0
ALPHA = (2.0 * DEPTH) ** 0.25
BETA = (8.0 * DEPTH) ** -0.25
EPS = 1e-5
NEG = -1e30
EVEN_IN = S5_WIDTH + MLA_Q_RANK + MLA_KV_RANK + MLA_ROPE
EVEN_OUT = S5_WIDTH + MLA_HEADS * MLA_V
ODD_IN = 3 * FOX_W + FOX_HEADS + 2 * RET_QK + 2 * RET_VW
ODD_OUT = FOX_W + RET_VW

kernel_name = 'hybrid_streaming_encoder_step'

F32 = jnp.float32


def _layer_norm(x, g, b):
    xf = x.astype(F32)
    mu = xf.mean(-1, keepdims=True)
    var = jnp.square(xf - mu).mean(-1, keepdims=True)
    return ((xf - mu) * lax.rsqrt(var + EPS) * g.astype(F32) + b.astype(F32)).astype(x.dtype)


def _rms_norm(x, g):
    xf = x.astype(F32)
    return (xf * lax.rsqrt(jnp.square(xf).mean(-1, keepdims=True) + EPS) * g.astype(F32)).astype(x.dtype)


def _post_ln(x, sub, g, b):
    return _layer_norm(ALPHA * x + sub.astype(x.dtype), g, b)


def _swiglu(x, wg, wu, wd):
    hid = jax.nn.silu(jnp.einsum('bld,df->blf', x, wg)) * jnp.einsum('bld,df->blf', x, wu)
    return jnp.einsum('blf,fd->bld', hid, wd)


def _rope(x, pos):
    half = x.shape[-1] // 2
    inv = ROPE_BASE ** (-jnp.arange(half, dtype=F32) / half)
    ang = pos.astype(F32)[:, None] * inv[None, :]
    cos, sin = jnp.cos(ang)[:, None, :], jnp.sin(ang)[:, None, :]
    x1, x2 = x[..., :half].astype(F32), x[..., half:].astype(F32)
    return jnp.concatenate([x1 * cos - x2 * sin, x1 * sin + x2 * cos], axis=-1).astype(x.dtype)


def _attend(q, k, v, scale, bias, mask):
    s = jnp.einsum('bhqd,bhkd->bhqk', q, k, preferred_element_type=F32) * scale + bias
    p = jax.nn.softmax(jnp.where(mask, s, NEG), axis=-1)
    return jnp.einsum('bhqk,bhkd->bhqd', p, v.astype(F32))


def _sweep(block_fn, q_args, q_axes, n_q):
    n_blk = -(-n_q // Q_BLOCK)
    pad = n_blk * Q_BLOCK - n_q

    def split(a, ax):
        a = jnp.pad(a, [(0, pad) if i == ax else (0, 0) for i in range(a.ndim)])
        a = a.reshape(a.shape[:ax] + (n_blk, Q_BLOCK) + a.shape[ax + 1:])
        return jnp.moveaxis(a, ax, 0)

    blocks = tuple(split(a, ax) for a, ax in zip(q_args, q_axes))
    out = lax.map(lambda bl: block_fn(*bl), blocks)
    out = jnp.moveaxis(out, 0, 2)
    return out.reshape(out.shape[:2] + (n_blk * Q_BLOCK,) + out.shape[4:])[:, :, :n_q]


def _linear_combine(e1, e2):
    a1, b1 = e1
    a2, b2 = e2
    return a1 * a2, a2 * b1 + b2


def _s5(u, h0, a_re, a_im, b_re, b_im, c_re, c_im, d_skip, log_dt, w_glu, b_glu):
    bsz, seq, _ = u.shape
    uf = u.astype(F32)
    lam = lax.complex(a_re.astype(F32), a_im.astype(F32))
    lam_dt = lam * jnp.exp(log_dt.astype(F32))[:, None]
    a_bar = jnp.exp(lam_dt)
    b_bar = ((a_bar - 1.0) / lam)[..., None] * lax.complex(b_re.astype(F32), b_im.astype(F32))
    ug = uf.reshape(bsz, seq, S5_GROUPS, S5_CH).astype(jnp.complex64)
    bu = jnp.einsum('gnc,blgc->blgn', b_bar, ug)
    _, hs = lax.associative_scan(_linear_combine, (jnp.broadcast_to(a_bar, bu.shape), bu), axis=1)
    steps = jnp.arange(1, seq + 1, dtype=F32)[:, None, None]
    hs = hs + jnp.exp(lam_dt[None] * steps)[None] * h0[:, None]
    c = lax.complex(c_re.astype(F32), c_im.astype(F32))
    y = jnp.einsum('gcn,blgn->blgc', c, hs).real.reshape(bsz, seq, S5_WIDTH) + d_skip.astype(F32) * uf
    g = jax.nn.gelu(y)
    out = g * jax.nn.sigmoid(g @ w_glu.astype(F32) + b_glu.astype(F32))
    return out.astype(u.dtype), hs[:, -1]


def _mla(q_lat, c_kv, k_pe, pos, past_ckv, past_kpe, q_norm, kv_norm, w_uq, w_ukv):
    bsz, seq, _ = q_lat.shape
    q = jnp.einsum('blr,re->ble', _rms_norm(q_lat, q_norm), w_uq).reshape(bsz, seq, MLA_HEADS, MLA_NOPE + MLA_ROPE)
    q = jnp.concatenate([q[..., :MLA_NOPE], _rope(q[..., MLA_NOPE:], pos)], axis=-1)
    ckv_rows = _rms_norm(c_kv, kv_norm)
    kpe_rows = _rope(k_pe[:, :, None, :], pos)[:, :, 0, :]
    if past_ckv is None:
        ckv_all, kpe_all = ckv_rows, kpe_rows
    else:
        ckv_all = jnp.concatenate([past_ckv, ckv_rows], axis=1)
        kpe_all = jnp.concatenate([past_kpe, kpe_rows], axis=1)
    n_k = ckv_all.shape[1]
    kv = jnp.einsum('bkr,re->bke', ckv_all, w_ukv).reshape(bsz, n_k, MLA_HEADS, MLA_NOPE + MLA_V)
    k = jnp.concatenate([kv[..., :MLA_NOPE], jnp.broadcast_to(kpe_all[:, :, None, :], (bsz, n_k, MLA_HEADS, MLA_ROPE))], axis=-1)
    qt = q.transpose(0, 2, 1, 3)
    kt = k.transpose(0, 2, 1, 3)
    vt = kv[..., MLA_NOPE:].transpose(0, 2, 1, 3)
    scale = (MLA_NOPE + MLA_ROPE) ** -0.5
    if past_ckv is None:
        cid = (pos - N_META) // CHUNK
        o = _sweep(lambda qb, cb: _attend(qb, kt, vt, scale, 0.0, cb[:, None] >= cid[None, :]), (qt, cid), (2, 0), seq)
    else:
        o = _attend(qt, kt, vt, scale, 0.0, True)
    o = o.transpose(0, 2, 1, 3).reshape(bsz, seq, MLA_HEADS * MLA_V).astype(q_lat.dtype)
    return o, ckv_rows, kpe_rows


def _fox(q, k, v, f_logit, b_f, past_k, past_v, past_logf):
    bsz, seq = q.shape[:2]
    logf = jax.nn.log_sigmoid((f_logit + b_f).astype(F32))
    if past_k is None:
        k_all, v_all, lf_all = k, v, logf
    else:
        k_all = jnp.concatenate([past_k, k], axis=1)
        v_all = jnp.concatenate([past_v, v], axis=1)
        lf_all = jnp.concatenate([past_logf.astype(F32), logf], axis=1)
    n_k = k_all.shape[1]
    f_cum = jnp.cumsum(lf_all, axis=1).transpose(0, 2, 1)
    f_q = f_cum[:, :, n_k - seq:]
    k_idx = jnp.arange(n_k, dtype=jnp.int32)
    q_idx = jnp.arange(n_k - seq, n_k, dtype=jnp.int32)
    qt, kt, vt = q.transpose(0, 2, 1, 3), k_all.transpose(0, 2, 1, 3), v_all.transpose(0, 2, 1, 3)
    scale = FOX_DIM ** -0.5

    def blk(qb, fb, ib):
        return _attend(qb, kt, vt, scale, fb[..., :, None] - f_cum[..., None, :], ib[:, None] >= k_idx[None, :])

    if past_k is None:
        o = _sweep(blk, (qt, f_q, q_idx), (2, 2, 0), seq)
    else:
        o = blk(qt, f_q, q_idx)
    o = o.transpose(0, 2, 1, 3).reshape(bsz, seq, FOX_W).astype(q.dtype)
    return o, k, v, logf


def _retention_chunk(q, k, v, s0, log_gamma):
    c = q.shape[2]
    j = jnp.arange(c, dtype=F32)
    diff = j[:, None] - j[None, :]
    lg = log_gamma[:, None, None]
    decay = jnp.where(diff >= 0, jnp.exp(lg * jnp.maximum(diff, 0.0)), 0.0)
    scores = jnp.einsum('bhid,bhjd->bhij', q, k) * decay
    q_dec = q * jnp.exp(log_gamma[:, None] * (j + 1.0))[None, :, :, None]
    out = jnp.einsum('bhij,bhjv->bhiv', scores, v) + jnp.einsum('bhid,bhdv->bhiv', q_dec, s0)
    k_dec = k * jnp.exp(log_gamma[:, None] * (c - 1.0 - j))[None, :, :, None]
    s_new = jnp.exp(log_gamma * c)[None, :, None, None] * s0 + jnp.einsum('bhjd,bhjv->bhdv', k_dec, v)
    return out, s_new


def _retention(q, k, v, g, pos, s0):
    bsz, seq = q.shape[:2]
    qt = _rope(q, pos).astype(F32).transpose(0, 2, 1, 3)
    kt = (_rope(k, pos).astype(F32) * RET_DK ** -0.5).transpose(0, 2, 1, 3)
    vt = v.astype(F32).transpose(0, 2, 1, 3)
    log_gamma = jnp.log(1.0 - jnp.exp2(-5.0 - jnp.arange(RET_HEADS, dtype=F32)))
    if s0 is None:
        pad = CHUNK - N_META
        n_c = (seq + pad) // CHUNK

        def chunks(a):
            a = jnp.pad(a, ((0, 0), (0, 0), (pad, 0), (0, 0)))
            return jnp.moveaxis(a.reshape(bsz, RET_HEADS, n_c, CHUNK, a.shape[-1]), 2, 0)

        def step(s, qkv):
            o_c, s = _retention_chunk(qkv[0], qkv[1], qkv[2], s, log_gamma)
            return s, o_c

        s_last, o = lax.scan(step, jnp.zeros((bsz, RET_HEADS, RET_DK, RET_DV), F32), (chunks(qt), chunks(kt), chunks(vt)))
        o = jnp.moveaxis(o, 0, 2).reshape(bsz, RET_HEADS, n_c * CHUNK, RET_DV)[:, :, pad:]
    else:
        o, s_last = _retention_chunk(qt, kt, vt, s0.astype(F32), log_gamma)
    mu = o.mean(-1, keepdims=True)
    var = jnp.square(o - mu).mean(-1, keepdims=True)
    gn = ((o - mu) * lax.rsqrt(var + EPS)).transpose(0, 2, 1, 3).reshape(bsz, seq, RET_VW)
    return (jax.nn.silu(g.astype(F32)) * gn).astype(g.dtype), s_last


def _even_mixing(x, pos, e, past, w):
    bsz, seq, _ = x.shape
    h = jnp.einsum('bld,de->ble', x, w['even_w_in'][e])
    u, q_lat, c_kv, k_pe = jnp.split(h, [S5_WIDTH, S5_WIDTH + MLA_Q_RANK, S5_WIDTH + MLA_Q_RANK + MLA_KV_RANK], axis=-1)
    if past is None:
        h0 = jnp.zeros((bsz, S5_GROUPS, S5_STATE), jnp.complex64)
        p_ckv = p_kpe = None
    else:
        h0 = lax.complex(past['state_s5_re'][e].astype(F32), past['state_s5_im'][e].astype(F32))
        p_ckv, p_kpe = past['cache_mla_ckv'][e], past['cache_mla_kpe'][e]
    s5_out, h_last = _s5(u, h0, w['s5_a_re'][e], w['s5_a_im'][e], w['s5_b_re'][e], w['s5_b_im'][e],
                         w['s5_c_re'][e], w['s5_c_im'][e], w['s5_d'][e], w['s5_log_dt'][e],
                         w['s5_w_glu'][e], w['s5_b_glu'][e])
    mla_out, ckv_rows, kpe_rows = _mla(q_lat, c_kv, k_pe, pos, p_ckv, p_kpe, w['mla_q_norm'][e],
                                       w['mla_kv_norm'][e], w['mla_w_uq'][e], w['mla_w_ukv'][e])
    mix = jnp.einsum('ble,ed->bld', jnp.concatenate([s5_out, mla_out], axis=-1), w['even_w_out'][e])
    return mix, (ckv_rows, kpe_rows, jnp.real(h_last), jnp.imag(h_last))


def _odd_mixing(x, pos, o, past, w):
    bsz, seq, _ = x.shape
    h = jnp.einsum('bld,de->ble', x, w['odd_w_in'][o])
    cuts = np.cumsum([FOX_W, FOX_W, FOX_W, FOX_HEADS, RET_QK, RET_QK, RET_VW]).tolist()
    fq, fk, fv, f_logit, rq, rk, rv, rg = jnp.split(h, cuts, axis=-1)
    if past is None:
        pk = pv = plf = s0 = None
    else:
        pk, pv, plf = past['cache_fox_k'][o], past['cache_fox_v'][o], past['cache_fox_logf'][o]
        s0 = past['state_ret'][o]
    fox_out, k_rows, v_rows, lf_rows = _fox(fq.reshape(bsz, seq, FOX_HEADS, FOX_DIM), fk.reshape(bsz, seq, FOX_HEADS, FOX_DIM),
                                            fv.reshape(bsz, seq, FOX_HEADS, FOX_DIM), f_logit, w['fox_b_f'][o], pk, pv, plf)
    ret_out, s_last = _retention(rq.reshape(bsz, seq, RET_HEADS, RET_DK), rk.reshape(bsz, seq, RET_HEADS, RET_DK),
                                 rv.reshape(bsz, seq, RET_HEADS, RET_DV), rg, pos, s0)
    mix = jnp.einsum('ble,ed->bld', jnp.concatenate([fox_out, ret_out], axis=-1), w['odd_w_out'][o])
    return mix, (k_rows, v_rows, lf_rows, s_last)


def _trunk(x, pos, past, w):
    rows = {n: [] for n in ('mla_ckv', 'mla_kpe', 's5_re', 's5_im', 'fox_k', 'fox_v', 'fox_logf', 'ret')}
    for l in range(DEPTH):
        x = _post_ln(x, 0.5 * _swiglu(x, w['ffn_w_gate'][l, 0], w['ffn_w_up'][l, 0], w['ffn_w_down'][l, 0]),
                     w['ln_g'][l, 0], w['ln_b'][l, 0])
        if l % 2 == 0:
            mix, st = _even_mixing(x, pos, l // 2, past, w)
            names = ('mla_ckv', 'mla_kpe', 's5_re', 's5_im')
        else:
            mix, st = _odd_mixing(x, pos, l // 2, past, w)
            names = ('fox_k', 'fox_v', 'fox_logf', 'ret')
        for n, a in zip(names, st):
            rows[n].append(a)
        x = _post_ln(x, mix, w['ln_g'][l, 1], w['ln_b'][l, 1])
        x = _post_ln(x, 0.5 * _swiglu(x, w['ffn_w_gate'][l, 1], w['ffn_w_up'][l, 1], w['ffn_w_down'][l, 1]),
                     w['ln_g'][l, 2], w['ln_b'][l, 2])
    return x, {n: jnp.stack(a) for n, a in rows.items()}


def setup_inputs(seed: int = 0) -> dict:
    key = jax.random.key(seed)
    ks = jax.random.split(key, 40)

    def nrm(i, shape, scale=1.0):
        return scale * jax.random.normal(ks[i], shape, F32)

    def gain(i, shape):
        return 1.0 + 0.02 * jax.random.normal(ks[i], shape, F32)

    n_idx = jnp.arange(S5_STATE, dtype=F32)
    return {
        'x_prompt': nrm(0, (BATCH, SEQ, D_MODEL)),
        'x_sample': nrm(1, (DEC_BATCH, DEC_SEQ, D_MODEL)),
        'cache_mla_ckv': nrm(2, (N_EVEN, DEC_BATCH, PAST_LEN, MLA_KV_RANK)),
        'cache_mla_kpe': nrm(3, (N_EVEN, DEC_BATCH, PAST_LEN, MLA_ROPE)),
        'cache_fox_k': nrm(4, (N_ODD, DEC_BATCH, PAST_LEN, FOX_HEADS, FOX_DIM)),
        'cache_fox_v': nrm(5, (N_ODD, DEC_BATCH, PAST_LEN, FOX_HEADS, FOX_DIM)),
        'cache_fox_logf': jax.nn.log_sigmoid(jax.random.uniform(ks[6], (N_ODD, DEC_BATCH, PAST_LEN, FOX_HEADS), F32, 2.0, 5.0)),
        'state_s5_re': nrm(7, (N_EVEN, DEC_BATCH, S5_GROUPS, S5_STATE), 0.3),
        'state_s5_im': nrm(8, (N_EVEN, DEC_BATCH, S5_GROUPS, S5_STATE), 0.3),
        'state_ret': nrm(9, (N_ODD, DEC_BATCH, RET_HEADS, RET_DK, RET_DV), 0.3),
        'meta_tokens': nrm(10, (N_META, D_MODEL)),
        'ln_g': gain(11, (DEPTH, 3, D_MODEL)),
        'ln_b': nrm(12, (DEPTH, 3, D_MODEL), 0.02),
        'ffn_w_gate': nrm(13, (DEPTH, 2, D_MODEL, D_FF), D_MODEL ** -0.5),
        'ffn_w_up': nrm(14, (DEPTH, 2, D_MODEL, D_FF), D_MODEL ** -0.5),
        'ffn_w_down': nrm(15, (DEPTH, 2, D_FF, D_MODEL), BETA * D_FF ** -0.5),
        'even_w_in': nrm(16, (N_EVEN, D_MODEL, EVEN_IN), D_MODEL ** -0.5),
        'even_w_out': nrm(17, (N_EVEN, EVEN_OUT, D_MODEL), BETA * EVEN_OUT ** -0.5),
        's5_a_re': -0.5 + 0.01 * jax.random.normal(ks[18], (N_EVEN, S5_GROUPS, S5_STATE), F32),
        's5_a_im': math.pi * n_idx + 0.01 * jax.random.normal(ks[19], (N_EVEN, S5_GROUPS, S5_STATE), F32),
        's5_b_re': nrm(20, (N_EVEN, S5_GROUPS, S5_STATE, S5_CH), (2.0 * S5_CH) ** -0.5),
        's5_b_im': nrm(21, (N_EVEN, S5_GROUPS, S5_STATE, S5_CH), (2.0 * S5_CH) ** -0.5),
        's5_c_re': nrm(22, (N_EVEN, S5_GROUPS, S5_CH, S5_STATE), (2.0 * S5_STATE) ** -0.5),
        's5_c_im': nrm(23, (N_EVEN, S5_GROUPS, S5_CH, S5_STATE), (2.0 * S5_STATE) ** -0.5),
        's5_d': nrm(24, (N_EVEN, S5_WIDTH)),
        's5_log_dt': jax.random.uniform(ks[25], (N_EVEN, S5_GROUPS), F32, math.log(1e-3), math.log(1e-1)),
        's5_w_glu': nrm(26, (N_EVEN, S5_WIDTH, S5_WIDTH), S5_WIDTH ** -0.5),
        's5_b_glu': nrm(27, (N_EVEN, S5_WIDTH), 0.02),
        'mla_q_norm': gain(28, (N_EVEN, MLA_Q_RANK)),
        'mla_kv_norm': gain(29, (N_EVEN, MLA_KV_RANK)),
        'mla_w_uq': nrm(30, (N_EVEN, MLA_Q_RANK, MLA_HEADS * (MLA_NOPE + MLA_ROPE)), MLA_Q_RANK ** -0.5),
        'mla_w_ukv': nrm(31, (N_EVEN, MLA_KV_RANK, MLA_HEADS * (MLA_NOPE + MLA_V)), MLA_KV_RANK ** -0.5),
        'odd_w_in': nrm(32, (N_ODD, D_MODEL, ODD_IN), D_MODEL ** -0.5),
        'odd_w_out': nrm(33, (N_ODD, ODD_OUT, D_MODEL), BETA * ODD_OUT ** -0.5),
        'fox_b_f': jax.random.uniform(ks[34], (N_ODD, FOX_HEADS), F32, 2.0, 5.0),
    }


def reference(x_prompt, x_sample, cache_mla_ckv, cache_mla_kpe, cache_fox_k, cache_fox_v, cache_fox_logf,
              state_s5_re, state_s5_im, state_ret, meta_tokens, ln_g, ln_b, ffn_w_gate, ffn_w_up, ffn_w_down,
              even_w_in, even_w_out, s5_a_re, s5_a_im, s5_b_re, s5_b_im, s5_c_re, s5_c_im, s5_d, s5_log_dt,
              s5_w_glu, s5_b_glu, mla_q_norm, mla_kv_norm, mla_w_uq, mla_w_ukv, odd_w_in, odd_w_out, fox_b_f):
    w = dict(ln_g=ln_g, ln_b=ln_b, ffn_w_gate=ffn_w_gate, ffn_w_up=ffn_w_up, ffn_w_down=ffn_w_down,
             even_w_in=even_w_in, even_w_out=even_w_out, s5_a_re=s5_a_re, s5_a_im=s5_a_im, s5_b_re=s5_b_re,
             s5_b_im=s5_b_im, s5_c_re=s5_c_re, s5_c_im=s5_c_im, s5_d=s5_d, s5_log_dt=s5_log_dt,
             s5_w_glu=s5_w_glu, s5_b_glu=s5_b_glu, mla_q_norm=mla_q_norm, mla_kv_norm=mla_kv_norm,
             mla_w_uq=mla_w_uq, mla_w_ukv=mla_w_ukv, odd_w_in=odd_w_in, odd_w_out=odd_w_out, fox_b_f=fox_b_f)
    past = dict(cache_mla_ckv=cache_mla_ckv, cache_mla_kpe=cache_mla_kpe, cache_fox_k=cache_fox_k,
                cache_fox_v=cache_fox_v, cache_fox_logf=cache_fox_logf, state_s5_re=state_s5_re,
                state_s5_im=state_s5_im, state_ret=state_ret)
    bsz, seq, _ = x_prompt.shape
    meta = jnp.broadcast_to(meta_tokens[None].astype(x_prompt.dtype), (bsz, N_META, D_MODEL))
    pos_p = jnp.arange(N_META + seq, dtype=jnp.int32)
    y_p, st_p = _trunk(jnp.concatenate([meta, x_prompt], axis=1), pos_p, None, w)
    y_prompt = y_p[:, N_META:]
    past_len = cache_fox_k.shape[2]
    pos_s = N_META + past_len + jnp.arange(x_sample.shape[1], dtype=jnp.int32)
    y_sample, st_s = _trunk(x_sample, pos_s, past, w)
    return (y_prompt, y_sample,
            st_p['mla_ckv'], st_p['mla_kpe'], st_p['fox_k'], st_p['fox_v'], st_p['fox_logf'],
            st_p['s5_re'], st_p['s5_im'], st_p['ret'],
            st_s['mla_ckv'], st_s['mla_kpe'], st_s['fox_k'], st_s['fox_v'], st_s['fox_logf'],
            st_s['s5_re'], st_s['s5_im'], st_s['ret'])
```

```python
import math
import numpy as np
import ml_dtypes
from contextlib import ExitStack
import concourse.bass as bass
import concourse.mybir as mybir
from concourse.bass_utils import run_bass_kernel_spmd

F32 = mybir.dt.float32
BF16 = mybir.dt.bfloat16
AF = mybir.ActivationFunctionType
ALU = mybir.AluOpType
AX = mybir.AxisListType

D = 1024
DFF = 2816
NFT = DFF // 128
DEPTH = 4
ALPHA = (2.0 * DEPTH) ** 0.25
EPS = 1e-5
NMETA = 16
DSEQ = 16
NDS = 8


class Buf:
    __slots__ = ("w", "r", "excl")

    def __init__(self, excl=False):
        self.w = None
        self.r = {}
        self.excl = excl


class Sched:
    def __init__(self, nc, es):
        self.nc = nc
        self.engs = {"pe": nc.tensor, "act": nc.scalar, "dve": nc.vector, "pool": nc.gpsimd, "sp": nc.sync}
        self.sem = {k: es.enter_context(nc.semaphore("sem_" + k)) for k in ("pe", "act", "dve", "pool")}
        self.cnt = {k: 0 for k in self.sem}
        self.seen = {e: {} for e in self.engs}
        self.dq = {}
        for q in ("sp", "pool", "act"):
            self.dq[q] = [[es.enter_context(nc.semaphore("dma_%s_%d" % (q, i))), 0] for i in range(NDS)]
        self.dqi = {q: 0 for q in self.dq}
        self.out_toks = []
        self.nins = 0

    def _wait(self, e, tok):
        if tok is None:
            return
        key, sem, val = tok
        if e == "pe" and key == "pe":
            return
        if self.seen[e].get(key, 0) >= val:
            return
        self.engs[e].wait_ge(sem, val)
        self.seen[e][key] = val

    def _deps(self, e, reads, writes):
        for b in reads:
            self._wait(e, b.w)
            if b.excl:
                for k, t in list(b.r.items()):
                    if k != e:
                        self._wait(e, t)
        for b in writes:
            self._wait(e, b.w)
            for t in list(b.r.values()):
                self._wait(e, t)

    def _commit(self, tok, reads, writes):
        for b in reads:
            b.r[tok[0]] = tok
        for b in writes:
            b.w = tok
            b.r = {}

    def op(self, e, fn, reads=(), writes=()):
        self._deps(e, reads, writes)
        ins = fn(self.engs[e])
        self.cnt[e] += 1
        ins.then_inc(self.sem[e], 1)
        tok = (e, self.sem[e], self.cnt[e])
        self._commit(tok, reads, writes)
        self.nins += 1

    def dma(self, q, out, in_, reads=(), writes=(), is_output=False, **kw):
        idx = self.dqi[q] % NDS
        self.dqi[q] += 1
        slot = self.dq[q][idx]
        key = "d%s%d" % (q, idx)
        if slot[1] > 0:
            self._wait(q, (key, slot[0], slot[1]))
        self._deps(q, reads, writes)
        self.engs[q].dma_start(out=out, in_=in_, **kw).then_inc(slot[0], 16)
        slot[1] += 16
        tok = (key, slot[0], slot[1])
        self._commit(tok, reads, writes)
        self.nins += 1

    def barrier(self):
        toks = []
        for q in self.dq:
            for idx, slot in enumerate(self.dq[q]):
                if slot[1] > 0:
                    toks.append(("d%s%d" % (q, idx), slot[0], slot[1]))
        for e in ("pe", "act", "dve", "pool"):
            if self.cnt[e] > 0:
                toks.append((e, self.sem[e], self.cnt[e]))
        for e in self.engs:
            for t in toks:
                if e == "pe" and t[0] == "pe":
                    if self.seen[e].get("pe", 0) < t[2]:
                        self.engs[e].wait_ge(t[1], t[2])
                        self.seen[e]["pe"] = t[2]
                    continue
                self._wait(e, t)

    def finish(self):
        for q in self.dq:
            for idx, slot in enumerate(self.dq[q]):
                if slot[1] > 0:
                    self._wait("sp", ("d%s%d" % (q, idx), slot[0], slot[1]))
        for e in ("pe", "act", "dve", "pool"):
            if self.cnt[e] > 0:
                self._wait("sp", (e, self.sem[e], self.cnt[e]))


class Ctx:
    pass


def chunks(n, c):
    return [(o, min(c, n - o)) for o in range(0, n, c)]


def build(SEQ, PAST, stop_after=None):
    T = NMETA + SEQ
    R = T + DSEQ
    nc = bass.Bass("TRN2", target_bir_lowering=False)
    g = Ctx()
    g.nc, g.T, g.R, g.SEQ, g.PAST = nc, T, R, SEQ, PAST

    def din(name, shape, dt=F32):
        return nc.dram_tensor(name, list(shape), dt, kind="ExternalInput").ap()

    def dout(name, shape, dt=F32):
        return nc.dram_tensor(name, list(shape), dt, kind="ExternalOutput").ap()

    def dscr(name, shape, dt=F32):
        return nc.dram_tensor(name, list(shape), dt, kind="Internal").ap()

    I = {}
    for name, shape in input_shapes(SEQ, PAST).items():
        I[name] = din(name, shape, BF16 if name in BF16_CONSTS else F32)
    O = {}
    for name, shape in output_shapes(SEQ).items():
        O[name] = dout(name, shape)
    g.I, g.O = I, O
    g.X = dscr("X", [R, D])
    g.UT = dscr("UT", [4, 128, R], BF16)
    g.QT = dscr("QT", [8, 96, R], BF16)
    g.KTp = dscr("KTp", [8, 96, T], BF16)
    g.KTs = dscr("KTs", [8, 96, PAST + DSEQ], BF16)
    g.VAp = dscr("VAp", [T, 8 * 128], BF16)
    g.VAs = dscr("VAs", [PAST + DSEQ, 8 * 128], BF16)
    g.MIXT = dscr("MIXT", [8, 128, R], BF16)
    g.FQT = dscr("FQT", [4, 128, R], BF16)
    g.FKTp = dscr("FKTp", [4, 128, T], BF16)
    g.FKTs = dscr("FKTs", [4, 128, PAST + DSEQ], BF16)
    g.FTp = dscr("FTp", [8, T], F32)
    g.FTs = dscr("FTs", [8, PAST + DSEQ], F32)
    g.FT3p = dscr("FT3p", [8, 3, T], BF16)
    g.FT3s = dscr("FT3s", [8, 3, PAST + DSEQ], BF16)
    g.RK = dscr("RK", [R, 256], BF16)
    g.RQKT = dscr("RQKT", [4, 128, R], BF16)
    g.RV = dscr("RV", [R, 512], BF16)
    g.RG = dscr("RG", [R, 512], F32)
    g.scrb = Buf()
    g.outb = Buf()
    g.Xb = [Buf() for _ in range((R + 127) // 128)]
    with ExitStack() as es:
        s = Sched(nc, es)
        g.s = s
        g.ps = [es.enter_context(nc.psum_tensor("ps%d" % i, [128, 512], F32)) for i in range(6)]
        g.psb = [Buf(True) for _ in range(6)]
        g.pst = [es.enter_context(nc.psum_tensor("pst%d" % i, [128, 1024], BF16)) for i in range(2)]
        g.pstb = [Buf(True) for _ in range(2)]
        g.psi = 0
        g.psti = 0
        g.ident = es.enter_context(nc.sbuf_tensor("ident", [128, 128], BF16))
        g.identb = Buf()
        s.dma("sp", g.ident[:], I["ident_bf"], writes=[g.identb])
        s.dma("sp", g.X[0:NMETA, :], I["meta"], writes=xbufs(g, 0, NMETA))
        s.dma("sp", g.X[NMETA:T, :], I["xp"], writes=xbufs(g, NMETA, SEQ))
        s.dma("sp", g.X[T:R, :], I["xs"], writes=xbufs(g, T, DSEQ))
        done = False
        for l in range(DEPTH):
            ffn_stage(g, l, 0)
            if stop_after == ("ffn", l, 0):
                break
            if l % 2 == 0:
                even_stage(g, l // 2)
            else:
                odd_stage(g, l // 2)
            if stop_after == ("mix", l):
                break
            ffn_stage(g, l, 1)
            if stop_after == ("ffn", l, 1):
                break
        s.dma("sp", O["y_p"], g.X[NMETA:T, :], reads=xbufs(g, NMETA, SEQ))
        s.dma("sp", O["y_s"], g.X[T:R, :], reads=xbufs(g, T, DSEQ))
        s.finish()
    return nc


def staged(fn):
    def w(g, *a, **k):
        r = fn(g, *a, **k)
        g.s.barrier()
        return r
    return w


def xbufs(g, r0, n):
    return g.Xb[r0 // 128:(r0 + n - 1) // 128 + 1]


def next_ps(g):
    i = g.psi % len(g.ps)
    g.psi += 1
    return g.ps[i], g.psb[i]


def next_pst(g):
    i = g.psti % len(g.pst)
    g.psti += 1
    return g.pst[i], g.pstb[i]


BF16_CONSTS = ("ident_bf", "mask_mla", "mask_fox")


def input_shapes(SEQ, PAST):
    T = NMETA + SEQ
    R = T + DSEQ
    return {
        "xp": (SEQ, D), "xs": (DSEQ, D), "meta": (NMETA, D),
        "c_ckv": (2, PAST, 128), "c_kpe": (2, PAST, 32), "c_fk": (2, PAST, 512), "c_fv": (2, PAST, 512),
        "c_flf": (2, PAST, 8), "st_re": (2, 32, 64), "st_im": (2, 32, 64), "st_ret": (2, 4, 64, 128),
        "ln_g": (4, 3, D), "ln_b": (4, 3, D),
        "wg": (4, 2, D, DFF), "wu": (4, 2, D, DFF), "wd": (4, 2, DFF, D),
        "ewin": (2, D, 928), "ewout": (2, D, D),
        "s5_a_re": (2, 32, 64), "s5_a_im": (2, 32, 64), "s5_b_re": (2, 32, 64, 16), "s5_b_im": (2, 32, 64, 16),
        "s5_c_re": (2, 32, 16, 64), "s5_c_im": (2, 32, 16, 64), "s5_d": (2, 512), "s5_log_dt": (2, 32),
        "s5_w_glu": (2, 512, 512), "s5_b_glu": (2, 512),
        "mla_q_norm": (2, 256), "mla_kv_norm": (2, 128), "mla_w_uq": (2, 256, 768), "mla_w_ukv": (2, 128, 1024),
        "owin": (2, D, 3080), "owout": (2, D, D), "fox_b_f": (2, 8),
        "ident_bf": (128, 128), "ident_f": (128, 128), "mask_mla": (5, 128, 512), "mask_fox": (4, 128, 512),
        "ropeA16": (R, 4, 32), "ropeB16": (R, 4, 32), "tau": (128, 512),
        "ropeA32": (R, 8, 64), "ropeB32": (R, 8, 64), "ret_decT": (4, 128, 128), "ret_gq": (4, 128, 128), "ret_ginv": (128, 4),
    }


def output_shapes(SEQ):
    T = NMETA + SEQ
    return {
        "y_p": (SEQ, D), "y_s": (DSEQ, D),
        "p_ckv": (2, T, 128), "p_kpe": (2, T, 32), "p_fk": (2, T, 512), "p_fv": (2, T, 512), "p_flf": (2, T, 8),
        "p_s5re": (2, 32, 64), "p_s5im": (2, 32, 64), "p_ret": (2, 4, 64, 128),
        "s_ckv": (2, DSEQ, 128), "s_kpe": (2, DSEQ, 32), "s_fk": (2, DSEQ, 512), "s_fv": (2, DSEQ, 512),
        "s_flf": (2, DSEQ, 8), "s_s5re": (2, 32, 64), "s_s5im": (2, 32, 64), "s_ret": (2, 4, 64, 128),
    }


def x_load_tile(g, blk, ti, xf, xfb):
    if blk is None:
        return
    r0, nb = blk
    tl = chunks(nb, 128)
    if ti >= len(tl):
        return
    o, n = tl[ti]
    g.s.dma("sp", xf[0:n, ti, :], g.X[r0 + o:r0 + o + n, :], reads=xbufs(g, r0 + o, n), writes=[xfb[ti]])


def load_xT(g, es_tiles, r0, nb, xf, xfb, xb, xbb, xT, xTb, preloaded=False):
    s = g.s
    for ti, (o, n) in enumerate(chunks(nb, 128)):
        if not preloaded:
            s.dma("sp", xf[0:n, ti, :], g.X[r0 + o:r0 + o + n, :], reads=xbufs(g, r0 + o, n), writes=[xfb[ti]])
        xi = ti % 2
        s.op("act", lambda e, xi=xi, n=n, ti=ti: e.copy(out=xb[0:n, xi, :], in_=xf[0:n, ti, :]), reads=[xfb[ti]], writes=[xbb[xi]])
        for half in range(2):
            pt, ptb = next_pst(g)

            def tr(e, half=half, pt=pt, n=n, ti=xi):
                last = None
                for j in range(4):
                    kt = half * 4 + j
                    last = e.transpose(out=pt[:, j * 128:j * 128 + n], in_=xb[0:n, ti, kt * 128:(kt + 1) * 128],
                                       identity=g.ident[0:n, 0:n])
                return last
            s.op("pe", tr, reads=[xbb[xi], g.identb], writes=[ptb])
            s.op("dve", lambda e, half=half, pt=pt, n=n, o=o: e.tensor_copy(
                out=xT[:, half * 4:half * 4 + 4, o:o + n],
                in_=pt[:, 0:512].rearrange("p (j c) -> p j c", j=4)[:, :, 0:n]),
                reads=[ptb], writes=[xTb])


def layer_norm_rows(g, y, yb, n, gt, bt, eps, out, outb, tmp):
    s = g.s
    st, mv, rs, nm = tmp["st"], tmp["mv"], tmp["rs"], tmp["nm"]
    sb = tmp["b"]

    s.op("dve", lambda e: e.bn_stats(out=st[0:n, 0, :], in_=y[0:n, 0:512]), reads=[yb], writes=[tmp["b0"]])
    s.op("dve", lambda e: e.bn_stats(out=st[0:n, 1, :], in_=y[0:n, 512:1024]), reads=[yb], writes=[tmp["b1"]])
    s.op("dve", lambda e: e.bn_aggr(out=mv[0:n, :], in_=st[0:n, :, :].rearrange("p a b -> p (a b)")),
         reads=[tmp["b0"], tmp["b1"]], writes=[sb])
    s.op("dve", lambda e: e.tensor_scalar(out=rs[0:n, :], in0=mv[0:n, 1:2], scalar1=eps, scalar2=None,
                                          op0=ALU.add), reads=[sb], writes=[sb])
    s.op("act", lambda e: e.sqrt(out=rs[0:n, :], in_=rs[0:n, :]), reads=[sb], writes=[sb])
    s.op("dve", lambda e: e.reciprocal(out=rs[0:n, :], in_=rs[0:n, :]), reads=[sb], writes=[sb])
    s.op("dve", lambda e: e.scalar_tensor_tensor(out=nm[0:n, :], in0=mv[0:n, 0:1], scalar=-1.0, in1=rs[0:n, :],
                                                 op0=ALU.mult, op1=ALU.mult), reads=[sb], writes=[sb])
    s.op("act", lambda e: e.activation(out=y[0:n, :], in_=y[0:n, :], func=AF.Identity, bias=nm[0:n, 0:1],
                                       scale=rs[0:n, 0:1]), reads=[sb, yb], writes=[yb])
    s.op("pool", lambda e: e.tensor_tensor(out=y[0:n, :], in0=y[0:n, :], in1=gt[0:n, :], op=ALU.mult),
         reads=[yb, tmp["gb"]], writes=[yb])
    s.op("pool", lambda e: e.tensor_tensor(out=out[0:n, :], in0=y[0:n, :], in1=bt[0:n, :], op=ALU.add),
         reads=[yb, tmp["gb"]], writes=[outb])


def alloc_ln(g, es, l, i, tag):
    nc, s = g.nc, g.s
    t = {}
    t["g"] = es.enter_context(nc.sbuf_tensor("lng" + tag, [128, D], F32))
    t["bt"] = es.enter_context(nc.sbuf_tensor("lnb" + tag, [128, D], F32))
    t["st"] = es.enter_context(nc.sbuf_tensor("lnst" + tag, [128, 2, 6], F32))
    t["mv"] = es.enter_context(nc.sbuf_tensor("lnmv" + tag, [128, 2], F32))
    t["rs"] = es.enter_context(nc.sbuf_tensor("lnrs" + tag, [128, 1], F32))
    t["nm"] = es.enter_context(nc.sbuf_tensor("lnnm" + tag, [128, 1], F32))
    t["b"] = Buf()
    t["b0"] = Buf()
    t["b1"] = Buf()
    t["gb"] = Buf()
    s.dma("sp", t["g"][:], g.I["ln_g"][l, i:i + 1, :].broadcast_to([128, D]), writes=[t["gb"]])
    s.dma("sp", t["bt"][:], g.I["ln_b"][l, i:i + 1, :].broadcast_to([128, D]), writes=[t["gb"]])
    return t


@staged
def ffn_stage(g, l, i):
    nc, s, R = g.nc, g.s, g.R
    NB = 512
    with ExitStack() as es:
        tag = "f%d%d" % (l, i)
        wg = es.enter_context(nc.sbuf_tensor("wg" + tag, [128, 8, DFF], BF16))
        wu = es.enter_context(nc.sbuf_tensor("wu" + tag, [128, 8, DFF], BF16))
        wd = es.enter_context(nc.sbuf_tensor("wd" + tag, [128, NFT, D], BF16))
        wgb, wub, wdb = Buf(), Buf(), Buf()
        wgs = g.I["wg"][l, i].rearrange("(kt p) f -> p kt f", p=128)
        wus = g.I["wu"][l, i].rearrange("(kt p) f -> p kt f", p=128)
        wds = g.I["wd"][l, i].rearrange("(ft p) d -> p ft d", p=128)
        for kt in range(8):
            s.dma("pool", wg[:, kt, :], wgs[:, kt, :], writes=[wgb])
            s.dma("pool", wu[:, kt, :], wus[:, kt, :], writes=[wub])
        for ft in range(0, NFT, 2):
            s.dma("pool", wd[:, ft:ft + 2, :], wds[:, ft:ft + 2, :], writes=[wdb])
        ln = alloc_ln(g, es, l, 0 if i == 0 else 2, tag)
        xf = es.enter_context(nc.sbuf_tensor("xf" + tag, [128, 4, D], F32))
        xb = es.enter_context(nc.sbuf_tensor("xb" + tag, [128, 2, D], BF16))
        xT = es.enter_context(nc.sbuf_tensor("xT" + tag, [128, 8, NB], BF16))
        hT = es.enter_context(nc.sbuf_tensor("hT" + tag, [128, NFT, NB], BF16))
        sg = [es.enter_context(nc.sbuf_tensor("sg%d" % k + tag, [128, NB], F32)) for k in range(2)]
        y = [es.enter_context(nc.sbuf_tensor("y%d" % k + tag, [128, D], F32)) for k in range(2)]
        xfb = [Buf() for _ in range(4)]
        xbb = [Buf() for _ in range(4)]
        xTb, hTb = Buf(), Buf()
        sgb = [Buf(), Buf()]
        yb = [Buf(), Buf()]
        yi = 0
        blks = chunks(R, NB)
        for ti in range(4):
            x_load_tile(g, blks[0], ti, xf, xfb)
        for bi, (r0, nb) in enumerate(blks):
            nxt = blks[bi + 1] if bi + 1 < len(blks) else None
            load_xT(g, es, r0, nb, xf, xfb, xb, xbb, xT, xTb, preloaded=True)
            for ft in range(NFT):
                pg, pgb = next_ps(g)
                pu, pub = next_ps(g)

                def mmg(e, w=wg, p=pg, ft=ft, nb=nb):
                    last = None
                    for kt in range(8):
                        last = e.matmul(p[:, 0:nb], lhsT=w[:, kt, ft * 128:(ft + 1) * 128], rhs=xT[:, kt, 0:nb],
                                        start=(kt == 0), stop=(kt == 7))
                    return last
                s.op("pe", mmg, reads=[wgb, xTb], writes=[pgb])
                s.op("pe", lambda e, ft=ft, nb=nb, pu=pu: mmg(e, wu, pu, ft, nb), reads=[wub, xTb], writes=[pub])
                k = ft % 2
                s.op("act", lambda e, k=k, pg=pg, nb=nb: e.activation(out=sg[k][:, 0:nb], in_=pg[:, 0:nb], func=AF.Silu),
                     reads=[pgb], writes=[sgb[k]])
                s.op("dve", lambda e, k=k, pu=pu, nb=nb, ft=ft: e.tensor_tensor(out=hT[:, ft, 0:nb], in0=sg[k][:, 0:nb],
                                                                                 in1=pu[:, 0:nb], op=ALU.mult),
                     reads=[sgb[k], pub], writes=[hTb])
            for ti, (o, n) in enumerate(chunks(nb, 128)):
                yy, yyb = y[yi % 2], yb[yi % 2]
                yi += 1
                for half in range(2):
                    po, pob = next_ps(g)

                    def mmd(e, po=po, o=o, n=n, half=half):
                        last = None
                        for ft in range(NFT):
                            last = e.matmul(po[0:n, :], lhsT=hT[:, ft, o:o + n], rhs=wd[:, ft, half * 512:(half + 1) * 512],
                                            start=(ft == 0), stop=(ft == NFT - 1))
                        return last
                    s.op("pe", mmd, reads=[hTb, wdb], writes=[pob])
                    s.op("dve", lambda e, po=po, n=n, half=half, ti=ti, yy=yy: e.scalar_tensor_tensor(
                        out=yy[0:n, half * 512:(half + 1) * 512], in0=xf[0:n, ti, half * 512:(half + 1) * 512],
                        scalar=2.0 * ALPHA, in1=po[0:n, :], op0=ALU.mult, op1=ALU.add),
                        reads=[pob, xfb[ti]], writes=[yyb])
                x_load_tile(g, nxt, ti, xf, xfb)
                layer_norm_rows(g, yy, yyb, n, ln["g"], ln["bt"], 4.0 * EPS, yy, yyb, ln)
                s.dma("sp", g.X[r0 + o:r0 + o + n, :], yy[0:n, :], reads=[yyb], writes=xbufs(g, r0 + o, n))
            for ti in range(len(chunks(nb, 128)), 4):
                x_load_tile(g, nxt, ti, xf, xfb)


TWO_PI = 2.0 * math.pi
MAGIC = 12582912.0
SCALE_MLA = 96.0 ** -0.5


def T_(g, es, name, shape, dt):
    return es.enter_context(g.nc.sbuf_tensor(name, list(shape), dt))


def range_reduce(s, e1, out, x, tmpb, n=128):
    s.op(e1, lambda e: e.tensor_scalar(out=out, in0=x, scalar1=1.0 / TWO_PI, scalar2=MAGIC, op0=ALU.mult, op1=ALU.add),
         reads=[tmpb], writes=[tmpb])
    s.op(e1, lambda e: e.tensor_scalar(out=out, in0=out, scalar1=MAGIC, scalar2=TWO_PI, op0=ALU.subtract, op1=ALU.mult),
         reads=[tmpb], writes=[tmpb])
    s.op(e1, lambda e: e.tensor_tensor(out=out, in0=x, in1=out, op=ALU.subtract), reads=[tmpb], writes=[tmpb])
    s.op(e1, lambda e: e.tensor_scalar(out=out, in0=out, scalar1=3.1415925, scalar2=-3.1415925, op0=ALU.min, op1=ALU.max),
         reads=[tmpb], writes=[tmpb])


def rms_rows(g, src, n, width, gt, out, tmp, tb, reads, writes):
    s = g.s
    sq, ss = tmp["sq"], tmp["ss"]
    s.op("act", lambda e: e.square(out=sq[0:n, 0:width], in_=src), reads=reads, writes=[tb])
    s.op("dve", lambda e: e.reduce_sum(out=ss[0:n, :], in_=sq[0:n, 0:width], axis=AX.X), reads=[tb], writes=[tb])
    s.op("dve", lambda e: e.tensor_scalar(out=ss[0:n, :], in0=ss[0:n, :], scalar1=1.0 / width, scalar2=EPS,
                                          op0=ALU.mult, op1=ALU.add), reads=[tb], writes=[tb])
    s.op("act", lambda e: e.sqrt(out=ss[0:n, :], in_=ss[0:n, :]), reads=[tb], writes=[tb])
    s.op("dve", lambda e: e.reciprocal(out=ss[0:n, :], in_=ss[0:n, :]), reads=[tb], writes=[tb])
    s.op("dve", lambda e: e.scalar_tensor_tensor(out=out, in0=src, scalar=ss[0:n, 0:1], in1=gt[0:n, 0:width],
                                                 op0=ALU.mult, op1=ALU.mult), reads=list(reads) + [tb], writes=writes)


def rope_rows(g, src, n, H, half, ra, rb, out, tmpA, tmpB, tb, reads, writes):
    s = g.s
    s.op("dve", lambda e: e.tensor_tensor(out=tmpA[0:n], in0=src, in1=ra[0:n], op=ALU.mult), reads=reads, writes=[tb])
    s.op("dve", lambda e: e.tensor_tensor(out=tmpB[0:n, :, 0:half], in0=src[:, :, half:2 * half], in1=rb[0:n, :, 0:half],
                                          op=ALU.mult), reads=list(reads) + [tb], writes=[tb])
    s.op("dve", lambda e: e.tensor_tensor(out=tmpB[0:n, :, half:2 * half], in0=src[:, :, 0:half],
                                          in1=rb[0:n, :, half:2 * half], op=ALU.mult), reads=list(reads) + [tb], writes=[tb])
    s.op("pool", lambda e: e.tensor_tensor(out=out, in0=tmpA[0:n], in1=tmpB[0:n], op=ALU.add), reads=[tb], writes=writes)


def split_rows(g, r0, n):
    T = g.T
    out = []
    a, b = r0, min(r0 + n, T)
    if b > a:
        out.append(("p", a - r0, a, b - a))
    a, b = max(r0, T), r0 + n
    if b > a:
        out.append(("s", a - r0, a - T, b - a))
    return out


@staged
def attention(g, name, H, dq, QT, q0, nq, KT, VA, nk, kind, scale, out_base, fox=None):
    nc, s = g.nc, g.s
    with ExitStack() as es:
        nkt = (nk + 127) // 128
        Qh = [T_(g, es, name + "Q%d" % k, [128, nq], BF16) for k in range(2)]
        Kh = [T_(g, es, name + "K%d" % k, [128, nk], BF16) for k in range(2)]
        Vh = [T_(g, es, name + "V%d" % k, [128, nkt, 128], BF16) for k in range(2)]
        Qb, Kb, Vb = [Buf(), Buf()], [Buf(), Buf()], [Buf(), Buf()]
        NPT = 4
        pT = [T_(g, es, name + "pT%d" % k, [128, 512], BF16) for k in range(NPT)]
        pTb = [Buf() for _ in range(NPT)]
        osb = [T_(g, es, name + "o%d" % k, [128, 512], F32) for k in range(2)]
        osbb = [Buf(), Buf()]
        obf = [T_(g, es, name + "ob%d" % k, [64, 512], BF16) for k in range(2)]
        obfb = [Buf(), Buf()]
        ones = T_(g, es, name + "ones", [128, 128], F32)
        ones3 = T_(g, es, name + "ones3", [4, 128], BF16)
        onesb = Buf()
        s.op("pool", lambda e: e.memset(ones[:], 1.0), writes=[onesb])
        s.op("pool", lambda e: e.memset(ones3[:], 1.0), reads=[onesb], writes=[onesb])
        nmask = 5 if kind.startswith("mla") else 4
        msk = T_(g, es, name + "msk", [128, nmask, 512], BF16)
        mskb = Buf()
        s.dma("sp", msk[:], g.I["mask_mla" if kind.startswith("mla") else "mask_fox"].rearrange("m p c -> p m c"), writes=[mskb])
        if fox is not None:
            fq = [T_(g, es, name + "fq%d" % k, [4, nq], BF16) for k in range(2)]
            nfk = [T_(g, es, name + "nfk%d" % k, [128, nkt], F32) for k in range(2)]
            fqb, nfkb = [Buf(), Buf()], [Buf(), Buf()]
            for k in range(2):
                s.op("pool", lambda e, k=k: e.memset(nfk[k][:], 0.0), writes=[nfkb[k]])
        for k in range(2):
            s.op("pool", lambda e, k=k: e.memset(Qh[k][64:128, :], 0.0), writes=[Qb[k]])
            s.op("pool", lambda e, k=k: e.memset(Kh[k][64:128, :], 1.0 if fox is not None else 0.0), writes=[Kb[k]])
        si = 0
        oi = 0
        deferred = []
        nfull = nk // 128

        def head_loads(h):
            if h >= H:
                return
            hb = h % 2
            Q_, K_, V_ = Qh[hb], Kh[hb], Vh[hb]
            s.dma("sp", Q_[0:dq, :], QT[h, :, q0:q0 + nq], reads=[g.scrb], writes=[Qb[hb]])
            s.dma("sp", K_[0:dq, :], KT[h, :, 0:nk], reads=[g.scrb], writes=[Kb[hb]])
            if nfull:
                s.dma("sp", V_[:, 0:nfull, :], VA[0:nfull * 128, h * 128:(h + 1) * 128].rearrange("(t p) c -> p t c", p=128),
                      reads=[g.scrb], writes=[Vb[hb]])
            if nk % 128:
                s.dma("sp", V_[0:nk % 128, nfull, :], VA[nfull * 128:nk, h * 128:(h + 1) * 128], reads=[g.scrb], writes=[Vb[hb]])
            if fox is not None:
                nfk_ = nfk[hb]
                s.dma("sp", Q_[64:67, :], fox["FT3"][h, :, fox["qoff"]:fox["qoff"] + nq], reads=[g.scrb], writes=[Qb[hb]])
                for t0 in range(0, nfull, 8):
                    t1 = min(nfull, t0 + 8)
                    s.dma("sp", nfk_[:, t0:t1], fox["FT"][h, t0 * 128:t1 * 128].rearrange("(t p) -> p t", p=128),
                          reads=[g.scrb], writes=[nfkb[hb]], allow_slow_non_contiguous=True)
                if nk % 128:
                    s.dma("sp", nfk_[0:nk % 128, nfull:nfull + 1], fox["FT"][h, nfull * 128:nk].rearrange("(p o) -> p o", o=1),
                          reads=[g.scrb], writes=[nfkb[hb]], allow_slow_non_contiguous=True)
                s.op("dve", lambda e, nfk_=nfk_: e.tensor_scalar(out=nfk_[:, :], in0=nfk_[:, :], scalar1=-1.0, scalar2=None, op0=ALU.mult),
                     reads=[nfkb[hb]], writes=[nfkb[hb]])
        head_loads(0)
        for h in range(H):
            hb = h % 2
            Q_, K_, V_ = Qh[hb], Kh[hb], Vh[hb]
            if fox is not None:
                nfk_ = nfk[hb]
            head_loads(h + 1)
            flat = []
            for (qo, nqb) in chunks(nq, 512):
                j = qo // 512
                vis = []
                for i in range(nkt):
                    kn = min(128, nk - i * 128)
                    if kind in ("mla_p", "fox_p"):
                        d = i - 4 * j
                        if d < 0:
                            vis.append((i, kn, None))
                        elif d < nmask:
                            vis.append((i, kn, d))
                    elif kind == "mla_s":
                        vis.append((i, kn, None))
                    else:
                        vis.append((i, kn, 0 if i * 128 >= fox["qoff"] else None))
                for vi, (i, kn, d) in enumerate(vis):
                    flat.append((qo, nqb, vi, len(vis), i, kn, d))
            base = si

            def issue_S(idx):
                qo, nqb, vi, nv, i, kn, d = flat[idx]
                ps_, psb_ = g.ps[(base + idx) % 4], g.psb[(base + idx) % 4]

                def mm_s(e):
                    last = e.matmul(ps_[0:kn, 0:nqb], lhsT=K_[:, i * 128:i * 128 + kn], rhs=Q_[:, qo:qo + nqb], start=True, stop=(d is None))
                    if d is not None:
                        last = e.matmul(ps_[0:kn, 0:nqb], lhsT=g.ident[0:kn, 0:kn], rhs=msk[0:kn, d, 0:nqb], start=False, stop=True)
                    return last
                rd = [Qb[hb], Kb[hb], mskb, g.identb]
                s.op("pe", mm_s, reads=rd, writes=[psb_])

            LOOK = 3
            for idx in range(min(LOOK, len(flat))):
                issue_S(idx)
            for idx, (qo, nqb, vi, nv, i, kn, d) in enumerate(flat):
                ps_, psb_ = g.ps[(base + idx) % 4], g.psb[(base + idx) % 4]
                pt_, ptb_ = pT[(base + idx) % NPT], pTb[(base + idx) % NPT]
                if vi == 0:
                    po, pob = g.ps[4 + oi % 2], g.psb[4 + oi % 2]
                    o_, ob_ = osb[oi % 2], osbb[oi % 2]
                    f_, fb_ = obf[oi % 2], obfb[oi % 2]
                    oi += 1
                if fox is None:
                    s.op("act", lambda e, pt_=pt_, ps_=ps_, kn=kn, nqb=nqb: e.activation(
                        out=pt_[0:kn, 0:nqb], in_=ps_[0:kn, 0:nqb], func=AF.Exp, scale=scale), reads=[psb_], writes=[ptb_])
                else:
                    s.op("act", lambda e, pt_=pt_, ps_=ps_, kn=kn, nqb=nqb, i=i: e.activation(
                        out=pt_[0:kn, 0:nqb], in_=ps_[0:kn, 0:nqb], func=AF.Exp, scale=1.0, bias=nfk_[0:kn, i:i + 1]),
                        reads=[psb_, nfkb[hb]], writes=[ptb_])
                if idx + LOOK < len(flat):
                    issue_S(idx + LOOK)
                for fn in deferred:
                    fn()
                deferred = []
                s.op("pe", lambda e, po=po, pt_=pt_, i=i, kn=kn, nqb=nqb, vi=vi, nv=nv: e.matmul(
                    po[:, 0:nqb], lhsT=V_[0:kn, i, :], rhs=pt_[0:kn, 0:nqb], start=(vi == 0), stop=(vi == nv - 1)),
                    reads=[Vb[hb], ptb_], writes=[pob])
                if vi == nv - 1:
                    s.op("act", lambda e, o_=o_, po=po, nqb=nqb: e.copy(out=o_[0:65, 0:nqb], in_=po[0:65, 0:nqb]),
                         reads=[pob], writes=[ob_])
                    s.op("dve", lambda e, o_=o_, nqb=nqb: e.reciprocal(out=o_[64:65, 0:nqb], in_=o_[64:65, 0:nqb]),
                         reads=[ob_], writes=[ob_])

                    def fin(po=po, pob=pob, o_=o_, ob_=ob_, f_=f_, fb_=fb_, nqb=nqb, qo=qo, h=h):
                        s.op("pe", lambda e: e.matmul(po[0:64, 0:nqb], lhsT=ones[64:65, 0:64], rhs=o_[64:65, 0:nqb], start=True, stop=True),
                             reads=[ob_, onesb], writes=[pob])
                        s.op("dve", lambda e: e.tensor_tensor(out=f_[0:64, 0:nqb], in0=o_[0:64, 0:nqb], in1=po[0:64, 0:nqb], op=ALU.mult),
                             reads=[ob_, pob], writes=[fb_])
                        fr = out_base + 64 * h
                        s.dma("sp", g.MIXT[fr // 128, fr % 128:fr % 128 + 64, q0 + qo:q0 + qo + nqb], f_[0:64, 0:nqb],
                              reads=[fb_], writes=[g.scrb])
                    deferred.append(fin)
            si = base + len(flat)
        for fn in deferred:
            fn()


@staged
def out_proj_ln(g, tag, wsrc, l, eps):
    nc, s, R = g.nc, g.s, g.R
    with ExitStack() as es:
        wo = T_(g, es, "wo" + tag, [128, 8, D], BF16)
        wob = Buf()
        ws = wsrc.rearrange("(kt p) d -> p kt d", p=128)
        for kt in range(8):
            s.dma("pool", wo[:, kt, :], ws[:, kt, :], writes=[wob])
        ln = alloc_ln(g, es, l, 1, tag)
        NBUF = 4
        mt = [T_(g, es, "mt%d" % k + tag, [128, 8, 128], BF16) for k in range(NBUF)]
        xf = [T_(g, es, "xo%d" % k + tag, [128, D], F32) for k in range(NBUF)]
        y = [T_(g, es, "yo%d" % k + tag, [128, D], F32) for k in range(NBUF)]
        mtb, xfb, yb = [Buf() for _ in range(NBUF)], [Buf() for _ in range(NBUF)], [Buf() for _ in range(NBUF)]
        tiles = chunks(R, 128)

        def loads(ti):
            if ti >= len(tiles):
                return
            r0, n = tiles[ti]
            k = ti % NBUF
            s.dma("sp", mt[k][:, :, 0:n], g.MIXT[:, :, r0:r0 + n].rearrange("kt p c -> p kt c"), reads=[g.scrb], writes=[mtb[k]])
            s.dma("sp", xf[k][0:n, :], g.X[r0:r0 + n, :], reads=xbufs(g, r0, n), writes=[xfb[k]])
        loads(0)
        loads(1)
        for ti, (r0, n) in enumerate(tiles):
            k = ti % NBUF
            loads(ti + 2)
            for half in range(2):
                po, pob = next_ps(g)

                def mm(e, po=po, k=k, n=n, half=half):
                    last = None
                    for kt in range(8):
                        last = e.matmul(po[0:n, :], lhsT=mt[k][:, kt, 0:n], rhs=wo[:, kt, half * 512:(half + 1) * 512],
                                        start=(kt == 0), stop=(kt == 7))
                    return last
                s.op("pe", mm, reads=[mtb[k], wob], writes=[pob])
                s.op("dve", lambda e, po=po, k=k, n=n, half=half: e.scalar_tensor_tensor(
                    out=y[k][0:n, half * 512:(half + 1) * 512], in0=xf[k][0:n, half * 512:(half + 1) * 512], scalar=ALPHA,
                    in1=po[0:n, :], op0=ALU.mult, op1=ALU.add), reads=[pob, xfb[k]], writes=[yb[k]])
            layer_norm_rows(g, y[k], yb[k], n, ln["g"], ln["bt"], eps, y[k], yb[k], ln)
            s.dma("sp", g.X[r0:r0 + n, :], y[k][0:n, :], reads=[yb[k]], writes=xbufs(g, r0, n))


def even_stage(g, e):
    import os
    lim = int(os.environ.get("EV_STOP", "9"))
    even_proj(g, e)
    if lim <= 1:
        return
    s5_stage(g, e)
    if lim <= 2:
        return
    T, R, PAST = g.T, g.R, g.PAST
    attention(g, "ap%d" % e, 8, 96, g.QT, 0, T, g.KTp, g.VAp, T, "mla_p", SCALE_MLA, 512)
    attention(g, "as%d" % e, 8, 96, g.QT, T, DSEQ, g.KTs, g.VAs, PAST + DSEQ, "mla_s", SCALE_MLA, 512)
    out_proj_ln(g, "eo%d" % e, g.I["ewout"][e], 2 * e, EPS)


@staged
def even_proj(g, e):
    nc, s, R, T, PAST = g.nc, g.s, g.R, g.T, g.PAST
    NB = 512
    I, O = g.I, g.O
    with ExitStack() as es:
        tag = "ep%d" % e
        win = T_(g, es, "win" + tag, [128, 8, 928], BF16)
        wuq = T_(g, es, "wuq" + tag, [128, 2, 768], BF16)
        wukv = T_(g, es, "wukv" + tag, [128, 1024], BF16)
        wb = Buf()
        wins = I["ewin"][e].rearrange("(kt p) f -> p kt f", p=128)
        for kt in range(8):
            s.dma("pool", win[:, kt, :], wins[:, kt, :], writes=[wb])
        s.dma("pool", wuq[:], I["mla_w_uq"][e].rearrange("(kt p) f -> p kt f", p=128), writes=[wb])
        s.dma("pool", wukv[:], I["mla_w_ukv"][e], writes=[wb])
        gq = T_(g, es, "gq" + tag, [128, 256], F32)
        gkv = T_(g, es, "gkv" + tag, [128, 128], F32)
        s.dma("sp", gq[:], I["mla_q_norm"][e:e + 1, :].broadcast_to([128, 256]), writes=[wb])
        s.dma("sp", gkv[:], I["mla_kv_norm"][e:e + 1, :].broadcast_to([128, 128]), writes=[wb])
        xf = T_(g, es, "xf" + tag, [128, 4, D], F32)
        xb = T_(g, es, "xb" + tag, [128, 2, D], BF16)
        xT = T_(g, es, "xT" + tag, [128, 8, NB], BF16)
        xfb, xbb, xTb = [Buf() for _ in range(4)], [Buf(), Buf()], Buf()
        uT = T_(g, es, "uT" + tag, [128, 4, NB], BF16)
        uTb = Buf()
        cT = T_(g, es, "cT" + tag, [128, 3, NB], BF16)
        kpT = T_(g, es, "kpT" + tag, [128, NB], BF16)
        cTb = Buf()
        sq = T_(g, es, "sq" + tag, [128, 256], F32)
        ss = T_(g, es, "ss" + tag, [128, 1], F32)
        tmp = {"sq": sq, "ss": ss}
        tb = Buf()
        qn = T_(g, es, "qn" + tag, [128, 256], BF16)
        ckf = T_(g, es, "ckf" + tag, [128, 128], F32)
        ckb = T_(g, es, "ckb" + tag, [128, 128], BF16)
        kpf = T_(g, es, "kpf" + tag, [128, 1, 32], F32)
        kq = T_(g, es, "kq" + tag, [128, 96], BF16)
        rwb = Buf()
        s.op("pool", lambda e_: e_.memset(kq[:], 0.0), writes=[rwb])
        ra4 = T_(g, es, "ra" + tag, [128, 4, 4, 32], F32)
        rbt4 = T_(g, es, "rb" + tag, [128, 4, 4, 32], F32)
        rab4 = [Buf() for _ in range(4)]
        tA = T_(g, es, "tA" + tag, [128, 4, 32], F32)
        tB = T_(g, es, "tB" + tag, [128, 4, 32], F32)
        tAb = Buf()
        qb = T_(g, es, "qb" + tag, [128, 8, 96], BF16)
        qbb = Buf()
        QTs = T_(g, es, "QTs" + tag, [128, 8, NB], BF16)
        KTs_ = T_(g, es, "KTs" + tag, [128, 8, NB], BF16)
        VAs_ = T_(g, es, "VAs" + tag, [128, 4, 8, 128], BF16)
        QTb, KTb, VAb = Buf(), Buf(), Buf()
        s.op("pool", lambda e_: e_.memset(VAs_[:], 1.0), writes=[VAb])
        pcf = T_(g, es, "pcf" + tag, [128, 128], F32)
        pkf = T_(g, es, "pkf" + tag, [128, 32], F32)
        pcb = Buf()

        def kv_project(nb, dests):
            if "nokv" in dbg:
                return
            kv_project_(nb, dests)

        def kv_project_(nb, dests):
            for h in range(8):
                pk, pkb = next_ps(g)
                s.op("pe", lambda e_, pk=pk, h=h: e_.matmul(pk[0:64, 0:nb], lhsT=wukv[:, h * 128:h * 128 + 64], rhs=cT[:, 2, 0:nb],
                                                             start=True, stop=True), reads=[wb, cTb], writes=[pkb])
                s.op("act", lambda e_, pk=pk, h=h: e_.copy(out=KTs_[0:64, h, 0:nb], in_=pk[0:64, 0:nb]), reads=[pkb], writes=[KTb])
                s.op("pool", lambda e_, h=h: e_.tensor_copy(out=KTs_[64:96, h, 0:nb], in_=kpT[64:96, 0:nb]), reads=[cTb], writes=[KTb])
            for ti, (o, n) in enumerate(chunks(nb, 128)):
                pv, pvb = next_ps(g)
                s.op("pe", lambda e_, pv=pv, o=o, n=n: e_.matmul(
                    pv[0:n, :], lhsT=cT[:, 2, o:o + n], rhs=wukv[:, :].rearrange("p (h c) -> p h c", c=128)[:, :, 64:128],
                    start=True, stop=True), reads=[wb, cTb], writes=[pvb])
                s.op("dve", lambda e_, pv=pv, n=n, ti=ti: e_.tensor_copy(
                    out=VAs_[0:n, ti, :, 0:64], in_=pv[0:n, :].rearrange("p (h c) -> p h c", c=64)), reads=[pvb], writes=[VAb])
            for (lo, KTd, VAd, dr, cnt) in dests:
                s.dma("sp", KTd[:, :, dr:dr + cnt].rearrange("h p c -> p h c"), KTs_[0:96, :, lo:lo + cnt], reads=[KTb], writes=[g.scrb])
                a = lo
                while a < lo + cnt:
                    ti = a // 128
                    b = min(lo + cnt, (ti + 1) * 128)
                    s.dma("sp", VAd[dr + a - lo:dr + b - lo, :], VAs_[a - ti * 128:b - ti * 128, ti, :, :].rearrange("p h c -> p (h c)"),
                          reads=[VAb], writes=[g.scrb])
                    a = b

        import os
        dbg = os.environ.get("EP_DBG", "")
        pblks = chunks(PAST if "nopast" not in dbg else 0, NB)
        pcf4 = [T_(g, es, "pcf4_%d" % k + tag, [128, 4, 128], F32) for k in range(2)]
        pkf4 = [T_(g, es, "pkf4_%d" % k + tag, [128, 4, 32], F32) for k in range(2)]
        pcb4 = [[Buf() for _ in range(4)] for _ in range(2)]

        def past_loads(bi):
            if bi >= len(pblks):
                return
            k0_, nb_ = pblks[bi]
            for ti_, (o_, n_) in enumerate(chunks(nb_, 128)):
                s.dma("sp", pcf4[bi % 2][0:n_, ti_, :], I["c_ckv"][e, k0_ + o_:k0_ + o_ + n_, :], writes=[pcb4[bi % 2][ti_]])
                s.dma("sp", pkf4[bi % 2][0:n_, ti_, :], I["c_kpe"][e, k0_ + o_:k0_ + o_ + n_, :], writes=[pcb4[bi % 2][ti_]])
        past_loads(0)
        for bi, (k0, nb) in enumerate(pblks):
            past_loads(bi + 1)
            for ti, (o, n) in enumerate(chunks(nb, 128)):
                pcf, pkf, pcb = pcf4[bi % 2][:, ti, :], pkf4[bi % 2][:, ti, :], pcb4[bi % 2][ti]
                s.op("act", lambda e_, n=n, pcf=pcf: e_.copy(out=ckb[0:n, :], in_=pcf[0:n, :]), reads=[pcb], writes=[rwb])
                s.op("act", lambda e_, n=n, pkf=pkf: e_.copy(out=kq[0:n, 64:96], in_=pkf[0:n, :]), reads=[pcb], writes=[rwb])
                pt, ptb = next_pst(g)

                def tr(e_, pt=pt, n=n):
                    e_.transpose(out=pt[:, 0:n], in_=ckb[0:n, :], identity=g.ident[0:n, 0:n])
                    return e_.transpose(out=pt[0:96, 128:128 + n], in_=kq[0:n, :], identity=g.ident[0:n, 0:n])
                s.op("pe", tr, reads=[rwb, g.identb], writes=[ptb])
                s.op("dve", lambda e_, pt=pt, o=o, n=n: e_.tensor_copy(out=cT[:, 2, o:o + n], in_=pt[:, 0:n]), reads=[ptb], writes=[cTb])
                s.op("dve", lambda e_, pt=pt, o=o, n=n: e_.tensor_copy(out=kpT[64:96, o:o + n], in_=pt[64:96, 128:128 + n]),
                     reads=[ptb], writes=[cTb])
            kv_project(nb, [(0, g.KTs, g.VAs, k0, nb)])

        ublks = chunks(R, NB)
        for ti in range(4):
            x_load_tile(g, ublks[0], ti, xf, xfb)
        for bi, (r0, nb) in enumerate(ublks):
            load_xT(g, es, r0, nb, xf, xfb, xb, xbb, xT, xTb, preloaded=True)
            for ti in range(4):
                x_load_tile(g, ublks[bi + 1] if bi + 1 < len(ublks) else None, ti, xf, xfb)
            for ti, (o, n) in enumerate(chunks(nb, 128)):
                s.dma("sp", ra4[0:n, ti], I["ropeA16"][r0 + o:r0 + o + n, :, :], writes=[rab4[ti]])
                s.dma("sp", rbt4[0:n, ti], I["ropeB16"][r0 + o:r0 + o + n, :, :], writes=[rab4[ti]])
            for ft in range(4):
                pu, pub = next_ps(g)

                def mmu(e_, pu=pu, ft=ft):
                    last = None
                    for kt in range(8):
                        last = e_.matmul(pu[:, 0:nb], lhsT=win[:, kt, ft * 128:(ft + 1) * 128], rhs=xT[:, kt, 0:nb],
                                         start=(kt == 0), stop=(kt == 7))
                    return last
                s.op("pe", mmu, reads=[wb, xTb], writes=[pub])
                s.op("act", lambda e_, pu=pu, ft=ft: e_.copy(out=uT[:, ft, 0:nb], in_=pu[:, 0:nb]), reads=[pub], writes=[uTb])
            s.dma("sp", g.UT[:, :, r0:r0 + nb].rearrange("t p c -> p t c"), uT[:, :, 0:nb], reads=[uTb], writes=[g.scrb])
            def ph1a(ti, o, n):
                rr = r0 + o
                pm, pmb = next_ps(g)

                def mmt(e_, pm=pm, o=o, n=n):
                    last = None
                    for kt in range(8):
                        last = e_.matmul(pm[0:n, 0:416], lhsT=xT[:, kt, o:o + n], rhs=win[:, kt, 512:928],
                                         start=(kt == 0), stop=(kt == 7))
                    return last
                s.op("pe", mmt, reads=[wb, xTb], writes=[pmb])
                ra, rbt, rab = ra4[:, ti], rbt4[:, ti], rab4[ti]
                rms_rows(g, pm[0:n, 0:256], n, 256, gq, qn[0:n, :], tmp, tb, [pmb, wb], [rwb])
                rms_rows(g, pm[0:n, 256:384], n, 128, gkv, ckf[0:n, :], tmp, tb, [pmb, wb], [rwb])
                s.op("act", lambda e_, n=n: e_.copy(out=ckb[0:n, :], in_=ckf[0:n, :]), reads=[rwb], writes=[rwb])
                rope_rows(g, pm[0:n, 384:416].rearrange("p (h c) -> p h c", h=1), n, 1, 16, ra[:, 0:1, :], rbt[:, 0:1, :],
                          kpf[0:n], tA[:, 0:1, :], tB[:, 0:1, :], tAb, [pmb, rab], [rwb])
                s.op("act", lambda e_, n=n: e_.copy(out=kq[0:n, 64:96], in_=kpf[0:n, 0, :]), reads=[rwb], writes=[rwb])
                for (kd, lo, dr, cnt) in split_rows(g, rr, n):
                    pre = "p_" if kd == "p" else "s_"
                    s.dma("sp", O[pre + "ckv"][e, dr:dr + cnt, :], ckf[lo:lo + cnt, :], reads=[rwb], writes=[g.outb])
                    s.dma("sp", O[pre + "kpe"][e, dr:dr + cnt, :], kpf[lo:lo + cnt, 0, :], reads=[rwb], writes=[g.outb])
            def ph1b(ti, o, n):
                rr = r0 + o
                pt, ptb = next_pst(g)

                def tr2(e_, pt=pt, n=n):
                    e_.transpose(out=pt[:, 0:n], in_=qn[0:n, 0:128], identity=g.ident[0:n, 0:n])
                    e_.transpose(out=pt[:, 128:128 + n], in_=qn[0:n, 128:256], identity=g.ident[0:n, 0:n])
                    e_.transpose(out=pt[:, 256:256 + n], in_=ckb[0:n, :], identity=g.ident[0:n, 0:n])
                    return e_.transpose(out=pt[0:96, 384:384 + n], in_=kq[0:n, :], identity=g.ident[0:n, 0:n])
                s.op("pe", tr2, reads=[rwb, g.identb], writes=[ptb])
                s.op("dve", lambda e_, pt=pt, o=o, n=n: e_.tensor_copy(
                    out=cT[:, :, o:o + n], in_=pt[:, 0:384].rearrange("p (j c) -> p j c", j=3)[:, :, 0:n]), reads=[ptb], writes=[cTb])
                s.op("dve", lambda e_, pt=pt, o=o, n=n: e_.tensor_copy(out=kpT[64:96, o:o + n], in_=pt[64:96, 384:384 + n]),
                     reads=[ptb], writes=[cTb])
            def ph2(ti, o, n):
                rr = r0 + o
                ra, rbt, rab = ra4[:, ti], rbt4[:, ti], rab4[ti]
                for hf in range(2):
                    pq, pqb = next_ps(g)

                    def mmq(e_, pq=pq, o=o, n=n, hf=hf):
                        e_.matmul(pq[0:n, 0:384], lhsT=cT[:, 0, o:o + n], rhs=wuq[:, 0, hf * 384:(hf + 1) * 384], start=True, stop=False)
                        return e_.matmul(pq[0:n, 0:384], lhsT=cT[:, 1, o:o + n], rhs=wuq[:, 1, hf * 384:(hf + 1) * 384],
                                         start=False, stop=True)
                    s.op("pe", mmq, reads=[cTb, wb], writes=[pqb])
                    pq3 = pq[0:n, 0:384].rearrange("p (h c) -> p h c", c=96)
                    s.op("act", lambda e_, pq3=pq3, n=n, hf=hf: e_.copy(out=qb[0:n, hf * 4:hf * 4 + 4, 0:64], in_=pq3[:, :, 0:64]),
                         reads=[pqb], writes=[qbb])
                    rope_rows(g, pq3[:, :, 64:96], n, 4, 16, ra, rbt, qb[0:n, hf * 4:hf * 4 + 4, 64:96], tA, tB, tAb,
                              [pqb, rab], [qbb])
                pt, ptb = next_pst(g)

                def tr3(e_, pt=pt, n=n):
                    last = None
                    for h in range(8):
                        last = e_.transpose(out=pt[0:96, h * 128:h * 128 + n], in_=qb[0:n, h, :], identity=g.ident[0:n, 0:n])
                    return last
                s.op("pe", tr3, reads=[qbb, g.identb], writes=[ptb])
                s.op("dve", lambda e_, pt=pt, o=o, n=n: e_.tensor_copy(
                    out=QTs[0:96, :, o:o + n], in_=pt[0:96, :].rearrange("p (h c) -> p h c", h=8)[:, :, 0:n]), reads=[ptb], writes=[QTb])
            tl = chunks(nb, 128)
            ph1a(0, *tl[0])
            ph1b(0, *tl[0])
            for idx in range(len(tl)):
                if idx + 1 < len(tl):
                    ph1a(idx + 1, *tl[idx + 1])
                ph2(idx, *tl[idx])
                if idx + 1 < len(tl):
                    ph1b(idx + 1, *tl[idx + 1])
            s.dma("sp", g.QT[:, :, r0:r0 + nb].rearrange("h p c -> p h c"), QTs[0:96, :, 0:nb], reads=[QTb], writes=[g.scrb])
            dests = []
            for (kd, lo, dr, cnt) in split_rows(g, r0, nb):
                if kd == "p":
                    dests.append((lo, g.KTp, g.VAp, dr, cnt))
                else:
                    dests.append((lo, g.KTs, g.VAs, PAST + dr, cnt))
            kv_project(nb, dests)


@staged
def s5_stage(g, e):
    nc, s, R, T = g.nc, g.s, g.R, g.T
    I, O = g.I, g.O
    TC = 512
    with ExitStack() as es:
        tag = "s5%d" % e
        sm = lambda nm, w=16: T_(g, es, nm + tag, [128, w], F32)
        are, aim, ldt = sm("are"), sm("aim"), sm("ldt")
        ard, aid, mag, cs, sn, t0, t1 = sm("ard"), sm("aid"), sm("mag"), sm("cs"), sm("sn"), sm("t0"), sm("t1")
        abr, abi, nr, ni, den, fr, fi = sm("abr"), sm("abi"), sm("nr"), sm("ni"), sm("den"), sm("fr"), sm("fi")
        pb = Buf()
        vw = lambda ap: ap.rearrange("(pr g2) n -> (g2 n) pr", g2=2)
        for hp in range(2):
            s.dma("sp", are[:, hp * 8:hp * 8 + 8], vw(I["s5_a_re"][e])[:, hp * 8:hp * 8 + 8], writes=[pb], allow_slow_non_contiguous=True)
            s.dma("sp", aim[:, hp * 8:hp * 8 + 8], vw(I["s5_a_im"][e])[:, hp * 8:hp * 8 + 8], writes=[pb], allow_slow_non_contiguous=True)
        for g2 in range(2):
            s.dma("sp", ldt[g2 * 64:(g2 + 1) * 64, :],
                  I["s5_log_dt"][e:e + 1, :].rearrange("o (pr g2) -> o pr g2", g2=2)[:, :, g2].broadcast_to([64, 16]),
                  writes=[pb], allow_slow_non_contiguous=True)
        P = lambda eng, fn: s.op(eng, fn, reads=[pb], writes=[pb])
        P("act", lambda e_: e_.activation(out=ldt[:], in_=ldt[:], func=AF.Exp))
        P("dve", lambda e_: e_.tensor_tensor(out=ard[:], in0=are[:], in1=ldt[:], op=ALU.mult))
        P("dve", lambda e_: e_.tensor_tensor(out=aid[:], in0=aim[:], in1=ldt[:], op=ALU.mult))
        P("act", lambda e_: e_.activation(out=mag[:], in_=ard[:], func=AF.Exp))
        range_reduce(s, "dve", t0[:], aid[:], pb)
        P("act", lambda e_: e_.activation(out=sn[:], in_=t0[:], func=AF.Sin))
        P("dve", lambda e_: e_.tensor_scalar(out=t1[:], in0=aid[:], scalar1=math.pi / 2, scalar2=None, op0=ALU.add))
        range_reduce(s, "dve", t0[:], t1[:], pb)
        P("act", lambda e_: e_.activation(out=cs[:], in_=t0[:], func=AF.Sin))
        P("dve", lambda e_: e_.tensor_tensor(out=abr[:], in0=mag[:], in1=cs[:], op=ALU.mult))
        P("dve", lambda e_: e_.tensor_tensor(out=abi[:], in0=mag[:], in1=sn[:], op=ALU.mult))
        P("dve", lambda e_: e_.tensor_scalar(out=t0[:], in0=abr[:], scalar1=-1.0, scalar2=None, op0=ALU.add))
        P("dve", lambda e_: e_.tensor_tensor(out=nr[:], in0=t0[:], in1=are[:], op=ALU.mult))
        P("dve", lambda e_: e_.tensor_tensor(out=t1[:], in0=abi[:], in1=aim[:], op=ALU.mult))
        P("dve", lambda e_: e_.tensor_tensor(out=nr[:], in0=nr[:], in1=t1[:], op=ALU.add))
        P("dve", lambda e_: e_.tensor_tensor(out=ni[:], in0=abi[:], in1=are[:], op=ALU.mult))
        P("dve", lambda e_: e_.tensor_tensor(out=t1[:], in0=t0[:], in1=aim[:], op=ALU.mult))
        P("dve", lambda e_: e_.tensor_tensor(out=ni[:], in0=ni[:], in1=t1[:], op=ALU.subtract))
        P("dve", lambda e_: e_.tensor_tensor(out=den[:], in0=are[:], in1=are[:], op=ALU.mult))
        P("dve", lambda e_: e_.tensor_tensor(out=t1[:], in0=aim[:], in1=aim[:], op=ALU.mult))
        P("dve", lambda e_: e_.tensor_tensor(out=den[:], in0=den[:], in1=t1[:], op=ALU.add))
        P("dve", lambda e_: e_.reciprocal(out=den[:], in_=den[:]))
        P("dve", lambda e_: e_.tensor_tensor(out=fr[:], in0=nr[:], in1=den[:], op=ALU.mult))
        P("dve", lambda e_: e_.tensor_tensor(out=fi[:], in0=ni[:], in1=den[:], op=ALU.mult))
        Br = T_(g, es, "Br" + tag, [128, 16, 16], F32)
        Bi = T_(g, es, "Bi" + tag, [128, 16, 16], F32)
        Bbr = T_(g, es, "Bbr" + tag, [128, 16, 16], F32)
        Bbi = T_(g, es, "Bbi" + tag, [128, 16, 16], F32)
        Bt = T_(g, es, "Bt" + tag, [128, 16, 16], F32)
        vb = lambda ap: ap.rearrange("(pr g2) n c -> (g2 n) pr c", g2=2)
        s.dma("sp", Br[:], vb(I["s5_b_re"][e]), writes=[pb])
        s.dma("sp", Bi[:], vb(I["s5_b_im"][e]), writes=[pb])
        frb = fr[:, :].unsqueeze(2).to_broadcast([128, 16, 16])
        fib = fi[:, :].unsqueeze(2).to_broadcast([128, 16, 16])
        P("dve", lambda e_: e_.tensor_tensor(out=Bbr[:], in0=Br[:], in1=frb, op=ALU.mult))
        P("dve", lambda e_: e_.tensor_tensor(out=Bt[:], in0=Bi[:], in1=fib, op=ALU.mult))
        P("dve", lambda e_: e_.tensor_tensor(out=Bbr[:], in0=Bbr[:], in1=Bt[:], op=ALU.subtract))
        P("dve", lambda e_: e_.tensor_tensor(out=Bbi[:], in0=Bi[:], in1=frb, op=ALU.mult))
        P("dve", lambda e_: e_.tensor_tensor(out=Bt[:], in0=Br[:], in1=fib, op=ALU.mult))
        P("dve", lambda e_: e_.tensor_tensor(out=Bbi[:], in0=Bbi[:], in1=Bt[:], op=ALU.add))
        BP = [T_(g, es, "BP%d" % k + tag, [128, 16, 128], F32) for k in range(2)]
        LB = [T_(g, es, "LB%d" % k + tag, [128, 16, 128], BF16) for k in range(2)]
        identf = T_(g, es, "idf" + tag, [128, 128], F32)
        s.dma("sp", identf[:], I["ident_f"], writes=[pb])
        for k, Bb in enumerate((Bbr, Bbi)):
            P("pool", lambda e_, k=k: e_.memset(BP[k][:], 0.0))
            for g2 in range(2):
                for r in range(4):
                    c0 = 32 * r + 16 * g2
                    P("pool", lambda e_, k=k, Bb=Bb, g2=g2, r=r, c0=c0: e_.tensor_copy(
                        out=BP[k][g2 * 64:(g2 + 1) * 64, r::4, c0:c0 + 16], in_=Bb[g2 * 64:(g2 + 1) * 64, r::4, :]))
            for q4 in range(4):
                pp, ppb = next_ps(g)

                def trb(e_, pp=pp, k=k, q4=q4):
                    last = None
                    for jj in range(4):
                        last = e_.transpose(out=pp[:, jj * 128:(jj + 1) * 128], in_=BP[k][:, q4 * 4 + jj, :], identity=identf[:])
                    return last
                s.op("pe", trb, reads=[pb], writes=[ppb])
                s.op("act", lambda e_, pp=pp, k=k, q4=q4: e_.copy(
                    out=LB[k][:, q4 * 4:q4 * 4 + 4, :], in_=pp[:, :].rearrange("p (j c) -> p j c", j=4)), reads=[ppb], writes=[pb])
        CPf = [T_(g, es, "CPf%d" % k + tag, [128, 16, 128], F32) for k in range(2)]
        CP = [T_(g, es, "CP%d" % k + tag, [128, 16, 128], BF16) for k in range(2)]
        for k, nm in enumerate(("s5_c_re", "s5_c_im")):
            P("pool", lambda e_, k=k: e_.memset(CPf[k][:], 0.0))
            src = I[nm][e].rearrange("(pr g2) c n -> g2 n pr c", g2=2)
            for g2 in range(2):
                for r in range(4):
                    c0 = 32 * r + 16 * g2
                    for q in range(4):
                        s.dma("sp", CPf[k][g2 * 64:(g2 + 1) * 64, r + 4 * q, c0:c0 + 16], src[g2][:, r + 4 * q, :], reads=[pb],
                              writes=[pb], allow_slow_non_contiguous=True)
        P("act", lambda e_: e_.copy(out=CP[0][:], in_=CPf[0][:]))
        P("act", lambda e_: e_.mul(out=CP[1][:], in_=CPf[1][:], mul=-1.0))
        CP.append(T_(g, es, "CP2" + tag, [128, 16, 128], BF16))
        P("act", lambda e_: e_.mul(out=CP[2][:], in_=CPf[0][:], mul=-1.0))
        dsk = sm("dsk", 4)
        bgl = sm("bgl", 4)
        s.dma("sp", dsk[:], I["s5_d"][e].rearrange("(t p) -> p t", p=128), writes=[pb], allow_slow_non_contiguous=True)
        s.dma("sp", bgl[:], I["s5_b_glu"][e].rearrange("(t p) -> p t", p=128), writes=[pb], allow_slow_non_contiguous=True)
        wgl = T_(g, es, "wgl" + tag, [128, 4, 512], BF16)
        s.dma("pool", wgl[:], I["s5_w_glu"][e].rearrange("(kt p) f -> p kt f", p=128), writes=[pb])
        tau = T_(g, es, "tau" + tag, [128, TC], F32)
        s.dma("sp", tau[:], I["tau"], writes=[pb])
        cosT = T_(g, es, "cosT" + tag, [128, 16, TC], F32)
        sinT = T_(g, es, "sinT" + tag, [128, 16, TC], F32)
        ang = T_(g, es, "ang" + tag, [128, TC], F32)
        ang2 = T_(g, es, "ang2" + tag, [128, TC], F32)
        angb = Buf()
        tabb = Buf()
        for pr in range(16):
            s.op("dve", lambda e_, pr=pr: e_.tensor_scalar(out=ang[:], in0=tau[:], scalar1=aid[:, pr:pr + 1], scalar2=None, op0=ALU.mult),
                 reads=[pb], writes=[angb])
            range_reduce(s, "dve", ang2[:], ang[:], angb)
            s.op("act", lambda e_, pr=pr: e_.activation(out=sinT[:, pr, :], in_=ang2[:], func=AF.Sin), reads=[angb], writes=[tabb])
            s.op("dve", lambda e_: e_.tensor_scalar(out=ang[:], in0=ang[:], scalar1=math.pi / 2, scalar2=None, op0=ALU.add),
                 reads=[angb], writes=[angb])
            range_reduce(s, "dve", ang2[:], ang[:], angb)
            s.op("act", lambda e_, pr=pr: e_.activation(out=cosT[:, pr, :], in_=ang2[:], func=AF.Sin), reads=[angb], writes=[tabb])
        h0r, h0i = sm("h0r"), sm("h0i")
        hb = Buf()
        uT = T_(g, es, "uTs" + tag, [128, 4, TC], BF16)
        uTb = Buf()
        Wset = [[T_(g, es, "w%d_%d" % (ss_, k) + tag, [128, TC], F32) for k in range(6)] for ss_ in range(2)]
        Wbset = [[Buf() for _ in range(6)] for _ in range(2)]
        hb4s = [T_(g, es, "hb4_%d" % k + tag, [128, 4, 4, TC], BF16) for k in range(2)]
        hbfbs = [Buf(), Buf()]
        glr, gli, ht = sm("glr"), sm("gli"), sm("ht")
        glb = Buf()
        ysb = T_(g, es, "ysb" + tag, [128, TC], F32)
        y2 = T_(g, es, "y2" + tag, [128, TC], F32)
        ysbb = Buf()
        gT = T_(g, es, "gT" + tag, [128, 4, TC], BF16)
        gTb = Buf()
        sg = T_(g, es, "sgl" + tag, [128, TC], F32)
        oT = T_(g, es, "oT" + tag, [128, 4, TC], BF16)
        oTb = Buf()
        vs = lambda ap: ap.rearrange("(pr g2) n -> (g2 n) pr", g2=2)
        for seg, (c0, c1) in enumerate(((0, T), (T, R))):
            if seg == 0:
                s.op("pool", lambda e_: e_.memset(h0r[:], 0.0), reads=[hb], writes=[hb])
                s.op("pool", lambda e_: e_.memset(h0i[:], 0.0), reads=[hb], writes=[hb])
            else:
                for hp in range(2):
                    s.dma("sp", h0r[:, hp * 8:hp * 8 + 8], vs(I["st_re"][e])[:, hp * 8:hp * 8 + 8], reads=[hb], writes=[hb], allow_slow_non_contiguous=True)
                    s.dma("sp", h0i[:, hp * 8:hp * 8 + 8], vs(I["st_im"][e])[:, hp * 8:hp * 8 + 8], reads=[hb], writes=[hb], allow_slow_non_contiguous=True)
            for (co, tc) in chunks(c1 - c0, TC):
                col = c0 + co
                s.dma("sp", uT[:, :, 0:tc], g.UT[:, :, col:col + tc].rearrange("t p c -> p t c"), reads=[g.scrb], writes=[uTb])
                TT = lambda eng, o, a, b, op, rd, wr: s.op(eng, lambda e_: e_.tensor_tensor(out=o, in0=a, in1=b, op=op), reads=rd, writes=wr)

                def stageA(pr):
                    ft = pr // 4
                    W, Wb = Wset[pr % 2], Wbset[pr % 2]
                    pbr, pbrb = g.ps[2 * (pr % 2)], g.psb[2 * (pr % 2)]
                    pbi, pbib = g.ps[2 * (pr % 2) + 1], g.psb[2 * (pr % 2) + 1]
                    s.op("pe", lambda e_: e_.matmul(pbr[:, 0:tc], lhsT=LB[0][:, pr, :], rhs=uT[:, ft, 0:tc], start=True, stop=True),
                         reads=[pb, uTb], writes=[pbrb])
                    s.op("pe", lambda e_: e_.matmul(pbi[:, 0:tc], lhsT=LB[1][:, pr, :], rhs=uT[:, ft, 0:tc], start=True, stop=True),
                         reads=[pb, uTb], writes=[pbib])
                    c_, s_ = cosT[:, pr, 0:tc], sinT[:, pr, 0:tc]
                    TT("dve", W[0][:, 0:tc], pbr[:, 0:tc], c_, ALU.mult, [pbrb, tabb], [Wb[0]])
                    TT("dve", W[1][:, 0:tc], pbi[:, 0:tc], s_, ALU.mult, [pbib, tabb], [Wb[1]])
                    TT("pool", W[0][:, 0:tc], W[0][:, 0:tc], W[1][:, 0:tc], ALU.add, [Wb[0], Wb[1]], [Wb[0]])
                    TT("dve", W[2][:, 0:tc], pbi[:, 0:tc], c_, ALU.mult, [pbib, tabb], [Wb[2]])
                    TT("dve", W[5][:, 0:tc], pbr[:, 0:tc], s_, ALU.mult, [pbrb, tabb, Wb[5]], [Wb[5]])
                    TT("pool", W[2][:, 0:tc], W[2][:, 0:tc], W[5][:, 0:tc], ALU.subtract, [Wb[2], Wb[5]], [Wb[2]])

                def stageB(pr):
                    ft, p4 = pr // 4, pr % 4
                    W, Wb = Wset[pr % 2], Wbset[pr % 2]
                    hb4, hbfb = hb4s[ft % 2], hbfbs[ft % 2]
                    c_, s_ = cosT[:, pr, 0:tc], sinT[:, pr, 0:tc]
                    dec = mag[:, pr:pr + 1].to_broadcast([128, tc])
                    s.op("dve", lambda e_: e_.tensor_tensor_scan(
                        out=W[3][:, 0:tc], data0=dec, data1=W[0][:, 0:tc], initial=h0r[:, pr:pr + 1], op0=ALU.mult, op1=ALU.add),
                        reads=[Wb[0], hb, pb], writes=[Wb[3]])
                    s.op("dve", lambda e_: e_.tensor_tensor_scan(
                        out=W[4][:, 0:tc], data0=dec, data1=W[2][:, 0:tc], initial=h0i[:, pr:pr + 1], op0=ALU.mult, op1=ALU.add),
                        reads=[Wb[2], hb, pb], writes=[Wb[4]])
                    TT("dve", hb4[:, p4, 0, 0:tc], W[3][:, 0:tc], c_, ALU.mult, [Wb[3], tabb, hbfb], [hbfb])
                    TT("pool", hb4[:, p4, 1, 0:tc], W[4][:, 0:tc], s_, ALU.mult, [Wb[4], tabb, hbfb], [hbfb])
                    TT("dve", hb4[:, p4, 2, 0:tc], W[3][:, 0:tc], s_, ALU.mult, [Wb[3], tabb, hbfb], [hbfb])
                    TT("pool", hb4[:, p4, 3, 0:tc], W[4][:, 0:tc], c_, ALU.mult, [Wb[4], tabb, hbfb], [hbfb])
                    s.op("act", lambda e_: e_.copy(out=glr[:, pr:pr + 1], in_=W[3][:, tc - 1:tc]), reads=[Wb[3], glb], writes=[glb])
                    s.op("act", lambda e_: e_.copy(out=gli[:, pr:pr + 1], in_=W[4][:, tc - 1:tc]), reads=[Wb[4], glb], writes=[glb])

                def stageY(ft):
                    hb4, hbfb = hb4s[ft % 2], hbfbs[ft % 2]
                    py, pyb = g.ps[4 + ft % 2], g.psb[4 + ft % 2]

                    def mmy(e_):
                        last = None
                        for p4 in range(4):
                            for q_, ci in enumerate((0, 2, 1, 1)):
                                last = e_.matmul(py[:, 0:tc], lhsT=CP[ci][:, ft * 4 + p4, :], rhs=hb4[:, p4, q_, 0:tc],
                                                 start=(p4 == 0 and q_ == 0), stop=(p4 == 3 and q_ == 3))
                        return last
                    s.op("pe", mmy, reads=[pb, hbfb], writes=[pyb])
                    s.op("dve", lambda e_: e_.scalar_tensor_tensor(
                        out=ysb[:, 0:tc], in0=uT[:, ft, 0:tc], scalar=dsk[:, ft:ft + 1], in1=py[:, 0:tc], op0=ALU.mult, op1=ALU.add),
                        reads=[pyb, uTb, pb], writes=[ysbb])
                    s.op("pool", lambda e_: e_.tensor_tensor(out=y2[:, 0:tc], in0=ysb[:, 0:tc], in1=ysb[:, 0:tc], op=ALU.mult), reads=[ysbb], writes=[ysbb])
                    s.op("pool", lambda e_: e_.tensor_scalar(out=y2[:, 0:tc], in0=y2[:, 0:tc], scalar1=0.044715, scalar2=1.0,
                                                             op0=ALU.mult, op1=ALU.add), reads=[ysbb], writes=[ysbb])
                    s.op("pool", lambda e_: e_.tensor_tensor(out=y2[:, 0:tc], in0=y2[:, 0:tc], in1=ysb[:, 0:tc], op=ALU.mult), reads=[ysbb], writes=[ysbb])
                    s.op("act", lambda e_: e_.activation(out=y2[:, 0:tc], in_=y2[:, 0:tc], func=AF.Sigmoid, scale=1.5957691216), reads=[ysbb], writes=[ysbb])
                    s.op("dve", lambda e_: e_.tensor_tensor(out=gT[:, ft, 0:tc], in0=ysb[:, 0:tc], in1=y2[:, 0:tc], op=ALU.mult),
                         reads=[ysbb], writes=[gTb])

                stageA(0)
                for pr in range(16):
                    if pr + 1 < 16:
                        stageA(pr + 1)
                    stageB(pr)
                    if pr % 4 == 3:
                        stageY(pr // 4)
                cl, sl = cosT[:, :, tc - 1], sinT[:, :, tc - 1]
                H_ = lambda o_, a_, b_, op: s.op("dve", lambda e_: e_.tensor_tensor(out=o_, in0=a_, in1=b_, op=op),
                                                 reads=[glb, hb, tabb], writes=[hb])
                H_(h0r[:], glr[:], cl, ALU.mult)
                H_(ht[:], gli[:], sl, ALU.mult)
                H_(h0r[:], h0r[:], ht[:], ALU.subtract)
                H_(h0i[:], glr[:], sl, ALU.mult)
                H_(ht[:], gli[:], cl, ALU.mult)
                H_(h0i[:], h0i[:], ht[:], ALU.add)
                for ot in range(4):
                    pz, pzb = g.ps[4 + ot % 2], g.psb[4 + ot % 2]

                    def mmz(e_, pz=pz, ot=ot):
                        last = None
                        for kt in range(4):
                            last = e_.matmul(pz[:, 0:tc], lhsT=wgl[:, kt, ot * 128:(ot + 1) * 128], rhs=gT[:, kt, 0:tc], start=(kt == 0), stop=(kt == 3))
                        return last
                    s.op("pe", mmz, reads=[pb, gTb], writes=[pzb])
                    s.op("act", lambda e_, pz=pz, ot=ot: e_.activation(out=sg[:, 0:tc], in_=pz[:, 0:tc], func=AF.Sigmoid, bias=bgl[:, ot:ot + 1]),
                         reads=[pzb, pb, ysbb], writes=[ysbb])
                    s.op("dve", lambda e_, ot=ot: e_.tensor_tensor(out=oT[:, ot, 0:tc], in0=gT[:, ot, 0:tc], in1=sg[:, 0:tc], op=ALU.mult),
                         reads=[ysbb, gTb], writes=[oTb])
                s.dma("sp", g.MIXT[0:4, :, col:col + tc].rearrange("t p c -> p t c"), oT[:, :, 0:tc], reads=[oTb], writes=[g.scrb])
            pre = "p_" if seg == 0 else "s_"
            for hp in range(2):
                s.dma("sp", vs(O[pre + "s5re"][e])[:, hp * 8:hp * 8 + 8], h0r[:, hp * 8:hp * 8 + 8], reads=[hb], writes=[g.outb], allow_slow_non_contiguous=True)
                s.dma("sp", vs(O[pre + "s5im"][e])[:, hp * 8:hp * 8 + 8], h0i[:, hp * 8:hp * 8 + 8], reads=[hb], writes=[g.outb], allow_slow_non_contiguous=True)


def odd_stage(g, o):
    odd_proj(g, o)
    T, R, PAST = g.T, g.R, g.PAST
    v8 = lambda ap: ap.rearrange("t (hh p) c -> (t hh) p c", hh=2)
    attention(g, "fp%d" % o, 8, 64, v8(g.FQT), 0, T, v8(g.FKTp), g.VAp, T, "fox_p", 1.0, 0, fox=dict(FT=g.FTp, FT3=g.FT3p, qoff=0))
    attention(g, "fs%d" % o, 8, 64, v8(g.FQT), T, DSEQ, v8(g.FKTs), g.VAs, PAST + DSEQ, "fox_s", 1.0, 0,
              fox=dict(FT=g.FTs, FT3=g.FT3s, qoff=PAST))
    retention(g, o)
    out_proj_ln(g, "oo%d" % o, g.I["owout"][o], 2 * o + 1, EPS)


@staged
def odd_proj(g, o):
    nc, s, R, T, PAST = g.nc, g.s, g.R, g.T, g.PAST
    NB = 512
    I, O = g.I, g.O
    with ExitStack() as es:
        tag = "op%d" % o
        win = T_(g, es, "win" + tag, [128, 8, 3080], BF16)
        wb = Buf()
        wins = I["owin"][o].rearrange("(kt p) f -> p kt f", p=128)
        for kt in range(8):
            s.dma("pool", win[:, kt, :], wins[:, kt, :], writes=[wb])
        nbf = T_(g, es, "nbf" + tag, [8, 1], F32)
        s.dma("sp", nbf[:], I["fox_b_f"][o].rearrange("(p o) -> p o", o=1), writes=[wb], allow_slow_non_contiguous=True)
        s.op("dve", lambda e: e.tensor_scalar(out=nbf[:], in0=nbf[:], scalar1=-1.0, scalar2=None, op0=ALU.mult), reads=[wb], writes=[wb])
        xf = T_(g, es, "xf" + tag, [128, 4, D], F32)
        xb = T_(g, es, "xb" + tag, [128, 2, D], BF16)
        xT = T_(g, es, "xT" + tag, [128, 8, NB], BF16)
        xfb, xbb, xTb = [Buf() for _ in range(4)], [Buf(), Buf()], Buf()
        FQs = T_(g, es, "FQs" + tag, [128, 4, NB], BF16)
        FKs = T_(g, es, "FKs" + tag, [128, 4, NB], BF16)
        FQb, FKb = Buf(), Buf()
        lf = T_(g, es, "lf" + tag, [8, NB], F32)
        fc = T_(g, es, "fc" + tag, [8, NB], F32)
        one8 = T_(g, es, "one8" + tag, [8, NB], F32)
        car = T_(g, es, "car" + tag, [8, 1], F32)
        lfb, carb = Buf(), Buf()
        s.op("pool", lambda e: e.memset(one8[:], 1.0), writes=[wb])
        kf = [T_(g, es, "kf%d" % k + tag, [128, 512], F32) for k in range(2)]
        kfb = [Buf(), Buf()]
        VAs_ = T_(g, es, "VAo" + tag, [128, 8, 128], BF16)
        VAb = Buf()
        s.op("pool", lambda e: e.memset(VAs_[:], 1.0), writes=[VAb])
        ra4 = T_(g, es, "ra" + tag, [128, 4, 8, 64], F32)
        rbt4 = T_(g, es, "rb" + tag, [128, 4, 8, 64], F32)
        rab4 = [Buf() for _ in range(4)]
        tA = T_(g, es, "tA" + tag, [128, 8, 64], F32)
        tB = T_(g, es, "tB" + tag, [128, 8, 64], F32)
        tAb = Buf()
        qk = T_(g, es, "qk" + tag, [128, 8, 64], BF16)
        qkb = Buf()
        qkT = T_(g, es, "qkT" + tag, [128, 4, 128], BF16)
        qkTb = Buf()
        rvb = T_(g, es, "rvb" + tag, [128, 512], BF16)
        rgf = T_(g, es, "rgf" + tag, [128, 512], F32)
        rvbb = Buf()
        pkf = T_(g, es, "pkf" + tag, [128, 512], F32)
        pkb_ = T_(g, es, "pkb" + tag, [128, 512], BF16)
        pcb, pcb2 = Buf(), Buf()

        f3 = T_(g, es, "f3" + tag, [8, 3, NB], BF16)
        f3t = T_(g, es, "f3t" + tag, [8, NB], F32)
        f3b = Buf()

        def split3(lo, cnt, dst, d0):
            sl = slice(lo, lo + cnt)
            s.op("act", lambda e: e.copy(out=f3[:, 0, sl], in_=fc[:, sl]), reads=[lfb, f3b], writes=[f3b])
            s.op("dve", lambda e: e.tensor_tensor(out=f3t[:, sl], in0=fc[:, sl], in1=f3[:, 0, sl], op=ALU.subtract), reads=[lfb, f3b], writes=[f3b])
            s.op("act", lambda e: e.copy(out=f3[:, 1, sl], in_=f3t[:, sl]), reads=[f3b], writes=[f3b])
            s.op("dve", lambda e: e.tensor_tensor(out=f3t[:, sl], in0=f3t[:, sl], in1=f3[:, 1, sl], op=ALU.subtract), reads=[f3b], writes=[f3b])
            s.op("act", lambda e: e.copy(out=f3[:, 2, sl], in_=f3t[:, sl]), reads=[f3b], writes=[f3b])
            s.dma("sp", dst[:, :, d0:d0 + cnt], f3[:, :, sl], reads=[f3b], writes=[g.scrb])

        s.op("pool", lambda e: e.memset(car[:], 0.0), writes=[carb])
        pblks = chunks(PAST, NB)
        pk4 = [T_(g, es, "pk4_%d" % k + tag, [128, 4, 512], F32) for k in range(2)]
        pv4 = [T_(g, es, "pv4_%d" % k + tag, [128, 4, 512], F32) for k in range(2)]
        pb4 = [[Buf() for _ in range(4)] for _ in range(2)]

        def past_loads(bi):
            if bi >= len(pblks):
                return
            k0_, nb_ = pblks[bi]
            for ti_, (o_, n_) in enumerate(chunks(nb_, 128)):
                s.dma("sp", pk4[bi % 2][0:n_, ti_, :], I["c_fk"][o, k0_ + o_:k0_ + o_ + n_, :], writes=[pb4[bi % 2][ti_]])
                s.dma("sp", pv4[bi % 2][0:n_, ti_, :], I["c_fv"][o, k0_ + o_:k0_ + o_ + n_, :], writes=[pb4[bi % 2][ti_]])
        past_loads(0)
        for bi, (k0, nb) in enumerate(pblks):
            past_loads(bi + 1)
            for ti, (oo, n) in enumerate(chunks(nb, 128)):
                kk = k0 + oo
                pkf, pvf, pcb = pk4[bi % 2][:, ti, :], pv4[bi % 2][:, ti, :], pb4[bi % 2][ti]
                s.op("act", lambda e, n=n, pkf=pkf: e.copy(out=pkb_[0:n, :], in_=pkf[0:n, :]), reads=[pcb], writes=[pcb2])
                pt, ptb = next_pst(g)

                def tr(e, pt=pt, n=n):
                    last = None
                    for j in range(4):
                        last = e.transpose(out=pt[:, j * 128:j * 128 + n], in_=pkb_[0:n, j * 128:(j + 1) * 128], identity=g.ident[0:n, 0:n])
                    return last
                s.op("pe", tr, reads=[pcb2, g.identb], writes=[ptb])
                s.op("dve", lambda e, pt=pt, oo=oo, n=n: e.tensor_copy(
                    out=FKs[:, :, oo:oo + n], in_=pt[:, 0:512].rearrange("p (j c) -> p j c", j=4)[:, :, 0:n]), reads=[ptb], writes=[FKb])
                s.op("act", lambda e, n=n, pvf=pvf: e.copy(out=VAs_[0:n, :, 0:64], in_=pvf[0:n, :].rearrange("p (h c) -> p h c", c=64)),
                     reads=[pcb], writes=[VAb])
                s.dma("sp", g.VAs[kk:kk + n, :], VAs_[0:n, :, :].rearrange("p h c -> p (h c)"), reads=[VAb], writes=[g.scrb])
            s.dma("sp", g.FKTs[:, :, k0:k0 + nb].rearrange("t p c -> p t c"), FKs[:, :, 0:nb], reads=[FKb], writes=[g.scrb])
            for (c_, cn_) in chunks(nb, 128):
                s.dma("sp", lf[:, c_:c_ + cn_], I["c_flf"][o, k0 + c_:k0 + c_ + cn_, :].rearrange("k h -> h k"), reads=[lfb], writes=[lfb],
                      allow_slow_non_contiguous=True)
            s.op("dve", lambda e, nb=nb: e.tensor_tensor_scan(out=fc[:, 0:nb], data0=one8[:, 0:nb], data1=lf[:, 0:nb], initial=car[:, 0:1],
                                                               op0=ALU.mult, op1=ALU.add), reads=[lfb, carb, wb], writes=[lfb])
            s.op("act", lambda e, nb=nb: e.copy(out=car[:, 0:1], in_=fc[:, nb - 1:nb]), reads=[lfb, carb], writes=[carb])
            s.dma("sp", g.FTs[:, k0:k0 + nb], fc[:, 0:nb], reads=[lfb], writes=[g.scrb])
            split3(0, nb, g.FT3s, k0)
        carp = T_(g, es, "carp" + tag, [8, 1], F32)
        carpb = Buf()
        s.op("pool", lambda e: e.memset(carp[:], 0.0), writes=[carpb])

        ublks = chunks(R, NB)
        for ti in range(4):
            x_load_tile(g, ublks[0], ti, xf, xfb)
        for bi, (r0, nb) in enumerate(ublks):
            load_xT(g, es, r0, nb, xf, xfb, xb, xbb, xT, xTb, preloaded=True)
            for ti in range(4):
                x_load_tile(g, ublks[bi + 1] if bi + 1 < len(ublks) else None, ti, xf, xfb)
            for ti, (oo, n) in enumerate(chunks(nb, 128)):
                s.dma("sp", ra4[0:n, ti], I["ropeA32"][r0 + oo:r0 + oo + n], writes=[rab4[ti]])
                s.dma("sp", rbt4[0:n, ti], I["ropeB32"][r0 + oo:r0 + oo + n], writes=[rab4[ti]])
            for which, (dst, dstb, c0, sc) in enumerate(((FQs, FQb, 0, 0.125), (FKs, FKb, 512, 1.0))):
                for ft in range(4):
                    pu, pub = next_ps(g)

                    def mmu(e, pu=pu, ft=ft, c0=c0):
                        last = None
                        for kt in range(8):
                            last = e.matmul(pu[:, 0:nb], lhsT=win[:, kt, c0 + ft * 128:c0 + (ft + 1) * 128], rhs=xT[:, kt, 0:nb],
                                            start=(kt == 0), stop=(kt == 7))
                        return last
                    s.op("pe", mmu, reads=[wb, xTb], writes=[pub])
                    s.op("act", lambda e, pu=pu, ft=ft, dst=dst, sc=sc: e.mul(out=dst[:, ft, 0:nb], in_=pu[:, 0:nb], mul=sc),
                         reads=[pub], writes=[dstb])
            s.dma("sp", g.FQT[:, :, r0:r0 + nb].rearrange("t p c -> p t c"), FQs[:, :, 0:nb], reads=[FQb], writes=[g.scrb])
            for (kd, lo, dr, cnt) in split_rows(g, r0, nb):
                dstT = g.FKTp if kd == "p" else g.FKTs
                d0 = dr if kd == "p" else PAST + dr
                s.dma("sp", dstT[:, :, d0:d0 + cnt].rearrange("t p c -> p t c"), FKs[:, :, lo:lo + cnt], reads=[FKb], writes=[g.scrb])
            pl, plb = next_ps(g)

            def mml(e, pl=pl):
                last = None
                for kt in range(8):
                    last = e.matmul(pl[0:8, 0:nb], lhsT=win[:, kt, 1536:1544], rhs=xT[:, kt, 0:nb], start=(kt == 0), stop=(kt == 7))
                return last
            s.op("pe", mml, reads=[wb, xTb], writes=[plb])
            s.op("act", lambda e, pl=pl: e.activation(out=lf[:, 0:nb], in_=pl[0:8, 0:nb], func=AF.Exp, scale=-1.0, bias=nbf[:, 0:1]),
                 reads=[plb, wb, lfb], writes=[lfb])
            s.op("act", lambda e: e.activation(out=lf[:, 0:nb], in_=lf[:, 0:nb], func=AF.Ln, bias=1.0), reads=[lfb], writes=[lfb])
            s.op("dve", lambda e: e.tensor_scalar(out=lf[:, 0:nb], in0=lf[:, 0:nb], scalar1=-1.0, scalar2=None, op0=ALU.mult),
                 reads=[lfb], writes=[lfb])
            for (kd, lo, dr, cnt) in split_rows(g, r0, nb):
                cr, crb = (carp, carpb) if kd == "p" else (car, carb)
                pre = "p_" if kd == "p" else "s_"
                for (c_, cn_) in chunks(cnt, 128):
                    s.dma("sp", O[pre + "flf"][o, dr + c_:dr + c_ + cn_, :].rearrange("k h -> h k"), lf[:, lo + c_:lo + c_ + cn_],
                          reads=[lfb], writes=[g.outb], allow_slow_non_contiguous=True)
                s.op("dve", lambda e, lo=lo, cnt=cnt, cr=cr: e.tensor_tensor_scan(
                    out=fc[:, lo:lo + cnt], data0=one8[:, lo:lo + cnt], data1=lf[:, lo:lo + cnt], initial=cr[:, 0:1],
                    op0=ALU.mult, op1=ALU.add), reads=[lfb, crb, wb], writes=[lfb])
                s.op("act", lambda e, lo=lo, cnt=cnt, cr=cr: e.copy(out=cr[:, 0:1], in_=fc[:, lo + cnt - 1:lo + cnt]), reads=[lfb, crb], writes=[crb])
                dF = g.FTp if kd == "p" else g.FTs
                dF3 = g.FT3p if kd == "p" else g.FT3s
                d0 = dr if kd == "p" else PAST + dr
                s.dma("sp", dF[:, d0:d0 + cnt], fc[:, lo:lo + cnt], reads=[lfb], writes=[g.scrb])
                split3(lo, cnt, dF3, d0)
            for ti, (oo, n) in enumerate(chunks(nb, 128)):
                rr = r0 + oo
                pieces = split_rows(g, rr, n)

                def mmt(e, pm, c0, oo=oo, n=n):
                    last = None
                    for kt in range(8):
                        last = e.matmul(pm[0:n, :], lhsT=xT[:, kt, oo:oo + n], rhs=win[:, kt, c0:c0 + 512], start=(kt == 0), stop=(kt == 7))
                    return last
                pm, pmb = next_ps(g)
                s.op("pe", lambda e, pm=pm: mmt(e, pm, 512), reads=[wb, xTb], writes=[pmb])
                s.op("act", lambda e, pm=pm, n=n: e.copy(out=kf[0][0:n, :], in_=pm[0:n, :]), reads=[pmb], writes=[kfb[0]])
                for (kd, lo, dr, cnt) in pieces:
                    s.dma("sp", O[("p_" if kd == "p" else "s_") + "fk"][o, dr:dr + cnt, :], kf[0][lo:lo + cnt, :], reads=[kfb[0]], writes=[g.outb])
                pm, pmb = next_ps(g)
                s.op("pe", lambda e, pm=pm: mmt(e, pm, 1024), reads=[wb, xTb], writes=[pmb])
                s.op("act", lambda e, pm=pm, n=n: e.copy(out=kf[1][0:n, :], in_=pm[0:n, :]), reads=[pmb], writes=[kfb[1]])
                s.op("dve", lambda e, n=n: e.tensor_copy(out=VAs_[0:n, :, 0:64], in_=kf[1][0:n, :].rearrange("p (h c) -> p h c", c=64)),
                     reads=[kfb[1]], writes=[VAb])
                for (kd, lo, dr, cnt) in pieces:
                    s.dma("sp", O[("p_" if kd == "p" else "s_") + "fv"][o, dr:dr + cnt, :], kf[1][lo:lo + cnt, :], reads=[kfb[1]], writes=[g.outb])
                    dV = g.VAp if kd == "p" else g.VAs
                    d0 = dr if kd == "p" else PAST + dr
                    s.dma("sp", dV[d0:d0 + cnt, :], VAs_[lo:lo + cnt, :, :].rearrange("p h c -> p (h c)"), reads=[VAb], writes=[g.scrb])
                pm, pmb = next_ps(g)
                s.op("pe", lambda e, pm=pm: mmt(e, pm, 1544), reads=[wb, xTb], writes=[pmb])
                ra, rbt, rab = ra4[:, ti], rbt4[:, ti], rab4[ti]
                rope_rows(g, pm[0:n, :].rearrange("p (h c) -> p h c", c=64), n, 8, 32, ra, rbt, qk[0:n], tA, tB, tAb, [pmb, rab], [qkb])
                s.dma("sp", g.RK[rr:rr + n, :], qk[0:n, 4:8, :].rearrange("p h c -> p (h c)"), reads=[qkb], writes=[g.scrb])
                pm, pmb = next_ps(g)
                s.op("pe", lambda e, pm=pm: mmt(e, pm, 2056), reads=[wb, xTb], writes=[pmb])
                s.op("act", lambda e, pm=pm, n=n: e.copy(out=rvb[0:n, :], in_=pm[0:n, :]), reads=[pmb], writes=[rvbb])
                s.dma("sp", g.RV[rr:rr + n, :], rvb[0:n, :], reads=[rvbb], writes=[g.scrb])
                pm, pmb = next_ps(g)
                s.op("pe", lambda e, pm=pm: mmt(e, pm, 2568), reads=[wb, xTb], writes=[pmb])
                s.op("act", lambda e, pm=pm, n=n: e.activation(out=rgf[0:n, :], in_=pm[0:n, :], func=AF.Silu), reads=[pmb], writes=[rvbb])
                s.dma("sp", g.RG[rr:rr + n, :], rgf[0:n, :], reads=[rvbb], writes=[g.scrb])
                pt, ptb = next_pst(g)

                def tr4(e, pt=pt, n=n):
                    last = None
                    for j in range(4):
                        last = e.transpose(out=pt[:, j * 128:j * 128 + n], in_=qk[0:n, 2 * j:2 * j + 2, :].rearrange("p h c -> p (h c)"),
                                           identity=g.ident[0:n, 0:n])
                    return last
                s.op("pe", tr4, reads=[qkb, g.identb], writes=[ptb])
                s.op("dve", lambda e, pt=pt, n=n: e.tensor_copy(out=qkT[:, :, 0:n], in_=pt[:, 0:512].rearrange("p (j c) -> p j c", j=4)[:, :, 0:n]),
                     reads=[ptb], writes=[qkTb])
                s.dma("sp", g.RQKT[:, :, rr:rr + n].rearrange("t p c -> p t c"), qkT[:, :, 0:n], reads=[qkTb], writes=[g.scrb])


@staged
def retention(g, o):
    nc, s, R, T = g.nc, g.s, g.R, g.T
    I, O = g.I, g.O
    with ExitStack() as es:
        tag = "rt%d" % o
        decT = T_(g, es, "decT" + tag, [128, 4, 128], F32)
        gq = T_(g, es, "gqd" + tag, [128, 4, 128], F32)
        ginv = T_(g, es, "ginv" + tag, [128, 4], F32)
        cb = Buf()
        s.dma("sp", decT[:], I["ret_decT"].rearrange("h j i -> j h i"), writes=[cb])
        s.dma("sp", gq[:], I["ret_gq"].rearrange("h d i -> d h i"), writes=[cb])
        s.dma("sp", ginv[:], I["ret_ginv"], writes=[cb])
        S = [T_(g, es, "S%d" % h + tag, [64, 128], F32) for h in range(4)]
        Sb = [T_(g, es, "Sb%d" % h + tag, [64, 128], BF16) for h in range(4)]
        Sbuf = [Buf() for _ in range(4)]
        qT2 = [T_(g, es, "qT%d" % k + tag, [64, 8, 128], BF16) for k in range(2)]
        kt2 = [T_(g, es, "kt%d" % k + tag, [128, 256], BF16) for k in range(2)]
        v2 = [T_(g, es, "v%d" % k + tag, [128, 512], BF16) for k in range(2)]
        gg2 = [T_(g, es, "gg%d" % k + tag, [128, 512], F32) for k in range(2)]
        lb2 = [Buf(), Buf()]
        ci = 0
        PT = [T_(g, es, "PT%d" % k + tag, [128, 128], BF16) for k in range(2)]
        qd = [T_(g, es, "qd%d" % k + tag, [128, 128], BF16) for k in range(2)]
        kd_ = [T_(g, es, "kd%d" % k + tag, [128, 64], BF16) for k in range(2)]
        wbf = [Buf(), Buf()]
        st4 = [T_(g, es, "st%d" % h + tag, [128, 6], F32) for h in range(4)]
        mv4 = [T_(g, es, "mv%d" % h + tag, [128, 2], F32) for h in range(4)]
        rs4 = [T_(g, es, "rs%d" % h + tag, [128, 1], F32) for h in range(4)]
        nm4 = [T_(g, es, "nm%d" % h + tag, [128, 1], F32) for h in range(4)]
        stb4 = [Buf() for _ in range(4)]
        on4 = [T_(g, es, "on%d" % h + tag, [128, 128], F32) for h in range(4)]
        ro = T_(g, es, "ro" + tag, [128, 4, 128], BF16)
        rob = Buf()
        roT = T_(g, es, "roT" + tag, [128, 4, 128], BF16)
        roTb = Buf()
        gam = [1.0 - 2.0 ** (-5.0 - h) for h in range(4)]
        it = 0
        for seg, (c0, c1) in enumerate(((0, T), (T, R))):
            for h in range(4):
                if seg == 0:
                    s.op("pool", lambda e, h=h: e.memset(S[h][:], 0.0), reads=[Sbuf[h]], writes=[Sbuf[h]])
                else:
                    s.dma("sp", S[h][:], I["st_ret"][o, h], reads=[Sbuf[h]], writes=[Sbuf[h]])
                s.op("act", lambda e, h=h: e.copy(out=Sb[h][:], in_=S[h][:]), reads=[Sbuf[h]], writes=[Sbuf[h]])
            cks = chunks(c1 - c0, 128)

            def ch_loads(j, cj):
                if j >= len(cks):
                    return
                co_, n_ = cks[j]
                rr_ = c0 + co_
                s.dma("sp", qT2[cj % 2][:, :, 0:n_], g.RQKT[:, :, rr_:rr_ + n_].rearrange("t (hh p) c -> p (t hh) c", hh=2),
                      reads=[g.scrb], writes=[lb2[cj % 2]])
                s.dma("sp", kt2[cj % 2][0:n_, :], g.RK[rr_:rr_ + n_, :], reads=[g.scrb], writes=[lb2[cj % 2]])
                s.dma("sp", v2[cj % 2][0:n_, :], g.RV[rr_:rr_ + n_, :], reads=[g.scrb], writes=[lb2[cj % 2]])
                s.dma("sp", gg2[cj % 2][0:n_, :], g.RG[rr_:rr_ + n_, :], reads=[g.scrb], writes=[lb2[cj % 2]])
            ch_loads(0, ci)
            for j, (co, n) in enumerate(cks):
                rr = c0 + co
                qT, kt_, v, gg, lb = qT2[ci % 2], kt2[ci % 2], v2[ci % 2], gg2[ci % 2], lb2[ci % 2]
                ci += 1
                ch_loads(j + 1, ci)
                for h in range(4):
                    k = it % 2
                    it += 1
                    st, mv, rs, nm, stb, on = st4[h], mv4[h], rs4[h], nm4[h], stb4[h], on4[h]
                    p0 = 0
                    qh = qT[0:64, h, 0:n]
                    kh = qT[0:64, 4 + h, 0:n]
                    psc, pscb = next_ps(g)
                    s.op("pe", lambda e, psc=psc, kh=kh, qh=qh: e.matmul(psc[0:n, 0:n], lhsT=kh, rhs=qh, start=True, stop=True),
                         reads=[lb], writes=[pscb])
                    s.op("dve", lambda e, psc=psc, k=k, h=h: e.tensor_tensor(out=PT[k][0:n, 0:n], in0=psc[0:n, 0:n], in1=decT[0:n, h, 0:n],
                                                                              op=ALU.mult), reads=[pscb, cb], writes=[wbf[k]])
                    s.op("pool", lambda e, k=k, h=h, qh=qh, p0=p0: e.tensor_tensor(out=qd[k][p0:p0 + 64, 0:n], in0=qh, in1=gq[p0:p0 + 64, h, 0:n],
                                                                                   op=ALU.mult), reads=[lb, cb], writes=[wbf[k]])
                    s.op("dve", lambda e, k=k, h=h, kt_=kt_: e.tensor_scalar(out=kd_[k][0:n, :], in0=kt_[0:n, h * 64:(h + 1) * 64],
                                                                    scalar1=ginv[0:n, h:h + 1], scalar2=float(gam[h] ** (n - 1)),
                                                                    op0=ALU.mult, op1=ALU.mult), reads=[lb, cb], writes=[wbf[k]])
                    po, pob = next_ps(g)

                    def mmo(e, po=po, k=k, h=h, p0=p0, v=v):
                        e.matmul(po[0:n, 0:128], lhsT=PT[k][0:n, 0:n], rhs=v[0:n, h * 128:(h + 1) * 128], start=True, stop=False)
                        return e.matmul(po[0:n, 0:128], lhsT=qd[k][p0:p0 + 64, 0:n], rhs=Sb[h][:, :], start=False, stop=True)
                    s.op("pe", mmo, reads=[wbf[k], lb, Sbuf[h]], writes=[pob])
                    pst_, pstb_ = next_ps(g)
                    s.op("pe", lambda e, pst_=pst_, k=k, h=h, v=v: e.matmul(pst_[0:64, 0:128], lhsT=kd_[k][0:n, :], rhs=v[0:n, h * 128:(h + 1) * 128],
                                                                       start=True, stop=True), reads=[wbf[k], lb], writes=[pstb_])
                    s.op("dve", lambda e, pst_=pst_, h=h: e.scalar_tensor_tensor(out=S[h][:], in0=S[h][:], scalar=float(gam[h] ** n),
                                                                                 in1=pst_[0:64, 0:128], op0=ALU.mult, op1=ALU.add),
                         reads=[pstb_, Sbuf[h]], writes=[Sbuf[h]])
                    s.op("act", lambda e, h=h: e.copy(out=Sb[h][:], in_=S[h][:]), reads=[Sbuf[h]], writes=[Sbuf[h]])
                    s.op("dve", lambda e, po=po, st=st: e.bn_stats(out=st[0:n, :], in_=po[0:n, 0:128]), reads=[pob], writes=[stb])
                    s.op("dve", lambda e, mv=mv, st=st: e.bn_aggr(out=mv[0:n, :], in_=st[0:n, :]), reads=[stb], writes=[stb])
                    s.op("dve", lambda e, rs=rs, mv=mv: e.tensor_scalar(out=rs[0:n, :], in0=mv[0:n, 1:2], scalar1=EPS, scalar2=None, op0=ALU.add),
                         reads=[stb], writes=[stb])
                    s.op("act", lambda e, rs=rs: e.sqrt(out=rs[0:n, :], in_=rs[0:n, :]), reads=[stb], writes=[stb])
                    s.op("dve", lambda e, rs=rs: e.reciprocal(out=rs[0:n, :], in_=rs[0:n, :]), reads=[stb], writes=[stb])
                    s.op("dve", lambda e, nm=nm, mv=mv, rs=rs: e.scalar_tensor_tensor(out=nm[0:n, :], in0=mv[0:n, 0:1], scalar=-1.0, in1=rs[0:n, :],
                                                                 op0=ALU.mult, op1=ALU.mult), reads=[stb], writes=[stb])
                    s.op("act", lambda e, po=po, on=on, nm=nm, rs=rs: e.activation(out=on[0:n, :], in_=po[0:n, 0:128], func=AF.Identity, bias=nm[0:n, 0:1],
                                                              scale=rs[0:n, 0:1]), reads=[stb, pob], writes=[stb])
                    s.op("pool", lambda e, h=h, on=on, gg=gg: e.tensor_tensor(out=ro[0:n, h, :], in0=on[0:n, :], in1=gg[0:n, h * 128:(h + 1) * 128], op=ALU.mult),
                         reads=[stb, lb], writes=[rob])
                pt, ptb = next_pst(g)

                def tr5(e, pt=pt):
                    last = None
                    for h in range(4):
                        last = e.transpose(out=pt[:, h * 128:h * 128 + n], in_=ro[0:n, h, :], identity=g.ident[0:n, 0:n])
                    return last
                s.op("pe", tr5, reads=[rob, g.identb], writes=[ptb])
                s.op("dve", lambda e, pt=pt: e.tensor_copy(out=roT[:, :, 0:n], in_=pt[:, 0:512].rearrange("p (j c) -> p j c", j=4)[:, :, 0:n]),
                     reads=[ptb], writes=[roTb])
                s.dma("sp", g.MIXT[4:8, :, rr:rr + n].rearrange("t p c -> p t c"), roT[:, :, 0:n], reads=[roTb], writes=[g.scrb])
            pre = "p_" if seg == 0 else "s_"
            for h in range(4):
                s.dma("sp", O[pre + "ret"][o, h], S[h][:], reads=[Sbuf[h]], writes=[g.outb])


def make_consts(SEQ, PAST):
    T = NMETA + SEQ
    R = T + DSEQ
    bf = ml_dtypes.bfloat16
    c = {"ident_bf": np.eye(128, dtype=np.float32).astype(bf), "ident_f": np.eye(128, dtype=np.float32)}
    kk = np.arange(128)[:, None]
    qq = np.arange(512)[None, :]
    lim = NMETA + 64 * (np.floor_divide(qq - NMETA, 64) + 1)
    NEGM = -30000.0
    c["mask_mla"] = np.stack([np.where((128 * d + kk) < lim, 0.0, NEGM) for d in range(5)]).astype(np.float32).astype(bf)
    c["mask_fox"] = np.stack([np.where((128 * d + kk) <= qq, 0.0, NEGM) for d in range(4)]).astype(np.float32).astype(bf)
    pos = np.concatenate([np.arange(T), NMETA + PAST + np.arange(DSEQ)]).astype(np.float32)
    inv = (10000.0 ** (-np.arange(16, dtype=np.float32) / 16)).astype(np.float32)
    ang = (pos[:, None] * inv[None, :]).astype(np.float32)
    cs, sn = np.cos(ang).astype(np.float32), np.sin(ang).astype(np.float32)
    A = np.concatenate([cs, cs], -1)
    B = np.concatenate([-sn, sn], -1)
    c["ropeA16"] = np.ascontiguousarray(np.broadcast_to(A[:, None, :], (R, 4, 32))).astype(np.float32)
    c["ropeB16"] = np.ascontiguousarray(np.broadcast_to(B[:, None, :], (R, 4, 32))).astype(np.float32)
    inv2 = (10000.0 ** (-np.arange(32, dtype=np.float32) / 32)).astype(np.float32)
    ang2 = (pos[:, None] * inv2[None, :]).astype(np.float32)
    c2, s2 = np.cos(ang2).astype(np.float32), np.sin(ang2).astype(np.float32)
    A2 = np.concatenate([c2, c2], -1)
    B2 = np.concatenate([-s2, s2], -1)
    scl = np.array([1.0] * 4 + [0.125] * 4, np.float32)[None, :, None]
    c["ropeA32"] = np.ascontiguousarray(A2[:, None, :] * scl).astype(np.float32)
    c["ropeB32"] = np.ascontiguousarray(B2[:, None, :] * scl).astype(np.float32)
    gam = np.array([1.0 - 2.0 ** (-5.0 - h) for h in range(4)], np.float64)
    jj = np.arange(128)
    dif = jj[None, :] - jj[:, None]
    c["ret_decT"] = np.stack([np.where(dif >= 0, gam[h] ** np.maximum(dif, 0), 0.0) for h in range(4)]).astype(np.float32)
    c["ret_gq"] = np.stack([np.broadcast_to((gam[h] ** (jj + 1.0))[None, :], (128, 128)) for h in range(4)]).astype(np.float32)
    c["ret_ginv"] = np.stack([gam[h] ** (-jj.astype(np.float64)) for h in range(4)], axis=1).astype(np.float32)
    c["tau"] = np.ascontiguousarray(np.broadcast_to(np.arange(1, 513, dtype=np.float32)[None, :], (128, 512)))
    return c


def make_in_maps(inp, n_cores=8):
    f = lambda a: np.ascontiguousarray(np.asarray(a, dtype=np.float32))
    consts = make_consts(inp["x_prompt"].shape[1], inp["cache_mla_ckv"].shape[2])
    nb = inp["x_prompt"].shape[0]
    maps = []
    for c in range(n_cores):
        bp = PROMPT_OF_CORE.get(c) if n_cores == 8 else c % nb
        xp = f(inp["x_prompt"][bp]) if bp is not None else np.zeros(inp["x_prompt"].shape[1:], np.float32)
        meta = f(inp["meta_tokens"]) if bp is not None else np.zeros(inp["meta_tokens"].shape, np.float32)
        m = {
            "xp": xp, "xs": f(inp["x_sample"][c]), "meta": meta,
            "c_ckv": f(inp["cache_mla_ckv"][:, c]), "c_kpe": f(inp["cache_mla_kpe"][:, c]),
            "c_fk": f(np.asarray(inp["cache_fox_k"])[:, c].reshape(2, -1, 512)),
            "c_fv": f(np.asarray(inp["cache_fox_v"])[:, c].reshape(2, -1, 512)),
            "c_flf": f(inp["cache_fox_logf"][:, c]),
            "st_re": f(inp["state_s5_re"][:, c]), "st_im": f(inp["state_s5_im"][:, c]), "st_ret": f(inp["state_ret"][:, c]),
            "ln_g": f(inp["ln_g"]), "ln_b": f(inp["ln_b"]),
            "wg": f(inp["ffn_w_gate"]), "wu": f(inp["ffn_w_up"]), "wd": f(inp["ffn_w_down"]),
            "ewin": f(inp["even_w_in"]), "ewout": f(inp["even_w_out"]),
            "owin": f(inp["odd_w_in"]), "owout": f(inp["odd_w_out"]), "fox_b_f": f(inp["fox_b_f"]),
        }
        for k in ("s5_a_re", "s5_a_im", "s5_b_re", "s5_b_im", "s5_c_re", "s5_c_im", "s5_d", "s5_log_dt", "s5_w_glu",
                  "s5_b_glu", "mla_q_norm", "mla_kv_norm", "mla_w_uq", "mla_w_ukv"):
            m[k] = f(inp[k])
        m.update(consts)
        maps.append(m)
    return maps


PROMPT_OF_CORE = {0: 0, 1: 1, 4: 2, 5: 3}
CORE_OF_PROMPT = {b: c for c, b in PROMPT_OF_CORE.items()}


def gather(res, SEQ, nb_p=4, n_cores=8):
    T = NMETA + SEQ
    r = res.results
    P = lambda k: np.stack([r[CORE_OF_PROMPT[b] if n_cores == 8 else b][k] for b in range(nb_p)])
    S = lambda k: np.stack([r[c][k] for c in range(n_cores)])
    sw = lambda a: np.swapaxes(a, 0, 1)
    outs = [P("y_p"), S("y_s"),
            sw(P("p_ckv")), sw(P("p_kpe")), sw(P("p_fk")).reshape(2, nb_p, T, 8, 64), sw(P("p_fv")).reshape(2, nb_p, T, 8, 64),
            sw(P("p_flf")), sw(P("p_s5re")), sw(P("p_s5im")), sw(P("p_ret")),
            sw(S("s_ckv")), sw(S("s_kpe")), sw(S("s_fk")).reshape(2, n_cores, DSEQ, 8, 64),
            sw(S("s_fv")).reshape(2, n_cores, DSEQ, 8, 64), sw(S("s_flf")), sw(S("s_s5re")), sw(S("s_s5im")), sw(S("s_ret"))]
    return tuple(np.ascontiguousarray(o.astype(np.float32)) for o in outs)


def kernel(**inputs):
    SEQ = inputs["x_prompt"].shape[1]
    PAST = inputs["cache_mla_ckv"].shape[2]
    nc = build(SEQ, PAST)
    in_maps = make_in_maps(inputs)
    res = run_bass_kernel_spmd(nc, in_maps, core_ids=list(range(8)))
    return gather(res, SEQ)
```

```python
import math
import numpy as np
import ml_dtypes
from contextlib import ExitStack
import concourse.bass as bass
import concourse.mybir as mybir
from concourse.bass_utils import run_bass_kernel_spmd

F32 = mybir.dt.float32
BF16 = mybir.dt.bfloat16
AF = mybir.ActivationFunctionType
ALU = mybir.AluOpType
AX = mybir.AxisListType

D = 1024
DFF = 2816
NFT = DFF // 128
DEPTH = 4
ALPHA = (2.0 * DEPTH) ** 0.25
EPS = 1e-5
NMETA = 16
DSEQ = 16
NDS = 8


class Buf:
    __slots__ = ("w", "r", "excl")

    def __init__(self, excl=False):
        self.w = None
        self.r = {}
        self.excl = excl


class Sched:
    def __init__(self, nc, es):
        self.nc = nc
        self.engs = {"pe": nc.tensor, "act": nc.scalar, "dve": nc.vector, "pool": nc.gpsimd, "sp": nc.sync}
        self.sem = {k: es.enter_context(nc.semaphore("sem_" + k)) for k in ("pe", "act", "dve", "pool")}
        self.cnt = {k: 0 for k in self.sem}
        self.seen = {e: {} for e in self.engs}
        self.dq = {}
        for q in ("sp", "pool", "act"):
            self.dq[q] = [[es.enter_context(nc.semaphore("dma_%s_%d" % (q, i))), 0] for i in range(NDS)]
        self.dqi = {q: 0 for q in self.dq}
        self.out_toks = []
        self.nins = 0

    def _wait(self, e, tok):
        if tok is None:
            return
        key, sem, val = tok
        if e == "pe" and key == "pe":
            return
        if self.seen[e].get(key, 0) >= val:
            return
        self.engs[e].wait_ge(sem, val)
        self.seen[e][key] = val

    def _deps(self, e, reads, writes):
        for b in reads:
            self._wait(e, b.w)
            if b.excl:
                for k, t in list(b.r.items()):
                    if k != e:
                        self._wait(e, t)
        for b in writes:
            self._wait(e, b.w)
            for t in list(b.r.values()):
                self._wait(e, t)

    def _commit(self, tok, reads, writes):
        for b in reads:
            b.r[tok[0]] = tok
        for b in writes:
            b.w = tok
            b.r = {}

    def op(self, e, fn, reads=(), writes=()):
        self._deps(e, reads, writes)
        ins = fn(self.engs[e])
        self.cnt[e] += 1
        ins.then_inc(self.sem[e], 1)
        tok = (e, self.sem[e], self.cnt[e])
        self._commit(tok, reads, writes)
        self.nins += 1

    def dma(self, q, out, in_, reads=(), writes=(), is_output=False, **kw):
        idx = self.dqi[q] % NDS
        self.dqi[q] += 1
        slot = self.dq[q][idx]
        key = "d%s%d" % (q, idx)
        if slot[1] > 0:
            self._wait(q, (key, slot[0], slot[1]))
        self._deps(q, reads, writes)
        self.engs[q].dma_start(out=out, in_=in_, **kw).then_inc(slot[0], 16)
        slot[1] += 16
        tok = (key, slot[0], slot[1])
        self._commit(tok, reads, writes)
        self.nins += 1

    def barrier(self):
        toks = []
        for q in self.dq:
            for idx, slot in enumerate(self.dq[q]):
                if slot[1] > 0:
                    toks.append(("d%s%d" % (q, idx), slot[0], slot[1]))
        for e in ("pe", "act", "dve", "pool"):
            if self.cnt[e] > 0:
                toks.append((e, self.sem[e], self.cnt[e]))
        for e in self.engs:
            for t in toks:
                if e == "pe" and t[0] == "pe":
                    if self.seen[e].get("pe", 0) < t[2]:
                        self.engs[e].wait_ge(t[1], t[2])
                        self.seen[e]["pe"] = t[2]
                    continue
                self._wait(e, t)

    def finish(self):
        for q in self.dq:
            for idx, slot in enumerate(self.dq[q]):
                if slot[1] > 0:
                    self._wait("sp", ("d%s%d" % (q, idx), slot[0], slot[1]))
        for e in ("pe", "act", "dve", "pool"):
            if self.cnt[e] > 0:
                self._wait("sp", (e, self.sem[e], self.cnt[e]))


class Ctx:
    pass


def chunks(n, c):
    return [(o, min(c, n - o)) for o in range(0, n, c)]


def build(SEQ, PAST, stop_after=None):
    T = NMETA + SEQ
    R = T + DSEQ
    nc = bass.Bass("TRN2", target_bir_lowering=False)
    g = Ctx()
    g.nc, g.T, g.R, g.SEQ, g.PAST = nc, T, R, SEQ, PAST

    def din(name, shape, dt=F32):
        return nc.dram_tensor(name, list(shape), dt, kind="ExternalInput").ap()

    def dout(name, shape, dt=F32):
        return nc.dram_tensor(name, list(shape), dt, kind="ExternalOutput").ap()

    def dscr(name, shape, dt=F32):
        return nc.dram_tensor(name, list(shape), dt, kind="Internal").ap()

    I = {}
    for name, shape in input_shapes(SEQ, PAST).items():
        I[name] = din(name, shape, BF16 if name in BF16_CONSTS else F32)
    O = {}
    for name, shape in output_shapes(SEQ).items():
        O[name] = dout(name, shape)
    g.I, g.O = I, O
    g.X = dscr("X", [R, D])
    g.UT = dscr("UT", [4, 128, R], BF16)
    g.QT = dscr("QT", [8, 96, R], BF16)
    g.KTp = dscr("KTp", [8, 96, T], BF16)
    g.KTs = dscr("KTs", [8, 96, PAST + DSEQ], BF16)
    g.VAp = dscr("VAp", [T, 8 * 128], BF16)
    g.VAs = dscr("VAs", [PAST + DSEQ, 8 * 128], BF16)
    g.MIXT = dscr("MIXT", [8, 128, R], BF16)
    g.FQT = dscr("FQT", [4, 128, R], BF16)
    g.FKTp = dscr("FKTp", [4, 128, T], BF16)
    g.FKTs = dscr("FKTs", [4, 128, PAST + DSEQ], BF16)
    g.FTp = dscr("FTp", [8, T], F32)
    g.FTs = dscr("FTs", [8, PAST + DSEQ], F32)
    g.FT3p = dscr("FT3p", [8, 3, T], BF16)
    g.FT3s = dscr("FT3s", [8, 3, PAST + DSEQ], BF16)
    g.RK = dscr("RK", [R, 256], BF16)
    g.RQKT = dscr("RQKT", [4, 128, R], BF16)
    g.RV = dscr("RV", [R, 512], BF16)
    g.RG = dscr("RG", [R, 512], F32)
    g.scrb = Buf()
    g.outb = Buf()
    g.Xb = [Buf() for _ in range((R + 127) // 128)]
    with ExitStack() as es:
        s = Sched(nc, es)
        g.s = s
        g.ps = [es.enter_context(nc.psum_tensor("ps%d" % i, [128, 512], F32)) for i in range(6)]
        g.psb = [Buf(True) for _ in range(6)]
        g.pst = [es.enter_context(nc.psum_tensor("pst%d" % i, [128, 1024], BF16)) for i in range(2)]
        g.pstb = [Buf(True) for _ in range(2)]
        g.psi = 0
        g.psti = 0
        g.ident = es.enter_context(nc.sbuf_tensor("ident", [128, 128], BF16))
        g.identb = Buf()
        s.dma("sp", g.ident[:], I["ident_bf"], writes=[g.identb])
        s.dma("sp", g.X[0:NMETA, :], I["meta"], writes=xbufs(g, 0, NMETA))
        s.dma("sp", g.X[NMETA:T, :], I["xp"], writes=xbufs(g, NMETA, SEQ))
        s.dma("sp", g.X[T:R, :], I["xs"], writes=xbufs(g, T, DSEQ))
        done = False
        for l in range(DEPTH):
            ffn_stage(g, l, 0)
            if stop_after == ("ffn", l, 0):
                break
            if l % 2 == 0:
                even_stage(g, l // 2)
            else:
                odd_stage(g, l // 2)
            if stop_after == ("mix", l):
                break
            ffn_stage(g, l, 1)
            if stop_after == ("ffn", l, 1):
                break
        s.dma("sp", O["y_p"], g.X[NMETA:T, :], reads=xbufs(g, NMETA, SEQ))
        s.dma("sp", O["y_s"], g.X[T:R, :], reads=xbufs(g, T, DSEQ))
        s.finish()
    return nc


def staged(fn):
    def w(g, *a, **k):
        r = fn(g, *a, **k)
        g.s.barrier()
        return r
    return w


def xbufs(g, r0, n):
    return g.Xb[r0 // 128:(r0 + n - 1) // 128 + 1]


def next_ps(g):
    i = g.psi % len(g.ps)
    g.psi += 1
    return g.ps[i], g.psb[i]


def next_pst(g):
    i = g.psti % len(g.pst)
    g.psti += 1
    return g.pst[i], g.pstb[i]


BF16_CONSTS = ("ident_bf", "mask_mla", "mask_fox")


def input_shapes(SEQ, PAST):
    T = NMETA + SEQ
    R = T + DSEQ
    return {
        "xp": (SEQ, D), "xs": (DSEQ, D), "meta": (NMETA, D),
        "c_ckv": (2, PAST, 128), "c_kpe": (2, PAST, 32), "c_fk": (2, PAST, 512), "c_fv": (2, PAST, 512),
        "c_flf": (2, PAST, 8), "st_re": (2, 32, 64), "st_im": (2, 32, 64), "st_ret": (2, 4, 64, 128),
        "ln_g": (4, 3, D), "ln_b": (4, 3, D),
        "wg": (4, 2, D, DFF), "wu": (4, 2, D, DFF), "wd": (4, 2, DFF, D),
        "ewin": (2, D, 928), "ewout": (2, D, D),
        "s5_a_re": (2, 32, 64), "s5_a_im": (2, 32, 64), "s5_b_re": (2, 32, 64, 16), "s5_b_im": (2, 32, 64, 16),
        "s5_c_re": (2, 32, 16, 64), "s5_c_im": (2, 32, 16, 64), "s5_d": (2, 512), "s5_log_dt": (2, 32),
        "s5_w_glu": (2, 512, 512), "s5_b_glu": (2, 512),
        "mla_q_norm": (2, 256), "mla_kv_norm": (2, 128), "mla_w_uq": (2, 256, 768), "mla_w_ukv": (2, 128, 1024),
        "owin": (2, D, 3080), "owout": (2, D, D), "fox_b_f": (2, 8),
        "ident_bf": (128, 128), "ident_f": (128, 128), "mask_mla": (5, 128, 512), "mask_fox": (4, 128, 512),
        "ropeA16": (R, 4, 32), "ropeB16": (R, 4, 32), "tau": (128, 512),
        "ropeA32": (R, 8, 64), "ropeB32": (R, 8, 64), "ret_decT": (4, 128, 128), "ret_gq": (4, 128, 128), "ret_ginv": (128, 4),
    }


def output_shapes(SEQ):
    T = NMETA + SEQ
    return {
        "y_p": (SEQ, D), "y_s": (DSEQ, D),
        "p_ckv": (2, T, 128), "p_kpe": (2, T, 32), "p_fk": (2, T, 512), "p_fv": (2, T, 512), "p_flf": (2, T, 8),
        "p_s5re": (2, 32, 64), "p_s5im": (2, 32, 64), "p_ret": (2, 4, 64, 128),
        "s_ckv": (2, DSEQ, 128), "s_kpe": (2, DSEQ, 32), "s_fk": (2, DSEQ, 512), "s_fv": (2, DSEQ, 512),
        "s_flf": (2, DSEQ, 8), "s_s5re": (2, 32, 64), "s_s5im": (2, 32, 64), "s_ret": (2, 4, 64, 128),
    }


def x_load_tile(g, blk, ti, xf, xfb):
    if blk is None:
        return
    r0, nb = blk
    tl = chunks(nb, 128)
    if ti >= len(tl):
        return
    o, n = tl[ti]
    g.s.dma("sp", xf[0:n, ti, :], g.X[r0 + o:r0 + o + n, :], reads=xbufs(g, r0 + o, n), writes=[xfb[ti]])


def load_xT(g, es_tiles, r0, nb, xf, xfb, xb, xbb, xT, xTb, preloaded=False):
    s = g.s
    for ti, (o, n) in enumerate(chunks(nb, 128)):
        if not preloaded:
            s.dma("sp", xf[0:n, ti, :], g.X[r0 + o:r0 + o + n, :], reads=xbufs(g, r0 + o, n), writes=[xfb[ti]])
        xi = ti % 2
        s.op("act", lambda e, xi=xi, n=n, ti=ti: e.copy(out=xb[0:n, xi, :], in_=xf[0:n, ti, :]), reads=[xfb[ti]], writes=[xbb[xi]])
        for half in range(2):
            pt, ptb = next_pst(g)

            def tr(e, half=half, pt=pt, n=n, ti=xi):
                last = None
                for j in range(4):
                    kt = half * 4 + j
                    last = e.transpose(out=pt[:, j * 128:j * 128 + n], in_=xb[0:n, ti, kt * 128:(kt + 1) * 128],
                                       identity=g.ident[0:n, 0:n])
                return last
            s.op("pe", tr, reads=[xbb[xi], g.identb], writes=[ptb])
            s.op("dve", lambda e, half=half, pt=pt, n=n, o=o: e.tensor_copy(
                out=xT[:, half * 4:half * 4 + 4, o:o + n],
                in_=pt[:, 0:512].rearrange("p (j c) -> p j c", j=4)[:, :, 0:n]),
                reads=[ptb], writes=[xTb])


def layer_norm_rows(g, y, yb, n, gt, bt, eps, out, outb, tmp):
    s = g.s
    st, mv, rs, nm = tmp["st"], tmp["mv"], tmp["rs"], tmp["nm"]
    sb = tmp["b"]

    s.op("dve", lambda e: e.bn_stats(out=st[0:n, 0, :], in_=y[0:n, 0:512]), reads=[yb], writes=[tmp["b0"]])
    s.op("dve", lambda e: e.bn_stats(out=st[0:n, 1, :], in_=y[0:n, 512:1024]), reads=[yb], writes=[tmp["b1"]])
    s.op("dve", lambda e: e.bn_aggr(out=mv[0:n, :], in_=st[0:n, :, :].rearrange("p a b -> p (a b)")),
         reads=[tmp["b0"], tmp["b1"]], writes=[sb])
    s.op("dve", lambda e: e.tensor_scalar(out=rs[0:n, :], in0=mv[0:n, 1:2], scalar1=eps, scalar2=None,
                                          op0=ALU.add), reads=[sb], writes=[sb])
    s.op("act", lambda e: e.sqrt(out=rs[0:n, :], in_=rs[0:n, :]), reads=[sb], writes=[sb])
    s.op("dve", lambda e: e.reciprocal(out=rs[0:n, :], in_=rs[0:n, :]), reads=[sb], writes=[sb])
    s.op("dve", lambda e: e.scalar_tensor_tensor(out=nm[0:n, :], in0=mv[0:n, 0:1], scalar=-1.0, in1=rs[0:n, :],
                                                 op0=ALU.mult, op1=ALU.mult), reads=[sb], writes=[sb])
    s.op("act", lambda e: e.activation(out=y[0:n, :], in_=y[0:n, :], func=AF.Identity, bias=nm[0:n, 0:1],
                                       scale=rs[0:n, 0:1]), reads=[sb, yb], writes=[yb])
    s.op("pool", lambda e: e.tensor_tensor(out=y[0:n, :], in0=y[0:n, :], in1=gt[0:n, :], op=ALU.mult),
         reads=[yb, tmp["gb"]], writes=[yb])
    s.op("pool", lambda e: e.tensor_tensor(out=out[0:n, :], in0=y[0:n, :], in1=bt[0:n, :], op=ALU.add),
         reads=[yb, tmp["gb"]], writes=[outb])


def alloc_ln(g, es, l, i, tag):
    nc, s = g.nc, g.s
    t = {}
    t["g"] = es.enter_context(nc.sbuf_tensor("lng" + tag, [128, D], F32))
    t["bt"] = es.enter_context(nc.sbuf_tensor("lnb" + tag, [128, D], F32))
    t["st"] = es.enter_context(nc.sbuf_tensor("lnst" + tag, [128, 2, 6], F32))
    t["mv"] = es.enter_context(nc.sbuf_tensor("lnmv" + tag, [128, 2], F32))
    t["rs"] = es.enter_context(nc.sbuf_tensor("lnrs" + tag, [128, 1], F32))
    t["nm"] = es.enter_context(nc.sbuf_tensor("lnnm" + tag, [128, 1], F32))
    t["b"] = Buf()
    t["b0"] = Buf()
    t["b1"] = Buf()
    t["gb"] = Buf()
    s.dma("sp", t["g"][:], g.I["ln_g"][l, i:i + 1, :].broadcast_to([128, D]), writes=[t["gb"]])
    s.dma("sp", t["bt"][:], g.I["ln_b"][l, i:i + 1, :].broadcast_to([128, D]), writes=[t["gb"]])
    return t


@staged
def ffn_stage(g, l, i):
    nc, s, R = g.nc, g.s, g.R
    NB = 512
    with ExitStack() as es:
        tag = "f%d%d" % (l, i)
        wg = es.enter_context(nc.sbuf_tensor("wg" + tag, [128, 8, DFF], BF16))
        wu = es.enter_context(nc.sbuf_tensor("wu" + tag, [128, 8, DFF], BF16))
        wd = es.enter_context(nc.sbuf_tensor("wd" + tag, [128, NFT, D], BF16))
        wgb, wub, wdb = Buf(), Buf(), Buf()
        wgs = g.I["wg"][l, i].rearrange("(kt p) f -> p kt f", p=128)
        wus = g.I["wu"][l, i].rearrange("(kt p) f -> p kt f", p=128)
        wds = g.I["wd"][l, i].rearrange("(ft p) d -> p ft d", p=128)
        for kt in range(8):
            s.dma("pool", wg[:, kt, :], wgs[:, kt, :], writes=[wgb])
            s.dma("pool", wu[:, kt, :], wus[:, kt, :], writes=[wub])
        for ft in range(0, NFT, 2):
            s.dma("pool", wd[:, ft:ft + 2, :], wds[:, ft:ft + 2, :], writes=[wdb])
        ln = alloc_ln(g, es, l, 0 if i == 0 else 2, tag)
        xf = es.enter_context(nc.sbuf_tensor("xf" + tag, [128, 4, D], F32))
        xb = es.enter_context(nc.sbuf_tensor("xb" + tag, [128, 2, D], BF16))
        xT = es.enter_context(nc.sbuf_tensor("xT" + tag, [128, 8, NB], BF16))
        hT = es.enter_context(nc.sbuf_tensor("hT" + tag, [128, NFT, NB], BF16))
        sg = [es.enter_context(nc.sbuf_tensor("sg%d" % k + tag, [128, NB], F32)) for k in range(2)]
        y = [es.enter_context(nc.sbuf_tensor("y%d" % k + tag, [128, D], F32)) for k in range(2)]
        xfb = [Buf() for _ in range(4)]
        xbb = [Buf() for _ in range(4)]
        xTb, hTb = Buf(), Buf()
        sgb = [Buf(), Buf()]
        yb = [Buf(), Buf()]
        yi = 0
        blks = chunks(R, NB)
        for ti in range(4):
            x_load_tile(g, blks[0], ti, xf, xfb)
        for bi, (r0, nb) in enumerate(blks):
            nxt = blks[bi + 1] if bi + 1 < len(blks) else None
            load_xT(g, es, r0, nb, xf, xfb, xb, xbb, xT, xTb, preloaded=True)
            for ft in range(NFT):
                pg, pgb = next_ps(g)
                pu, pub = next_ps(g)

                def mmg(e, w=wg, p=pg, ft=ft, nb=nb):
                    last = None
                    for kt in range(8):
                        last = e.matmul(p[:, 0:nb], lhsT=w[:, kt, ft * 128:(ft + 1) * 128], rhs=xT[:, kt, 0:nb],
                                        start=(kt == 0), stop=(kt == 7))
                    return last
                s.op("pe", mmg, reads=[wgb, xTb], writes=[pgb])
                s.op("pe", lambda e, ft=ft, nb=nb, pu=pu: mmg(e, wu, pu, ft, nb), reads=[wub, xTb], writes=[pub])
                k = ft % 2
                s.op("act", lambda e, k=k, pg=pg, nb=nb: e.activation(out=sg[k][:, 0:nb], in_=pg[:, 0:nb], func=AF.Silu),
                     reads=[pgb], writes=[sgb[k]])
                s.op("dve", lambda e, k=k, pu=pu, nb=nb, ft=ft: e.tensor_tensor(out=hT[:, ft, 0:nb], in0=sg[k][:, 0:nb],
                                                                                 in1=pu[:, 0:nb], op=ALU.mult),
                     reads=[sgb[k], pub], writes=[hTb])
            for ti, (o, n) in enumerate(chunks(nb, 128)):
                yy, yyb = y[yi % 2], yb[yi % 2]
                yi += 1
                for half in range(2):
                    po, pob = next_ps(g)

                    def mmd(e, po=po, o=o, n=n, half=half):
                        last = None
                        for ft in range(NFT):
                            last = e.matmul(po[0:n, :], lhsT=hT[:, ft, o:o + n], rhs=wd[:, ft, half * 512:(half + 1) * 512],
                                            start=(ft == 0), stop=(ft == NFT - 1))
                        return last
                    s.op("pe", mmd, reads=[hTb, wdb], writes=[pob])
                    s.op("dve", lambda e, po=po, n=n, half=half, ti=ti, yy=yy: e.scalar_tensor_tensor(
                        out=yy[0:n, half * 512:(half + 1) * 512], in0=xf[0:n, ti, half * 512:(half + 1) * 512],
                        scalar=2.0 * ALPHA, in1=po[0:n, :], op0=ALU.mult, op1=ALU.add),
                        reads=[pob, xfb[ti]], writes=[yyb])
                x_load_tile(g, nxt, ti, xf, xfb)
                layer_norm_rows(g, yy, yyb, n, ln["g"], ln["bt"], 4.0 * EPS, yy, yyb, ln)
                s.dma("sp", g.X[r0 + o:r0 + o + n, :], yy[0:n, :], reads=[yyb], writes=xbufs(g, r0 + o, n))
            for ti in range(len(chunks(nb, 128)), 4):
                x_load_tile(g, nxt, ti, xf, xfb)


TWO_PI = 2.0 * math.pi
MAGIC = 12582912.0
SCALE_MLA = 96.0 ** -0.5


def T_(g, es, name, shape, dt):
    return es.enter_context(g.nc.sbuf_tensor(name, list(shape), dt))


def range_reduce(s, e1, out, x, tmpb, n=128):
    s.op(e1, lambda e: e.tensor_scalar(out=out, in0=x, scalar1=1.0 / TWO_PI, scalar2=MAGIC, op0=ALU.mult, op1=ALU.add),
         reads=[tmpb], writes=[tmpb])
    s.op(e1, lambda e: e.tensor_scalar(out=out, in0=out, scalar1=MAGIC, scalar2=TWO_PI, op0=ALU.subtract, op1=ALU.mult),
         reads=[tmpb], writes=[tmpb])
    s.op(e1, lambda e: e.tensor_tensor(out=out, in0=x, in1=out, op=ALU.subtract), reads=[tmpb], writes=[tmpb])
    s.op(e1, lambda e: e.tensor_scalar(out=out, in0=out, scalar1=3.1415925, scalar2=-3.1415925, op0=ALU.min, op1=ALU.max),
         reads=[tmpb], writes=[tmpb])


def rms_rows(g, src, n, width, gt, out, tmp, tb, reads, writes):
    s = g.s
    sq, ss = tmp["sq"], tmp["ss"]
    s.op("act", lambda e: e.square(out=sq[0:n, 0:width], in_=src), reads=reads, writes=[tb])
    s.op("dve", lambda e: e.reduce_sum(out=ss[0:n, :], in_=sq[0:n, 0:width], axis=AX.X), reads=[tb], writes=[tb])
    s.op("dve", lambda e: e.tensor_scalar(out=ss[0:n, :], in0=ss[0:n, :], scalar1=1.0 / width, scalar2=EPS,
                                          op0=ALU.mult, op1=ALU.add), reads=[tb], writes=[tb])
    s.op("act", lambda e: e.sqrt(out=ss[0:n, :], in_=ss[0:n, :]), reads=[tb], writes=[tb])
    s.op("dve", lambda e: e.reciprocal(out=ss[0:n, :], in_=ss[0:n, :]), reads=[tb], writes=[tb])
    s.op("dve", lambda e: e.scalar_tensor_tensor(out=out, in0=src, scalar=ss[0:n, 0:1], in1=gt[0:n, 0:width],
                                                 op0=ALU.mult, op1=ALU.mult), reads=list(reads) + [tb], writes=writes)


def rope_rows(g, src, n, H, half, ra, rb, out, tmpA, tmpB, tb, reads, writes):
    s = g.s
    s.op("dve", lambda e: e.tensor_tensor(out=tmpA[0:n], in0=src, in1=ra[0:n], op=ALU.mult), reads=reads, writes=[tb])
    s.op("dve", lambda e: e.tensor_tensor(out=tmpB[0:n, :, 0:half], in0=src[:, :, half:2 * half], in1=rb[0:n, :, 0:half],
                                          op=ALU.mult), reads=list(reads) + [tb], writes=[tb])
    s.op("dve", lambda e: e.tensor_tensor(out=tmpB[0:n, :, half:2 * half], in0=src[:, :, 0:half],
                                          in1=rb[0:n, :, half:2 * half], op=ALU.mult), reads=list(reads) + [tb], writes=[tb])
    s.op("pool", lambda e: e.tensor_tensor(out=out, in0=tmpA[0:n], in1=tmpB[0:n], op=ALU.add), reads=[tb], writes=writes)


def split_rows(g, r0, n):
    T = g.T
    out = []
    a, b = r0, min(r0 + n, T)
    if b > a:
        out.append(("p", a - r0, a, b - a))
    a, b = max(r0, T), r0 + n
    if b > a:
        out.append(("s", a - r0, a - T, b - a))
    return out


@staged
def attention(g, name, H, dq, QT, q0, nq, KT, VA, nk, kind, scale, out_base, fox=None):
    nc, s = g.nc, g.s
    with ExitStack() as es:
        nkt = (nk + 127) // 128
        Qh = [T_(g, es, name + "Q%d" % k, [128, nq], BF16) for k in range(2)]
        Kh = [T_(g, es, name + "K%d" % k, [128, nk], BF16) for k in range(2)]
        Vh = [T_(g, es, name + "V%d" % k, [128, nkt, 128], BF16) for k in range(2)]
        Qb, Kb, Vb = [Buf(), Buf()], [Buf(), Buf()], [Buf(), Buf()]
        NPT = 4
        pT = [T_(g, es, name + "pT%d" % k, [128, 512], BF16) for k in range(NPT)]
        pTb = [Buf() for _ in range(NPT)]
        osb = [T_(g, es, name + "o%d" % k, [128, 512], F32) for k in range(2)]
        osbb = [Buf(), Buf()]
        obf = [T_(g, es, name + "ob%d" % k, [64, 512], BF16) for k in range(2)]
        obfb = [Buf(), Buf()]
        ones = T_(g, es, name + "ones", [128, 128], F32)
        ones3 = T_(g, es, name + "ones3", [4, 128], BF16)
        onesb = Buf()
        s.op("pool", lambda e: e.memset(ones[:], 1.0), writes=[onesb])
        s.op("pool", lambda e: e.memset(ones3[:], 1.0), reads=[onesb], writes=[onesb])
        nmask = 5 if kind.startswith("mla") else 4
        msk = T_(g, es, name + "msk", [128, nmask, 512], BF16)
        mskb = Buf()
        s.dma("sp", msk[:], g.I["mask_mla" if kind.startswith("mla") else "mask_fox"].rearrange("m p c -> p m c"), writes=[mskb])
        if fox is not None:
            fq = [T_(g, es, name + "fq%d" % k, [4, nq], BF16) for k in range(2)]
            nfk = [T_(g, es, name + "nfk%d" % k, [128, nkt], F32) for k in range(2)]
            fqb, nfkb = [Buf(), Buf()], [Buf(), Buf()]
            for k in range(2):
                s.op("pool", lambda e, k=k: e.memset(nfk[k][:], 0.0), writes=[nfkb[k]])
        for k in range(2):
            s.op("pool", lambda e, k=k: e.memset(Qh[k][64:128, :], 0.0), writes=[Qb[k]])
            s.op("pool", lambda e, k=k: e.memset(Kh[k][64:128, :], 1.0 if fox is not None else 0.0), writes=[Kb[k]])
        si = 0
        oi = 0
        deferred = []
        nfull = nk // 128

        def head_loads(h):
            if h >= H:
                return
            hb = h % 2
            Q_, K_, V_ = Qh[hb], Kh[hb], Vh[hb]
            s.dma("sp", Q_[0:dq, :], QT[h, :, q0:q0 + nq], reads=[g.scrb], writes=[Qb[hb]])
            s.dma("sp", K_[0:dq, :], KT[h, :, 0:nk], reads=[g.scrb], writes=[Kb[hb]])
            if nfull:
                s.dma("sp", V_[:, 0:nfull, :], VA[0:nfull * 128, h * 128:(h + 1) * 128].rearrange("(t p) c -> p t c", p=128),
                      reads=[g.scrb], writes=[Vb[hb]])
            if nk % 128:
                s.dma("sp", V_[0:nk % 128, nfull, :], VA[nfull * 128:nk, h * 128:(h + 1) * 128], reads=[g.scrb], writes=[Vb[hb]])
            if fox is not None:
                nfk_ = nfk[hb]
                s.dma("sp", Q_[64:67, :], fox["FT3"][h, :, fox["qoff"]:fox["qoff"] + nq], reads=[g.scrb], writes=[Qb[hb]])
                if nfull:
                    s.dma("sp", nfk_[:, 0:nfull], fox["FT"][h, 0:nfull * 128].rearrange("(t p) -> p t", p=128),
                          reads=[g.scrb], writes=[nfkb[hb]], allow_slow_non_contiguous=True)
                if nk % 128:
                    s.dma("sp", nfk_[0:nk % 128, nfull:nfull + 1], fox["FT"][h, nfull * 128:nk].rearrange("(p o) -> p o", o=1),
                          reads=[g.scrb], writes=[nfkb[hb]], allow_slow_non_contiguous=True)
                s.op("dve", lambda e, nfk_=nfk_: e.tensor_scalar(out=nfk_[:, :], in0=nfk_[:, :], scalar1=-1.0, scalar2=None, op0=ALU.mult),
                     reads=[nfkb[hb]], writes=[nfkb[hb]])
        head_loads(0)
        for h in range(H):
            hb = h % 2
            Q_, K_, V_ = Qh[hb], Kh[hb], Vh[hb]
            if fox is not None:
                nfk_ = nfk[hb]
            head_loads(h + 1)
            flat = []
            for (qo, nqb) in chunks(nq, 512):
                j = qo // 512
                vis = []
                for i in range(nkt):
                    kn = min(128, nk - i * 128)
                    if kind in ("mla_p", "fox_p"):
                        d = i - 4 * j
                        if d < 0:
                            vis.append((i, kn, None))
                        elif d < nmask:
                            vis.append((i, kn, d))
                    elif kind == "mla_s":
                        vis.append((i, kn, None))
                    else:
                        vis.append((i, kn, 0 if i * 128 >= fox["qoff"] else None))
                for vi, (i, kn, d) in enumerate(vis):
                    flat.append((qo, nqb, vi, len(vis), i, kn, d))
            base = si

            def issue_S(idx):
                qo, nqb, vi, nv, i, kn, d = flat[idx]
                ps_, psb_ = g.ps[(base + idx) % 4], g.psb[(base + idx) % 4]

                def mm_s(e):
                    last = e.matmul(ps_[0:kn, 0:nqb], lhsT=K_[:, i * 128:i * 128 + kn], rhs=Q_[:, qo:qo + nqb], start=True, stop=(d is None))
                    if d is not None:
                        last = e.matmul(ps_[0:kn, 0:nqb], lhsT=g.ident[0:kn, 0:kn], rhs=msk[0:kn, d, 0:nqb], start=False, stop=True)
                    return last
                rd = [Qb[hb], Kb[hb], mskb, g.identb]
                s.op("pe", mm_s, reads=rd, writes=[psb_])

            LOOK = 3
            for idx in range(min(LOOK, len(flat))):
                issue_S(idx)
            for idx, (qo, nqb, vi, nv, i, kn, d) in enumerate(flat):
                ps_, psb_ = g.ps[(base + idx) % 4], g.psb[(base + idx) % 4]
                pt_, ptb_ = pT[(base + idx) % NPT], pTb[(base + idx) % NPT]
                if vi == 0:
                    po, pob = g.ps[4 + oi % 2], g.psb[4 + oi % 2]
                    o_, ob_ = osb[oi % 2], osbb[oi % 2]
                    f_, fb_ = obf[oi % 2], obfb[oi % 2]
                    oi += 1
                if fox is None:
                    s.op("act", lambda e, pt_=pt_, ps_=ps_, kn=kn, nqb=nqb: e.activation(
                        out=pt_[0:kn, 0:nqb], in_=ps_[0:kn, 0:nqb], func=AF.Exp, scale=scale), reads=[psb_], writes=[ptb_])
                else:
                    s.op("act", lambda e, pt_=pt_, ps_=ps_, kn=kn, nqb=nqb, i=i: e.activation(
                        out=pt_[0:kn, 0:nqb], in_=ps_[0:kn, 0:nqb], func=AF.Exp, scale=1.0, bias=nfk_[0:kn, i:i + 1]),
                        reads=[psb_, nfkb[hb]], writes=[ptb_])
                if idx + LOOK < len(flat):
                    issue_S(idx + LOOK)
                for fn in deferred:
                    fn()
                deferred = []
                s.op("pe", lambda e, po=po, pt_=pt_, i=i, kn=kn, nqb=nqb, vi=vi, nv=nv: e.matmul(
                    po[:, 0:nqb], lhsT=V_[0:kn, i, :], rhs=pt_[0:kn, 0:nqb], start=(vi == 0), stop=(vi == nv - 1)),
                    reads=[Vb[hb], ptb_], writes=[pob])
                if vi == nv - 1:
                    s.op("act", lambda e, o_=o_, po=po, nqb=nqb: e.copy(out=o_[0:65, 0:nqb], in_=po[0:65, 0:nqb]),
                         reads=[pob], writes=[ob_])
                    s.op("dve", lambda e, o_=o_, nqb=nqb: e.reciprocal(out=o_[64:65, 0:nqb], in_=o_[64:65, 0:nqb]),
                         reads=[ob_], writes=[ob_])

                    def fin(po=po, pob=pob, o_=o_, ob_=ob_, f_=f_, fb_=fb_, nqb=nqb, qo=qo, h=h):
                        s.op("pe", lambda e: e.matmul(po[0:64, 0:nqb], lhsT=ones[64:65, 0:64], rhs=o_[64:65, 0:nqb], start=True, stop=True),
                             reads=[ob_, onesb], writes=[pob])
                        s.op("dve", lambda e: e.tensor_tensor(out=f_[0:64, 0:nqb], in0=o_[0:64, 0:nqb], in1=po[0:64, 0:nqb], op=ALU.mult),
                             reads=[ob_, pob], writes=[fb_])
                        fr = out_base + 64 * h
                        s.dma("sp", g.MIXT[fr // 128, fr % 128:fr % 128 + 64, q0 + qo:q0 + qo + nqb], f_[0:64, 0:nqb],
                              reads=[fb_], writes=[g.scrb])
                    deferred.append(fin)
            si = base + len(flat)
        for fn in deferred:
            fn()


@staged
def out_proj_ln(g, tag, wsrc, l, eps):
    nc, s, R = g.nc, g.s, g.R
    with ExitStack() as es:
        wo = T_(g, es, "wo" + tag, [128, 8, D], BF16)
        wob = Buf()
        ws = wsrc.rearrange("(kt p) d -> p kt d", p=128)
        for kt in range(8):
            s.dma("pool", wo[:, kt, :], ws[:, kt, :], writes=[wob])
        ln = alloc_ln(g, es, l, 1, tag)
        NBUF = 4
        mt = [T_(g, es, "mt%d" % k + tag, [128, 8, 128], BF16) for k in range(NBUF)]
        xf = [T_(g, es, "xo%d" % k + tag, [128, D], F32) for k in range(NBUF)]
        y = [T_(g, es, "yo%d" % k + tag, [128, D], F32) for k in range(NBUF)]
        mtb, xfb, yb = [Buf() for _ in range(NBUF)], [Buf() for _ in range(NBUF)], [Buf() for _ in range(NBUF)]
        tiles = chunks(R, 128)

        def loads(ti):
            if ti >= len(tiles):
                return
            r0, n = tiles[ti]
            k = ti % NBUF
            s.dma("sp", mt[k][:, :, 0:n], g.MIXT[:, :, r0:r0 + n].rearrange("kt p c -> p kt c"), reads=[g.scrb], writes=[mtb[k]])
            s.dma("sp", xf[k][0:n, :], g.X[r0:r0 + n, :], reads=xbufs(g, r0, n), writes=[xfb[k]])
        loads(0)
        loads(1)
        for ti, (r0, n) in enumerate(tiles):
            k = ti % NBUF
            loads(ti + 2)
            for half in range(2):
                po, pob = next_ps(g)

                def mm(e, po=po, k=k, n=n, half=half):
                    last = None
                    for kt in range(8):
                        last = e.matmul(po[0:n, :], lhsT=mt[k][:, kt, 0:n], rhs=wo[:, kt, half * 512:(half + 1) * 512],
                                        start=(kt == 0), stop=(kt == 7))
                    return last
                s.op("pe", mm, reads=[mtb[k], wob], writes=[pob])
                s.op("dve", lambda e, po=po, k=k, n=n, half=half: e.scalar_tensor_tensor(
                    out=y[k][0:n, half * 512:(half + 1) * 512], in0=xf[k][0:n, half * 512:(half + 1) * 512], scalar=ALPHA,
                    in1=po[0:n, :], op0=ALU.mult, op1=ALU.add), reads=[pob, xfb[k]], writes=[yb[k]])
            layer_norm_rows(g, y[k], yb[k], n, ln["g"], ln["bt"], eps, y[k], yb[k], ln)
            s.dma("sp", g.X[r0:r0 + n, :], y[k][0:n, :], reads=[yb[k]], writes=xbufs(g, r0, n))


def even_stage(g, e):
    import os
    lim = int(os.environ.get("EV_STOP", "9"))
    even_proj(g, e)
    if lim <= 1:
        return
    s5_stage(g, e)
    if lim <= 2:
        return
    T, R, PAST = g.T, g.R, g.PAST
    attention(g, "ap%d" % e, 8, 96, g.QT, 0, T, g.KTp, g.VAp, T, "mla_p", SCALE_MLA, 512)
    attention(g, "as%d" % e, 8, 96, g.QT, T, DSEQ, g.KTs, g.VAs, PAST + DSEQ, "mla_s", SCALE_MLA, 512)
    out_proj_ln(g, "eo%d" % e, g.I["ewout"][e], 2 * e, EPS)


@staged
def even_proj(g, e):
    nc, s, R, T, PAST = g.nc, g.s, g.R, g.T, g.PAST
    NB = 512
    I, O = g.I, g.O
    with ExitStack() as es:
        tag = "ep%d" % e
        win = T_(g, es, "win" + tag, [128, 8, 928], BF16)
        wuq = T_(g, es, "wuq" + tag, [128, 2, 768], BF16)
        wukv = T_(g, es, "wukv" + tag, [128, 1024], BF16)
        wb = Buf()
        wins = I["ewin"][e].rearrange("(kt p) f -> p kt f", p=128)
        for kt in range(8):
            s.dma("pool", win[:, kt, :], wins[:, kt, :], writes=[wb])
        s.dma("pool", wuq[:], I["mla_w_uq"][e].rearrange("(kt p) f -> p kt f", p=128), writes=[wb])
        s.dma("pool", wukv[:], I["mla_w_ukv"][e], writes=[wb])
        gq = T_(g, es, "gq" + tag, [128, 256], F32)
        gkv = T_(g, es, "gkv" + tag, [128, 128], F32)
        s.dma("sp", gq[:], I["mla_q_norm"][e:e + 1, :].broadcast_to([128, 256]), writes=[wb])
        s.dma("sp", gkv[:], I["mla_kv_norm"][e:e + 1, :].broadcast_to([128, 128]), writes=[wb])
        xf = T_(g, es, "xf" + tag, [128, 4, D], F32)
        xb = T_(g, es, "xb" + tag, [128, 2, D], BF16)
        xT = T_(g, es, "xT" + tag, [128, 8, NB], BF16)
        xfb, xbb, xTb = [Buf() for _ in range(4)], [Buf(), Buf()], Buf()
        uT = T_(g, es, "uT" + tag, [128, 4, NB], BF16)
        uTb = Buf()
        cT = T_(g, es, "cT" + tag, [128, 3, NB], BF16)
        kpT = T_(g, es, "kpT" + tag, [128, NB], BF16)
        cTb = Buf()
        sq = T_(g, es, "sq" + tag, [128, 256], F32)
        ss = T_(g, es, "ss" + tag, [128, 1], F32)
        tmp = {"sq": sq, "ss": ss}
        tb = Buf()
        qn = T_(g, es, "qn" + tag, [128, 256], BF16)
        ckf = T_(g, es, "ckf" + tag, [128, 128], F32)
        ckb = T_(g, es, "ckb" + tag, [128, 128], BF16)
        kpf = T_(g, es, "kpf" + tag, [128, 1, 32], F32)
        kq = T_(g, es, "kq" + tag, [128, 96], BF16)
        rwb = Buf()
        s.op("pool", lambda e_: e_.memset(kq[:], 0.0), writes=[rwb])
        ra4 = T_(g, es, "ra" + tag, [128, 4, 4, 32], F32)
        rbt4 = T_(g, es, "rb" + tag, [128, 4, 4, 32], F32)
        rab4 = [Buf() for _ in range(4)]
        tA = T_(g, es, "tA" + tag, [128, 4, 32], F32)
        tB = T_(g, es, "tB" + tag, [128, 4, 32], F32)
        tAb = Buf()
        qb = T_(g, es, "qb" + tag, [128, 8, 96], BF16)
        qbb = Buf()
        QTs = T_(g, es, "QTs" + tag, [128, 8, NB], BF16)
        KTs_ = T_(g, es, "KTs" + tag, [128, 8, NB], BF16)
        VAs_ = T_(g, es, "VAs" + tag, [128, 4, 8, 128], BF16)
        QTb, KTb, VAb = Buf(), Buf(), Buf()
        s.op("pool", lambda e_: e_.memset(VAs_[:], 1.0), writes=[VAb])
        pcf = T_(g, es, "pcf" + tag, [128, 128], F32)
        pkf = T_(g, es, "pkf" + tag, [128, 32], F32)
        pcb = Buf()

        def kv_project(nb, dests):
            if "nokv" in dbg:
                return
            kv_project_(nb, dests)

        def kv_project_(nb, dests):
            for h in range(8):
                pk, pkb = next_ps(g)
                s.op("pe", lambda e_, pk=pk, h=h: e_.matmul(pk[0:64, 0:nb], lhsT=wukv[:, h * 128:h * 128 + 64], rhs=cT[:, 2, 0:nb],
                                                             start=True, stop=True), reads=[wb, cTb], writes=[pkb])
                s.op("act", lambda e_, pk=pk, h=h: e_.copy(out=KTs_[0:64, h, 0:nb], in_=pk[0:64, 0:nb]), reads=[pkb], writes=[KTb])
                s.op("pool", lambda e_, h=h: e_.tensor_copy(out=KTs_[64:96, h, 0:nb], in_=kpT[64:96, 0:nb]), reads=[cTb], writes=[KTb])
            for ti, (o, n) in enumerate(chunks(nb, 128)):
                pv, pvb = next_ps(g)
                s.op("pe", lambda e_, pv=pv, o=o, n=n: e_.matmul(
                    pv[0:n, :], lhsT=cT[:, 2, o:o + n], rhs=wukv[:, :].rearrange("p (h c) -> p h c", c=128)[:, :, 64:128],
                    start=True, stop=True), reads=[wb, cTb], writes=[pvb])
                s.op("dve", lambda e_, pv=pv, n=n, ti=ti: e_.tensor_copy(
                    out=VAs_[0:n, ti, :, 0:64], in_=pv[0:n, :].rearrange("p (h c) -> p h c", c=64)), reads=[pvb], writes=[VAb])
            for (lo, KTd, VAd, dr, cnt) in dests:
                s.dma("sp", KTd[:, :, dr:dr + cnt].rearrange("h p c -> p h c"), KTs_[0:96, :, lo:lo + cnt], reads=[KTb], writes=[g.scrb])
                a = lo
                while a < lo + cnt:
                    ti = a // 128
                    b = min(lo + cnt, (ti + 1) * 128)
                    s.dma("sp", VAd[dr + a - lo:dr + b - lo, :], VAs_[a - ti * 128:b - ti * 128, ti, :, :].rearrange("p h c -> p (h c)"),
                          reads=[VAb], writes=[g.scrb])
                    a = b

        import os
        dbg = os.environ.get("EP_DBG", "")
        pblks = chunks(PAST if "nopast" not in dbg else 0, NB)
        pcf4 = [T_(g, es, "pcf4_%d" % k + tag, [128, 4, 128], F32) for k in range(2)]
        pkf4 = [T_(g, es, "pkf4_%d" % k + tag, [128, 4, 32], F32) for k in range(2)]
        pcb4 = [[Buf() for _ in range(4)] for _ in range(2)]

        def past_loads(bi):
            if bi >= len(pblks):
                return
            k0_, nb_ = pblks[bi]
            for ti_, (o_, n_) in enumerate(chunks(nb_, 128)):
                s.dma("sp", pcf4[bi % 2][0:n_, ti_, :], I["c_ckv"][e, k0_ + o_:k0_ + o_ + n_, :], writes=[pcb4[bi % 2][ti_]])
                s.dma("sp", pkf4[bi % 2][0:n_, ti_, :], I["c_kpe"][e, k0_ + o_:k0_ + o_ + n_, :], writes=[pcb4[bi % 2][ti_]])
        past_loads(0)
        for bi, (k0, nb) in enumerate(pblks):
            past_loads(bi + 1)
            for ti, (o, n) in enumerate(chunks(nb, 128)):
                pcf, pkf, pcb = pcf4[bi % 2][:, ti, :], pkf4[bi % 2][:, ti, :], pcb4[bi % 2][ti]
                s.op("act", lambda e_, n=n, pcf=pcf: e_.copy(out=ckb[0:n, :], in_=pcf[0:n, :]), reads=[pcb], writes=[rwb])
                s.op("act", lambda e_, n=n, pkf=pkf: e_.copy(out=kq[0:n, 64:96], in_=pkf[0:n, :]), reads=[pcb], writes=[rwb])
                pt, ptb = next_pst(g)

                def tr(e_, pt=pt, n=n):
                    e_.transpose(out=pt[:, 0:n], in_=ckb[0:n, :], identity=g.ident[0:n, 0:n])
                    return e_.transpose(out=pt[0:96, 128:128 + n], in_=kq[0:n, :], identity=g.ident[0:n, 0:n])
                s.op("pe", tr, reads=[rwb, g.identb], writes=[ptb])
                s.op("dve", lambda e_, pt=pt, o=o, n=n: e_.tensor_copy(out=cT[:, 2, o:o + n], in_=pt[:, 0:n]), reads=[ptb], writes=[cTb])
                s.op("dve", lambda e_, pt=pt, o=o, n=n: e_.tensor_copy(out=kpT[64:96, o:o + n], in_=pt[64:96, 128:128 + n]),
                     reads=[ptb], writes=[cTb])
            kv_project(nb, [(0, g.KTs, g.VAs, k0, nb)])

        ublks = chunks(R, NB)
        for ti in range(4):
            x_load_tile(g, ublks[0], ti, xf, xfb)
        for bi, (r0, nb) in enumerate(ublks):
            load_xT(g, es, r0, nb, xf, xfb, xb, xbb, xT, xTb, preloaded=True)
            for ti in range(4):
                x_load_tile(g, ublks[bi + 1] if bi + 1 < len(ublks) else None, ti, xf, xfb)
            for ti, (o, n) in enumerate(chunks(nb, 128)):
                s.dma("sp", ra4[0:n, ti], I["ropeA16"][r0 + o:r0 + o + n, :, :], writes=[rab4[ti]])
                s.dma("sp", rbt4[0:n, ti], I["ropeB16"][r0 + o:r0 + o + n, :, :], writes=[rab4[ti]])
            for ft in range(4):
                pu, pub = next_ps(g)

                def mmu(e_, pu=pu, ft=ft):
                    last = None
                    for kt in range(8):
                        last = e_.matmul(pu[:, 0:nb], lhsT=win[:, kt, ft * 128:(ft + 1) * 128], rhs=xT[:, kt, 0:nb],
                                         start=(kt == 0), stop=(kt == 7))
                    return last
                s.op("pe", mmu, reads=[wb, xTb], writes=[pub])
                s.op("act", lambda e_, pu=pu, ft=ft: e_.copy(out=uT[:, ft, 0:nb], in_=pu[:, 0:nb]), reads=[pub], writes=[uTb])
            s.dma("sp", g.UT[:, :, r0:r0 + nb].rearrange("t p c -> p t c"), uT[:, :, 0:nb], reads=[uTb], writes=[g.scrb])
            def ph1a(ti, o, n):
                rr = r0 + o
                pm, pmb = next_ps(g)

                def mmt(e_, pm=pm, o=o, n=n):
                    last = None
                    for kt in range(8):
                        last = e_.matmul(pm[0:n, 0:416], lhsT=xT[:, kt, o:o + n], rhs=win[:, kt, 512:928],
                                         start=(kt == 0), stop=(kt == 7))
                    return last
                s.op("pe", mmt, reads=[wb, xTb], writes=[pmb])
                ra, rbt, rab = ra4[:, ti], rbt4[:, ti], rab4[ti]
                rms_rows(g, pm[0:n, 0:256], n, 256, gq, qn[0:n, :], tmp, tb, [pmb, wb], [rwb])
                rms_rows(g, pm[0:n, 256:384], n, 128, gkv, ckf[0:n, :], tmp, tb, [pmb, wb], [rwb])
                s.op("act", lambda e_, n=n: e_.copy(out=ckb[0:n, :], in_=ckf[0:n, :]), reads=[rwb], writes=[rwb])
                rope_rows(g, pm[0:n, 384:416].rearrange("p (h c) -> p h c", h=1), n, 1, 16, ra[:, 0:1, :], rbt[:, 0:1, :],
                          kpf[0:n], tA[:, 0:1, :], tB[:, 0:1, :], tAb, [pmb, rab], [rwb])
                s.op("act", lambda e_, n=n: e_.copy(out=kq[0:n, 64:96], in_=kpf[0:n, 0, :]), reads=[rwb], writes=[rwb])
                for (kd, lo, dr, cnt) in split_rows(g, rr, n):
                    pre = "p_" if kd == "p" else "s_"
                    s.dma("sp", O[pre + "ckv"][e, dr:dr + cnt, :], ckf[lo:lo + cnt, :], reads=[rwb], writes=[g.outb])
                    s.dma("sp", O[pre + "kpe"][e, dr:dr + cnt, :], kpf[lo:lo + cnt, 0, :], reads=[rwb], writes=[g.outb])
            def ph1b(ti, o, n):
                rr = r0 + o
                pt, ptb = next_pst(g)

                def tr2(e_, pt=pt, n=n):
                    e_.transpose(out=pt[:, 0:n], in_=qn[0:n, 0:128], identity=g.ident[0:n, 0:n])
                    e_.transpose(out=pt[:, 128:128 + n], in_=qn[0:n, 128:256], identity=g.ident[0:n, 0:n])
                    e_.transpose(out=pt[:, 256:256 + n], in_=ckb[0:n, :], identity=g.ident[0:n, 0:n])
                    return e_.transpose(out=pt[0:96, 384:384 + n], in_=kq[0:n, :], identity=g.ident[0:n, 0:n])
                s.op("pe", tr2, reads=[rwb, g.identb], writes=[ptb])
                s.op("dve", lambda e_, pt=pt, o=o, n=n: e_.tensor_copy(
                    out=cT[:, :, o:o + n], in_=pt[:, 0:384].rearrange("p (j c) -> p j c", j=3)[:, :, 0:n]), reads=[ptb], writes=[cTb])
                s.op("dve", lambda e_, pt=pt, o=o, n=n: e_.tensor_copy(out=kpT[64:96, o:o + n], in_=pt[64:96, 384:384 + n]),
                     reads=[ptb], writes=[cTb])
            def ph2(ti, o, n):
                rr = r0 + o
                ra, rbt, rab = ra4[:, ti], rbt4[:, ti], rab4[ti]
                for hf in range(2):
                    pq, pqb = next_ps(g)

                    def mmq(e_, pq=pq, o=o, n=n, hf=hf):
                        e_.matmul(pq[0:n, 0:384], lhsT=cT[:, 0, o:o + n], rhs=wuq[:, 0, hf * 384:(hf + 1) * 384], start=True, stop=False)
                        return e_.matmul(pq[0:n, 0:384], lhsT=cT[:, 1, o:o + n], rhs=wuq[:, 1, hf * 384:(hf + 1) * 384],
                                         start=False, stop=True)
                    s.op("pe", mmq, reads=[cTb, wb], writes=[pqb])
                    pq3 = pq[0:n, 0:384].rearrange("p (h c) -> p h c", c=96)
                    s.op("act", lambda e_, pq3=pq3, n=n, hf=hf: e_.copy(out=qb[0:n, hf * 4:hf * 4 + 4, 0:64], in_=pq3[:, :, 0:64]),
                         reads=[pqb], writes=[qbb])
                    rope_rows(g, pq3[:, :, 64:96], n, 4, 16, ra, rbt, qb[0:n, hf * 4:hf * 4 + 4, 64:96], tA, tB, tAb,
                              [pqb, rab], [qbb])
                pt, ptb = next_pst(g)

                def tr3(e_, pt=pt, n=n):
                    last = None
                    for h in range(8):
                        last = e_.transpose(out=pt[0:96, h * 128:h * 128 + n], in_=qb[0:n, h, :], identity=g.ident[0:n, 0:n])
                    return last
                s.op("pe", tr3, reads=[qbb, g.identb], writes=[ptb])
                s.op("dve", lambda e_, pt=pt, o=o, n=n: e_.tensor_copy(
                    out=QTs[0:96, :, o:o + n], in_=pt[0:96, :].rearrange("p (h c) -> p h c", h=8)[:, :, 0:n]), reads=[ptb], writes=[QTb])
            tl = chunks(nb, 128)
            ph1a(0, *tl[0])
            ph1b(0, *tl[0])
            for idx in range(len(tl)):
                if idx + 1 < len(tl):
                    ph1a(idx + 1, *tl[idx + 1])
                ph2(idx, *tl[idx])
                if idx + 1 < len(tl):
                    ph1b(idx + 1, *tl[idx + 1])
            s.dma("sp", g.QT[:, :, r0:r0 + nb].rearrange("h p c -> p h c"), QTs[0:96, :, 0:nb], reads=[QTb], writes=[g.scrb])
            dests = []
            for (kd, lo, dr, cnt) in split_rows(g, r0, nb):
                if kd == "p":
                    dests.append((lo, g.KTp, g.VAp, dr, cnt))
                else:
                    dests.append((lo, g.KTs, g.VAs, PAST + dr, cnt))
            kv_project(nb, dests)


@staged
def s5_stage(g, e):
    nc, s, R, T = g.nc, g.s, g.R, g.T
    I, O = g.I, g.O
    TC = 512
    with ExitStack() as es:
        tag = "s5%d" % e
        sm = lambda nm, w=16: T_(g, es, nm + tag, [128, w], F32)
        are, aim, ldt = sm("are"), sm("aim"), sm("ldt")
        ard, aid, mag, cs, sn, t0, t1 = sm("ard"), sm("aid"), sm("mag"), sm("cs"), sm("sn"), sm("t0"), sm("t1")
        abr, abi, nr, ni, den, fr, fi = sm("abr"), sm("abi"), sm("nr"), sm("ni"), sm("den"), sm("fr"), sm("fi")
        pb = Buf()
        vw = lambda ap: ap.rearrange("(pr g2) n -> (g2 n) pr", g2=2)
        for hp in range(2):
            s.dma("sp", are[:, hp * 8:hp * 8 + 8], vw(I["s5_a_re"][e])[:, hp * 8:hp * 8 + 8], writes=[pb], allow_slow_non_contiguous=True)
            s.dma("sp", aim[:, hp * 8:hp * 8 + 8], vw(I["s5_a_im"][e])[:, hp * 8:hp * 8 + 8], writes=[pb], allow_slow_non_contiguous=True)
        for g2 in range(2):
            s.dma("sp", ldt[g2 * 64:(g2 + 1) * 64, :],
                  I["s5_log_dt"][e:e + 1, :].rearrange("o (pr g2) -> o pr g2", g2=2)[:, :, g2].broadcast_to([64, 16]),
                  writes=[pb], allow_slow_non_contiguous=True)
        P = lambda eng, fn: s.op(eng, fn, reads=[pb], writes=[pb])
        P("act", lambda e_: e_.activation(out=ldt[:], in_=ldt[:], func=AF.Exp))
        P("dve", lambda e_: e_.tensor_tensor(out=ard[:], in0=are[:], in1=ldt[:], op=ALU.mult))
        P("dve", lambda e_: e_.tensor_tensor(out=aid[:], in0=aim[:], in1=ldt[:], op=ALU.mult))
        P("act", lambda e_: e_.activation(out=mag[:], in_=ard[:], func=AF.Exp))
        range_reduce(s, "dve", t0[:], aid[:], pb)
        P("act", lambda e_: e_.activation(out=sn[:], in_=t0[:], func=AF.Sin))
        P("dve", lambda e_: e_.tensor_scalar(out=t1[:], in0=aid[:], scalar1=math.pi / 2, scalar2=None, op0=ALU.add))
        range_reduce(s, "dve", t0[:], t1[:], pb)
        P("act", lambda e_: e_.activation(out=cs[:], in_=t0[:], func=AF.Sin))
        P("dve", lambda e_: e_.tensor_tensor(out=abr[:], in0=mag[:], in1=cs[:], op=ALU.mult))
        P("dve", lambda e_: e_.tensor_tensor(out=abi[:], in0=mag[:], in1=sn[:], op=ALU.mult))
        P("dve", lambda e_: e_.tensor_scalar(out=t0[:], in0=abr[:], scalar1=-1.0, scalar2=None, op0=ALU.add))
        P("dve", lambda e_: e_.tensor_tensor(out=nr[:], in0=t0[:], in1=are[:], op=ALU.mult))
        P("dve", lambda e_: e_.tensor_tensor(out=t1[:], in0=abi[:], in1=aim[:], op=ALU.mult))
        P("dve", lambda e_: e_.tensor_tensor(out=nr[:], in0=nr[:], in1=t1[:], op=ALU.add))
        P("dve", lambda e_: e_.tensor_tensor(out=ni[:], in0=abi[:], in1=are[:], op=ALU.mult))
        P("dve", lambda e_: e_.tensor_tensor(out=t1[:], in0=t0[:], in1=aim[:], op=ALU.mult))
        P("dve", lambda e_: e_.tensor_tensor(out=ni[:], in0=ni[:], in1=t1[:], op=ALU.subtract))
        P("dve", lambda e_: e_.tensor_tensor(out=den[:], in0=are[:], in1=are[:], op=ALU.mult))
        P("dve", lambda e_: e_.tensor_tensor(out=t1[:], in0=aim[:], in1=aim[:], op=ALU.mult))
        P("dve", lambda e_: e_.tensor_tensor(out=den[:], in0=den[:], in1=t1[:], op=ALU.add))
        P("dve", lambda e_: e_.reciprocal(out=den[:], in_=den[:]))
        P("dve", lambda e_: e_.tensor_tensor(out=fr[:], in0=nr[:], in1=den[:], op=ALU.mult))
        P("dve", lambda e_: e_.tensor_tensor(out=fi[:], in0=ni[:], in1=den[:], op=ALU.mult))
        Br = T_(g, es, "Br" + tag, [128, 16, 16], F32)
        Bi = T_(g, es, "Bi" + tag, [128, 16, 16], F32)
        Bbr = T_(g, es, "Bbr" + tag, [128, 16, 16], F32)
        Bbi = T_(g, es, "Bbi" + tag, [128, 16, 16], F32)
        Bt = T_(g, es, "Bt" + tag, [128, 16, 16], F32)
        vb = lambda ap: ap.rearrange("(pr g2) n c -> (g2 n) pr c", g2=2)
        s.dma("sp", Br[:], vb(I["s5_b_re"][e]), writes=[pb])
        s.dma("sp", Bi[:], vb(I["s5_b_im"][e]), writes=[pb])
        frb = fr[:, :].unsqueeze(2).to_broadcast([128, 16, 16])
        fib = fi[:, :].unsqueeze(2).to_broadcast([128, 16, 16])
        P("dve", lambda e_: e_.tensor_tensor(out=Bbr[:], in0=Br[:], in1=frb, op=ALU.mult))
        P("dve", lambda e_: e_.tensor_tensor(out=Bt[:], in0=Bi[:], in1=fib, op=ALU.mult))
        P("dve", lambda e_: e_.tensor_tensor(out=Bbr[:], in0=Bbr[:], in1=Bt[:], op=ALU.subtract))
        P("dve", lambda e_: e_.tensor_tensor(out=Bbi[:], in0=Bi[:], in1=frb, op=ALU.mult))
        P("dve", lambda e_: e_.tensor_tensor(out=Bt[:], in0=Br[:], in1=fib, op=ALU.mult))
        P("dve", lambda e_: e_.tensor_tensor(out=Bbi[:], in0=Bbi[:], in1=Bt[:], op=ALU.add))
        BP = [T_(g, es, "BP%d" % k + tag, [128, 16, 128], F32) for k in range(2)]
        LB = [T_(g, es, "LB%d" % k + tag, [128, 16, 128], BF16) for k in range(2)]
        identf = T_(g, es, "idf" + tag, [128, 128], F32)
        s.dma("sp", identf[:], I["ident_f"], writes=[pb])
        for k, Bb in enumerate((Bbr, Bbi)):
            P("pool", lambda e_, k=k: e_.memset(BP[k][:], 0.0))
            for g2 in range(2):
                for r in range(4):
                    c0 = 32 * r + 16 * g2
                    P("pool", lambda e_, k=k, Bb=Bb, g2=g2, r=r, c0=c0: e_.tensor_copy(
                        out=BP[k][g2 * 64:(g2 + 1) * 64, r::4, c0:c0 + 16], in_=Bb[g2 * 64:(g2 + 1) * 64, r::4, :]))
            for q4 in range(4):
                pp, ppb = next_ps(g)

                def trb(e_, pp=pp, k=k, q4=q4):
                    last = None
                    for jj in range(4):
                        last = e_.transpose(out=pp[:, jj * 128:(jj + 1) * 128], in_=BP[k][:, q4 * 4 + jj, :], identity=identf[:])
                    return last
                s.op("pe", trb, reads=[pb], writes=[ppb])
                s.op("act", lambda e_, pp=pp, k=k, q4=q4: e_.copy(
                    out=LB[k][:, q4 * 4:q4 * 4 + 4, :], in_=pp[:, :].rearrange("p (j c) -> p j c", j=4)), reads=[ppb], writes=[pb])
        CPf = [T_(g, es, "CPf%d" % k + tag, [128, 16, 128], F32) for k in range(2)]
        CP = [T_(g, es, "CP%d" % k + tag, [128, 16, 128], BF16) for k in range(2)]
        for k, nm in enumerate(("s5_c_re", "s5_c_im")):
            P("pool", lambda e_, k=k: e_.memset(CPf[k][:], 0.0))
            src = I[nm][e].rearrange("(pr g2) c n -> g2 n pr c", g2=2)
            for g2 in range(2):
                for r in range(4):
                    c0 = 32 * r + 16 * g2
                    for q in range(4):
                        s.dma("sp", CPf[k][g2 * 64:(g2 + 1) * 64, r + 4 * q, c0:c0 + 16], src[g2][:, r + 4 * q, :], reads=[pb],
                              writes=[pb], allow_slow_non_contiguous=True)
        P("act", lambda e_: e_.copy(out=CP[0][:], in_=CPf[0][:]))
        P("act", lambda e_: e_.mul(out=CP[1][:], in_=CPf[1][:], mul=-1.0))
        CP.append(T_(g, es, "CP2" + tag, [128, 16, 128], BF16))
        P("act", lambda e_: e_.mul(out=CP[2][:], in_=CPf[0][:], mul=-1.0))
        dsk = sm("dsk", 4)
        bgl = sm("bgl", 4)
        s.dma("sp", dsk[:], I["s5_d"][e].rearrange("(t p) -> p t", p=128), writes=[pb], allow_slow_non_contiguous=True)
        s.dma("sp", bgl[:], I["s5_b_glu"][e].rearrange("(t p) -> p t", p=128), writes=[pb], allow_slow_non_contiguous=True)
        wgl = T_(g, es, "wgl" + tag, [128, 4, 512], BF16)
        s.dma("pool", wgl[:], I["s5_w_glu"][e].rearrange("(kt p) f -> p kt f", p=128), writes=[pb])
        tau = T_(g, es, "tau" + tag, [128, TC], F32)
        s.dma("sp", tau[:], I["tau"], writes=[pb])
        cosT = T_(g, es, "cosT" + tag, [128, 16, TC], F32)
        sinT = T_(g, es, "sinT" + tag, [128, 16, TC], F32)
        ang = T_(g, es, "ang" + tag, [128, TC], F32)
        ang2 = T_(g, es, "ang2" + tag, [128, TC], F32)
        angb = Buf()
        tabb = Buf()
        for pr in range(16):
            s.op("dve", lambda e_, pr=pr: e_.tensor_scalar(out=ang[:], in0=tau[:], scalar1=aid[:, pr:pr + 1], scalar2=None, op0=ALU.mult),
                 reads=[pb], writes=[angb])
            range_reduce(s, "dve", ang2[:], ang[:], angb)
            s.op("act", lambda e_, pr=pr: e_.activation(out=sinT[:, pr, :], in_=ang2[:], func=AF.Sin), reads=[angb], writes=[tabb])
            s.op("dve", lambda e_: e_.tensor_scalar(out=ang[:], in0=ang[:], scalar1=math.pi / 2, scalar2=None, op0=ALU.add),
                 reads=[angb], writes=[angb])
            range_reduce(s, "dve", ang2[:], ang[:], angb)
            s.op("act", lambda e_, pr=pr: e_.activation(out=cosT[:, pr, :], in_=ang2[:], func=AF.Sin), reads=[angb], writes=[tabb])
        h0r, h0i = sm("h0r"), sm("h0i")
        hb = Buf()
        uT = T_(g, es, "uTs" + tag, [128, 4, TC], BF16)
        uTb = Buf()
        Wset = [[T_(g, es, "w%d_%d" % (ss_, k) + tag, [128, TC], F32) for k in range(6)] for ss_ in range(2)]
        Wbset = [[Buf() for _ in range(6)] for _ in range(2)]
        hb4s = [T_(g, es, "hb4_%d" % k + tag, [128, 4, 4, TC], BF16) for k in range(2)]
        hbfbs = [Buf(), Buf()]
        glr, gli, ht = sm("glr"), sm("gli"), sm("ht")
        glb = Buf()
        ysb = T_(g, es, "ysb" + tag, [128, TC], F32)
        y2 = T_(g, es, "y2" + tag, [128, TC], F32)
        ysbb = Buf()
        gT = T_(g, es, "gT" + tag, [128, 4, TC], BF16)
        gTb = Buf()
        sg = T_(g, es, "sgl" + tag, [128, TC], F32)
        oT = T_(g, es, "oT" + tag, [128, 4, TC], BF16)
        oTb = Buf()
        vs = lambda ap: ap.rearrange("(pr g2) n -> (g2 n) pr", g2=2)
        for seg, (c0, c1) in enumerate(((0, T), (T, R))):
            if seg == 0:
                s.op("pool", lambda e_: e_.memset(h0r[:], 0.0), reads=[hb], writes=[hb])
                s.op("pool", lambda e_: e_.memset(h0i[:], 0.0), reads=[hb], writes=[hb])
            else:
                for hp in range(2):
                    s.dma("sp", h0r[:, hp * 8:hp * 8 + 8], vs(I["st_re"][e])[:, hp * 8:hp * 8 + 8], reads=[hb], writes=[hb], allow_slow_non_contiguous=True)
                    s.dma("sp", h0i[:, hp * 8:hp * 8 + 8], vs(I["st_im"][e])[:, hp * 8:hp * 8 + 8], reads=[hb], writes=[hb], allow_slow_non_contiguous=True)
            for (co, tc) in chunks(c1 - c0, TC):
                col = c0 + co
                s.dma("sp", uT[:, :, 0:tc], g.UT[:, :, col:col + tc].rearrange("t p c -> p t c"), reads=[g.scrb], writes=[uTb])
                TT = lambda eng, o, a, b, op, rd, wr: s.op(eng, lambda e_: e_.tensor_tensor(out=o, in0=a, in1=b, op=op), reads=rd, writes=wr)

                def stageA(pr):
                    ft = pr // 4
                    W, Wb = Wset[pr % 2], Wbset[pr % 2]
                    pbr, pbrb = g.ps[2 * (pr % 2)], g.psb[2 * (pr % 2)]
                    pbi, pbib = g.ps[2 * (pr % 2) + 1], g.psb[2 * (pr % 2) + 1]
                    s.op("pe", lambda e_: e_.matmul(pbr[:, 0:tc], lhsT=LB[0][:, pr, :], rhs=uT[:, ft, 0:tc], start=True, stop=True),
                         reads=[pb, uTb], writes=[pbrb])
                    s.op("pe", lambda e_: e_.matmul(pbi[:, 0:tc], lhsT=LB[1][:, pr, :], rhs=uT[:, ft, 0:tc], start=True, stop=True),
                         reads=[pb, uTb], writes=[pbib])
                    c_, s_ = cosT[:, pr, 0:tc], sinT[:, pr, 0:tc]
                    TT("dve", W[0][:, 0:tc], pbr[:, 0:tc], c_, ALU.mult, [pbrb, tabb], [Wb[0]])
                    TT("dve", W[1][:, 0:tc], pbi[:, 0:tc], s_, ALU.mult, [pbib, tabb], [Wb[1]])
                    TT("pool", W[0][:, 0:tc], W[0][:, 0:tc], W[1][:, 0:tc], ALU.add, [Wb[0], Wb[1]], [Wb[0]])
                    TT("dve", W[2][:, 0:tc], pbi[:, 0:tc], c_, ALU.mult, [pbib, tabb], [Wb[2]])
                    TT("dve", W[5][:, 0:tc], pbr[:, 0:tc], s_, ALU.mult, [pbrb, tabb, Wb[5]], [Wb[5]])
                    TT("pool", W[2][:, 0:tc], W[2][:, 0:tc], W[5][:, 0:tc], ALU.subtract, [Wb[2], Wb[5]], [Wb[2]])

                def stageB(pr):
                    ft, p4 = pr // 4, pr % 4
                    W, Wb = Wset[pr % 2], Wbset[pr % 2]
                    hb4, hbfb = hb4s[ft % 2], hbfbs[ft % 2]
                    c_, s_ = cosT[:, pr, 0:tc], sinT[:, pr, 0:tc]
                    dec = mag[:, pr:pr + 1].to_broadcast([128, tc])
                    s.op("dve", lambda e_: e_.tensor_tensor_scan(
                        out=W[3][:, 0:tc], data0=dec, data1=W[0][:, 0:tc], initial=h0r[:, pr:pr + 1], op0=ALU.mult, op1=ALU.add),
                        reads=[Wb[0], hb, pb], writes=[Wb[3]])
                    s.op("dve", lambda e_: e_.tensor_tensor_scan(
                        out=W[4][:, 0:tc], data0=dec, data1=W[2][:, 0:tc], initial=h0i[:, pr:pr + 1], op0=ALU.mult, op1=ALU.add),
                        reads=[Wb[2], hb, pb], writes=[Wb[4]])
                    TT("dve", hb4[:, p4, 0, 0:tc], W[3][:, 0:tc], c_, ALU.mult, [Wb[3], tabb, hbfb], [hbfb])
                    TT("pool", hb4[:, p4, 1, 0:tc], W[4][:, 0:tc], s_, ALU.mult, [Wb[4], tabb, hbfb], [hbfb])
                    TT("dve", hb4[:, p4, 2, 0:tc], W[3][:, 0:tc], s_, ALU.mult, [Wb[3], tabb, hbfb], [hbfb])
                    TT("pool", hb4[:, p4, 3, 0:tc], W[4][:, 0:tc], c_, ALU.mult, [Wb[4], tabb, hbfb], [hbfb])
                    s.op("act", lambda e_: e_.copy(out=glr[:, pr:pr + 1], in_=W[3][:, tc - 1:tc]), reads=[Wb[3], glb], writes=[glb])
                    s.op("act", lambda e_: e_.copy(out=gli[:, pr:pr + 1], in_=W[4][:, tc - 1:tc]), reads=[Wb[4], glb], writes=[glb])

                def stageY(ft):
                    hb4, hbfb = hb4s[ft % 2], hbfbs[ft % 2]
                    py, pyb = g.ps[4 + ft % 2], g.psb[4 + ft % 2]

                    def mmy(e_):
                        last = None
                        for p4 in range(4):
                            for q_, ci in enumerate((0, 2, 1, 1)):
                                last = e_.matmul(py[:, 0:tc], lhsT=CP[ci][:, ft * 4 + p4, :], rhs=hb4[:, p4, q_, 0:tc],
                                                 start=(p4 == 0 and q_ == 0), stop=(p4 == 3 and q_ == 3))
                        return last
                    s.op("pe", mmy, reads=[pb, hbfb], writes=[pyb])
                    s.op("dve", lambda e_: e_.scalar_tensor_tensor(
                        out=ysb[:, 0:tc], in0=uT[:, ft, 0:tc], scalar=dsk[:, ft:ft + 1], in1=py[:, 0:tc], op0=ALU.mult, op1=ALU.add),
                        reads=[pyb, uTb, pb], writes=[ysbb])
                    s.op("pool", lambda e_: e_.tensor_tensor(out=y2[:, 0:tc], in0=ysb[:, 0:tc], in1=ysb[:, 0:tc], op=ALU.mult), reads=[ysbb], writes=[ysbb])
                    s.op("pool", lambda e_: e_.tensor_scalar(out=y2[:, 0:tc], in0=y2[:, 0:tc], scalar1=0.044715, scalar2=1.0,
                                                             op0=ALU.mult, op1=ALU.add), reads=[ysbb], writes=[ysbb])
                    s.op("pool", lambda e_: e_.tensor_tensor(out=y2[:, 0:tc], in0=y2[:, 0:tc], in1=ysb[:, 0:tc], op=ALU.mult), reads=[ysbb], writes=[ysbb])
                    s.op("act", lambda e_: e_.activation(out=y2[:, 0:tc], in_=y2[:, 0:tc], func=AF.Sigmoid, scale=1.5957691216), reads=[ysbb], writes=[ysbb])
                    s.op("dve", lambda e_: e_.tensor_tensor(out=gT[:, ft, 0:tc], in0=ysb[:, 0:tc], in1=y2[:, 0:tc], op=ALU.mult),
                         reads=[ysbb], writes=[gTb])

                stageA(0)
                for pr in range(16):
                    if pr + 1 < 16:
                        stageA(pr + 1)
                    stageB(pr)
                    if pr % 4 == 3:
                        stageY(pr // 4)
                cl, sl = cosT[:, :, tc - 1], sinT[:, :, tc - 1]
                H_ = lambda o_, a_, b_, op: s.op("dve", lambda e_: e_.tensor_tensor(out=o_, in0=a_, in1=b_, op=op),
                                                 reads=[glb, hb, tabb], writes=[hb])
                H_(h0r[:], glr[:], cl, ALU.mult)
                H_(ht[:], gli[:], sl, ALU.mult)
                H_(h0r[:], h0r[:], ht[:], ALU.subtract)
                H_(h0i[:], glr[:], sl, ALU.mult)
                H_(ht[:], gli[:], cl, ALU.mult)
                H_(h0i[:], h0i[:], ht[:], ALU.add)
                for ot in range(4):
                    pz, pzb = g.ps[4 + ot % 2], g.psb[4 + ot % 2]

                    def mmz(e_, pz=pz, ot=ot):
                        last = None
                        for kt in range(4):
                            last = e_.matmul(pz[:, 0:tc], lhsT=wgl[:, kt, ot * 128:(ot + 1) * 128], rhs=gT[:, kt, 0:tc], start=(kt == 0), stop=(kt == 3))
                        return last
                    s.op("pe", mmz, reads=[pb, gTb], writes=[pzb])
                    s.op("act", lambda e_, pz=pz, ot=ot: e_.activation(out=sg[:, 0:tc], in_=pz[:, 0:tc], func=AF.Sigmoid, bias=bgl[:, ot:ot + 1]),
                         reads=[pzb, pb, ysbb], writes=[ysbb])
                    s.op("dve", lambda e_, ot=ot: e_.tensor_tensor(out=oT[:, ot, 0:tc], in0=gT[:, ot, 0:tc], in1=sg[:, 0:tc], op=ALU.mult),
                         reads=[ysbb, gTb], writes=[oTb])
                s.dma("sp", g.MIXT[0:4, :, col:col + tc].rearrange("t p c -> p t c"), oT[:, :, 0:tc], reads=[oTb], writes=[g.scrb])
            pre = "p_" if seg == 0 else "s_"
            for hp in range(2):
                s.dma("sp", vs(O[pre + "s5re"][e])[:, hp * 8:hp * 8 + 8], h0r[:, hp * 8:hp * 8 + 8], reads=[hb], writes=[g.outb], allow_slow_non_contiguous=True)
                s.dma("sp", vs(O[pre + "s5im"][e])[:, hp * 8:hp * 8 + 8], h0i[:, hp * 8:hp * 8 + 8], reads=[hb], writes=[g.outb], allow_slow_non_contiguous=True)


def odd_stage(g, o):
    odd_proj(g, o)
    T, R, PAST = g.T, g.R, g.PAST
    v8 = lambda ap: ap.rearrange("t (hh p) c -> (t hh) p c", hh=2)
    attention(g, "fp%d" % o, 8, 64, v8(g.FQT), 0, T, v8(g.FKTp), g.VAp, T, "fox_p", 1.0, 0, fox=dict(FT=g.FTp, FT3=g.FT3p, qoff=0))
    attention(g, "fs%d" % o, 8, 64, v8(g.FQT), T, DSEQ, v8(g.FKTs), g.VAs, PAST + DSEQ, "fox_s", 1.0, 0,
              fox=dict(FT=g.FTs, FT3=g.FT3s, qoff=PAST))
    retention(g, o)
    out_proj_ln(g, "oo%d" % o, g.I["owout"][o], 2 * o + 1, EPS)


@staged
def odd_proj(g, o):
    nc, s, R, T, PAST = g.nc, g.s, g.R, g.T, g.PAST
    NB = 512
    I, O = g.I, g.O
    with ExitStack() as es:
        tag = "op%d" % o
        win = T_(g, es, "win" + tag, [128, 8, 3080], BF16)
        wb = Buf()
        wins = I["owin"][o].rearrange("(kt p) f -> p kt f", p=128)
        for kt in range(8):
            s.dma("pool", win[:, kt, :], wins[:, kt, :], writes=[wb])
        nbf = T_(g, es, "nbf" + tag, [8, 1], F32)
        s.dma("sp", nbf[:], I["fox_b_f"][o].rearrange("(p o) -> p o", o=1), writes=[wb], allow_slow_non_contiguous=True)
        s.op("dve", lambda e: e.tensor_scalar(out=nbf[:], in0=nbf[:], scalar1=-1.0, scalar2=None, op0=ALU.mult), reads=[wb], writes=[wb])
        xf = T_(g, es, "xf" + tag, [128, 4, D], F32)
        xb = T_(g, es, "xb" + tag, [128, 2, D], BF16)
        xT = T_(g, es, "xT" + tag, [128, 8, NB], BF16)
        xfb, xbb, xTb = [Buf() for _ in range(4)], [Buf(), Buf()], Buf()
        FQs = T_(g, es, "FQs" + tag, [128, 4, NB], BF16)
        FKs = T_(g, es, "FKs" + tag, [128, 4, NB], BF16)
        FQb, FKb = Buf(), Buf()
        lf = T_(g, es, "lf" + tag, [8, NB], F32)
        fc = T_(g, es, "fc" + tag, [8, NB], F32)
        one8 = T_(g, es, "one8" + tag, [8, NB], F32)
        car = T_(g, es, "car" + tag, [8, 1], F32)
        lfb, carb = Buf(), Buf()
        s.op("pool", lambda e: e.memset(one8[:], 1.0), writes=[wb])
        kf = [T_(g, es, "kf%d" % k + tag, [128, 512], F32) for k in range(2)]
        kfb = [Buf(), Buf()]
        VAs_ = T_(g, es, "VAo" + tag, [128, 8, 128], BF16)
        VAb = Buf()
        s.op("pool", lambda e: e.memset(VAs_[:], 1.0), writes=[VAb])
        ra4 = T_(g, es, "ra" + tag, [128, 4, 8, 64], F32)
        rbt4 = T_(g, es, "rb" + tag, [128, 4, 8, 64], F32)
        rab4 = [Buf() for _ in range(4)]
        tA = T_(g, es, "tA" + tag, [128, 8, 64], F32)
        tB = T_(g, es, "tB" + tag, [128, 8, 64], F32)
        tAb = Buf()
        qk = T_(g, es, "qk" + tag, [128, 8, 64], BF16)
        qkb = Buf()
        qkT = T_(g, es, "qkT" + tag, [128, 4, 128], BF16)
        qkTb = Buf()
        rvb = T_(g, es, "rvb" + tag, [128, 512], BF16)
        rgf = T_(g, es, "rgf" + tag, [128, 512], F32)
        rvbb = Buf()
        pkf = T_(g, es, "pkf" + tag, [128, 512], F32)
        pkb_ = T_(g, es, "pkb" + tag, [128, 512], BF16)
        pcb, pcb2 = Buf(), Buf()

        f3 = T_(g, es, "f3" + tag, [8, 3, NB], BF16)
        f3t = T_(g, es, "f3t" + tag, [8, NB], F32)
        f3b = Buf()

        def split3(lo, cnt, dst, d0):
            sl = slice(lo, lo + cnt)
            s.op("act", lambda e: e.copy(out=f3[:, 0, sl], in_=fc[:, sl]), reads=[lfb, f3b], writes=[f3b])
            s.op("dve", lambda e: e.tensor_tensor(out=f3t[:, sl], in0=fc[:, sl], in1=f3[:, 0, sl], op=ALU.subtract), reads=[lfb, f3b], writes=[f3b])
            s.op("act", lambda e: e.copy(out=f3[:, 1, sl], in_=f3t[:, sl]), reads=[f3b], writes=[f3b])
            s.op("dve", lambda e: e.tensor_tensor(out=f3t[:, sl], in0=f3t[:, sl], in1=f3[:, 1, sl], op=ALU.subtract), reads=[f3b], writes=[f3b])
            s.op("act", lambda e: e.copy(out=f3[:, 2, sl], in_=f3t[:, sl]), reads=[f3b], writes=[f3b])
            s.dma("sp", dst[:, :, d0:d0 + cnt], f3[:, :, sl], reads=[f3b], writes=[g.scrb])

        s.op("pool", lambda e: e.memset(car[:], 0.0), writes=[carb])
        pblks = chunks(PAST, NB)
        pk4 = [T_(g, es, "pk4_%d" % k + tag, [128, 4, 512], F32) for k in range(2)]
        pv4 = [T_(g, es, "pv4_%d" % k + tag, [128, 4, 512], F32) for k in range(2)]
        pb4 = [[Buf() for _ in range(4)] for _ in range(2)]

        def past_loads(bi):
            if bi >= len(pblks):
                return
            k0_, nb_ = pblks[bi]
            for ti_, (o_, n_) in enumerate(chunks(nb_, 128)):
                s.dma("sp", pk4[bi % 2][0:n_, ti_, :], I["c_fk"][o, k0_ + o_:k0_ + o_ + n_, :], writes=[pb4[bi % 2][ti_]])
                s.dma("sp", pv4[bi % 2][0:n_, ti_, :], I["c_fv"][o, k0_ + o_:k0_ + o_ + n_, :], writes=[pb4[bi % 2][ti_]])
        past_loads(0)
        for bi, (k0, nb) in enumerate(pblks):
            past_loads(bi + 1)
            for ti, (oo, n) in enumerate(chunks(nb, 128)):
                kk = k0 + oo
                pkf, pvf, pcb = pk4[bi % 2][:, ti, :], pv4[bi % 2][:, ti, :], pb4[bi % 2][ti]
                s.op("act", lambda e, n=n, pkf=pkf: e.copy(out=pkb_[0:n, :], in_=pkf[0:n, :]), reads=[pcb], writes=[pcb2])
                pt, ptb = next_pst(g)

                def tr(e, pt=pt, n=n):
                    last = None
                    for j in range(4):
                        last = e.transpose(out=pt[:, j * 128:j * 128 + n], in_=pkb_[0:n, j * 128:(j + 1) * 128], identity=g.ident[0:n, 0:n])
                    return last
                s.op("pe", tr, reads=[pcb2, g.identb], writes=[ptb])
                s.op("dve", lambda e, pt=pt, oo=oo, n=n: e.tensor_copy(
                    out=FKs[:, :, oo:oo + n], in_=pt[:, 0:512].rearrange("p (j c) -> p j c", j=4)[:, :, 0:n]), reads=[ptb], writes=[FKb])
                s.op("act", lambda e, n=n, pvf=pvf: e.copy(out=VAs_[0:n, :, 0:64], in_=pvf[0:n, :].rearrange("p (h c) -> p h c", c=64)),
                     reads=[pcb], writes=[VAb])
                s.dma("sp", g.VAs[kk:kk + n, :], VAs_[0:n, :, :].rearrange("p h c -> p (h c)"), reads=[VAb], writes=[g.scrb])
            s.dma("sp", g.FKTs[:, :, k0:k0 + nb].rearrange("t p c -> p t c"), FKs[:, :, 0:nb], reads=[FKb], writes=[g.scrb])
            s.dma("sp", lf[:, 0:nb], I["c_flf"][o, k0:k0 + nb, :].rearrange("k h -> h k"), reads=[lfb], writes=[lfb],
                  allow_slow_non_contiguous=True)
            s.op("dve", lambda e, nb=nb: e.tensor_tensor_scan(out=fc[:, 0:nb], data0=one8[:, 0:nb], data1=lf[:, 0:nb], initial=car[:, 0:1],
                                                               op0=ALU.mult, op1=ALU.add), reads=[lfb, carb, wb], writes=[lfb])
            s.op("act", lambda e, nb=nb: e.copy(out=car[:, 0:1], in_=fc[:, nb - 1:nb]), reads=[lfb, carb], writes=[carb])
            s.dma("sp", g.FTs[:, k0:k0 + nb], fc[:, 0:nb], reads=[lfb], writes=[g.scrb])
            split3(0, nb, g.FT3s, k0)
        carp = T_(g, es, "carp" + tag, [8, 1], F32)
        carpb = Buf()
        s.op("pool", lambda e: e.memset(carp[:], 0.0), writes=[carpb])

        ublks = chunks(R, NB)
        for ti in range(4):
            x_load_tile(g, ublks[0], ti, xf, xfb)
        for bi, (r0, nb) in enumerate(ublks):
            load_xT(g, es, r0, nb, xf, xfb, xb, xbb, xT, xTb, preloaded=True)
            for ti in range(4):
                x_load_tile(g, ublks[bi + 1] if bi + 1 < len(ublks) else None, ti, xf, xfb)
            for ti, (oo, n) in enumerate(chunks(nb, 128)):
                s.dma("sp", ra4[0:n, ti], I["ropeA32"][r0 + oo:r0 + oo + n], writes=[rab4[ti]])
                s.dma("sp", rbt4[0:n, ti], I["ropeB32"][r0 + oo:r0 + oo + n], writes=[rab4[ti]])
            for which, (dst, dstb, c0, sc) in enumerate(((FQs, FQb, 0, 0.125), (FKs, FKb, 512, 1.0))):
                for ft in range(4):
                    pu, pub = next_ps(g)

                    def mmu(e, pu=pu, ft=ft, c0=c0):
                        last = None
                        for kt in range(8):
                            last = e.matmul(pu[:, 0:nb], lhsT=win[:, kt, c0 + ft * 128:c0 + (ft + 1) * 128], rhs=xT[:, kt, 0:nb],
                                            start=(kt == 0), stop=(kt == 7))
                        return last
                    s.op("pe", mmu, reads=[wb, xTb], writes=[pub])
                    s.op("act", lambda e, pu=pu, ft=ft, dst=dst, sc=sc: e.mul(out=dst[:, ft, 0:nb], in_=pu[:, 0:nb], mul=sc),
                         reads=[pub], writes=[dstb])
            s.dma("sp", g.FQT[:, :, r0:r0 + nb].rearrange("t p c -> p t c"), FQs[:, :, 0:nb], reads=[FQb], writes=[g.scrb])
            for (kd, lo, dr, cnt) in split_rows(g, r0, nb):
                dstT = g.FKTp if kd == "p" else g.FKTs
                d0 = dr if kd == "p" else PAST + dr
                s.dma("sp", dstT[:, :, d0:d0 + cnt].rearrange("t p c -> p t c"), FKs[:, :, lo:lo + cnt], reads=[FKb], writes=[g.scrb])
            pl, plb = next_ps(g)

            def mml(e, pl=pl):
                last = None
                for kt in range(8):
                    last = e.matmul(pl[0:8, 0:nb], lhsT=win[:, kt, 1536:1544], rhs=xT[:, kt, 0:nb], start=(kt == 0), stop=(kt == 7))
                return last
            s.op("pe", mml, reads=[wb, xTb], writes=[plb])
            s.op("act", lambda e, pl=pl: e.activation(out=lf[:, 0:nb], in_=pl[0:8, 0:nb], func=AF.Exp, scale=-1.0, bias=nbf[:, 0:1]),
                 reads=[plb, wb, lfb], writes=[lfb])
            s.op("act", lambda e: e.activation(out=lf[:, 0:nb], in_=lf[:, 0:nb], func=AF.Ln, bias=1.0), reads=[lfb], writes=[lfb])
            s.op("dve", lambda e: e.tensor_scalar(out=lf[:, 0:nb], in0=lf[:, 0:nb], scalar1=-1.0, scalar2=None, op0=ALU.mult),
                 reads=[lfb], writes=[lfb])
            for (kd, lo, dr, cnt) in split_rows(g, r0, nb):
                cr, crb = (carp, carpb) if kd == "p" else (car, carb)
                pre = "p_" if kd == "p" else "s_"
                s.dma("sp", O[pre + "flf"][o, dr:dr + cnt, :].rearrange("k h -> h k"), lf[:, lo:lo + cnt], reads=[lfb], writes=[g.outb],
                      allow_slow_non_contiguous=True)
                s.op("dve", lambda e, lo=lo, cnt=cnt, cr=cr: e.tensor_tensor_scan(
                    out=fc[:, lo:lo + cnt], data0=one8[:, lo:lo + cnt], data1=lf[:, lo:lo + cnt], initial=cr[:, 0:1],
                    op0=ALU.mult, op1=ALU.add), reads=[lfb, crb, wb], writes=[lfb])
                s.op("act", lambda e, lo=lo, cnt=cnt, cr=cr: e.copy(out=cr[:, 0:1], in_=fc[:, lo + cnt - 1:lo + cnt]), reads=[lfb, crb], writes=[crb])
                dF = g.FTp if kd == "p" else g.FTs
                dF3 = g.FT3p if kd == "p" else g.FT3s
                d0 = dr if kd == "p" else PAST + dr
                s.dma("sp", dF[:, d0:d0 + cnt], fc[:, lo:lo + cnt], reads=[lfb], writes=[g.scrb])
                split3(lo, cnt, dF3, d0)
            for ti, (oo, n) in enumerate(chunks(nb, 128)):
                rr = r0 + oo
                pieces = split_rows(g, rr, n)

                def mmt(e, pm, c0, oo=oo, n=n):
                    last = None
                    for kt in range(8):
                        last = e.matmul(pm[0:n, :], lhsT=xT[:, kt, oo:oo + n], rhs=win[:, kt, c0:c0 + 512], start=(kt == 0), stop=(kt == 7))
                    return last
                pm, pmb = next_ps(g)
                s.op("pe", lambda e, pm=pm: mmt(e, pm, 512), reads=[wb, xTb], writes=[pmb])
                s.op("act", lambda e, pm=pm, n=n: e.copy(out=kf[0][0:n, :], in_=pm[0:n, :]), reads=[pmb], writes=[kfb[0]])
                for (kd, lo, dr, cnt) in pieces:
                    s.dma("sp", O[("p_" if kd == "p" else "s_") + "fk"][o, dr:dr + cnt, :], kf[0][lo:lo + cnt, :], reads=[kfb[0]], writes=[g.outb])
                pm, pmb = next_ps(g)
                s.op("pe", lambda e, pm=pm: mmt(e, pm, 1024), reads=[wb, xTb], writes=[pmb])
                s.op("act", lambda e, pm=pm, n=n: e.copy(out=kf[1][0:n, :], in_=pm[0:n, :]), reads=[pmb], writes=[kfb[1]])
                s.op("dve", lambda e, n=n: e.tensor_copy(out=VAs_[0:n, :, 0:64], in_=kf[1][0:n, :].rearrange("p (h c) -> p h c", c=64)),
                     reads=[kfb[1]], writes=[VAb])
                for (kd, lo, dr, cnt) in pieces:
                    s.dma("sp", O[("p_" if kd == "p" else "s_") + "fv"][o, dr:dr + cnt, :], kf[1][lo:lo + cnt, :], reads=[kfb[1]], writes=[g.outb])
                    dV = g.VAp if kd == "p" else g.VAs
                    d0 = dr if kd == "p" else PAST + dr
                    s.dma("sp", dV[d0:d0 + cnt, :], VAs_[lo:lo + cnt, :, :].rearrange("p h c -> p (h c)"), reads=[VAb], writes=[g.scrb])
                pm, pmb = next_ps(g)
                s.op("pe", lambda e, pm=pm: mmt(e, pm, 1544), reads=[wb, xTb], writes=[pmb])
                ra, rbt, rab = ra4[:, ti], rbt4[:, ti], rab4[ti]
                rope_rows(g, pm[0:n, :].rearrange("p (h c) -> p h c", c=64), n, 8, 32, ra, rbt, qk[0:n], tA, tB, tAb, [pmb, rab], [qkb])
                s.dma("sp", g.RK[rr:rr + n, :], qk[0:n, 4:8, :].rearrange("p h c -> p (h c)"), reads=[qkb], writes=[g.scrb])
                pm, pmb = next_ps(g)
                s.op("pe", lambda e, pm=pm: mmt(e, pm, 2056), reads=[wb, xTb], writes=[pmb])
                s.op("act", lambda e, pm=pm, n=n: e.copy(out=rvb[0:n, :], in_=pm[0:n, :]), reads=[pmb], writes=[rvbb])
                s.dma("sp", g.RV[rr:rr + n, :], rvb[0:n, :], reads=[rvbb], writes=[g.scrb])
                pm, pmb = next_ps(g)
                s.op("pe", lambda e, pm=pm: mmt(e, pm, 2568), reads=[wb, xTb], writes=[pmb])
                s.op("act", lambda e, pm=pm, n=n: e.activation(out=rgf[0:n, :], in_=pm[0:n, :], func=AF.Silu), reads=[pmb], writes=[rvbb])
                s.dma("sp", g.RG[rr:rr + n, :], rgf[0:n, :], reads=[rvbb], writes=[g.scrb])
                pt, ptb = next_pst(g)

                def tr4(e, pt=pt, n=n):
                    last = None
                    for j in range(4):
                        last = e.transpose(out=pt[:, j * 128:j * 128 + n], in_=qk[0:n, 2 * j:2 * j + 2, :].rearrange("p h c -> p (h c)"),
                                           identity=g.ident[0:n, 0:n])
                    return last
                s.op("pe", tr4, reads=[qkb, g.identb], writes=[ptb])
                s.op("dve", lambda e, pt=pt, n=n: e.tensor_copy(out=qkT[:, :, 0:n], in_=pt[:, 0:512].rearrange("p (j c) -> p j c", j=4)[:, :, 0:n]),
                     reads=[ptb], writes=[qkTb])
                s.dma("sp", g.RQKT[:, :, rr:rr + n].rearrange("t p c -> p t c"), qkT[:, :, 0:n], reads=[qkTb], writes=[g.scrb])


@staged
def retention(g, o):
    nc, s, R, T = g.nc, g.s, g.R, g.T
    I, O = g.I, g.O
    with ExitStack() as es:
        tag = "rt%d" % o
        decT = T_(g, es, "decT" + tag, [128, 4, 128], F32)
        gq = T_(g, es, "gqd" + tag, [128, 4, 128], F32)
        ginv = T_(g, es, "ginv" + tag, [128, 4], F32)
        cb = Buf()
        s.dma("sp", decT[:], I["ret_decT"].rearrange("h j i -> j h i"), writes=[cb])
        s.dma("sp", gq[:], I["ret_gq"].rearrange("h d i -> d h i"), writes=[cb])
        s.dma("sp", ginv[:], I["ret_ginv"], writes=[cb])
        S = [T_(g, es, "S%d" % h + tag, [64, 128], F32) for h in range(4)]
        Sb = [T_(g, es, "Sb%d" % h + tag, [64, 128], BF16) for h in range(4)]
        Sbuf = [Buf() for _ in range(4)]
        qT2 = [T_(g, es, "qT%d" % k + tag, [64, 8, 128], BF16) for k in range(2)]
        kt2 = [T_(g, es, "kt%d" % k + tag, [128, 256], BF16) for k in range(2)]
        v2 = [T_(g, es, "v%d" % k + tag, [128, 512], BF16) for k in range(2)]
        gg2 = [T_(g, es, "gg%d" % k + tag, [128, 512], F32) for k in range(2)]
        lb2 = [Buf(), Buf()]
        ci = 0
        PT = [T_(g, es, "PT%d" % k + tag, [128, 128], BF16) for k in range(2)]
        qd = [T_(g, es, "qd%d" % k + tag, [128, 128], BF16) for k in range(2)]
        kd_ = [T_(g, es, "kd%d" % k + tag, [128, 64], BF16) for k in range(2)]
        wbf = [Buf(), Buf()]
        st4 = [T_(g, es, "st%d" % h + tag, [128, 6], F32) for h in range(4)]
        mv4 = [T_(g, es, "mv%d" % h + tag, [128, 2], F32) for h in range(4)]
        rs4 = [T_(g, es, "rs%d" % h + tag, [128, 1], F32) for h in range(4)]
        nm4 = [T_(g, es, "nm%d" % h + tag, [128, 1], F32) for h in range(4)]
        stb4 = [Buf() for _ in range(4)]
        on4 = [T_(g, es, "on%d" % h + tag, [128, 128], F32) for h in range(4)]
        ro = T_(g, es, "ro" + tag, [128, 4, 128], BF16)
        rob = Buf()
        roT = T_(g, es, "roT" + tag, [128, 4, 128], BF16)
        roTb = Buf()
        gam = [1.0 - 2.0 ** (-5.0 - h) for h in range(4)]
        it = 0
        for seg, (c0, c1) in enumerate(((0, T), (T, R))):
            for h in range(4):
                if seg == 0:
                    s.op("pool", lambda e, h=h: e.memset(S[h][:], 0.0), reads=[Sbuf[h]], writes=[Sbuf[h]])
                else:
                    s.dma("sp", S[h][:], I["st_ret"][o, h], reads=[Sbuf[h]], writes=[Sbuf[h]])
                s.op("act", lambda e, h=h: e.copy(out=Sb[h][:], in_=S[h][:]), reads=[Sbuf[h]], writes=[Sbuf[h]])
            cks = chunks(c1 - c0, 128)

            def ch_loads(j, cj):
                if j >= len(cks):
                    return
                co_, n_ = cks[j]
                rr_ = c0 + co_
                s.dma("sp", qT2[cj % 2][:, :, 0:n_], g.RQKT[:, :, rr_:rr_ + n_].rearrange("t (hh p) c -> p (t hh) c", hh=2),
                      reads=[g.scrb], writes=[lb2[cj % 2]])
                s.dma("sp", kt2[cj % 2][0:n_, :], g.RK[rr_:rr_ + n_, :], reads=[g.scrb], writes=[lb2[cj % 2]])
                s.dma("sp", v2[cj % 2][0:n_, :], g.RV[rr_:rr_ + n_, :], reads=[g.scrb], writes=[lb2[cj % 2]])
                s.dma("sp", gg2[cj % 2][0:n_, :], g.RG[rr_:rr_ + n_, :], reads=[g.scrb], writes=[lb2[cj % 2]])
            ch_loads(0, ci)
            for j, (co, n) in enumerate(cks):
                rr = c0 + co
                qT, kt_, v, gg, lb = qT2[ci % 2], kt2[ci % 2], v2[ci % 2], gg2[ci % 2], lb2[ci % 2]
                ci += 1
                ch_loads(j + 1, ci)
                for h in range(4):
                    k = it % 2
                    it += 1
                    st, mv, rs, nm, stb, on = st4[h], mv4[h], rs4[h], nm4[h], stb4[h], on4[h]
                    p0 = 0
                    qh = qT[0:64, h, 0:n]
                    kh = qT[0:64, 4 + h, 0:n]
                    psc, pscb = next_ps(g)
                    s.op("pe", lambda e, psc=psc, kh=kh, qh=qh: e.matmul(psc[0:n, 0:n], lhsT=kh, rhs=qh, start=True, stop=True),
                         reads=[lb], writes=[pscb])
                    s.op("dve", lambda e, psc=psc, k=k, h=h: e.tensor_tensor(out=PT[k][0:n, 0:n], in0=psc[0:n, 0:n], in1=decT[0:n, h, 0:n],
                                                                              op=ALU.mult), reads=[pscb, cb], writes=[wbf[k]])
                    s.op("pool", lambda e, k=k, h=h, qh=qh, p0=p0: e.tensor_tensor(out=qd[k][p0:p0 + 64, 0:n], in0=qh, in1=gq[p0:p0 + 64, h, 0:n],
                                                                                   op=ALU.mult), reads=[lb, cb], writes=[wbf[k]])
                    s.op("dve", lambda e, k=k, h=h, kt_=kt_: e.tensor_scalar(out=kd_[k][0:n, :], in0=kt_[0:n, h * 64:(h + 1) * 64],
                                                                    scalar1=ginv[0:n, h:h + 1], scalar2=float(gam[h] ** (n - 1)),
                                                                    op0=ALU.mult, op1=ALU.mult), reads=[lb, cb], writes=[wbf[k]])
                    po, pob = next_ps(g)

                    def mmo(e, po=po, k=k, h=h, p0=p0, v=v):
                        e.matmul(po[0:n, 0:128], lhsT=PT[k][0:n, 0:n], rhs=v[0:n, h * 128:(h + 1) * 128], start=True, stop=False)
                        return e.matmul(po[0:n, 0:128], lhsT=qd[k][p0:p0 + 64, 0:n], rhs=Sb[h][:, :], start=False, stop=True)
                    s.op("pe", mmo, reads=[wbf[k], lb, Sbuf[h]], writes=[pob])
                    pst_, pstb_ = next_ps(g)
                    s.op("pe", lambda e, pst_=pst_, k=k, h=h, v=v: e.matmul(pst_[0:64, 0:128], lhsT=kd_[k][0:n, :], rhs=v[0:n, h * 128:(h + 1) * 128],
                                                                       start=True, stop=True), reads=[wbf[k], lb], writes=[pstb_])
                    s.op("dve", lambda e, pst_=pst_, h=h: e.scalar_tensor_tensor(out=S[h][:], in0=S[h][:], scalar=float(gam[h] ** n),
                                                                                 in1=pst_[0:64, 0:128], op0=ALU.mult, op1=ALU.add),
                         reads=[pstb_, Sbuf[h]], writes=[Sbuf[h]])
                    s.op("act", lambda e, h=h: e.copy(out=Sb[h][:], in_=S[h][:]), reads=[Sbuf[h]], writes=[Sbuf[h]])
                    s.op("dve", lambda e, po=po, st=st: e.bn_stats(out=st[0:n, :], in_=po[0:n, 0:128]), reads=[pob], writes=[stb])
                    s.op("dve", lambda e, mv=mv, st=st: e.bn_aggr(out=mv[0:n, :], in_=st[0:n, :]), reads=[stb], writes=[stb])
                    s.op("dve", lambda e, rs=rs, mv=mv: e.tensor_scalar(out=rs[0:n, :], in0=mv[0:n, 1:2], scalar1=EPS, scalar2=None, op0=ALU.add),
                         reads=[stb], writes=[stb])
                    s.op("act", lambda e, rs=rs: e.sqrt(out=rs[0:n, :], in_=rs[0:n, :]), reads=[stb], writes=[stb])
                    s.op("dve", lambda e, rs=rs: e.reciprocal(out=rs[0:n, :], in_=rs[0:n, :]), reads=[stb], writes=[stb])
                    s.op("dve", lambda e, nm=nm, mv=mv, rs=rs: e.scalar_tensor_tensor(out=nm[0:n, :], in0=mv[0:n, 0:1], scalar=-1.0, in1=rs[0:n, :],
                                                                 op0=ALU.mult, op1=ALU.mult), reads=[stb], writes=[stb])
                    s.op("act", lambda e, po=po, on=on, nm=nm, rs=rs: e.activation(out=on[0:n, :], in_=po[0:n, 0:128], func=AF.Identity, bias=nm[0:n, 0:1],
                                                              scale=rs[0:n, 0:1]), reads=[stb, pob], writes=[stb])
                    s.op("pool", lambda e, h=h, on=on, gg=gg: e.tensor_tensor(out=ro[0:n, h, :], in0=on[0:n, :], in1=gg[0:n, h * 128:(h + 1) * 128], op=ALU.mult),
                         reads=[stb, lb], writes=[rob])
                pt, ptb = next_pst(g)

                def tr5(e, pt=pt):
                    last = None
                    for h in range(4):
                        last = e.transpose(out=pt[:, h * 128:h * 128 + n], in_=ro[0:n, h, :], identity=g.ident[0:n, 0:n])
                    return last
                s.op("pe", tr5, reads=[rob, g.identb], writes=[ptb])
                s.op("dve", lambda e, pt=pt: e.tensor_copy(out=roT[:, :, 0:n], in_=pt[:, 0:512].rearrange("p (j c) -> p j c", j=4)[:, :, 0:n]),
                     reads=[ptb], writes=[roTb])
                s.dma("sp", g.MIXT[4:8, :, rr:rr + n].rearrange("t p c -> p t c"), roT[:, :, 0:n], reads=[roTb], writes=[g.scrb])
            pre = "p_" if seg == 0 else "s_"
            for h in range(4):
                s.dma("sp", O[pre + "ret"][o, h], S[h][:], reads=[Sbuf[h]], writes=[g.outb])


def make_consts(SEQ, PAST):
    T = NMETA + SEQ
    R = T + DSEQ
    bf = ml_dtypes.bfloat16
    c = {"ident_bf": np.eye(128, dtype=np.float32).astype(bf), "ident_f": np.eye(128, dtype=np.float32)}
    kk = np.arange(128)[:, None]
    qq = np.arange(512)[None, :]
    lim = NMETA + 64 * (np.floor_divide(qq - NMETA, 64) + 1)
    NEGM = -30000.0
    c["mask_mla"] = np.stack([np.where((128 * d + kk) < lim, 0.0, NEGM) for d in range(5)]).astype(np.float32).astype(bf)
    c["mask_fox"] = np.stack([np.where((128 * d + kk) <= qq, 0.0, NEGM) for d in range(4)]).astype(np.float32).astype(bf)
    pos = np.concatenate([np.arange(T), NMETA + PAST + np.arange(DSEQ)]).astype(np.float32)
    inv = (10000.0 ** (-np.arange(16, dtype=np.float32) / 16)).astype(np.float32)
    ang = (pos[:, None] * inv[None, :]).astype(np.float32)
    cs, sn = np.cos(ang).astype(np.float32), np.sin(ang).astype(np.float32)
    A = np.concatenate([cs, cs], -1)
    B = np.concatenate([-sn, sn], -1)
    c["ropeA16"] = np.ascontiguousarray(np.broadcast_to(A[:, None, :], (R, 4, 32))).astype(np.float32)
    c["ropeB16"] = np.ascontiguousarray(np.broadcast_to(B[:, None, :], (R, 4, 32))).astype(np.float32)
    inv2 = (10000.0 ** (-np.arange(32, dtype=np.float32) / 32)).astype(np.float32)
    ang2 = (pos[:, None] * inv2[None, :]).astype(np.float32)
    c2, s2 = np.cos(ang2).astype(np.float32), np.sin(ang2).astype(np.float32)
    A2 = np.concatenate([c2, c2], -1)
    B2 = np.concatenate([-s2, s2], -1)
    scl = np.array([1.0] * 4 + [0.125] * 4, np.float32)[None, :, None]
    c["ropeA32"] = np.ascontiguousarray(A2[:, None, :] * scl).astype(np.float32)
    c["ropeB32"] = np.ascontiguousarray(B2[:, None, :] * scl).astype(np.float32)
    gam = np.array([1.0 - 2.0 ** (-5.0 - h) for h in range(4)], np.float64)
    jj = np.arange(128)
    dif = jj[None, :] - jj[:, None]
    c["ret_decT"] = np.stack([np.where(dif >= 0, gam[h] ** np.maximum(dif, 0), 0.0) for h in range(4)]).astype(np.float32)
    c["ret_gq"] = np.stack([np.broadcast_to((gam[h] ** (jj + 1.0))[None, :], (128, 128)) for h in range(4)]).astype(np.float32)
    c["ret_ginv"] = np.stack([gam[h] ** (-jj.astype(np.float64)) for h in range(4)], axis=1).astype(np.float32)
    c["tau"] = np.ascontiguousarray(np.broadcast_to(np.arange(1, 513, dtype=np.float32)[None, :], (128, 512)))
    return c


def make_in_maps(inp, n_cores=8):
    f = lambda a: np.ascontiguousarray(np.asarray(a, dtype=np.float32))
    consts = make_consts(inp["x_prompt"].shape[1], inp["cache_mla_ckv"].shape[2])
    nb = inp["x_prompt"].shape[0]
    maps = []
    for c in range(n_cores):
        bp = PROMPT_OF_CORE.get(c) if n_cores == 8 else c % nb
        xp = f(inp["x_prompt"][bp]) if bp is not None else np.zeros(inp["x_prompt"].shape[1:], np.float32)
        meta = f(inp["meta_tokens"]) if bp is not None else np.zeros(inp["meta_tokens"].shape, np.float32)
        m = {
            "xp": xp, "xs": f(inp["x_sample"][c]), "meta": meta,
            "c_ckv": f(inp["cache_mla_ckv"][:, c]), "c_kpe": f(inp["cache_mla_kpe"][:, c]),
            "c_fk": f(np.asarray(inp["cache_fox_k"])[:, c].reshape(2, -1, 512)),
            "c_fv": f(np.asarray(inp["cache_fox_v"])[:, c].reshape(2, -1, 512)),
            "c_flf": f(inp["cache_fox_logf"][:, c]),
            "st_re": f(inp["state_s5_re"][:, c]), "st_im": f(inp["state_s5_im"][:, c]), "st_ret": f(inp["state_ret"][:, c]),
            "ln_g": f(inp["ln_g"]), "ln_b": f(inp["ln_b"]),
            "wg": f(inp["ffn_w_gate"]), "wu": f(inp["ffn_w_up"]), "wd": f(inp["ffn_w_down"]),
            "ewin": f(inp["even_w_in"]), "ewout": f(inp["even_w_out"]),
            "owin": f(inp["odd_w_in"]), "owout": f(inp["odd_w_out"]), "fox_b_f": f(inp["fox_b_f"]),
        }
        for k in ("s5_a_re", "s5_a_im", "s5_b_re", "s5_b_im", "s5_c_re", "s5_c_im", "s5_d", "s5_log_dt", "s5_w_glu",
                  "s5_b_glu", "mla_q_norm", "mla_kv_norm", "mla_w_uq", "mla_w_ukv"):
            m[k] = f(inp[k])
        m.update(consts)
        maps.append(m)
    return maps


PROMPT_OF_CORE = {0: 0, 1: 1, 4: 2, 5: 3}
CORE_OF_PROMPT = {b: c for c, b in PROMPT_OF_CORE.items()}


def gather(res, SEQ, nb_p=4, n_cores=8):
    T = NMETA + SEQ
    r = res.results
    P = lambda k: np.stack([r[CORE_OF_PROMPT[b] if n_cores == 8 else b][k] for b in range(nb_p)])
    S = lambda k: np.stack([r[c][k] for c in range(n_cores)])
    sw = lambda a: np.swapaxes(a, 0, 1)
    outs = [P("y_p"), S("y_s"),
            sw(P("p_ckv")), sw(P("p_kpe")), sw(P("p_fk")).reshape(2, nb_p, T, 8, 64), sw(P("p_fv")).reshape(2, nb_p, T, 8, 64),
            sw(P("p_flf")), sw(P("p_s5re")), sw(P("p_s5im")), sw(P("p_ret")),
            sw(S("s_ckv")), sw(S("s_kpe")), sw(S("s_fk")).reshape(2, n_cores, DSEQ, 8, 64),
            sw(S("s_fv")).reshape(2, n_cores, DSEQ, 8, 64), sw(S("s_flf")), sw(S("s_s5re")), sw(S("s_s5im")), sw(S("s_ret"))]
    return tuple(np.ascontiguousarray(o.astype(np.float32)) for o in outs)


def kernel(**inputs):
    SEQ = inputs["x_prompt"].shape[1]
    PAST = inputs["cache_mla_ckv"].shape[2]
    nc = build(SEQ, PAST)
    in_maps = make_in_maps(inputs)
    res = run_bass_kernel_spmd(nc, in_maps, core_ids=list(range(8)))
    return gather(res, SEQ)
```

```python
import math
import numpy as np
import ml_dtypes
from contextlib import ExitStack
import concourse.bass as bass
import concourse.mybir as mybir
from concourse.bass_utils import run_bass_kernel_spmd

F32 = mybir.dt.float32
BF16 = mybir.dt.bfloat16
AF = mybir.ActivationFunctionType
ALU = mybir.AluOpType
AX = mybir.AxisListType

D = 1024
DFF = 2816
NFT = DFF // 128
DEPTH = 4
ALPHA = (2.0 * DEPTH) ** 0.25
EPS = 1e-5
NMETA = 16
DSEQ = 16
NDS = 8


class Buf:
    __slots__ = ("w", "r", "excl")

    def __init__(self, excl=False):
        self.w = None
        self.r = {}
        self.excl = excl


class Sched:
    def __init__(self, nc, es):
        self.nc = nc
        self.engs = {"pe": nc.tensor, "act": nc.scalar, "dve": nc.vector, "pool": nc.gpsimd, "sp": nc.sync}
        self.sem = {k: es.enter_context(nc.semaphore("sem_" + k)) for k in ("pe", "act", "dve", "pool")}
        self.cnt = {k: 0 for k in self.sem}
        self.seen = {e: {} for e in self.engs}
        self.dq = {}
        for q in ("sp", "pool", "act"):
            self.dq[q] = [[es.enter_context(nc.semaphore("dma_%s_%d" % (q, i))), 0] for i in range(NDS)]
        self.dqi = {q: 0 for q in self.dq}
        self.out_toks = []
        self.nins = 0

    def _wait(self, e, tok):
        if tok is None:
            return
        key, sem, val = tok
        if e == "pe" and key == "pe":
            return
        if self.seen[e].get(key, 0) >= val:
            return
        self.engs[e].wait_ge(sem, val)
        self.seen[e][key] = val

    def _deps(self, e, reads, writes):
        for b in reads:
            self._wait(e, b.w)
            if b.excl:
                for k, t in list(b.r.items()):
                    if k != e:
                        self._wait(e, t)
        for b in writes:
            self._wait(e, b.w)
            for t in list(b.r.values()):
                self._wait(e, t)

    def _commit(self, tok, reads, writes):
        for b in reads:
            b.r[tok[0]] = tok
        for b in writes:
            b.w = tok
            b.r = {}

    def op(self, e, fn, reads=(), writes=()):
        self._deps(e, reads, writes)
        ins = fn(self.engs[e])
        self.cnt[e] += 1
        ins.then_inc(self.sem[e], 1)
        tok = (e, self.sem[e], self.cnt[e])
        self._commit(tok, reads, writes)
        self.nins += 1

    def dma(self, q, out, in_, reads=(), writes=(), is_output=False, **kw):
        idx = self.dqi[q] % NDS
        self.dqi[q] += 1
        slot = self.dq[q][idx]
        key = "d%s%d" % (q, idx)
        if slot[1] > 0:
            self._wait(q, (key, slot[0], slot[1]))
        self._deps(q, reads, writes)
        self.engs[q].dma_start(out=out, in_=in_, **kw).then_inc(slot[0], 16)
        slot[1] += 16
        tok = (key, slot[0], slot[1])
        self._commit(tok, reads, writes)
        self.nins += 1

    def barrier(self):
        toks = []
        for q in self.dq:
            for idx, slot in enumerate(self.dq[q]):
                if slot[1] > 0:
                    toks.append(("d%s%d" % (q, idx), slot[0], slot[1]))
        for e in ("pe", "act", "dve", "pool"):
            if self.cnt[e] > 0:
                toks.append((e, self.sem[e], self.cnt[e]))
        for e in self.engs:
            for t in toks:
                if e == "pe" and t[0] == "pe":
                    if self.seen[e].get("pe", 0) < t[2]:
                        self.engs[e].wait_ge(t[1], t[2])
                        self.seen[e]["pe"] = t[2]
                    continue
                self._wait(e, t)

    def finish(self):
        for q in self.dq:
            for idx, slot in enumerate(self.dq[q]):
                if slot[1] > 0:
                    self._wait("sp", ("d%s%d" % (q, idx), slot[0], slot[1]))
        for e in ("pe", "act", "dve", "pool"):
            if self.cnt[e] > 0:
                self._wait("sp", (e, self.sem[e], self.cnt[e]))


class Ctx:
    pass


def chunks(n, c):
    return [(o, min(c, n - o)) for o in range(0, n, c)]


def build(SEQ, PAST, stop_after=None):
    T = NMETA + SEQ
    R = T + DSEQ
    nc = bass.Bass("TRN2", target_bir_lowering=False)
    g = Ctx()
    g.nc, g.T, g.R, g.SEQ, g.PAST = nc, T, R, SEQ, PAST

    def din(name, shape, dt=F32):
        return nc.dram_tensor(name, list(shape), dt, kind="ExternalInput").ap()

    def dout(name, shape, dt=F32):
        return nc.dram_tensor(name, list(shape), dt, kind="ExternalOutput").ap()

    def dscr(name, shape, dt=F32):
        return nc.dram_tensor(name, list(shape), dt, kind="Internal").ap()

    I = {}
    for name, shape in input_shapes(SEQ, PAST).items():
        I[name] = din(name, shape, BF16 if name in BF16_CONSTS else F32)
    O = {}
    for name, shape in output_shapes(SEQ).items():
        O[name] = dout(name, shape)
    g.I, g.O = I, O
    g.X = dscr("X", [R, D])
    g.UT = dscr("UT", [4, 128, R], BF16)
    g.QT = dscr("QT", [8, 96, R], BF16)
    g.KTp = dscr("KTp", [8, 96, T], BF16)
    g.KTs = dscr("KTs", [8, 96, PAST + DSEQ], BF16)
    g.VAp = dscr("VAp", [T, 8 * 128], BF16)
    g.VAs = dscr("VAs", [PAST + DSEQ, 8 * 128], BF16)
    g.MIXT = dscr("MIXT", [8, 128, R], BF16)
    g.FQT = dscr("FQT", [4, 128, R], BF16)
    g.FKTp = dscr("FKTp", [4, 128, T], BF16)
    g.FKTs = dscr("FKTs", [4, 128, PAST + DSEQ], BF16)
    g.FTp = dscr("FTp", [8, T], F32)
    g.FTs = dscr("FTs", [8, PAST + DSEQ], F32)
    g.FT3p = dscr("FT3p", [8, 3, T], BF16)
    g.FT3s = dscr("FT3s", [8, 3, PAST + DSEQ], BF16)
    g.RK = dscr("RK", [R, 256], BF16)
    g.RQKT = dscr("RQKT", [4, 128, R], BF16)
    g.RV = dscr("RV", [R, 512], BF16)
    g.RG = dscr("RG", [R, 512], F32)
    g.scrb = Buf()
    g.outb = Buf()
    g.Xb = [Buf() for _ in range((R + 127) // 128)]
    with ExitStack() as es:
        s = Sched(nc, es)
        g.s = s
        g.ps = [es.enter_context(nc.psum_tensor("ps%d" % i, [128, 512], F32)) for i in range(6)]
        g.psb = [Buf(True) for _ in range(6)]
        g.pst = [es.enter_context(nc.psum_tensor("pst%d" % i, [128, 1024], BF16)) for i in range(2)]
        g.pstb = [Buf(True) for _ in range(2)]
        g.psi = 0
        g.psti = 0
        g.ident = es.enter_context(nc.sbuf_tensor("ident", [128, 128], BF16))
        g.identb = Buf()
        s.dma("sp", g.ident[:], I["ident_bf"], writes=[g.identb])
        s.dma("sp", g.X[0:NMETA, :], I["meta"], writes=xbufs(g, 0, NMETA))
        s.dma("sp", g.X[NMETA:T, :], I["xp"], writes=xbufs(g, NMETA, SEQ))
        s.dma("sp", g.X[T:R, :], I["xs"], writes=xbufs(g, T, DSEQ))
        done = False
        for l in range(DEPTH):
            ffn_stage(g, l, 0)
            if stop_after == ("ffn", l, 0):
                break
            if l % 2 == 0:
                even_stage(g, l // 2)
            else:
                odd_stage(g, l // 2)
            if stop_after == ("mix", l):
                break
            ffn_stage(g, l, 1)
            if stop_after == ("ffn", l, 1):
                break
        s.dma("sp", O["y_p"], g.X[NMETA:T, :], reads=xbufs(g, NMETA, SEQ))
        s.dma("sp", O["y_s"], g.X[T:R, :], reads=xbufs(g, T, DSEQ))
        s.finish()
    return nc


def staged(fn):
    def w(g, *a, **k):
        r = fn(g, *a, **k)
        g.s.barrier()
        return r
    return w


def xbufs(g, r0, n):
    return g.Xb[r0 // 128:(r0 + n - 1) // 128 + 1]


def next_ps(g):
    i = g.psi % len(g.ps)
    g.psi += 1
    return g.ps[i], g.psb[i]


def next_pst(g):
    i = g.psti % len(g.pst)
    g.psti += 1
    return g.pst[i], g.pstb[i]


BF16_CONSTS = ("ident_bf", "mask_mla", "mask_fox")


def input_shapes(SEQ, PAST):
    T = NMETA + SEQ
    R = T + DSEQ
    return {
        "xp": (SEQ, D), "xs": (DSEQ, D), "meta": (NMETA, D),
        "c_ckv": (2, PAST, 128), "c_kpe": (2, PAST, 32), "c_fk": (2, PAST, 512), "c_fv": (2, PAST, 512),
        "c_flf": (2, PAST, 8), "st_re": (2, 32, 64), "st_im": (2, 32, 64), "st_ret": (2, 4, 64, 128),
        "ln_g": (4, 3, D), "ln_b": (4, 3, D),
        "wg": (4, 2, D, DFF), "wu": (4, 2, D, DFF), "wd": (4, 2, DFF, D),
        "ewin": (2, D, 928), "ewout": (2, D, D),
        "s5_a_re": (2, 32, 64), "s5_a_im": (2, 32, 64), "s5_b_re": (2, 32, 64, 16), "s5_b_im": (2, 32, 64, 16),
        "s5_c_re": (2, 32, 16, 64), "s5_c_im": (2, 32, 16, 64), "s5_d": (2, 512), "s5_log_dt": (2, 32),
        "s5_w_glu": (2, 512, 512), "s5_b_glu": (2, 512),
        "mla_q_norm": (2, 256), "mla_kv_norm": (2, 128), "mla_w_uq": (2, 256, 768), "mla_w_ukv": (2, 128, 1024),
        "owin": (2, D, 3080), "owout": (2, D, D), "fox_b_f": (2, 8),
        "ident_bf": (128, 128), "ident_f": (128, 128), "mask_mla": (5, 128, 512), "mask_fox": (4, 128, 512),
        "ropeA16": (R, 4, 32), "ropeB16": (R, 4, 32), "tau": (128, 512),
        "ropeA32": (R, 8, 64), "ropeB32": (R, 8, 64), "ret_decT": (4, 128, 128), "ret_gq": (4, 128, 128), "ret_ginv": (128, 4),
    }


def output_shapes(SEQ):
    T = NMETA + SEQ
    return {
        "y_p": (SEQ, D), "y_s": (DSEQ, D),
        "p_ckv": (2, T, 128), "p_kpe": (2, T, 32), "p_fk": (2, T, 512), "p_fv": (2, T, 512), "p_flf": (2, T, 8),
        "p_s5re": (2, 32, 64), "p_s5im": (2, 32, 64), "p_ret": (2, 4, 64, 128),
        "s_ckv": (2, DSEQ, 128), "s_kpe": (2, DSEQ, 32), "s_fk": (2, DSEQ, 512), "s_fv": (2, DSEQ, 512),
        "s_flf": (2, DSEQ, 8), "s_s5re": (2, 32, 64), "s_s5im": (2, 32, 64), "s_ret": (2, 4, 64, 128),
    }


def x_load_tile(g, blk, ti, xf, xfb):
    if blk is None:
        return
    r0, nb = blk
    tl = chunks(nb, 128)
    if ti >= len(tl):
        return
    o, n = tl[ti]
    g.s.dma("sp", xf[0:n, ti, :], g.X[r0 + o:r0 + o + n, :], reads=xbufs(g, r0 + o, n), writes=[xfb[ti]])


def load_xT(g, es_tiles, r0, nb, xf, xfb, xb, xbb, xT, xTb, preloaded=False):
    s = g.s
    for ti, (o, n) in enumerate(chunks(nb, 128)):
        if not preloaded:
            s.dma("sp", xf[0:n, ti, :], g.X[r0 + o:r0 + o + n, :], reads=xbufs(g, r0 + o, n), writes=[xfb[ti]])
        xi = ti % 2
        s.op("act", lambda e, xi=xi, n=n, ti=ti: e.copy(out=xb[0:n, xi, :], in_=xf[0:n, ti, :]), reads=[xfb[ti]], writes=[xbb[xi]])
        for half in range(2):
            pt, ptb = next_pst(g)

            def tr(e, half=half, pt=pt, n=n, ti=xi):
                last = None
                for j in range(4):
                    kt = half * 4 + j
                    last = e.transpose(out=pt[:, j * 128:j * 128 + n], in_=xb[0:n, ti, kt * 128:(kt + 1) * 128],
                                       identity=g.ident[0:n, 0:n])
                return last
            s.op("pe", tr, reads=[xbb[xi], g.identb], writes=[ptb])
            s.op("dve", lambda e, half=half, pt=pt, n=n, o=o: e.tensor_copy(
                out=xT[:, half * 4:half * 4 + 4, o:o + n],
                in_=pt[:, 0:512].rearrange("p (j c) -> p j c", j=4)[:, :, 0:n]),
                reads=[ptb], writes=[xTb])


def layer_norm_rows(g, y, yb, n, gt, bt, eps, out, outb, tmp):
    s = g.s
    st, mv, rs, nm = tmp["st"], tmp["mv"], tmp["rs"], tmp["nm"]
    sb = tmp["b"]

    s.op("dve", lambda e: e.bn_stats(out=st[0:n, 0, :], in_=y[0:n, 0:512]), reads=[yb], writes=[tmp["b0"]])
    s.op("dve", lambda e: e.bn_stats(out=st[0:n, 1, :], in_=y[0:n, 512:1024]), reads=[yb], writes=[tmp["b1"]])
    s.op("dve", lambda e: e.bn_aggr(out=mv[0:n, :], in_=st[0:n, :, :].rearrange("p a b -> p (a b)")),
         reads=[tmp["b0"], tmp["b1"]], writes=[sb])
    s.op("dve", lambda e: e.tensor_scalar(out=rs[0:n, :], in0=mv[0:n, 1:2], scalar1=eps, scalar2=None,
                                          op0=ALU.add), reads=[sb], writes=[sb])
    s.op("act", lambda e: e.sqrt(out=rs[0:n, :], in_=rs[0:n, :]), reads=[sb], writes=[sb])
    s.op("dve", lambda e: e.reciprocal(out=rs[0:n, :], in_=rs[0:n, :]), reads=[sb], writes=[sb])
    s.op("dve", lambda e: e.scalar_tensor_tensor(out=nm[0:n, :], in0=mv[0:n, 0:1], scalar=-1.0, in1=rs[0:n, :],
                                                 op0=ALU.mult, op1=ALU.mult), reads=[sb], writes=[sb])
    s.op("act", lambda e: e.activation(out=y[0:n, :], in_=y[0:n, :], func=AF.Identity, bias=nm[0:n, 0:1],
                                       scale=rs[0:n, 0:1]), reads=[sb, yb], writes=[yb])
    s.op("pool", lambda e: e.tensor_tensor(out=y[0:n, :], in0=y[0:n, :], in1=gt[0:n, :], op=ALU.mult),
         reads=[yb, tmp["gb"]], writes=[yb])
    s.op("pool", lambda e: e.tensor_tensor(out=out[0:n, :], in0=y[0:n, :], in1=bt[0:n, :], op=ALU.add),
         reads=[yb, tmp["gb"]], writes=[outb])


def alloc_ln(g, es, l, i, tag):
    nc, s = g.nc, g.s
    t = {}
    t["g"] = es.enter_context(nc.sbuf_tensor("lng" + tag, [128, D], F32))
    t["bt"] = es.enter_context(nc.sbuf_tensor("lnb" + tag, [128, D], F32))
    t["st"] = es.enter_context(nc.sbuf_tensor("lnst" + tag, [128, 2, 6], F32))
    t["mv"] = es.enter_context(nc.sbuf_tensor("lnmv" + tag, [128, 2], F32))
    t["rs"] = es.enter_context(nc.sbuf_tensor("lnrs" + tag, [128, 1], F32))
    t["nm"] = es.enter_context(nc.sbuf_tensor("lnnm" + tag, [128, 1], F32))
    t["b"] = Buf()
    t["b0"] = Buf()
    t["b1"] = Buf()
    t["gb"] = Buf()
    s.dma("sp", t["g"][:], g.I["ln_g"][l, i:i + 1, :].broadcast_to([128, D]), writes=[t["gb"]])
    s.dma("sp", t["bt"][:], g.I["ln_b"][l, i:i + 1, :].broadcast_to([128, D]), writes=[t["gb"]])
    return t


@staged
def ffn_stage(g, l, i):
    nc, s, R = g.nc, g.s, g.R
    NB = 512
    with ExitStack() as es:
        tag = "f%d%d" % (l, i)
        wg = es.enter_context(nc.sbuf_tensor("wg" + tag, [128, 8, DFF], BF16))
        wu = es.enter_context(nc.sbuf_tensor("wu" + tag, [128, 8, DFF], BF16))
        wd = es.enter_context(nc.sbuf_tensor("wd" + tag, [128, NFT, D], BF16))
        wgb, wub, wdb = Buf(), Buf(), Buf()
        wgs = g.I["wg"][l, i].rearrange("(kt p) f -> p kt f", p=128)
        wus = g.I["wu"][l, i].rearrange("(kt p) f -> p kt f", p=128)
        wds = g.I["wd"][l, i].rearrange("(ft p) d -> p ft d", p=128)
        for kt in range(8):
            s.dma("pool", wg[:, kt, :], wgs[:, kt, :], writes=[wgb])
            s.dma("pool", wu[:, kt, :], wus[:, kt, :], writes=[wub])
        for ft in range(0, NFT, 2):
            s.dma("pool", wd[:, ft:ft + 2, :], wds[:, ft:ft + 2, :], writes=[wdb])
        ln = alloc_ln(g, es, l, 0 if i == 0 else 2, tag)
        xf = es.enter_context(nc.sbuf_tensor("xf" + tag, [128, 4, D], F32))
        xb = es.enter_context(nc.sbuf_tensor("xb" + tag, [128, 2, D], BF16))
        xT = es.enter_context(nc.sbuf_tensor("xT" + tag, [128, 8, NB], BF16))
        hT = es.enter_context(nc.sbuf_tensor("hT" + tag, [128, NFT, NB], BF16))
        sg = [es.enter_context(nc.sbuf_tensor("sg%d" % k + tag, [128, NB], F32)) for k in range(2)]
        y = [es.enter_context(nc.sbuf_tensor("y%d" % k + tag, [128, D], F32)) for k in range(2)]
        xfb = [Buf() for _ in range(4)]
        xbb = [Buf() for _ in range(4)]
        xTb, hTb = Buf(), Buf()
        sgb = [Buf(), Buf()]
        yb = [Buf(), Buf()]
        yi = 0
        blks = chunks(R, NB)
        for ti in range(4):
            x_load_tile(g, blks[0], ti, xf, xfb)
        for bi, (r0, nb) in enumerate(blks):
            nxt = blks[bi + 1] if bi + 1 < len(blks) else None
            load_xT(g, es, r0, nb, xf, xfb, xb, xbb, xT, xTb, preloaded=True)
            for ft in range(NFT):
                pg, pgb = next_ps(g)
                pu, pub = next_ps(g)

                def mmg(e, w=wg, p=pg, ft=ft, nb=nb):
                    last = None
                    for kt in range(8):
                        last = e.matmul(p[:, 0:nb], lhsT=w[:, kt, ft * 128:(ft + 1) * 128], rhs=xT[:, kt, 0:nb],
                                        start=(kt == 0), stop=(kt == 7))
                    return last
                s.op("pe", mmg, reads=[wgb, xTb], writes=[pgb])
                s.op("pe", lambda e, ft=ft, nb=nb, pu=pu: mmg(e, wu, pu, ft, nb), reads=[wub, xTb], writes=[pub])
                k = ft % 2
                s.op("act", lambda e, k=k, pg=pg, nb=nb: e.activation(out=sg[k][:, 0:nb], in_=pg[:, 0:nb], func=AF.Silu),
                     reads=[pgb], writes=[sgb[k]])
                s.op("dve", lambda e, k=k, pu=pu, nb=nb, ft=ft: e.tensor_tensor(out=hT[:, ft, 0:nb], in0=sg[k][:, 0:nb],
                                                                                 in1=pu[:, 0:nb], op=ALU.mult),
                     reads=[sgb[k], pub], writes=[hTb])
            for ti, (o, n) in enumerate(chunks(nb, 128)):
                yy, yyb = y[yi % 2], yb[yi % 2]
                yi += 1
                for half in range(2):
                    po, pob = next_ps(g)

                    def mmd(e, po=po, o=o, n=n, half=half):
                        last = None
                        for ft in range(NFT):
                            last = e.matmul(po[0:n, :], lhsT=hT[:, ft, o:o + n], rhs=wd[:, ft, half * 512:(half + 1) * 512],
                                            start=(ft == 0), stop=(ft == NFT - 1))
                        return last
                    s.op("pe", mmd, reads=[hTb, wdb], writes=[pob])
                    s.op("dve", lambda e, po=po, n=n, half=half, ti=ti, yy=yy: e.scalar_tensor_tensor(
                        out=yy[0:n, half * 512:(half + 1) * 512], in0=xf[0:n, ti, half * 512:(half + 1) * 512],
                        scalar=2.0 * ALPHA, in1=po[0:n, :], op0=ALU.mult, op1=ALU.add),
                        reads=[pob, xfb[ti]], writes=[yyb])
                x_load_tile(g, nxt, ti, xf, xfb)
                layer_norm_rows(g, yy, yyb, n, ln["g"], ln["bt"], 4.0 * EPS, yy, yyb, ln)
                s.dma("sp", g.X[r0 + o:r0 + o + n, :], yy[0:n, :], reads=[yyb], writes=xbufs(g, r0 + o, n))
            for ti in range(len(chunks(nb, 128)), 4):
                x_load_tile(g, nxt, ti, xf, xfb)


TWO_PI = 2.0 * math.pi
MAGIC = 12582912.0
SCALE_MLA = 96.0 ** -0.5


def T_(g, es, name, shape, dt):
    return es.enter_context(g.nc.sbuf_tensor(name, list(shape), dt))


def range_reduce(s, e1, out, x, tmpb, n=128):
    s.op(e1, lambda e: e.tensor_scalar(out=out, in0=x, scalar1=1.0 / TWO_PI, scalar2=MAGIC, op0=ALU.mult, op1=ALU.add),
         reads=[tmpb], writes=[tmpb])
    s.op(e1, lambda e: e.tensor_scalar(out=out, in0=out, scalar1=MAGIC, scalar2=TWO_PI, op0=ALU.subtract, op1=ALU.mult),
         reads=[tmpb], writes=[tmpb])
    s.op(e1, lambda e: e.tensor_tensor(out=out, in0=x, in1=out, op=ALU.subtract), reads=[tmpb], writes=[tmpb])
    s.op(e1, lambda e: e.tensor_scalar(out=out, in0=out, scalar1=3.1415925, scalar2=-3.1415925, op0=ALU.min, op1=ALU.max),
         reads=[tmpb], writes=[tmpb])


def rms_rows(g, src, n, width, gt, out, tmp, tb, reads, writes):
    s = g.s
    sq, ss = tmp["sq"], tmp["ss"]
    s.op("act", lambda e: e.square(out=sq[0:n, 0:width], in_=src), reads=reads, writes=[tb])
    s.op("dve", lambda e: e.reduce_sum(out=ss[0:n, :], in_=sq[0:n, 0:width], axis=AX.X), reads=[tb], writes=[tb])
    s.op("dve", lambda e: e.tensor_scalar(out=ss[0:n, :], in0=ss[0:n, :], scalar1=1.0 / width, scalar2=EPS,
                                          op0=ALU.mult, op1=ALU.add), reads=[tb], writes=[tb])
    s.op("act", lambda e: e.sqrt(out=ss[0:n, :], in_=ss[0:n, :]), reads=[tb], writes=[tb])
    s.op("dve", lambda e: e.reciprocal(out=ss[0:n, :], in_=ss[0:n, :]), reads=[tb], writes=[tb])
    s.op("dve", lambda e: e.scalar_tensor_tensor(out=out, in0=src, scalar=ss[0:n, 0:1], in1=gt[0:n, 0:width],
                                                 op0=ALU.mult, op1=ALU.mult), reads=list(reads) + [tb], writes=writes)


def rope_rows(g, src, n, H, half, ra, rb, out, tmpA, tmpB, tb, reads, writes):
    s = g.s
    s.op("dve", lambda e: e.tensor_tensor(out=tmpA[0:n], in0=src, in1=ra[0:n], op=ALU.mult), reads=reads, writes=[tb])
    s.op("dve", lambda e: e.tensor_tensor(out=tmpB[0:n, :, 0:half], in0=src[:, :, half:2 * half], in1=rb[0:n, :, 0:half],
                                          op=ALU.mult), reads=list(reads) + [tb], writes=[tb])
    s.op("dve", lambda e: e.tensor_tensor(out=tmpB[0:n, :, half:2 * half], in0=src[:, :, 0:half],
                                          in1=rb[0:n, :, half:2 * half], op=ALU.mult), reads=list(reads) + [tb], writes=[tb])
    s.op("pool", lambda e: e.tensor_tensor(out=out, in0=tmpA[0:n], in1=tmpB[0:n], op=ALU.add), reads=[tb], writes=writes)


def split_rows(g, r0, n):
    T = g.T
    out = []
    a, b = r0, min(r0 + n, T)
    if b > a:
        out.append(("p", a - r0, a, b - a))
    a, b = max(r0, T), r0 + n
    if b > a:
        out.append(("s", a - r0, a - T, b - a))
    return out


@staged
def attention(g, name, H, dq, QT, q0, nq, KT, VA, nk, kind, scale, out_base, fox=None):
    nc, s = g.nc, g.s
    with ExitStack() as es:
        nkt = (nk + 127) // 128
        Qh = [T_(g, es, name + "Q%d" % k, [128, nq], BF16) for k in range(2)]
        Kh = [T_(g, es, name + "K%d" % k, [128, nk], BF16) for k in range(2)]
        Vh = [T_(g, es, name + "V%d" % k, [128, nkt, 128], BF16) for k in range(2)]
        Qb, Kb, Vb = [Buf(), Buf()], [Buf(), Buf()], [Buf(), Buf()]
        NPT = 4
        pT = [T_(g, es, name + "pT%d" % k, [128, 512], BF16) for k in range(NPT)]
        pTb = [Buf() for _ in range(NPT)]
        osb = [T_(g, es, name + "o%d" % k, [128, 512], F32) for k in range(2)]
        osbb = [Buf(), Buf()]
        obf = [T_(g, es, name + "ob%d" % k, [64, 512], BF16) for k in range(2)]
        obfb = [Buf(), Buf()]
        rcp = [T_(g, es, name + "rc%d" % k, [64, 512], F32) for k in range(2)]
        rcpb = [Buf(), Buf()]
        ones = T_(g, es, name + "ones", [128, 128], F32)
        ones3 = T_(g, es, name + "ones3", [4, 128], BF16)
        onesb = Buf()
        s.op("pool", lambda e: e.memset(ones[:], 1.0), writes=[onesb])
        s.op("pool", lambda e: e.memset(ones3[:], 1.0), reads=[onesb], writes=[onesb])
        nmask = 5 if kind.startswith("mla") else 4
        msk = T_(g, es, name + "msk", [128, nmask, 512], BF16)
        mskb = Buf()
        s.dma("sp", msk[:], g.I["mask_mla" if kind.startswith("mla") else "mask_fox"].rearrange("m p c -> p m c"), writes=[mskb])
        if fox is not None:
            fq = [T_(g, es, name + "fq%d" % k, [4, nq], BF16) for k in range(2)]
            nfk = [T_(g, es, name + "nfk%d" % k, [128, nkt], F32) for k in range(2)]
            fqb, nfkb = [Buf(), Buf()], [Buf(), Buf()]
            for k in range(2):
                s.op("pool", lambda e, k=k: e.memset(nfk[k][:], 0.0), writes=[nfkb[k]])
        for k in range(2):
            s.op("pool", lambda e, k=k: e.memset(Qh[k][64:128, :], 0.0), writes=[Qb[k]])
            s.op("pool", lambda e, k=k: e.memset(Kh[k][64:128, :], 1.0 if fox is not None else 0.0), writes=[Kb[k]])
        si = 0
        oi = 0
        deferred = []
        nfull = nk // 128

        def head_loads(h):
            if h >= H:
                return
            hb = h % 2
            Q_, K_, V_ = Qh[hb], Kh[hb], Vh[hb]
            s.dma("sp", Q_[0:dq, :], QT[h, :, q0:q0 + nq], reads=[g.scrb], writes=[Qb[hb]])
            s.dma("sp", K_[0:dq, :], KT[h, :, 0:nk], reads=[g.scrb], writes=[Kb[hb]])
            if nfull:
                s.dma("sp", V_[:, 0:nfull, :], VA[0:nfull * 128, h * 128:(h + 1) * 128].rearrange("(t p) c -> p t c", p=128),
                      reads=[g.scrb], writes=[Vb[hb]])
            if nk % 128:
                s.dma("sp", V_[0:nk % 128, nfull, :], VA[nfull * 128:nk, h * 128:(h + 1) * 128], reads=[g.scrb], writes=[Vb[hb]])
            if fox is not None:
                nfk_ = nfk[hb]
                s.dma("sp", Q_[64:67, :], fox["FT3"][h, :, fox["qoff"]:fox["qoff"] + nq], reads=[g.scrb], writes=[Qb[hb]])
                if nfull:
                    s.dma("sp", nfk_[:, 0:nfull], fox["FT"][h, 0:nfull * 128].rearrange("(t p) -> p t", p=128),
                          reads=[g.scrb], writes=[nfkb[hb]], allow_slow_non_contiguous=True)
                if nk % 128:
                    s.dma("sp", nfk_[0:nk % 128, nfull:nfull + 1], fox["FT"][h, nfull * 128:nk].rearrange("(p o) -> p o", o=1),
                          reads=[g.scrb], writes=[nfkb[hb]], allow_slow_non_contiguous=True)
                s.op("dve", lambda e, nfk_=nfk_: e.tensor_scalar(out=nfk_[:, :], in0=nfk_[:, :], scalar1=-1.0, scalar2=None, op0=ALU.mult),
                     reads=[nfkb[hb]], writes=[nfkb[hb]])
        head_loads(0)
        for h in range(H):
            hb = h % 2
            Q_, K_, V_ = Qh[hb], Kh[hb], Vh[hb]
            if fox is not None:
                nfk_ = nfk[hb]
            head_loads(h + 1)
            flat = []
            for (qo, nqb) in chunks(nq, 512):
                j = qo // 512
                vis = []
                for i in range(nkt):
                    kn = min(128, nk - i * 128)
                    if kind in ("mla_p", "fox_p"):
                        d = i - 4 * j
                        if d < 0:
                            vis.append((i, kn, None))
                        elif d < nmask:
                            vis.append((i, kn, d))
                    elif kind == "mla_s":
                        vis.append((i, kn, None))
                    else:
                        vis.append((i, kn, 0 if i * 128 >= fox["qoff"] else None))
                for vi, (i, kn, d) in enumerate(vis):
                    flat.append((qo, nqb, vi, len(vis), i, kn, d))
            base = si

            def issue_S(idx):
                qo, nqb, vi, nv, i, kn, d = flat[idx]
                ps_, psb_ = g.ps[(base + idx) % 4], g.psb[(base + idx) % 4]

                def mm_s(e):
                    last = e.matmul(ps_[0:kn, 0:nqb], lhsT=K_[:, i * 128:i * 128 + kn], rhs=Q_[:, qo:qo + nqb], start=True, stop=(d is None))
                    if d is not None:
                        last = e.matmul(ps_[0:kn, 0:nqb], lhsT=g.ident[0:kn, 0:kn], rhs=msk[0:kn, d, 0:nqb], start=False, stop=True)
                    return last
                rd = [Qb[hb], Kb[hb], mskb, g.identb]
                s.op("pe", mm_s, reads=rd, writes=[psb_])

            LOOK = 3
            for idx in range(min(LOOK, len(flat))):
                issue_S(idx)
            for idx, (qo, nqb, vi, nv, i, kn, d) in enumerate(flat):
                ps_, psb_ = g.ps[(base + idx) % 4], g.psb[(base + idx) % 4]
                pt_, ptb_ = pT[(base + idx) % NPT], pTb[(base + idx) % NPT]
                if vi == 0:
                    po, pob = g.ps[4 + oi % 2], g.psb[4 + oi % 2]
                    o_, ob_ = osb[oi % 2], osbb[oi % 2]
                    f_, fb_ = obf[oi % 2], obfb[oi % 2]
                    oi += 1
                if fox is None:
                    s.op("act", lambda e, pt_=pt_, ps_=ps_, kn=kn, nqb=nqb: e.activation(
                        out=pt_[0:kn, 0:nqb], in_=ps_[0:kn, 0:nqb], func=AF.Exp, scale=scale), reads=[psb_], writes=[ptb_])
                else:
                    s.op("act", lambda e, pt_=pt_, ps_=ps_, kn=kn, nqb=nqb, i=i: e.activation(
                        out=pt_[0:kn, 0:nqb], in_=ps_[0:kn, 0:nqb], func=AF.Exp, scale=1.0, bias=nfk_[0:kn, i:i + 1]),
                        reads=[psb_, nfkb[hb]], writes=[ptb_])
                if idx + LOOK < len(flat):
                    issue_S(idx + LOOK)
                for fn in deferred:
                    fn()
                deferred = []
                s.op("pe", lambda e, po=po, pt_=pt_, i=i, kn=kn, nqb=nqb, vi=vi, nv=nv: e.matmul(
                    po[:, 0:nqb], lhsT=V_[0:kn, i, :], rhs=pt_[0:kn, 0:nqb], start=(vi == 0), stop=(vi == nv - 1)),
                    reads=[Vb[hb], ptb_], writes=[pob])
                if vi == nv - 1:
                    s.op("act", lambda e, o_=o_, po=po, nqb=nqb: e.copy(out=o_[0:65, 0:nqb], in_=po[0:65, 0:nqb]),
                         reads=[pob], writes=[ob_])
                    rc_, rcb_ = rcp[(oi - 1) % 2], rcpb[(oi - 1) % 2]

                    def fin(po=po, pob=pob, o_=o_, ob_=ob_, f_=f_, fb_=fb_, nqb=nqb, qo=qo, h=h, rc_=rc_, rcb_=rcb_):
                        s.op("pe", lambda e: e.matmul(po[0:64, 0:nqb], lhsT=ones[64:65, 0:64], rhs=o_[64:65, 0:nqb], start=True, stop=True),
                             reads=[ob_, onesb], writes=[pob])
                        s.op("dve", lambda e: e.reciprocal(out=rc_[0:64, 0:nqb], in_=po[0:64, 0:nqb]), reads=[pob], writes=[rcb_])
                        s.op("dve", lambda e: e.tensor_tensor(out=f_[0:64, 0:nqb], in0=o_[0:64, 0:nqb], in1=rc_[0:64, 0:nqb], op=ALU.mult),
                             reads=[ob_, rcb_], writes=[fb_])
                        fr = out_base + 64 * h
                        s.dma("sp", g.MIXT[fr // 128, fr % 128:fr % 128 + 64, q0 + qo:q0 + qo + nqb], f_[0:64, 0:nqb],
                              reads=[fb_], writes=[g.scrb])
                    deferred.append(fin)
            si = base + len(flat)
        for fn in deferred:
            fn()


@staged
def out_proj_ln(g, tag, wsrc, l, eps):
    nc, s, R = g.nc, g.s, g.R
    with ExitStack() as es:
        wo = T_(g, es, "wo" + tag, [128, 8, D], BF16)
        wob = Buf()
        ws = wsrc.rearrange("(kt p) d -> p kt d", p=128)
        for kt in range(8):
            s.dma("pool", wo[:, kt, :], ws[:, kt, :], writes=[wob])
        ln = alloc_ln(g, es, l, 1, tag)
        NBUF = 4
        mt = [T_(g, es, "mt%d" % k + tag, [128, 8, 128], BF16) for k in range(NBUF)]
        xf = [T_(g, es, "xo%d" % k + tag, [128, D], F32) for k in range(NBUF)]
        y = [T_(g, es, "yo%d" % k + tag, [128, D], F32) for k in range(NBUF)]
        mtb, xfb, yb = [Buf() for _ in range(NBUF)], [Buf() for _ in range(NBUF)], [Buf() for _ in range(NBUF)]
        tiles = chunks(R, 128)

        def loads(ti):
            if ti >= len(tiles):
                return
            r0, n = tiles[ti]
            k = ti % NBUF
            s.dma("sp", mt[k][:, :, 0:n], g.MIXT[:, :, r0:r0 + n].rearrange("kt p c -> p kt c"), reads=[g.scrb], writes=[mtb[k]])
            s.dma("sp", xf[k][0:n, :], g.X[r0:r0 + n, :], reads=xbufs(g, r0, n), writes=[xfb[k]])
        loads(0)
        loads(1)
        for ti, (r0, n) in enumerate(tiles):
            k = ti % NBUF
            loads(ti + 2)
            for half in range(2):
                po, pob = next_ps(g)

                def mm(e, po=po, k=k, n=n, half=half):
                    last = None
                    for kt in range(8):
                        last = e.matmul(po[0:n, :], lhsT=mt[k][:, kt, 0:n], rhs=wo[:, kt, half * 512:(half + 1) * 512],
                                        start=(kt == 0), stop=(kt == 7))
                    return last
                s.op("pe", mm, reads=[mtb[k], wob], writes=[pob])
                s.op("dve", lambda e, po=po, k=k, n=n, half=half: e.scalar_tensor_tensor(
                    out=y[k][0:n, half * 512:(half + 1) * 512], in0=xf[k][0:n, half * 512:(half + 1) * 512], scalar=ALPHA,
                    in1=po[0:n, :], op0=ALU.mult, op1=ALU.add), reads=[pob, xfb[k]], writes=[yb[k]])
            layer_norm_rows(g, y[k], yb[k], n, ln["g"], ln["bt"], eps, y[k], yb[k], ln)
            s.dma("sp", g.X[r0:r0 + n, :], y[k][0:n, :], reads=[yb[k]], writes=xbufs(g, r0, n))


def even_stage(g, e):
    import os
    lim = int(os.environ.get("EV_STOP", "9"))
    even_proj(g, e)
    if lim <= 1:
        return
    s5_stage(g, e)
    if lim <= 2:
        return
    T, R, PAST = g.T, g.R, g.PAST
    attention(g, "ap%d" % e, 8, 96, g.QT, 0, T, g.KTp, g.VAp, T, "mla_p", SCALE_MLA, 512)
    attention(g, "as%d" % e, 8, 96, g.QT, T, DSEQ, g.KTs, g.VAs, PAST + DSEQ, "mla_s", SCALE_MLA, 512)
    out_proj_ln(g, "eo%d" % e, g.I["ewout"][e], 2 * e, EPS)


@staged
def even_proj(g, e):
    nc, s, R, T, PAST = g.nc, g.s, g.R, g.T, g.PAST
    NB = 512
    I, O = g.I, g.O
    with ExitStack() as es:
        tag = "ep%d" % e
        win = T_(g, es, "win" + tag, [128, 8, 928], BF16)
        wuq = T_(g, es, "wuq" + tag, [128, 2, 768], BF16)
        wukv = T_(g, es, "wukv" + tag, [128, 1024], BF16)
        wb = Buf()
        wins = I["ewin"][e].rearrange("(kt p) f -> p kt f", p=128)
        for kt in range(8):
            s.dma("pool", win[:, kt, :], wins[:, kt, :], writes=[wb])
        s.dma("pool", wuq[:], I["mla_w_uq"][e].rearrange("(kt p) f -> p kt f", p=128), writes=[wb])
        s.dma("pool", wukv[:], I["mla_w_ukv"][e], writes=[wb])
        gq = T_(g, es, "gq" + tag, [128, 256], F32)
        gkv = T_(g, es, "gkv" + tag, [128, 128], F32)
        s.dma("sp", gq[:], I["mla_q_norm"][e:e + 1, :].broadcast_to([128, 256]), writes=[wb])
        s.dma("sp", gkv[:], I["mla_kv_norm"][e:e + 1, :].broadcast_to([128, 128]), writes=[wb])
        xf = T_(g, es, "xf" + tag, [128, 4, D], F32)
        xb = T_(g, es, "xb" + tag, [128, 2, D], BF16)
        xT = T_(g, es, "xT" + tag, [128, 8, NB], BF16)
        xfb, xbb, xTb = [Buf() for _ in range(4)], [Buf(), Buf()], Buf()
        uT = T_(g, es, "uT" + tag, [128, 4, NB], BF16)
        uTb = Buf()
        cT = T_(g, es, "cT" + tag, [128, 3, NB], BF16)
        kpT = T_(g, es, "kpT" + tag, [128, NB], BF16)
        cTb = Buf()
        sq = T_(g, es, "sq" + tag, [128, 256], F32)
        ss = T_(g, es, "ss" + tag, [128, 1], F32)
        tmp = {"sq": sq, "ss": ss}
        tb = Buf()
        qn = T_(g, es, "qn" + tag, [128, 256], BF16)
        ckf = T_(g, es, "ckf" + tag, [128, 128], F32)
        ckb = T_(g, es, "ckb" + tag, [128, 128], BF16)
        kpf = T_(g, es, "kpf" + tag, [128, 1, 32], F32)
        kq = T_(g, es, "kq" + tag, [128, 96], BF16)
        rwb = Buf()
        s.op("pool", lambda e_: e_.memset(kq[:], 0.0), writes=[rwb])
        ra4 = T_(g, es, "ra" + tag, [128, 4, 4, 32], F32)
        rbt4 = T_(g, es, "rb" + tag, [128, 4, 4, 32], F32)
        rab4 = [Buf() for _ in range(4)]
        tA = T_(g, es, "tA" + tag, [128, 4, 32], F32)
        tB = T_(g, es, "tB" + tag, [128, 4, 32], F32)
        tAb = Buf()
        qb = T_(g, es, "qb" + tag, [128, 8, 96], BF16)
        qbb = Buf()
        QTs = T_(g, es, "QTs" + tag, [128, 8, NB], BF16)
        KTs_ = T_(g, es, "KTs" + tag, [128, 8, NB], BF16)
        VAs_ = T_(g, es, "VAs" + tag, [128, 4, 8, 128], BF16)
        QTb, KTb, VAb = Buf(), Buf(), Buf()
        s.op("pool", lambda e_: e_.memset(VAs_[:], 1.0), writes=[VAb])
        pcf = T_(g, es, "pcf" + tag, [128, 128], F32)
        pkf = T_(g, es, "pkf" + tag, [128, 32], F32)
        pcb = Buf()

        def kv_project(nb, dests):
            if "nokv" in dbg:
                return
            kv_project_(nb, dests)

        def kv_project_(nb, dests):
            for h in range(8):
                pk, pkb = next_ps(g)
                s.op("pe", lambda e_, pk=pk, h=h: e_.matmul(pk[0:64, 0:nb], lhsT=wukv[:, h * 128:h * 128 + 64], rhs=cT[:, 2, 0:nb],
                                                             start=True, stop=True), reads=[wb, cTb], writes=[pkb])
                s.op("act", lambda e_, pk=pk, h=h: e_.copy(out=KTs_[0:64, h, 0:nb], in_=pk[0:64, 0:nb]), reads=[pkb], writes=[KTb])
                s.op("pool", lambda e_, h=h: e_.tensor_copy(out=KTs_[64:96, h, 0:nb], in_=kpT[64:96, 0:nb]), reads=[cTb], writes=[KTb])
            for ti, (o, n) in enumerate(chunks(nb, 128)):
                pv, pvb = next_ps(g)
                s.op("pe", lambda e_, pv=pv, o=o, n=n: e_.matmul(
                    pv[0:n, :], lhsT=cT[:, 2, o:o + n], rhs=wukv[:, :].rearrange("p (h c) -> p h c", c=128)[:, :, 64:128],
                    start=True, stop=True), reads=[wb, cTb], writes=[pvb])
                s.op("dve", lambda e_, pv=pv, n=n, ti=ti: e_.tensor_copy(
                    out=VAs_[0:n, ti, :, 0:64], in_=pv[0:n, :].rearrange("p (h c) -> p h c", c=64)), reads=[pvb], writes=[VAb])
            for (lo, KTd, VAd, dr, cnt) in dests:
                s.dma("sp", KTd[:, :, dr:dr + cnt].rearrange("h p c -> p h c"), KTs_[0:96, :, lo:lo + cnt], reads=[KTb], writes=[g.scrb])
                a = lo
                while a < lo + cnt:
                    ti = a // 128
                    b = min(lo + cnt, (ti + 1) * 128)
                    s.dma("sp", VAd[dr + a - lo:dr + b - lo, :], VAs_[a - ti * 128:b - ti * 128, ti, :, :].rearrange("p h c -> p (h c)"),
                          reads=[VAb], writes=[g.scrb])
                    a = b

        import os
        dbg = os.environ.get("EP_DBG", "")
        pblks = chunks(PAST if "nopast" not in dbg else 0, NB)
        pcf4 = [T_(g, es, "pcf4_%d" % k + tag, [128, 4, 128], F32) for k in range(2)]
        pkf4 = [T_(g, es, "pkf4_%d" % k + tag, [128, 4, 32], F32) for k in range(2)]
        pcb4 = [[Buf() for _ in range(4)] for _ in range(2)]

        def past_loads(bi):
            if bi >= len(pblks):
                return
            k0_, nb_ = pblks[bi]
            for ti_, (o_, n_) in enumerate(chunks(nb_, 128)):
                s.dma("sp", pcf4[bi % 2][0:n_, ti_, :], I["c_ckv"][e, k0_ + o_:k0_ + o_ + n_, :], writes=[pcb4[bi % 2][ti_]])
                s.dma("sp", pkf4[bi % 2][0:n_, ti_, :], I["c_kpe"][e, k0_ + o_:k0_ + o_ + n_, :], writes=[pcb4[bi % 2][ti_]])
        past_loads(0)
        for bi, (k0, nb) in enumerate(pblks):
            past_loads(bi + 1)
            for ti, (o, n) in enumerate(chunks(nb, 128)):
                pcf, pkf, pcb = pcf4[bi % 2][:, ti, :], pkf4[bi % 2][:, ti, :], pcb4[bi % 2][ti]
                s.op("act", lambda e_, n=n, pcf=pcf: e_.copy(out=ckb[0:n, :], in_=pcf[0:n, :]), reads=[pcb], writes=[rwb])
                s.op("act", lambda e_, n=n, pkf=pkf: e_.copy(out=kq[0:n, 64:96], in_=pkf[0:n, :]), reads=[pcb], writes=[rwb])
                pt, ptb = next_pst(g)

                def tr(e_, pt=pt, n=n):
                    e_.transpose(out=pt[:, 0:n], in_=ckb[0:n, :], identity=g.ident[0:n, 0:n])
                    return e_.transpose(out=pt[0:96, 128:128 + n], in_=kq[0:n, :], identity=g.ident[0:n, 0:n])
                s.op("pe", tr, reads=[rwb, g.identb], writes=[ptb])
                s.op("dve", lambda e_, pt=pt, o=o, n=n: e_.tensor_copy(out=cT[:, 2, o:o + n], in_=pt[:, 0:n]), reads=[ptb], writes=[cTb])
                s.op("dve", lambda e_, pt=pt, o=o, n=n: e_.tensor_copy(out=kpT[64:96, o:o + n], in_=pt[64:96, 128:128 + n]),
                     reads=[ptb], writes=[cTb])
            kv_project(nb, [(0, g.KTs, g.VAs, k0, nb)])

        ublks = chunks(R, NB)
        for ti in range(4):
            x_load_tile(g, ublks[0], ti, xf, xfb)
        for bi, (r0, nb) in enumerate(ublks):
            load_xT(g, es, r0, nb, xf, xfb, xb, xbb, xT, xTb, preloaded=True)
            for ti in range(4):
                x_load_tile(g, ublks[bi + 1] if bi + 1 < len(ublks) else None, ti, xf, xfb)
            for ti, (o, n) in enumerate(chunks(nb, 128)):
                s.dma("sp", ra4[0:n, ti], I["ropeA16"][r0 + o:r0 + o + n, :, :], writes=[rab4[ti]])
                s.dma("sp", rbt4[0:n, ti], I["ropeB16"][r0 + o:r0 + o + n, :, :], writes=[rab4[ti]])
            for ft in range(4):
                pu, pub = next_ps(g)

                def mmu(e_, pu=pu, ft=ft):
                    last = None
                    for kt in range(8):
                        last = e_.matmul(pu[:, 0:nb], lhsT=win[:, kt, ft * 128:(ft + 1) * 128], rhs=xT[:, kt, 0:nb],
                                         start=(kt == 0), stop=(kt == 7))
                    return last
                s.op("pe", mmu, reads=[wb, xTb], writes=[pub])
                s.op("act", lambda e_, pu=pu, ft=ft: e_.copy(out=uT[:, ft, 0:nb], in_=pu[:, 0:nb]), reads=[pub], writes=[uTb])
            s.dma("sp", g.UT[:, :, r0:r0 + nb].rearrange("t p c -> p t c"), uT[:, :, 0:nb], reads=[uTb], writes=[g.scrb])
            def ph1a(ti, o, n):
                rr = r0 + o
                pm, pmb = next_ps(g)

                def mmt(e_, pm=pm, o=o, n=n):
                    last = None
                    for kt in range(8):
                        last = e_.matmul(pm[0:n, 0:416], lhsT=xT[:, kt, o:o + n], rhs=win[:, kt, 512:928],
                                         start=(kt == 0), stop=(kt == 7))
                    return last
                s.op("pe", mmt, reads=[wb, xTb], writes=[pmb])
                ra, rbt, rab = ra4[:, ti], rbt4[:, ti], rab4[ti]
                rms_rows(g, pm[0:n, 0:256], n, 256, gq, qn[0:n, :], tmp, tb, [pmb, wb], [rwb])
                rms_rows(g, pm[0:n, 256:384], n, 128, gkv, ckf[0:n, :], tmp, tb, [pmb, wb], [rwb])
                s.op("act", lambda e_, n=n: e_.copy(out=ckb[0:n, :], in_=ckf[0:n, :]), reads=[rwb], writes=[rwb])
                rope_rows(g, pm[0:n, 384:416].rearrange("p (h c) -> p h c", h=1), n, 1, 16, ra[:, 0:1, :], rbt[:, 0:1, :],
                          kpf[0:n], tA[:, 0:1, :], tB[:, 0:1, :], tAb, [pmb, rab], [rwb])
                s.op("act", lambda e_, n=n: e_.copy(out=kq[0:n, 64:96], in_=kpf[0:n, 0, :]), reads=[rwb], writes=[rwb])
                for (kd, lo, dr, cnt) in split_rows(g, rr, n):
                    pre = "p_" if kd == "p" else "s_"
                    s.dma("sp", O[pre + "ckv"][e, dr:dr + cnt, :], ckf[lo:lo + cnt, :], reads=[rwb], writes=[g.outb])
                    s.dma("sp", O[pre + "kpe"][e, dr:dr + cnt, :], kpf[lo:lo + cnt, 0, :], reads=[rwb], writes=[g.outb])
            def ph1b(ti, o, n):
                rr = r0 + o
                pt, ptb = next_pst(g)

                def tr2(e_, pt=pt, n=n):
                    e_.transpose(out=pt[:, 0:n], in_=qn[0:n, 0:128], identity=g.ident[0:n, 0:n])
                    e_.transpose(out=pt[:, 128:128 + n], in_=qn[0:n, 128:256], identity=g.ident[0:n, 0:n])
                    e_.transpose(out=pt[:, 256:256 + n], in_=ckb[0:n, :], identity=g.ident[0:n, 0:n])
                    return e_.transpose(out=pt[0:96, 384:384 + n], in_=kq[0:n, :], identity=g.ident[0:n, 0:n])
                s.op("pe", tr2, reads=[rwb, g.identb], writes=[ptb])
                s.op("dve", lambda e_, pt=pt, o=o, n=n: e_.tensor_copy(
                    out=cT[:, :, o:o + n], in_=pt[:, 0:384].rearrange("p (j c) -> p j c", j=3)[:, :, 0:n]), reads=[ptb], writes=[cTb])
                s.op("dve", lambda e_, pt=pt, o=o, n=n: e_.tensor_copy(out=kpT[64:96, o:o + n], in_=pt[64:96, 384:384 + n]),
                     reads=[ptb], writes=[cTb])
            def ph2(ti, o, n):
                rr = r0 + o
                ra, rbt, rab = ra4[:, ti], rbt4[:, ti], rab4[ti]
                for hf in range(2):
                    pq, pqb = next_ps(g)

                    def mmq(e_, pq=pq, o=o, n=n, hf=hf):
                        e_.matmul(pq[0:n, 0:384], lhsT=cT[:, 0, o:o + n], rhs=wuq[:, 0, hf * 384:(hf + 1) * 384], start=True, stop=False)
                        return e_.matmul(pq[0:n, 0:384], lhsT=cT[:, 1, o:o + n], rhs=wuq[:, 1, hf * 384:(hf + 1) * 384],
                                         start=False, stop=True)
                    s.op("pe", mmq, reads=[cTb, wb], writes=[pqb])
                    pq3 = pq[0:n, 0:384].rearrange("p (h c) -> p h c", c=96)
                    s.op("act", lambda e_, pq3=pq3, n=n, hf=hf: e_.copy(out=qb[0:n, hf * 4:hf * 4 + 4, 0:64], in_=pq3[:, :, 0:64]),
                         reads=[pqb], writes=[qbb])
                    rope_rows(g, pq3[:, :, 64:96], n, 4, 16, ra, rbt, qb[0:n, hf * 4:hf * 4 + 4, 64:96], tA, tB, tAb,
                              [pqb, rab], [qbb])
                pt, ptb = next_pst(g)

                def tr3(e_, pt=pt, n=n):
                    last = None
                    for h in range(8):
                        last = e_.transpose(out=pt[0:96, h * 128:h * 128 + n], in_=qb[0:n, h, :], identity=g.ident[0:n, 0:n])
                    return last
                s.op("pe", tr3, reads=[qbb, g.identb], writes=[ptb])
                s.op("dve", lambda e_, pt=pt, o=o, n=n: e_.tensor_copy(
                    out=QTs[0:96, :, o:o + n], in_=pt[0:96, :].rearrange("p (h c) -> p h c", h=8)[:, :, 0:n]), reads=[ptb], writes=[QTb])
            tl = chunks(nb, 128)
            ph1a(0, *tl[0])
            ph1b(0, *tl[0])
            for idx in range(len(tl)):
                if idx + 1 < len(tl):
                    ph1a(idx + 1, *tl[idx + 1])
                ph2(idx, *tl[idx])
                if idx + 1 < len(tl):
                    ph1b(idx + 1, *tl[idx + 1])
            s.dma("sp", g.QT[:, :, r0:r0 + nb].rearrange("h p c -> p h c"), QTs[0:96, :, 0:nb], reads=[QTb], writes=[g.scrb])
            dests = []
            for (kd, lo, dr, cnt) in split_rows(g, r0, nb):
                if kd == "p":
                    dests.append((lo, g.KTp, g.VAp, dr, cnt))
                else:
                    dests.append((lo, g.KTs, g.VAs, PAST + dr, cnt))
            kv_project(nb, dests)


@staged
def s5_stage(g, e):
    nc, s, R, T = g.nc, g.s, g.R, g.T
    I, O = g.I, g.O
    TC = 512
    with ExitStack() as es:
        tag = "s5%d" % e
        sm = lambda nm, w=16: T_(g, es, nm + tag, [128, w], F32)
        are, aim, ldt = sm("are"), sm("aim"), sm("ldt")
        ard, aid, mag, cs, sn, t0, t1 = sm("ard"), sm("aid"), sm("mag"), sm("cs"), sm("sn"), sm("t0"), sm("t1")
        abr, abi, nr, ni, den, fr, fi = sm("abr"), sm("abi"), sm("nr"), sm("ni"), sm("den"), sm("fr"), sm("fi")
        pb = Buf()
        vw = lambda ap: ap.rearrange("(pr g2) n -> (g2 n) pr", g2=2)
        for hp in range(2):
            s.dma("sp", are[:, hp * 8:hp * 8 + 8], vw(I["s5_a_re"][e])[:, hp * 8:hp * 8 + 8], writes=[pb], allow_slow_non_contiguous=True)
            s.dma("sp", aim[:, hp * 8:hp * 8 + 8], vw(I["s5_a_im"][e])[:, hp * 8:hp * 8 + 8], writes=[pb], allow_slow_non_contiguous=True)
        for g2 in range(2):
            s.dma("sp", ldt[g2 * 64:(g2 + 1) * 64, :],
                  I["s5_log_dt"][e:e + 1, :].rearrange("o (pr g2) -> o pr g2", g2=2)[:, :, g2].broadcast_to([64, 16]),
                  writes=[pb], allow_slow_non_contiguous=True)
        P = lambda eng, fn: s.op(eng, fn, reads=[pb], writes=[pb])
        P("act", lambda e_: e_.activation(out=ldt[:], in_=ldt[:], func=AF.Exp))
        P("dve", lambda e_: e_.tensor_tensor(out=ard[:], in0=are[:], in1=ldt[:], op=ALU.mult))
        P("dve", lambda e_: e_.tensor_tensor(out=aid[:], in0=aim[:], in1=ldt[:], op=ALU.mult))
        P("act", lambda e_: e_.activation(out=mag[:], in_=ard[:], func=AF.Exp))
        range_reduce(s, "dve", t0[:], aid[:], pb)
        P("act", lambda e_: e_.activation(out=sn[:], in_=t0[:], func=AF.Sin))
        P("dve", lambda e_: e_.tensor_scalar(out=t1[:], in0=aid[:], scalar1=math.pi / 2, scalar2=None, op0=ALU.add))
        range_reduce(s, "dve", t0[:], t1[:], pb)
        P("act", lambda e_: e_.activation(out=cs[:], in_=t0[:], func=AF.Sin))
        P("dve", lambda e_: e_.tensor_tensor(out=abr[:], in0=mag[:], in1=cs[:], op=ALU.mult))
        P("dve", lambda e_: e_.tensor_tensor(out=abi[:], in0=mag[:], in1=sn[:], op=ALU.mult))
        P("dve", lambda e_: e_.tensor_scalar(out=t0[:], in0=abr[:], scalar1=-1.0, scalar2=None, op0=ALU.add))
        P("dve", lambda e_: e_.tensor_tensor(out=nr[:], in0=t0[:], in1=are[:], op=ALU.mult))
        P("dve", lambda e_: e_.tensor_tensor(out=t1[:], in0=abi[:], in1=aim[:], op=ALU.mult))
        P("dve", lambda e_: e_.tensor_tensor(out=nr[:], in0=nr[:], in1=t1[:], op=ALU.add))
        P("dve", lambda e_: e_.tensor_tensor(out=ni[:], in0=abi[:], in1=are[:], op=ALU.mult))
        P("dve", lambda e_: e_.tensor_tensor(out=t1[:], in0=t0[:], in1=aim[:], op=ALU.mult))
        P("dve", lambda e_: e_.tensor_tensor(out=ni[:], in0=ni[:], in1=t1[:], op=ALU.subtract))
        P("dve", lambda e_: e_.tensor_tensor(out=den[:], in0=are[:], in1=are[:], op=ALU.mult))
        P("dve", lambda e_: e_.tensor_tensor(out=t1[:], in0=aim[:], in1=aim[:], op=ALU.mult))
        P("dve", lambda e_: e_.tensor_tensor(out=den[:], in0=den[:], in1=t1[:], op=ALU.add))
        P("dve", lambda e_: e_.reciprocal(out=den[:], in_=den[:]))
        P("dve", lambda e_: e_.tensor_tensor(out=fr[:], in0=nr[:], in1=den[:], op=ALU.mult))
        P("dve", lambda e_: e_.tensor_tensor(out=fi[:], in0=ni[:], in1=den[:], op=ALU.mult))
        Br = T_(g, es, "Br" + tag, [128, 16, 16], F32)
        Bi = T_(g, es, "Bi" + tag, [128, 16, 16], F32)
        Bbr = T_(g, es, "Bbr" + tag, [128, 16, 16], F32)
        Bbi = T_(g, es, "Bbi" + tag, [128, 16, 16], F32)
        Bt = T_(g, es, "Bt" + tag, [128, 16, 16], F32)
        vb = lambda ap: ap.rearrange("(pr g2) n c -> (g2 n) pr c", g2=2)
        s.dma("sp", Br[:], vb(I["s5_b_re"][e]), writes=[pb])
        s.dma("sp", Bi[:], vb(I["s5_b_im"][e]), writes=[pb])
        frb = fr[:, :].unsqueeze(2).to_broadcast([128, 16, 16])
        fib = fi[:, :].unsqueeze(2).to_broadcast([128, 16, 16])
        P("dve", lambda e_: e_.tensor_tensor(out=Bbr[:], in0=Br[:], in1=frb, op=ALU.mult))
        P("dve", lambda e_: e_.tensor_tensor(out=Bt[:], in0=Bi[:], in1=fib, op=ALU.mult))
        P("dve", lambda e_: e_.tensor_tensor(out=Bbr[:], in0=Bbr[:], in1=Bt[:], op=ALU.subtract))
        P("dve", lambda e_: e_.tensor_tensor(out=Bbi[:], in0=Bi[:], in1=frb, op=ALU.mult))
        P("dve", lambda e_: e_.tensor_tensor(out=Bt[:], in0=Br[:], in1=fib, op=ALU.mult))
        P("dve", lambda e_: e_.tensor_tensor(out=Bbi[:], in0=Bbi[:], in1=Bt[:], op=ALU.add))
        BP = [T_(g, es, "BP%d" % k + tag, [128, 16, 128], F32) for k in range(2)]
        LB = [T_(g, es, "LB%d" % k + tag, [128, 16, 128], BF16) for k in range(2)]
        identf = T_(g, es, "idf" + tag, [128, 128], F32)
        s.dma("sp", identf[:], I["ident_f"], writes=[pb])
        for k, Bb in enumerate((Bbr, Bbi)):
            P("pool", lambda e_, k=k: e_.memset(BP[k][:], 0.0))
            for g2 in range(2):
                for r in range(4):
                    c0 = 32 * r + 16 * g2
                    P("pool", lambda e_, k=k, Bb=Bb, g2=g2, r=r, c0=c0: e_.tensor_copy(
                        out=BP[k][g2 * 64:(g2 + 1) * 64, r::4, c0:c0 + 16], in_=Bb[g2 * 64:(g2 + 1) * 64, r::4, :]))
            for q4 in range(4):
                pp, ppb = next_ps(g)

                def trb(e_, pp=pp, k=k, q4=q4):
                    last = None
                    for jj in range(4):
                        last = e_.transpose(out=pp[:, jj * 128:(jj + 1) * 128], in_=BP[k][:, q4 * 4 + jj, :], identity=identf[:])
                    return last
                s.op("pe", trb, reads=[pb], writes=[ppb])
                s.op("act", lambda e_, pp=pp, k=k, q4=q4: e_.copy(
                    out=LB[k][:, q4 * 4:q4 * 4 + 4, :], in_=pp[:, :].rearrange("p (j c) -> p j c", j=4)), reads=[ppb], writes=[pb])
        CPf = [T_(g, es, "CPf%d" % k + tag, [128, 16, 128], F32) for k in range(2)]
        CP = [T_(g, es, "CP%d" % k + tag, [128, 16, 128], BF16) for k in range(2)]
        for k, nm in enumerate(("s5_c_re", "s5_c_im")):
            P("pool", lambda e_, k=k: e_.memset(CPf[k][:], 0.0))
            src = I[nm][e].rearrange("(pr g2) c n -> g2 n pr c", g2=2)
            for g2 in range(2):
                for r in range(4):
                    c0 = 32 * r + 16 * g2
                    for q in range(4):
                        s.dma("sp", CPf[k][g2 * 64:(g2 + 1) * 64, r + 4 * q, c0:c0 + 16], src[g2][:, r + 4 * q, :], reads=[pb],
                              writes=[pb], allow_slow_non_contiguous=True)
        P("act", lambda e_: e_.copy(out=CP[0][:], in_=CPf[0][:]))
        P("act", lambda e_: e_.mul(out=CP[1][:], in_=CPf[1][:], mul=-1.0))
        CP.append(T_(g, es, "CP2" + tag, [128, 16, 128], BF16))
        P("act", lambda e_: e_.mul(out=CP[2][:], in_=CPf[0][:], mul=-1.0))
        dsk = sm("dsk", 4)
        bgl = sm("bgl", 4)
        s.dma("sp", dsk[:], I["s5_d"][e].rearrange("(t p) -> p t", p=128), writes=[pb], allow_slow_non_contiguous=True)
        s.dma("sp", bgl[:], I["s5_b_glu"][e].rearrange("(t p) -> p t", p=128), writes=[pb], allow_slow_non_contiguous=True)
        wgl = T_(g, es, "wgl" + tag, [128, 4, 512], BF16)
        s.dma("pool", wgl[:], I["s5_w_glu"][e].rearrange("(kt p) f -> p kt f", p=128), writes=[pb])
        tau = T_(g, es, "tau" + tag, [128, TC], F32)
        s.dma("sp", tau[:], I["tau"], writes=[pb])
        cosT = T_(g, es, "cosT" + tag, [128, 16, TC], F32)
        sinT = T_(g, es, "sinT" + tag, [128, 16, TC], F32)
        ang = T_(g, es, "ang" + tag, [128, TC], F32)
        ang2 = T_(g, es, "ang2" + tag, [128, TC], F32)
        angb = Buf()
        tabb = Buf()
        for pr in range(16):
            s.op("dve", lambda e_, pr=pr: e_.tensor_scalar(out=ang[:], in0=tau[:], scalar1=aid[:, pr:pr + 1], scalar2=None, op0=ALU.mult),
                 reads=[pb], writes=[angb])
            range_reduce(s, "dve", ang2[:], ang[:], angb)
            s.op("act", lambda e_, pr=pr: e_.activation(out=sinT[:, pr, :], in_=ang2[:], func=AF.Sin), reads=[angb], writes=[tabb])
            s.op("dve", lambda e_: e_.tensor_scalar(out=ang[:], in0=ang[:], scalar1=math.pi / 2, scalar2=None, op0=ALU.add),
                 reads=[angb], writes=[angb])
            range_reduce(s, "dve", ang2[:], ang[:], angb)
            s.op("act", lambda e_, pr=pr: e_.activation(out=cosT[:, pr, :], in_=ang2[:], func=AF.Sin), reads=[angb], writes=[tabb])
        h0r, h0i = sm("h0r"), sm("h0i")
        hb = Buf()
        uT = T_(g, es, "uTs" + tag, [128, 4, TC], BF16)
        uTb = Buf()
        Wset = [[T_(g, es, "w%d_%d" % (ss_, k) + tag, [128, TC], F32) for k in range(6)] for ss_ in range(2)]
        Wbset = [[Buf() for _ in range(6)] for _ in range(2)]
        hb4s = [T_(g, es, "hb4_%d" % k + tag, [128, 4, 4, TC], BF16) for k in range(2)]
        hbfbs = [Buf(), Buf()]
        glr, gli, ht = sm("glr"), sm("gli"), sm("ht")
        glb = Buf()
        ysb = T_(g, es, "ysb" + tag, [128, TC], F32)
        y2 = T_(g, es, "y2" + tag, [128, TC], F32)
        ysbb = Buf()
        gT = T_(g, es, "gT" + tag, [128, 4, TC], BF16)
        gTb = Buf()
        sg = T_(g, es, "sgl" + tag, [128, TC], F32)
        oT = T_(g, es, "oT" + tag, [128, 4, TC], BF16)
        oTb = Buf()
        vs = lambda ap: ap.rearrange("(pr g2) n -> (g2 n) pr", g2=2)
        for seg, (c0, c1) in enumerate(((0, T), (T, R))):
            if seg == 0:
                s.op("pool", lambda e_: e_.memset(h0r[:], 0.0), reads=[hb], writes=[hb])
                s.op("pool", lambda e_: e_.memset(h0i[:], 0.0), reads=[hb], writes=[hb])
            else:
                for hp in range(2):
                    s.dma("sp", h0r[:, hp * 8:hp * 8 + 8], vs(I["st_re"][e])[:, hp * 8:hp * 8 + 8], reads=[hb], writes=[hb], allow_slow_non_contiguous=True)
                    s.dma("sp", h0i[:, hp * 8:hp * 8 + 8], vs(I["st_im"][e])[:, hp * 8:hp * 8 + 8], reads=[hb], writes=[hb], allow_slow_non_contiguous=True)
            for (co, tc) in chunks(c1 - c0, TC):
                col = c0 + co
                s.dma("sp", uT[:, :, 0:tc], g.UT[:, :, col:col + tc].rearrange("t p c -> p t c"), reads=[g.scrb], writes=[uTb])
                TT = lambda eng, o, a, b, op, rd, wr: s.op(eng, lambda e_: e_.tensor_tensor(out=o, in0=a, in1=b, op=op), reads=rd, writes=wr)

                def stageA(pr):
                    ft = pr // 4
                    W, Wb = Wset[pr % 2], Wbset[pr % 2]
                    pbr, pbrb = g.ps[2 * (pr % 2)], g.psb[2 * (pr % 2)]
                    pbi, pbib = g.ps[2 * (pr % 2) + 1], g.psb[2 * (pr % 2) + 1]
                    s.op("pe", lambda e_: e_.matmul(pbr[:, 0:tc], lhsT=LB[0][:, pr, :], rhs=uT[:, ft, 0:tc], start=True, stop=True),
                         reads=[pb, uTb], writes=[pbrb])
                    s.op("pe", lambda e_: e_.matmul(pbi[:, 0:tc], lhsT=LB[1][:, pr, :], rhs=uT[:, ft, 0:tc], start=True, stop=True),
                         reads=[pb, uTb], writes=[pbib])
                    c_, s_ = cosT[:, pr, 0:tc], sinT[:, pr, 0:tc]
                    TT("dve", W[0][:, 0:tc], pbr[:, 0:tc], c_, ALU.mult, [pbrb, tabb], [Wb[0]])
                    TT("dve", W[1][:, 0:tc], pbi[:, 0:tc], s_, ALU.mult, [pbib, tabb], [Wb[1]])
                    TT("pool", W[0][:, 0:tc], W[0][:, 0:tc], W[1][:, 0:tc], ALU.add, [Wb[0], Wb[1]], [Wb[0]])
                    TT("dve", W[2][:, 0:tc], pbi[:, 0:tc], c_, ALU.mult, [pbib, tabb], [Wb[2]])
                    TT("dve", W[5][:, 0:tc], pbr[:, 0:tc], s_, ALU.mult, [pbrb, tabb, Wb[5]], [Wb[5]])
                    TT("pool", W[2][:, 0:tc], W[2][:, 0:tc], W[5][:, 0:tc], ALU.subtract, [Wb[2], Wb[5]], [Wb[2]])

                def stageB(pr):
                    ft, p4 = pr // 4, pr % 4
                    W, Wb = Wset[pr % 2], Wbset[pr % 2]
                    hb4, hbfb = hb4s[ft % 2], hbfbs[ft % 2]
                    c_, s_ = cosT[:, pr, 0:tc], sinT[:, pr, 0:tc]
                    dec = mag[:, pr:pr + 1].to_broadcast([128, tc])
                    s.op("dve", lambda e_: e_.tensor_tensor_scan(
                        out=W[3][:, 0:tc], data0=dec, data1=W[0][:, 0:tc], initial=h0r[:, pr:pr + 1], op0=ALU.mult, op1=ALU.add),
                        reads=[Wb[0], hb, pb], writes=[Wb[3]])
                    s.op("dve", lambda e_: e_.tensor_tensor_scan(
                        out=W[4][:, 0:tc], data0=dec, data1=W[2][:, 0:tc], initial=h0i[:, pr:pr + 1], op0=ALU.mult, op1=ALU.add),
                        reads=[Wb[2], hb, pb], writes=[Wb[4]])
                    TT("dve", hb4[:, p4, 0, 0:tc], W[3][:, 0:tc], c_, ALU.mult, [Wb[3], tabb, hbfb], [hbfb])
                    TT("pool", hb4[:, p4, 1, 0:tc], W[4][:, 0:tc], s_, ALU.mult, [Wb[4], tabb, hbfb], [hbfb])
                    TT("dve", hb4[:, p4, 2, 0:tc], W[3][:, 0:tc], s_, ALU.mult, [Wb[3], tabb, hbfb], [hbfb])
                    TT("pool", hb4[:, p4, 3, 0:tc], W[4][:, 0:tc], c_, ALU.mult, [Wb[4], tabb, hbfb], [hbfb])
                    s.op("act", lambda e_: e_.copy(out=glr[:, pr:pr + 1], in_=W[3][:, tc - 1:tc]), reads=[Wb[3], glb], writes=[glb])
                    s.op("act", lambda e_: e_.copy(out=gli[:, pr:pr + 1], in_=W[4][:, tc - 1:tc]), reads=[Wb[4], glb], writes=[glb])

                def stageY(ft):
                    hb4, hbfb = hb4s[ft % 2], hbfbs[ft % 2]
                    py, pyb = g.ps[4 + ft % 2], g.psb[4 + ft % 2]

                    def mmy(e_):
                        last = None
                        for p4 in range(4):
                            for q_, ci in enumerate((0, 2, 1, 1)):
                                last = e_.matmul(py[:, 0:tc], lhsT=CP[ci][:, ft * 4 + p4, :], rhs=hb4[:, p4, q_, 0:tc],
                                                 start=(p4 == 0 and q_ == 0), stop=(p4 == 3 and q_ == 3))
                        return last
                    s.op("pe", mmy, reads=[pb, hbfb], writes=[pyb])
                    s.op("dve", lambda e_: e_.scalar_tensor_tensor(
                        out=ysb[:, 0:tc], in0=uT[:, ft, 0:tc], scalar=dsk[:, ft:ft + 1], in1=py[:, 0:tc], op0=ALU.mult, op1=ALU.add),
                        reads=[pyb, uTb, pb], writes=[ysbb])
                    s.op("pool", lambda e_: e_.tensor_tensor(out=y2[:, 0:tc], in0=ysb[:, 0:tc], in1=ysb[:, 0:tc], op=ALU.mult), reads=[ysbb], writes=[ysbb])
                    s.op("pool", lambda e_: e_.tensor_scalar(out=y2[:, 0:tc], in0=y2[:, 0:tc], scalar1=0.044715, scalar2=1.0,
                                                             op0=ALU.mult, op1=ALU.add), reads=[ysbb], writes=[ysbb])
                    s.op("pool", lambda e_: e_.tensor_tensor(out=y2[:, 0:tc], in0=y2[:, 0:tc], in1=ysb[:, 0:tc], op=ALU.mult), reads=[ysbb], writes=[ysbb])
                    s.op("act", lambda e_: e_.activation(out=y2[:, 0:tc], in_=y2[:, 0:tc], func=AF.Sigmoid, scale=1.5957691216), reads=[ysbb], writes=[ysbb])
                    s.op("dve", lambda e_: e_.tensor_tensor(out=gT[:, ft, 0:tc], in0=ysb[:, 0:tc], in1=y2[:, 0:tc], op=ALU.mult),
                         reads=[ysbb], writes=[gTb])

                stageA(0)
                for pr in range(16):
                    if pr + 1 < 16:
                        stageA(pr + 1)
                    stageB(pr)
                    if pr % 4 == 3:
                        stageY(pr // 4)
                cl, sl = cosT[:, :, tc - 1], sinT[:, :, tc - 1]
                H_ = lambda o_, a_, b_, op: s.op("dve", lambda e_: e_.tensor_tensor(out=o_, in0=a_, in1=b_, op=op),
                                                 reads=[glb, hb, tabb], writes=[hb])
                H_(h0r[:], glr[:], cl, ALU.mult)
                H_(ht[:], gli[:], sl, ALU.mult)
                H_(h0r[:], h0r[:], ht[:], ALU.subtract)
                H_(h0i[:], glr[:], sl, ALU.mult)
                H_(ht[:], gli[:], cl, ALU.mult)
                H_(h0i[:], h0i[:], ht[:], ALU.add)
                for ot in range(4):
                    pz, pzb = g.ps[4 + ot % 2], g.psb[4 + ot % 2]

                    def mmz(e_, pz=pz, ot=ot):
                        last = None
                        for kt in range(4):
                            last = e_.matmul(pz[:, 0:tc], lhsT=wgl[:, kt, ot * 128:(ot + 1) * 128], rhs=gT[:, kt, 0:tc], start=(kt == 0), stop=(kt == 3))
                        return last
                    s.op("pe", mmz, reads=[pb, gTb], writes=[pzb])
                    s.op("act", lambda e_, pz=pz, ot=ot: e_.activation(out=sg[:, 0:tc], in_=pz[:, 0:tc], func=AF.Sigmoid, bias=bgl[:, ot:ot + 1]),
                         reads=[pzb, pb, ysbb], writes=[ysbb])
                    s.op("dve", lambda e_, ot=ot: e_.tensor_tensor(out=oT[:, ot, 0:tc], in0=gT[:, ot, 0:tc], in1=sg[:, 0:tc], op=ALU.mult),
                         reads=[ysbb, gTb], writes=[oTb])
                s.dma("sp", g.MIXT[0:4, :, col:col + tc].rearrange("t p c -> p t c"), oT[:, :, 0:tc], reads=[oTb], writes=[g.scrb])
            pre = "p_" if seg == 0 else "s_"
            for hp in range(2):
                s.dma("sp", vs(O[pre + "s5re"][e])[:, hp * 8:hp * 8 + 8], h0r[:, hp * 8:hp * 8 + 8], reads=[hb], writes=[g.outb], allow_slow_non_contiguous=True)
                s.dma("sp", vs(O[pre + "s5im"][e])[:, hp * 8:hp * 8 + 8], h0i[:, hp * 8:hp * 8 + 8], reads=[hb], writes=[g.outb], allow_slow_non_contiguous=True)


def odd_stage(g, o):
    odd_proj(g, o)
    T, R, PAST = g.T, g.R, g.PAST
    v8 = lambda ap: ap.rearrange("t (hh p) c -> (t hh) p c", hh=2)
    attention(g, "fp%d" % o, 8, 64, v8(g.FQT), 0, T, v8(g.FKTp), g.VAp, T, "fox_p", 1.0, 0, fox=dict(FT=g.FTp, FT3=g.FT3p, qoff=0))
    attention(g, "fs%d" % o, 8, 64, v8(g.FQT), T, DSEQ, v8(g.FKTs), g.VAs, PAST + DSEQ, "fox_s", 1.0, 0,
              fox=dict(FT=g.FTs, FT3=g.FT3s, qoff=PAST))
    retention(g, o)
    out_proj_ln(g, "oo%d" % o, g.I["owout"][o], 2 * o + 1, EPS)


@staged
def odd_proj(g, o):
    nc, s, R, T, PAST = g.nc, g.s, g.R, g.T, g.PAST
    NB = 512
    I, O = g.I, g.O
    with ExitStack() as es:
        tag = "op%d" % o
        win = T_(g, es, "win" + tag, [128, 8, 3080], BF16)
        wb = Buf()
        wins = I["owin"][o].rearrange("(kt p) f -> p kt f", p=128)
        for kt in range(8):
            s.dma("pool", win[:, kt, :], wins[:, kt, :], writes=[wb])
        nbf = T_(g, es, "nbf" + tag, [8, 1], F32)
        s.dma("sp", nbf[:], I["fox_b_f"][o].rearrange("(p o) -> p o", o=1), writes=[wb], allow_slow_non_contiguous=True)
        s.op("dve", lambda e: e.tensor_scalar(out=nbf[:], in0=nbf[:], scalar1=-1.0, scalar2=None, op0=ALU.mult), reads=[wb], writes=[wb])
        xf = T_(g, es, "xf" + tag, [128, 4, D], F32)
        xb = T_(g, es, "xb" + tag, [128, 2, D], BF16)
        xT = T_(g, es, "xT" + tag, [128, 8, NB], BF16)
        xfb, xbb, xTb = [Buf() for _ in range(4)], [Buf(), Buf()], Buf()
        FQs = T_(g, es, "FQs" + tag, [128, 4, NB], BF16)
        FKs = T_(g, es, "FKs" + tag, [128, 4, NB], BF16)
        FQb, FKb = Buf(), Buf()
        lf = T_(g, es, "lf" + tag, [8, NB], F32)
        fc = T_(g, es, "fc" + tag, [8, NB], F32)
        one8 = T_(g, es, "one8" + tag, [8, NB], F32)
        car = T_(g, es, "car" + tag, [8, 1], F32)
        lfb, carb = Buf(), Buf()
        s.op("pool", lambda e: e.memset(one8[:], 1.0), writes=[wb])
        kf = [T_(g, es, "kf%d" % k + tag, [128, 512], F32) for k in range(2)]
        kfb = [Buf(), Buf()]
        VAs_ = T_(g, es, "VAo" + tag, [128, 8, 128], BF16)
        VAb = Buf()
        s.op("pool", lambda e: e.memset(VAs_[:], 1.0), writes=[VAb])
        ra4 = T_(g, es, "ra" + tag, [128, 4, 8, 64], F32)
        rbt4 = T_(g, es, "rb" + tag, [128, 4, 8, 64], F32)
        rab4 = [Buf() for _ in range(4)]
        tA = T_(g, es, "tA" + tag, [128, 8, 64], F32)
        tB = T_(g, es, "tB" + tag, [128, 8, 64], F32)
        tAb = Buf()
        qk = T_(g, es, "qk" + tag, [128, 8, 64], BF16)
        qkb = Buf()
        qkT = T_(g, es, "qkT" + tag, [128, 4, 128], BF16)
        qkTb = Buf()
        rvb = T_(g, es, "rvb" + tag, [128, 512], BF16)
        rgf = T_(g, es, "rgf" + tag, [128, 512], F32)
        rvbb = Buf()
        pkf = T_(g, es, "pkf" + tag, [128, 512], F32)
        pkb_ = T_(g, es, "pkb" + tag, [128, 512], BF16)
        pcb, pcb2 = Buf(), Buf()

        f3 = T_(g, es, "f3" + tag, [8, 3, NB], BF16)
        f3t = T_(g, es, "f3t" + tag, [8, NB], F32)
        f3b = Buf()

        def split3(lo, cnt, dst, d0):
            sl = slice(lo, lo + cnt)
            s.op("act", lambda e: e.copy(out=f3[:, 0, sl], in_=fc[:, sl]), reads=[lfb, f3b], writes=[f3b])
            s.op("dve", lambda e: e.tensor_tensor(out=f3t[:, sl], in0=fc[:, sl], in1=f3[:, 0, sl], op=ALU.subtract), reads=[lfb, f3b], writes=[f3b])
            s.op("act", lambda e: e.copy(out=f3[:, 1, sl], in_=f3t[:, sl]), reads=[f3b], writes=[f3b])
            s.op("dve", lambda e: e.tensor_tensor(out=f3t[:, sl], in0=f3t[:, sl], in1=f3[:, 1, sl], op=ALU.subtract), reads=[f3b], writes=[f3b])
            s.op("act", lambda e: e.copy(out=f3[:, 2, sl], in_=f3t[:, sl]), reads=[f3b], writes=[f3b])
            s.dma("sp", dst[:, :, d0:d0 + cnt], f3[:, :, sl], reads=[f3b], writes=[g.scrb])

        s.op("pool", lambda e: e.memset(car[:], 0.0), writes=[carb])
        pblks = chunks(PAST, NB)
        pk4 = [T_(g, es, "pk4_%d" % k + tag, [128, 4, 512], F32) for k in range(2)]
        pv4 = [T_(g, es, "pv4_%d" % k + tag, [128, 4, 512], F32) for k in range(2)]
        pb4 = [[Buf() for _ in range(4)] for _ in range(2)]

        def past_loads(bi):
            if bi >= len(pblks):
                return
            k0_, nb_ = pblks[bi]
            for ti_, (o_, n_) in enumerate(chunks(nb_, 128)):
                s.dma("sp", pk4[bi % 2][0:n_, ti_, :], I["c_fk"][o, k0_ + o_:k0_ + o_ + n_, :], writes=[pb4[bi % 2][ti_]])
                s.dma("sp", pv4[bi % 2][0:n_, ti_, :], I["c_fv"][o, k0_ + o_:k0_ + o_ + n_, :], writes=[pb4[bi % 2][ti_]])
        past_loads(0)
        for bi, (k0, nb) in enumerate(pblks):
            past_loads(bi + 1)
            for ti, (oo, n) in enumerate(chunks(nb, 128)):
                kk = k0 + oo
                pkf, pvf, pcb = pk4[bi % 2][:, ti, :], pv4[bi % 2][:, ti, :], pb4[bi % 2][ti]
                s.op("act", lambda e, n=n, pkf=pkf: e.copy(out=pkb_[0:n, :], in_=pkf[0:n, :]), reads=[pcb], writes=[pcb2])
                pt, ptb = next_pst(g)

                def tr(e, pt=pt, n=n):
                    last = None
                    for j in range(4):
                        last = e.transpose(out=pt[:, j * 128:j * 128 + n], in_=pkb_[0:n, j * 128:(j + 1) * 128], identity=g.ident[0:n, 0:n])
                    return last
                s.op("pe", tr, reads=[pcb2, g.identb], writes=[ptb])
                s.op("dve", lambda e, pt=pt, oo=oo, n=n: e.tensor_copy(
                    out=FKs[:, :, oo:oo + n], in_=pt[:, 0:512].rearrange("p (j c) -> p j c", j=4)[:, :, 0:n]), reads=[ptb], writes=[FKb])
                s.op("act", lambda e, n=n, pvf=pvf: e.copy(out=VAs_[0:n, :, 0:64], in_=pvf[0:n, :].rearrange("p (h c) -> p h c", c=64)),
                     reads=[pcb], writes=[VAb])
                s.dma("sp", g.VAs[kk:kk + n, :], VAs_[0:n, :, :].rearrange("p h c -> p (h c)"), reads=[VAb], writes=[g.scrb])
            s.dma("sp", g.FKTs[:, :, k0:k0 + nb].rearrange("t p c -> p t c"), FKs[:, :, 0:nb], reads=[FKb], writes=[g.scrb])
            s.dma("sp", lf[:, 0:nb], I["c_flf"][o, k0:k0 + nb, :].rearrange("k h -> h k"), reads=[lfb], writes=[lfb],
                  allow_slow_non_contiguous=True)
            s.op("dve", lambda e, nb=nb: e.tensor_tensor_scan(out=fc[:, 0:nb], data0=one8[:, 0:nb], data1=lf[:, 0:nb], initial=car[:, 0:1],
                                                               op0=ALU.mult, op1=ALU.add), reads=[lfb, carb, wb], writes=[lfb])
            s.op("act", lambda e, nb=nb: e.copy(out=car[:, 0:1], in_=fc[:, nb - 1:nb]), reads=[lfb, carb], writes=[carb])
            s.dma("sp", g.FTs[:, k0:k0 + nb], fc[:, 0:nb], reads=[lfb], writes=[g.scrb])
            split3(0, nb, g.FT3s, k0)
        carp = T_(g, es, "carp" + tag, [8, 1], F32)
        carpb = Buf()
        s.op("pool", lambda e: e.memset(carp[:], 0.0), writes=[carpb])

        ublks = chunks(R, NB)
        for ti in range(4):
            x_load_tile(g, ublks[0], ti, xf, xfb)
        for bi, (r0, nb) in enumerate(ublks):
            load_xT(g, es, r0, nb, xf, xfb, xb, xbb, xT, xTb, preloaded=True)
            for ti in range(4):
                x_load_tile(g, ublks[bi + 1] if bi + 1 < len(ublks) else None, ti, xf, xfb)
            for ti, (oo, n) in enumerate(chunks(nb, 128)):
                s.dma("sp", ra4[0:n, ti], I["ropeA32"][r0 + oo:r0 + oo + n], writes=[rab4[ti]])
                s.dma("sp", rbt4[0:n, ti], I["ropeB32"][r0 + oo:r0 + oo + n], writes=[rab4[ti]])
            for which, (dst, dstb, c0, sc) in enumerate(((FQs, FQb, 0, 0.125), (FKs, FKb, 512, 1.0))):
                for ft in range(4):
                    pu, pub = next_ps(g)

                    def mmu(e, pu=pu, ft=ft, c0=c0):
                        last = None
                        for kt in range(8):
                            last = e.matmul(pu[:, 0:nb], lhsT=win[:, kt, c0 + ft * 128:c0 + (ft + 1) * 128], rhs=xT[:, kt, 0:nb],
                                            start=(kt == 0), stop=(kt == 7))
                        return last
                    s.op("pe", mmu, reads=[wb, xTb], writes=[pub])
                    s.op("act", lambda e, pu=pu, ft=ft, dst=dst, sc=sc: e.mul(out=dst[:, ft, 0:nb], in_=pu[:, 0:nb], mul=sc),
                         reads=[pub], writes=[dstb])
            s.dma("sp", g.FQT[:, :, r0:r0 + nb].rearrange("t p c -> p t c"), FQs[:, :, 0:nb], reads=[FQb], writes=[g.scrb])
            for (kd, lo, dr, cnt) in split_rows(g, r0, nb):
                dstT = g.FKTp if kd == "p" else g.FKTs
                d0 = dr if kd == "p" else PAST + dr
                s.dma("sp", dstT[:, :, d0:d0 + cnt].rearrange("t p c -> p t c"), FKs[:, :, lo:lo + cnt], reads=[FKb], writes=[g.scrb])
            pl, plb = next_ps(g)

            def mml(e, pl=pl):
                last = None
                for kt in range(8):
                    last = e.matmul(pl[0:8, 0:nb], lhsT=win[:, kt, 1536:1544], rhs=xT[:, kt, 0:nb], start=(kt == 0), stop=(kt == 7))
                return last
            s.op("pe", mml, reads=[wb, xTb], writes=[plb])
            s.op("act", lambda e, pl=pl: e.activation(out=lf[:, 0:nb], in_=pl[0:8, 0:nb], func=AF.Exp, scale=-1.0, bias=nbf[:, 0:1]),
                 reads=[plb, wb, lfb], writes=[lfb])
            s.op("act", lambda e: e.activation(out=lf[:, 0:nb], in_=lf[:, 0:nb], func=AF.Ln, bias=1.0), reads=[lfb], writes=[lfb])
            s.op("dve", lambda e: e.tensor_scalar(out=lf[:, 0:nb], in0=lf[:, 0:nb], scalar1=-1.0, scalar2=None, op0=ALU.mult),
                 reads=[lfb], writes=[lfb])
            for (kd, lo, dr, cnt) in split_rows(g, r0, nb):
                cr, crb = (carp, carpb) if kd == "p" else (car, carb)
                pre = "p_" if kd == "p" else "s_"
                s.dma("sp", O[pre + "flf"][o, dr:dr + cnt, :].rearrange("k h -> h k"), lf[:, lo:lo + cnt], reads=[lfb], writes=[g.outb],
                      allow_slow_non_contiguous=True)
                s.op("dve", lambda e, lo=lo, cnt=cnt, cr=cr: e.tensor_tensor_scan(
                    out=fc[:, lo:lo + cnt], data0=one8[:, lo:lo + cnt], data1=lf[:, lo:lo + cnt], initial=cr[:, 0:1],
                    op0=ALU.mult, op1=ALU.add), reads=[lfb, crb, wb], writes=[lfb])
                s.op("act", lambda e, lo=lo, cnt=cnt, cr=cr: e.copy(out=cr[:, 0:1], in_=fc[:, lo + cnt - 1:lo + cnt]), reads=[lfb, crb], writes=[crb])
                dF = g.FTp if kd == "p" else g.FTs
                dF3 = g.FT3p if kd == "p" else g.FT3s
                d0 = dr if kd == "p" else PAST + dr
                s.dma("sp", dF[:, d0:d0 + cnt], fc[:, lo:lo + cnt], reads=[lfb], writes=[g.scrb])
                split3(lo, cnt, dF3, d0)
            for ti, (oo, n) in enumerate(chunks(nb, 128)):
                rr = r0 + oo
                pieces = split_rows(g, rr, n)

                def mmt(e, pm, c0, oo=oo, n=n):
                    last = None
                    for kt in range(8):
                        last = e.matmul(pm[0:n, :], lhsT=xT[:, kt, oo:oo + n], rhs=win[:, kt, c0:c0 + 512], start=(kt == 0), stop=(kt == 7))
                    return last
                pm, pmb = next_ps(g)
                s.op("pe", lambda e, pm=pm: mmt(e, pm, 512), reads=[wb, xTb], writes=[pmb])
                s.op("act", lambda e, pm=pm, n=n: e.copy(out=kf[0][0:n, :], in_=pm[0:n, :]), reads=[pmb], writes=[kfb[0]])
                for (kd, lo, dr, cnt) in pieces:
                    s.dma("sp", O[("p_" if kd == "p" else "s_") + "fk"][o, dr:dr + cnt, :], kf[0][lo:lo + cnt, :], reads=[kfb[0]], writes=[g.outb])
                pm, pmb = next_ps(g)
                s.op("pe", lambda e, pm=pm: mmt(e, pm, 1024), reads=[wb, xTb], writes=[pmb])
                s.op("act", lambda e, pm=pm, n=n: e.copy(out=kf[1][0:n, :], in_=pm[0:n, :]), reads=[pmb], writes=[kfb[1]])
                s.op("dve", lambda e, n=n: e.tensor_copy(out=VAs_[0:n, :, 0:64], in_=kf[1][0:n, :].rearrange("p (h c) -> p h c", c=64)),
                     reads=[kfb[1]], writes=[VAb])
                for (kd, lo, dr, cnt) in pieces:
                    s.dma("sp", O[("p_" if kd == "p" else "s_") + "fv"][o, dr:dr + cnt, :], kf[1][lo:lo + cnt, :], reads=[kfb[1]], writes=[g.outb])
                    dV = g.VAp if kd == "p" else g.VAs
                    d0 = dr if kd == "p" else PAST + dr
                    s.dma("sp", dV[d0:d0 + cnt, :], VAs_[lo:lo + cnt, :, :].rearrange("p h c -> p (h c)"), reads=[VAb], writes=[g.scrb])
                pm, pmb = next_ps(g)
                s.op("pe", lambda e, pm=pm: mmt(e, pm, 1544), reads=[wb, xTb], writes=[pmb])
                ra, rbt, rab = ra4[:, ti], rbt4[:, ti], rab4[ti]
                rope_rows(g, pm[0:n, :].rearrange("p (h c) -> p h c", c=64), n, 8, 32, ra, rbt, qk[0:n], tA, tB, tAb, [pmb, rab], [qkb])
                s.dma("sp", g.RK[rr:rr + n, :], qk[0:n, 4:8, :].rearrange("p h c -> p (h c)"), reads=[qkb], writes=[g.scrb])
                pm, pmb = next_ps(g)
                s.op("pe", lambda e, pm=pm: mmt(e, pm, 2056), reads=[wb, xTb], writes=[pmb])
                s.op("act", lambda e, pm=pm, n=n: e.copy(out=rvb[0:n, :], in_=pm[0:n, :]), reads=[pmb], writes=[rvbb])
                s.dma("sp", g.RV[rr:rr + n, :], rvb[0:n, :], reads=[rvbb], writes=[g.scrb])
                pm, pmb = next_ps(g)
                s.op("pe", lambda e, pm=pm: mmt(e, pm, 2568), reads=[wb, xTb], writes=[pmb])
                s.op("act", lambda e, pm=pm, n=n: e.activation(out=rgf[0:n, :], in_=pm[0:n, :], func=AF.Silu), reads=[pmb], writes=[rvbb])
                s.dma("sp", g.RG[rr:rr + n, :], rgf[0:n, :], reads=[rvbb], writes=[g.scrb])
                pt, ptb = next_pst(g)

                def tr4(e, pt=pt, n=n):
                    last = None
                    for j in range(4):
                        last = e.transpose(out=pt[:, j * 128:j * 128 + n], in_=qk[0:n, 2 * j:2 * j + 2, :].rearrange("p h c -> p (h c)"),
                                           identity=g.ident[0:n, 0:n])
                    return last
                s.op("pe", tr4, reads=[qkb, g.identb], writes=[ptb])
                s.op("dve", lambda e, pt=pt, n=n: e.tensor_copy(out=qkT[:, :, 0:n], in_=pt[:, 0:512].rearrange("p (j c) -> p j c", j=4)[:, :, 0:n]),
                     reads=[ptb], writes=[qkTb])
                s.dma("sp", g.RQKT[:, :, rr:rr + n].rearrange("t p c -> p t c"), qkT[:, :, 0:n], reads=[qkTb], writes=[g.scrb])


@staged
def retention(g, o):
    nc, s, R, T = g.nc, g.s, g.R, g.T
    I, O = g.I, g.O
    with ExitStack() as es:
        tag = "rt%d" % o
        decT = T_(g, es, "decT" + tag, [128, 4, 128], F32)
        gq = T_(g, es, "gqd" + tag, [128, 4, 128], F32)
        ginv = T_(g, es, "ginv" + tag, [128, 4], F32)
        cb = Buf()
        s.dma("sp", decT[:], I["ret_decT"].rearrange("h j i -> j h i"), writes=[cb])
        s.dma("sp", gq[:], I["ret_gq"].rearrange("h d i -> d h i"), writes=[cb])
        s.dma("sp", ginv[:], I["ret_ginv"], writes=[cb])
        S = [T_(g, es, "S%d" % h + tag, [64, 128], F32) for h in range(4)]
        Sb = [T_(g, es, "Sb%d" % h + tag, [64, 128], BF16) for h in range(4)]
        Sbuf = [Buf() for _ in range(4)]
        qT2 = [T_(g, es, "qT%d" % k + tag, [64, 8, 128], BF16) for k in range(2)]
        kt2 = [T_(g, es, "kt%d" % k + tag, [128, 256], BF16) for k in range(2)]
        v2 = [T_(g, es, "v%d" % k + tag, [128, 512], BF16) for k in range(2)]
        gg2 = [T_(g, es, "gg%d" % k + tag, [128, 512], F32) for k in range(2)]
        lb2 = [Buf(), Buf()]
        ci = 0
        PT = [T_(g, es, "PT%d" % k + tag, [128, 128], BF16) for k in range(2)]
        qd = [T_(g, es, "qd%d" % k + tag, [128, 128], BF16) for k in range(2)]
        kd_ = [T_(g, es, "kd%d" % k + tag, [128, 64], BF16) for k in range(2)]
        wbf = [Buf(), Buf()]
        st4 = [T_(g, es, "st%d" % h + tag, [128, 6], F32) for h in range(4)]
        mv4 = [T_(g, es, "mv%d" % h + tag, [128, 2], F32) for h in range(4)]
        rs4 = [T_(g, es, "rs%d" % h + tag, [128, 1], F32) for h in range(4)]
        nm4 = [T_(g, es, "nm%d" % h + tag, [128, 1], F32) for h in range(4)]
        stb4 = [Buf() for _ in range(4)]
        on4 = [T_(g, es, "on%d" % h + tag, [128, 128], F32) for h in range(4)]
        ro = T_(g, es, "ro" + tag, [128, 4, 128], BF16)
        rob = Buf()
        roT = T_(g, es, "roT" + tag, [128, 4, 128], BF16)
        roTb = Buf()
        gam = [1.0 - 2.0 ** (-5.0 - h) for h in range(4)]
        it = 0
        for seg, (c0, c1) in enumerate(((0, T), (T, R))):
            for h in range(4):
                if seg == 0:
                    s.op("pool", lambda e, h=h: e.memset(S[h][:], 0.0), reads=[Sbuf[h]], writes=[Sbuf[h]])
                else:
                    s.dma("sp", S[h][:], I["st_ret"][o, h], reads=[Sbuf[h]], writes=[Sbuf[h]])
                s.op("act", lambda e, h=h: e.copy(out=Sb[h][:], in_=S[h][:]), reads=[Sbuf[h]], writes=[Sbuf[h]])
            cks = chunks(c1 - c0, 128)

            def ch_loads(j, cj):
                if j >= len(cks):
                    return
                co_, n_ = cks[j]
                rr_ = c0 + co_
                s.dma("sp", qT2[cj % 2][:, :, 0:n_], g.RQKT[:, :, rr_:rr_ + n_].rearrange("t (hh p) c -> p (t hh) c", hh=2),
                      reads=[g.scrb], writes=[lb2[cj % 2]])
                s.dma("sp", kt2[cj % 2][0:n_, :], g.RK[rr_:rr_ + n_, :], reads=[g.scrb], writes=[lb2[cj % 2]])
                s.dma("sp", v2[cj % 2][0:n_, :], g.RV[rr_:rr_ + n_, :], reads=[g.scrb], writes=[lb2[cj % 2]])
                s.dma("sp", gg2[cj % 2][0:n_, :], g.RG[rr_:rr_ + n_, :], reads=[g.scrb], writes=[lb2[cj % 2]])
            ch_loads(0, ci)
            for j, (co, n) in enumerate(cks):
                rr = c0 + co
                qT, kt_, v, gg, lb = qT2[ci % 2], kt2[ci % 2], v2[ci % 2], gg2[ci % 2], lb2[ci % 2]
                ci += 1
                ch_loads(j + 1, ci)
                for h in range(4):
                    k = it % 2
                    it += 1
                    st, mv, rs, nm, stb, on = st4[h], mv4[h], rs4[h], nm4[h], stb4[h], on4[h]
                    p0 = 0
                    qh = qT[0:64, h, 0:n]
                    kh = qT[0:64, 4 + h, 0:n]
                    psc, pscb = next_ps(g)
                    s.op("pe", lambda e, psc=psc, kh=kh, qh=qh: e.matmul(psc[0:n, 0:n], lhsT=kh, rhs=qh, start=True, stop=True),
                         reads=[lb], writes=[pscb])
                    s.op("dve", lambda e, psc=psc, k=k, h=h: e.tensor_tensor(out=PT[k][0:n, 0:n], in0=psc[0:n, 0:n], in1=decT[0:n, h, 0:n],
                                                                              op=ALU.mult), reads=[pscb, cb], writes=[wbf[k]])
                    s.op("pool", lambda e, k=k, h=h, qh=qh, p0=p0: e.tensor_tensor(out=qd[k][p0:p0 + 64, 0:n], in0=qh, in1=gq[p0:p0 + 64, h, 0:n],
                                                                                   op=ALU.mult), reads=[lb, cb], writes=[wbf[k]])
                    s.op("dve", lambda e, k=k, h=h, kt_=kt_: e.tensor_scalar(out=kd_[k][0:n, :], in0=kt_[0:n, h * 64:(h + 1) * 64],
                                                                    scalar1=ginv[0:n, h:h + 1], scalar2=float(gam[h] ** (n - 1)),
                                                                    op0=ALU.mult, op1=ALU.mult), reads=[lb, cb], writes=[wbf[k]])
                    po, pob = next_ps(g)

                    def mmo(e, po=po, k=k, h=h, p0=p0, v=v):
                        e.matmul(po[0:n, 0:128], lhsT=PT[k][0:n, 0:n], rhs=v[0:n, h * 128:(h + 1) * 128], start=True, stop=False)
                        return e.matmul(po[0:n, 0:128], lhsT=qd[k][p0:p0 + 64, 0:n], rhs=Sb[h][:, :], start=False, stop=True)
                    s.op("pe", mmo, reads=[wbf[k], lb, Sbuf[h]], writes=[pob])
                    pst_, pstb_ = next_ps(g)
                    s.op("pe", lambda e, pst_=pst_, k=k, h=h, v=v: e.matmul(pst_[0:64, 0:128], lhsT=kd_[k][0:n, :], rhs=v[0:n, h * 128:(h + 1) * 128],
                                                                       start=True, stop=True), reads=[wbf[k], lb], writes=[pstb_])
                    s.op("dve", lambda e, pst_=pst_, h=h: e.scalar_tensor_tensor(out=S[h][:], in0=S[h][:], scalar=float(gam[h] ** n),
                                                                                 in1=pst_[0:64, 0:128], op0=ALU.mult, op1=ALU.add),
                         reads=[pstb_, Sbuf[h]], writes=[Sbuf[h]])
                    s.op("act", lambda e, h=h: e.copy(out=Sb[h][:], in_=S[h][:]), reads=[Sbuf[h]], writes=[Sbuf[h]])
                    s.op("dve", lambda e, po=po, st=st: e.bn_stats(out=st[0:n, :], in_=po[0:n, 0:128]), reads=[pob], writes=[stb])
                    s.op("dve", lambda e, mv=mv, st=st: e.bn_aggr(out=mv[0:n, :], in_=st[0:n, :]), reads=[stb], writes=[stb])
                    s.op("dve", lambda e, rs=rs, mv=mv: e.tensor_scalar(out=rs[0:n, :], in0=mv[0:n, 1:2], scalar1=EPS, scalar2=None, op0=ALU.add),
                         reads=[stb], writes=[stb])
                    s.op("act", lambda e, rs=rs: e.sqrt(out=rs[0:n, :], in_=rs[0:n, :]), reads=[stb], writes=[stb])
                    s.op("dve", lambda e, rs=rs: e.reciprocal(out=rs[0:n, :], in_=rs[0:n, :]), reads=[stb], writes=[stb])
                    s.op("dve", lambda e, nm=nm, mv=mv, rs=rs: e.scalar_tensor_tensor(out=nm[0:n, :], in0=mv[0:n, 0:1], scalar=-1.0, in1=rs[0:n, :],
                                                                 op0=ALU.mult, op1=ALU.mult), reads=[stb], writes=[stb])
                    s.op("act", lambda e, po=po, on=on, nm=nm, rs=rs: e.activation(out=on[0:n, :], in_=po[0:n, 0:128], func=AF.Identity, bias=nm[0:n, 0:1],
                                                              scale=rs[0:n, 0:1]), reads=[stb, pob], writes=[stb])
                    s.op("pool", lambda e, h=h, on=on, gg=gg: e.tensor_tensor(out=ro[0:n, h, :], in0=on[0:n, :], in1=gg[0:n, h * 128:(h + 1) * 128], op=ALU.mult),
                         reads=[stb, lb], writes=[rob])
                pt, ptb = next_pst(g)

                def tr5(e, pt=pt):
                    last = None
                    for h in range(4):
                        last = e.transpose(out=pt[:, h * 128:h * 128 + n], in_=ro[0:n, h, :], identity=g.ident[0:n, 0:n])
                    return last
                s.op("pe", tr5, reads=[rob, g.identb], writes=[ptb])
                s.op("dve", lambda e, pt=pt: e.tensor_copy(out=roT[:, :, 0:n], in_=pt[:, 0:512].rearrange("p (j c) -> p j c", j=4)[:, :, 0:n]),
                     reads=[ptb], writes=[roTb])
                s.dma("sp", g.MIXT[4:8, :, rr:rr + n].rearrange("t p c -> p t c"), roT[:, :, 0:n], reads=[roTb], writes=[g.scrb])
            pre = "p_" if seg == 0 else "s_"
            for h in range(4):
                s.dma("sp", O[pre + "ret"][o, h], S[h][:], reads=[Sbuf[h]], writes=[g.outb])


def make_consts(SEQ, PAST):
    T = NMETA + SEQ
    R = T + DSEQ
    bf = ml_dtypes.bfloat16
    c = {"ident_bf": np.eye(128, dtype=np.float32).astype(bf), "ident_f": np.eye(128, dtype=np.float32)}
    kk = np.arange(128)[:, None]
    qq = np.arange(512)[None, :]
    lim = NMETA + 64 * (np.floor_divide(qq - NMETA, 64) + 1)
    NEGM = -30000.0
    c["mask_mla"] = np.stack([np.where((128 * d + kk) < lim, 0.0, NEGM) for d in range(5)]).astype(np.float32).astype(bf)
    c["mask_fox"] = np.stack([np.where((128 * d + kk) <= qq, 0.0, NEGM) for d in range(4)]).astype(np.float32).astype(bf)
    pos = np.concatenate([np.arange(T), NMETA + PAST + np.arange(DSEQ)]).astype(np.float32)
    inv = (10000.0 ** (-np.arange(16, dtype=np.float32) / 16)).astype(np.float32)
    ang = (pos[:, None] * inv[None, :]).astype(np.float32)
    cs, sn = np.cos(ang).astype(np.float32), np.sin(ang).astype(np.float32)
    A = np.concatenate([cs, cs], -1)
    B = np.concatenate([-sn, sn], -1)
    c["ropeA16"] = np.ascontiguousarray(np.broadcast_to(A[:, None, :], (R, 4, 32))).astype(np.float32)
    c["ropeB16"] = np.ascontiguousarray(np.broadcast_to(B[:, None, :], (R, 4, 32))).astype(np.float32)
    inv2 = (10000.0 ** (-np.arange(32, dtype=np.float32) / 32)).astype(np.float32)
    ang2 = (pos[:, None] * inv2[None, :]).astype(np.float32)
    c2, s2 = np.cos(ang2).astype(np.float32), np.sin(ang2).astype(np.float32)
    A2 = np.concatenate([c2, c2], -1)
    B2 = np.concatenate([-s2, s2], -1)
    scl = np.array([1.0] * 4 + [0.125] * 4, np.float32)[None, :, None]
    c["ropeA32"] = np.ascontiguousarray(A2[:, None, :] * scl).astype(np.float32)
    c["ropeB32"] = np.ascontiguousarray(B2[:, None, :] * scl).astype(np.float32)
    gam = np.array([1.0 - 2.0 ** (-5.0 - h) for h in range(4)], np.float64)
    jj = np.arange(128)
    dif = jj[None, :] - jj[:, None]
    c["ret_decT"] = np.stack([np.where(dif >= 0, gam[h] ** np.maximum(dif, 0), 0.0) for h in range(4)]).astype(np.float32)
    c["ret_gq"] = np.stack([np.broadcast_to((gam[h] ** (jj + 1.0))[None, :], (128, 128)) for h in range(4)]).astype(np.float32)
    c["ret_ginv"] = np.stack([gam[h] ** (-jj.astype(np.float64)) for h in range(4)], axis=1).astype(np.float32)
    c["tau"] = np.ascontiguousarray(np.broadcast_to(np.arange(1, 513, dtype=np.float32)[None, :], (128, 512)))
    return c


def make_in_maps(inp, n_cores=8):
    f = lambda a: np.ascontiguousarray(np.asarray(a, dtype=np.float32))
    consts = make_consts(inp["x_prompt"].shape[1], inp["cache_mla_ckv"].shape[2])
    nb = inp["x_prompt"].shape[0]
    maps = []
    for c in range(n_cores):
        bp = PROMPT_OF_CORE.get(c) if n_cores == 8 else c % nb
        xp = f(inp["x_prompt"][bp]) if bp is not None else np.zeros(inp["x_prompt"].shape[1:], np.float32)
        meta = f(inp["meta_tokens"]) if bp is not None else np.zeros(inp["meta_tokens"].shape, np.float32)
        m = {
            "xp": xp, "xs": f(inp["x_sample"][c]), "meta": meta,
            "c_ckv": f(inp["cache_mla_ckv"][:, c]), "c_kpe": f(inp["cache_mla_kpe"][:, c]),
            "c_fk": f(np.asarray(inp["cache_fox_k"])[:, c].reshape(2, -1, 512)),
            "c_fv": f(np.asarray(inp["cache_fox_v"])[:, c].reshape(2, -1, 512)),
            "c_flf": f(inp["cache_fox_logf"][:, c]),
            "st_re": f(inp["state_s5_re"][:, c]), "st_im": f(inp["state_s5_im"][:, c]), "st_ret": f(inp["state_ret"][:, c]),
            "ln_g": f(inp["ln_g"]), "ln_b": f(inp["ln_b"]),
            "wg": f(inp["ffn_w_gate"]), "wu": f(inp["ffn_w_up"]), "wd": f(inp["ffn_w_down"]),
            "ewin": f(inp["even_w_in"]), "ewout": f(inp["even_w_out"]),
            "owin": f(inp["odd_w_in"]), "owout": f(inp["odd_w_out"]), "fox_b_f": f(inp["fox_b_f"]),
        }
        for k in ("s5_a_re", "s5_a_im", "s5_b_re", "s5_b_im", "s5_c_re", "s5_c_im", "s5_d", "s5_log_dt", "s5_w_glu",
                  "s5_b_glu", "mla_q_norm", "mla_kv_norm", "mla_w_uq", "mla_w_ukv"):
            m[k] = f(inp[k])
        m.update(consts)
        maps.append(m)
    return maps


PROMPT_OF_CORE = {0: 0, 1: 1, 4: 2, 5: 3}
CORE_OF_PROMPT = {b: c for c, b in PROMPT_OF_CORE.items()}


def gather(res, SEQ, nb_p=4, n_cores=8):
    T = NMETA + SEQ
    r = res.results
    P = lambda k: np.stack([r[CORE_OF_PROMPT[b] if n_cores == 8 else b][k] for b in range(nb_p)])
    S = lambda k: np.stack([r[c][k] for c in range(n_cores)])
    sw = lambda a: np.swapaxes(a, 0, 1)
    outs = [P("y_p"), S("y_s"),
            sw(P("p_ckv")), sw(P("p_kpe")), sw(P("p_fk")).reshape(2, nb_p, T, 8, 64), sw(P("p_fv")).reshape(2, nb_p, T, 8, 64),
            sw(P("p_flf")), sw(P("p_s5re")), sw(P("p_s5im")), sw(P("p_ret")),
            sw(S("s_ckv")), sw(S("s_kpe")), sw(S("s_fk")).reshape(2, n_cores, DSEQ, 8, 64),
            sw(S("s_fv")).reshape(2, n_cores, DSEQ, 8, 64), sw(S("s_flf")), sw(S("s_s5re")), sw(S("s_s5im")), sw(S("s_ret"))]
    return tuple(np.ascontiguousarray(o.astype(np.float32)) for o in outs)


def kernel(**inputs):
    SEQ = inputs["x_prompt"].shape[1]
    PAST = inputs["cache_mla_ckv"].shape[2]
    nc = build(SEQ, PAST)
    in_maps = make_in_maps(inputs)
    res = run_bass_kernel_spmd(nc, in_maps, core_ids=list(range(8)))
    return gather(res, SEQ)
```

```python
import math
import numpy as np
import ml_dtypes
from contextlib import ExitStack
import concourse.bass as bass
import concourse.mybir as mybir
from concourse.bass_utils import run_bass_kernel_spmd

F32 = mybir.dt.float32
BF16 = mybir.dt.bfloat16
AF = mybir.ActivationFunctionType
ALU = mybir.AluOpType
AX = mybir.AxisListType

D = 1024
DFF = 2816
NFT = DFF // 128
DEPTH = 4
ALPHA = (2.0 * DEPTH) ** 0.25
EPS = 1e-5
NMETA = 16
DSEQ = 16
NDS = 8


class Buf:
    __slots__ = ("w", "r", "excl")

    def __init__(self, excl=False):
        self.w = None
        self.r = {}
        self.excl = excl


class Sched:
    def __init__(self, nc, es):
        self.nc = nc
        self.engs = {"pe": nc.tensor, "act": nc.scalar, "dve": nc.vector, "pool": nc.gpsimd, "sp": nc.sync}
        self.sem = {k: es.enter_context(nc.semaphore("sem_" + k)) for k in ("pe", "act", "dve", "pool")}
        self.cnt = {k: 0 for k in self.sem}
        self.seen = {e: {} for e in self.engs}
        self.dq = {}
        for q in ("sp", "pool", "act"):
            self.dq[q] = [[es.enter_context(nc.semaphore("dma_%s_%d" % (q, i))), 0] for i in range(NDS)]
        self.dqi = {q: 0 for q in self.dq}
        self.out_toks = []
        self.nins = 0

    def _wait(self, e, tok):
        if tok is None:
            return
        key, sem, val = tok
        if e == "pe" and key == "pe":
            return
        if self.seen[e].get(key, 0) >= val:
            return
        self.engs[e].wait_ge(sem, val)
        self.seen[e][key] = val

    def _deps(self, e, reads, writes):
        for b in reads:
            self._wait(e, b.w)
            if b.excl:
                for k, t in list(b.r.items()):
                    if k != e:
                        self._wait(e, t)
        for b in writes:
            self._wait(e, b.w)
            for t in list(b.r.values()):
                self._wait(e, t)

    def _commit(self, tok, reads, writes):
        for b in reads:
            b.r[tok[0]] = tok
        for b in writes:
            b.w = tok
            b.r = {}

    def op(self, e, fn, reads=(), writes=()):
        self._deps(e, reads, writes)
        ins = fn(self.engs[e])
        self.cnt[e] += 1
        ins.then_inc(self.sem[e], 1)
        tok = (e, self.sem[e], self.cnt[e])
        self._commit(tok, reads, writes)
        self.nins += 1

    def dma(self, q, out, in_, reads=(), writes=(), is_output=False, **kw):
        idx = self.dqi[q] % NDS
        self.dqi[q] += 1
        slot = self.dq[q][idx]
        key = "d%s%d" % (q, idx)
        if slot[1] > 0:
            self._wait(q, (key, slot[0], slot[1]))
        self._deps(q, reads, writes)
        self.engs[q].dma_start(out=out, in_=in_, **kw).then_inc(slot[0], 16)
        slot[1] += 16
        tok = (key, slot[0], slot[1])
        self._commit(tok, reads, writes)
        self.nins += 1

    def barrier(self):
        toks = []
        for q in self.dq:
            for idx, slot in enumerate(self.dq[q]):
                if slot[1] > 0:
                    toks.append(("d%s%d" % (q, idx), slot[0], slot[1]))
        for e in ("pe", "act", "dve", "pool"):
            if self.cnt[e] > 0:
                toks.append((e, self.sem[e], self.cnt[e]))
        for e in self.engs:
            for t in toks:
                if e == "pe" and t[0] == "pe":
                    if self.seen[e].get("pe", 0) < t[2]:
                        self.engs[e].wait_ge(t[1], t[2])
                        self.seen[e]["pe"] = t[2]
                    continue
                self._wait(e, t)

    def finish(self):
        for q in self.dq:
            for idx, slot in enumerate(self.dq[q]):
                if slot[1] > 0:
                    self._wait("sp", ("d%s%d" % (q, idx), slot[0], slot[1]))
        for e in ("pe", "act", "dve", "pool"):
            if self.cnt[e] > 0:
                self._wait("sp", (e, self.sem[e], self.cnt[e]))


class Ctx:
    pass


def chunks(n, c):
    return [(o, min(c, n - o)) for o in range(0, n, c)]


def build(SEQ, PAST, stop_after=None):
    T = NMETA + SEQ
    R = T + DSEQ
    nc = bass.Bass("TRN2", target_bir_lowering=False)
    g = Ctx()
    g.nc, g.T, g.R, g.SEQ, g.PAST = nc, T, R, SEQ, PAST

    def din(name, shape, dt=F32):
        return nc.dram_tensor(name, list(shape), dt, kind="ExternalInput").ap()

    def dout(name, shape, dt=F32):
        return nc.dram_tensor(name, list(shape), dt, kind="ExternalOutput").ap()

    def dscr(name, shape, dt=F32):
        return nc.dram_tensor(name, list(shape), dt, kind="Internal").ap()

    I = {}
    for name, shape in input_shapes(SEQ, PAST).items():
        I[name] = din(name, shape, BF16 if name in BF16_CONSTS else F32)
    O = {}
    for name, shape in output_shapes(SEQ).items():
        O[name] = dout(name, shape)
    g.I, g.O = I, O
    g.X = dscr("X", [R, D])
    g.UT = dscr("UT", [4, 128, R], BF16)
    g.QT = dscr("QT", [8, 96, R], BF16)
    g.KTp = dscr("KTp", [8, 96, T], BF16)
    g.KTs = dscr("KTs", [8, 96, PAST + DSEQ], BF16)
    g.VAp = dscr("VAp", [T, 8 * 128], BF16)
    g.VAs = dscr("VAs", [PAST + DSEQ, 8 * 128], BF16)
    g.MIXT = dscr("MIXT", [8, 128, R], BF16)
    g.FQT = dscr("FQT", [4, 128, R], BF16)
    g.FKTp = dscr("FKTp", [4, 128, T], BF16)
    g.FKTs = dscr("FKTs", [4, 128, PAST + DSEQ], BF16)
    g.FTp = dscr("FTp", [8, T], F32)
    g.FTs = dscr("FTs", [8, PAST + DSEQ], F32)
    g.FT3p = dscr("FT3p", [8, 3, T], BF16)
    g.FT3s = dscr("FT3s", [8, 3, PAST + DSEQ], BF16)
    g.RK = dscr("RK", [R, 256], BF16)
    g.RQKT = dscr("RQKT", [4, 128, R], BF16)
    g.RV = dscr("RV", [R, 512], BF16)
    g.RG = dscr("RG", [R, 512], F32)
    g.scrb = Buf()
    g.outb = Buf()
    g.Xb = [Buf() for _ in range((R + 127) // 128)]
    with ExitStack() as es:
        s = Sched(nc, es)
        g.s = s
        g.ps = [es.enter_context(nc.psum_tensor("ps%d" % i, [128, 512], F32)) for i in range(6)]
        g.psb = [Buf(True) for _ in range(6)]
        g.pst = [es.enter_context(nc.psum_tensor("pst%d" % i, [128, 1024], BF16)) for i in range(2)]
        g.pstb = [Buf(True) for _ in range(2)]
        g.psi = 0
        g.psti = 0
        g.ident = es.enter_context(nc.sbuf_tensor("ident", [128, 128], BF16))
        g.identb = Buf()
        s.dma("sp", g.ident[:], I["ident_bf"], writes=[g.identb])
        s.dma("sp", g.X[0:NMETA, :], I["meta"], writes=xbufs(g, 0, NMETA))
        s.dma("sp", g.X[NMETA:T, :], I["xp"], writes=xbufs(g, NMETA, SEQ))
        s.dma("sp", g.X[T:R, :], I["xs"], writes=xbufs(g, T, DSEQ))
        done = False
        for l in range(DEPTH):
            ffn_stage(g, l, 0)
            if stop_after == ("ffn", l, 0):
                break
            if l % 2 == 0:
                even_stage(g, l // 2)
            else:
                odd_stage(g, l // 2)
            if stop_after == ("mix", l):
                break
            ffn_stage(g, l, 1)
            if stop_after == ("ffn", l, 1):
                break
        s.dma("sp", O["y_p"], g.X[NMETA:T, :], reads=xbufs(g, NMETA, SEQ))
        s.dma("sp", O["y_s"], g.X[T:R, :], reads=xbufs(g, T, DSEQ))
        s.finish()
    return nc


def staged(fn):
    def w(g, *a, **k):
        r = fn(g, *a, **k)
        g.s.barrier()
        return r
    return w


def xbufs(g, r0, n):
    return g.Xb[r0 // 128:(r0 + n - 1) // 128 + 1]


def next_ps(g):
    i = g.psi % len(g.ps)
    g.psi += 1
    return g.ps[i], g.psb[i]


def next_pst(g):
    i = g.psti % len(g.pst)
    g.psti += 1
    return g.pst[i], g.pstb[i]


BF16_CONSTS = ("ident_bf", "mask_mla", "mask_fox")


def input_shapes(SEQ, PAST):
    T = NMETA + SEQ
    R = T + DSEQ
    return {
        "xp": (SEQ, D), "xs": (DSEQ, D), "meta": (NMETA, D),
        "c_ckv": (2, PAST, 128), "c_kpe": (2, PAST, 32), "c_fk": (2, PAST, 512), "c_fv": (2, PAST, 512),
        "c_flf": (2, PAST, 8), "st_re": (2, 32, 64), "st_im": (2, 32, 64), "st_ret": (2, 4, 64, 128),
        "ln_g": (4, 3, D), "ln_b": (4, 3, D),
        "wg": (4, 2, D, DFF), "wu": (4, 2, D, DFF), "wd": (4, 2, DFF, D),
        "ewin": (2, D, 928), "ewout": (2, D, D),
        "s5_a_re": (2, 32, 64), "s5_a_im": (2, 32, 64), "s5_b_re": (2, 32, 64, 16), "s5_b_im": (2, 32, 64, 16),
        "s5_c_re": (2, 32, 16, 64), "s5_c_im": (2, 32, 16, 64), "s5_d": (2, 512), "s5_log_dt": (2, 32),
        "s5_w_glu": (2, 512, 512), "s5_b_glu": (2, 512),
        "mla_q_norm": (2, 256), "mla_kv_norm": (2, 128), "mla_w_uq": (2, 256, 768), "mla_w_ukv": (2, 128, 1024),
        "owin": (2, D, 3080), "owout": (2, D, D), "fox_b_f": (2, 8),
        "ident_bf": (128, 128), "ident_f": (128, 128), "mask_mla": (5, 128, 512), "mask_fox": (4, 128, 512),
        "ropeA16": (R, 4, 32), "ropeB16": (R, 4, 32), "tau": (128, 512),
        "ropeA32": (R, 8, 64), "ropeB32": (R, 8, 64), "ret_decT": (4, 128, 128), "ret_gq": (4, 128, 128), "ret_ginv": (128, 4),
    }


def output_shapes(SEQ):
    T = NMETA + SEQ
    return {
        "y_p": (SEQ, D), "y_s": (DSEQ, D),
        "p_ckv": (2, T, 128), "p_kpe": (2, T, 32), "p_fk": (2, T, 512), "p_fv": (2, T, 512), "p_flf": (2, T, 8),
        "p_s5re": (2, 32, 64), "p_s5im": (2, 32, 64), "p_ret": (2, 4, 64, 128),
        "s_ckv": (2, DSEQ, 128), "s_kpe": (2, DSEQ, 32), "s_fk": (2, DSEQ, 512), "s_fv": (2, DSEQ, 512),
        "s_flf": (2, DSEQ, 8), "s_s5re": (2, 32, 64), "s_s5im": (2, 32, 64), "s_ret": (2, 4, 64, 128),
    }


def x_load_tile(g, blk, ti, xf, xfb):
    if blk is None:
        return
    r0, nb = blk
    tl = chunks(nb, 128)
    if ti >= len(tl):
        return
    o, n = tl[ti]
    g.s.dma("sp", xf[0:n, ti, :], g.X[r0 + o:r0 + o + n, :], reads=xbufs(g, r0 + o, n), writes=[xfb[ti]])


def load_xT(g, es_tiles, r0, nb, xf, xfb, xb, xbb, xT, xTb, preloaded=False):
    s = g.s
    for ti, (o, n) in enumerate(chunks(nb, 128)):
        if not preloaded:
            s.dma("sp", xf[0:n, ti, :], g.X[r0 + o:r0 + o + n, :], reads=xbufs(g, r0 + o, n), writes=[xfb[ti]])
        xi = ti % 2
        s.op("act", lambda e, xi=xi, n=n, ti=ti: e.copy(out=xb[0:n, xi, :], in_=xf[0:n, ti, :]), reads=[xfb[ti]], writes=[xbb[xi]])
        for half in range(2):
            pt, ptb = next_pst(g)

            def tr(e, half=half, pt=pt, n=n, ti=xi):
                last = None
                for j in range(4):
                    kt = half * 4 + j
                    last = e.transpose(out=pt[:, j * 128:j * 128 + n], in_=xb[0:n, ti, kt * 128:(kt + 1) * 128],
                                       identity=g.ident[0:n, 0:n])
                return last
            s.op("pe", tr, reads=[xbb[xi], g.identb], writes=[ptb])
            s.op("dve", lambda e, half=half, pt=pt, n=n, o=o: e.tensor_copy(
                out=xT[:, half * 4:half * 4 + 4, o:o + n],
                in_=pt[:, 0:512].rearrange("p (j c) -> p j c", j=4)[:, :, 0:n]),
                reads=[ptb], writes=[xTb])


def layer_norm_rows(g, y, yb, n, gt, bt, eps, out, outb, tmp):
    s = g.s
    st, mv, rs, nm = tmp["st"], tmp["mv"], tmp["rs"], tmp["nm"]
    sb = tmp["b"]

    s.op("dve", lambda e: e.bn_stats(out=st[0:n, 0, :], in_=y[0:n, 0:512]), reads=[yb], writes=[tmp["b0"]])
    s.op("dve", lambda e: e.bn_stats(out=st[0:n, 1, :], in_=y[0:n, 512:1024]), reads=[yb], writes=[tmp["b1"]])
    s.op("dve", lambda e: e.bn_aggr(out=mv[0:n, :], in_=st[0:n, :, :].rearrange("p a b -> p (a b)")),
         reads=[tmp["b0"], tmp["b1"]], writes=[sb])
    s.op("dve", lambda e: e.tensor_scalar(out=rs[0:n, :], in0=mv[0:n, 1:2], scalar1=eps, scalar2=None,
                                          op0=ALU.add), reads=[sb], writes=[sb])
    s.op("act", lambda e: e.sqrt(out=rs[0:n, :], in_=rs[0:n, :]), reads=[sb], writes=[sb])
    s.op("dve", lambda e: e.reciprocal(out=rs[0:n, :], in_=rs[0:n, :]), reads=[sb], writes=[sb])
    s.op("dve", lambda e: e.scalar_tensor_tensor(out=nm[0:n, :], in0=mv[0:n, 0:1], scalar=-1.0, in1=rs[0:n, :],
                                                 op0=ALU.mult, op1=ALU.mult), reads=[sb], writes=[sb])
    s.op("act", lambda e: e.activation(out=y[0:n, :], in_=y[0:n, :], func=AF.Identity, bias=nm[0:n, 0:1],
                                       scale=rs[0:n, 0:1]), reads=[sb, yb], writes=[yb])
    s.op("pool", lambda e: e.tensor_tensor(out=y[0:n, :], in0=y[0:n, :], in1=gt[0:n, :], op=ALU.mult),
         reads=[yb, tmp["gb"]], writes=[yb])
    s.op("pool", lambda e: e.tensor_tensor(out=out[0:n, :], in0=y[0:n, :], in1=bt[0:n, :], op=ALU.add),
         reads=[yb, tmp["gb"]], writes=[outb])


def alloc_ln(g, es, l, i, tag):
    nc, s = g.nc, g.s
    t = {}
    t["g"] = es.enter_context(nc.sbuf_tensor("lng" + tag, [128, D], F32))
    t["bt"] = es.enter_context(nc.sbuf_tensor("lnb" + tag, [128, D], F32))
    t["st"] = es.enter_context(nc.sbuf_tensor("lnst" + tag, [128, 2, 6], F32))
    t["mv"] = es.enter_context(nc.sbuf_tensor("lnmv" + tag, [128, 2], F32))
    t["rs"] = es.enter_context(nc.sbuf_tensor("lnrs" + tag, [128, 1], F32))
    t["nm"] = es.enter_context(nc.sbuf_tensor("lnnm" + tag, [128, 1], F32))
    t["b"] = Buf()
    t["b0"] = Buf()
    t["b1"] = Buf()
    t["gb"] = Buf()
    s.dma("sp", t["g"][:], g.I["ln_g"][l, i:i + 1, :].broadcast_to([128, D]), writes=[t["gb"]])
    s.dma("sp", t["bt"][:], g.I["ln_b"][l, i:i + 1, :].broadcast_to([128, D]), writes=[t["gb"]])
    return t


@staged
def ffn_stage(g, l, i):
    nc, s, R = g.nc, g.s, g.R
    NB = 512
    with ExitStack() as es:
        tag = "f%d%d" % (l, i)
        wg = es.enter_context(nc.sbuf_tensor("wg" + tag, [128, 8, DFF], BF16))
        wu = es.enter_context(nc.sbuf_tensor("wu" + tag, [128, 8, DFF], BF16))
        wd = es.enter_context(nc.sbuf_tensor("wd" + tag, [128, NFT, D], BF16))
        wgb, wub, wdb = Buf(), Buf(), Buf()
        wgs = g.I["wg"][l, i].rearrange("(kt p) f -> p kt f", p=128)
        wus = g.I["wu"][l, i].rearrange("(kt p) f -> p kt f", p=128)
        wds = g.I["wd"][l, i].rearrange("(ft p) d -> p ft d", p=128)
        for kt in range(8):
            s.dma("pool", wg[:, kt, :], wgs[:, kt, :], writes=[wgb])
            s.dma("pool", wu[:, kt, :], wus[:, kt, :], writes=[wub])
        for ft in range(0, NFT, 2):
            s.dma("pool", wd[:, ft:ft + 2, :], wds[:, ft:ft + 2, :], writes=[wdb])
        ln = alloc_ln(g, es, l, 0 if i == 0 else 2, tag)
        xf = es.enter_context(nc.sbuf_tensor("xf" + tag, [128, 4, D], F32))
        xb = es.enter_context(nc.sbuf_tensor("xb" + tag, [128, 2, D], BF16))
        xT = es.enter_context(nc.sbuf_tensor("xT" + tag, [128, 8, NB], BF16))
        hT = es.enter_context(nc.sbuf_tensor("hT" + tag, [128, NFT, NB], BF16))
        sg = [es.enter_context(nc.sbuf_tensor("sg%d" % k + tag, [128, NB], F32)) for k in range(2)]
        y = [es.enter_context(nc.sbuf_tensor("y%d" % k + tag, [128, D], F32)) for k in range(2)]
        xfb = [Buf() for _ in range(4)]
        xbb = [Buf() for _ in range(4)]
        xTb, hTb = Buf(), Buf()
        sgb = [Buf(), Buf()]
        yb = [Buf(), Buf()]
        yi = 0
        blks = chunks(R, NB)
        for ti in range(4):
            x_load_tile(g, blks[0], ti, xf, xfb)
        for bi, (r0, nb) in enumerate(blks):
            nxt = blks[bi + 1] if bi + 1 < len(blks) else None
            load_xT(g, es, r0, nb, xf, xfb, xb, xbb, xT, xTb, preloaded=True)
            for ft in range(NFT):
                pg, pgb = next_ps(g)
                pu, pub = next_ps(g)

                def mmg(e, w=wg, p=pg, ft=ft, nb=nb):
                    last = None
                    for kt in range(8):
                        last = e.matmul(p[:, 0:nb], lhsT=w[:, kt, ft * 128:(ft + 1) * 128], rhs=xT[:, kt, 0:nb],
                                        start=(kt == 0), stop=(kt == 7))
                    return last
                s.op("pe", mmg, reads=[wgb, xTb], writes=[pgb])
                s.op("pe", lambda e, ft=ft, nb=nb, pu=pu: mmg(e, wu, pu, ft, nb), reads=[wub, xTb], writes=[pub])
                k = ft % 2
                s.op("act", lambda e, k=k, pg=pg, nb=nb: e.activation(out=sg[k][:, 0:nb], in_=pg[:, 0:nb], func=AF.Silu),
                     reads=[pgb], writes=[sgb[k]])
                s.op("dve", lambda e, k=k, pu=pu, nb=nb, ft=ft: e.tensor_tensor(out=hT[:, ft, 0:nb], in0=sg[k][:, 0:nb],
                                                                                 in1=pu[:, 0:nb], op=ALU.mult),
                     reads=[sgb[k], pub], writes=[hTb])
            for ti, (o, n) in enumerate(chunks(nb, 128)):
                yy, yyb = y[yi % 2], yb[yi % 2]
                yi += 1
                for half in range(2):
                    po, pob = next_ps(g)

                    def mmd(e, po=po, o=o, n=n, half=half):
                        last = None
                        for ft in range(NFT):
                            last = e.matmul(po[0:n, :], lhsT=hT[:, ft, o:o + n], rhs=wd[:, ft, half * 512:(half + 1) * 512],
                                            start=(ft == 0), stop=(ft == NFT - 1))
                        return last
                    s.op("pe", mmd, reads=[hTb, wdb], writes=[pob])
                    s.op("dve", lambda e, po=po, n=n, half=half, ti=ti, yy=yy: e.scalar_tensor_tensor(
                        out=yy[0:n, half * 512:(half + 1) * 512], in0=xf[0:n, ti, half * 512:(half + 1) * 512],
                        scalar=2.0 * ALPHA, in1=po[0:n, :], op0=ALU.mult, op1=ALU.add),
                        reads=[pob, xfb[ti]], writes=[yyb])
                x_load_tile(g, nxt, ti, xf, xfb)
                layer_norm_rows(g, yy, yyb, n, ln["g"], ln["bt"], 4.0 * EPS, yy, yyb, ln)
                s.dma("sp", g.X[r0 + o:r0 + o + n, :], yy[0:n, :], reads=[yyb], writes=xbufs(g, r0 + o, n))
            for ti in range(len(chunks(nb, 128)), 4):
                x_load_tile(g, nxt, ti, xf, xfb)


TWO_PI = 2.0 * math.pi
MAGIC = 12582912.0
SCALE_MLA = 96.0 ** -0.5


def T_(g, es, name, shape, dt):
    return es.enter_context(g.nc.sbuf_tensor(name, list(shape), dt))


def range_reduce(s, e1, out, x, tmpb, n=128):
    s.op(e1, lambda e: e.tensor_scalar(out=out, in0=x, scalar1=1.0 / TWO_PI, scalar2=MAGIC, op0=ALU.mult, op1=ALU.add),
         reads=[tmpb], writes=[tmpb])
    s.op(e1, lambda e: e.tensor_scalar(out=out, in0=out, scalar1=MAGIC, scalar2=TWO_PI, op0=ALU.subtract, op1=ALU.mult),
         reads=[tmpb], writes=[tmpb])
    s.op(e1, lambda e: e.tensor_tensor(out=out, in0=x, in1=out, op=ALU.subtract), reads=[tmpb], writes=[tmpb])
    s.op(e1, lambda e: e.tensor_scalar(out=out, in0=out, scalar1=3.1415925, scalar2=-3.1415925, op0=ALU.min, op1=ALU.max),
         reads=[tmpb], writes=[tmpb])


def rms_rows(g, src, n, width, gt, out, tmp, tb, reads, writes):
    s = g.s
    sq, ss = tmp["sq"], tmp["ss"]
    s.op("act", lambda e: e.square(out=sq[0:n, 0:width], in_=src), reads=reads, writes=[tb])
    s.op("dve", lambda e: e.reduce_sum(out=ss[0:n, :], in_=sq[0:n, 0:width], axis=AX.X), reads=[tb], writes=[tb])
    s.op("dve", lambda e: e.tensor_scalar(out=ss[0:n, :], in0=ss[0:n, :], scalar1=1.0 / width, scalar2=EPS,
                                          op0=ALU.mult, op1=ALU.add), reads=[tb], writes=[tb])
    s.op("act", lambda e: e.sqrt(out=ss[0:n, :], in_=ss[0:n, :]), reads=[tb], writes=[tb])
    s.op("dve", lambda e: e.reciprocal(out=ss[0:n, :], in_=ss[0:n, :]), reads=[tb], writes=[tb])
    s.op("dve", lambda e: e.scalar_tensor_tensor(out=out, in0=src, scalar=ss[0:n, 0:1], in1=gt[0:n, 0:width],
                                                 op0=ALU.mult, op1=ALU.mult), reads=list(reads) + [tb], writes=writes)


def rope_rows(g, src, n, H, half, ra, rb, out, tmpA, tmpB, tb, reads, writes):
    s = g.s
    s.op("dve", lambda e: e.tensor_tensor(out=tmpA[0:n], in0=src, in1=ra[0:n], op=ALU.mult), reads=reads, writes=[tb])
    s.op("dve", lambda e: e.tensor_tensor(out=tmpB[0:n, :, 0:half], in0=src[:, :, half:2 * half], in1=rb[0:n, :, 0:half],
                                          op=ALU.mult), reads=list(reads) + [tb], writes=[tb])
    s.op("dve", lambda e: e.tensor_tensor(out=tmpB[0:n, :, half:2 * half], in0=src[:, :, 0:half],
                                          in1=rb[0:n, :, half:2 * half], op=ALU.mult), reads=list(reads) + [tb], writes=[tb])
    s.op("pool", lambda e: e.tensor_tensor(out=out, in0=tmpA[0:n], in1=tmpB[0:n], op=ALU.add), reads=[tb], writes=writes)


def split_rows(g, r0, n):
    T = g.T
    out = []
    a, b = r0, min(r0 + n, T)
    if b > a:
        out.append(("p", a - r0, a, b - a))
    a, b = max(r0, T), r0 + n
    if b > a:
        out.append(("s", a - r0, a - T, b - a))
    return out


@staged
def attention(g, name, H, dq, QT, q0, nq, KT, VA, nk, kind, scale, out_base, fox=None):
    nc, s = g.nc, g.s
    with ExitStack() as es:
        nkt = (nk + 127) // 128
        Qh = [T_(g, es, name + "Q%d" % k, [128, nq], BF16) for k in range(2)]
        Kh = [T_(g, es, name + "K%d" % k, [128, nk], BF16) for k in range(2)]
        Vh = [T_(g, es, name + "V%d" % k, [128, nkt, 128], BF16) for k in range(2)]
        Qb, Kb, Vb = [Buf(), Buf()], [Buf(), Buf()], [Buf(), Buf()]
        NPT = 4
        pT = [T_(g, es, name + "pT%d" % k, [128, 512], BF16) for k in range(NPT)]
        pTb = [Buf() for _ in range(NPT)]
        osb = [T_(g, es, name + "o%d" % k, [128, 512], F32) for k in range(2)]
        osbb = [Buf(), Buf()]
        obf = [T_(g, es, name + "ob%d" % k, [64, 512], BF16) for k in range(2)]
        obfb = [Buf(), Buf()]
        rcp = [T_(g, es, name + "rc%d" % k, [64, 512], F32) for k in range(2)]
        rcpb = [Buf(), Buf()]
        ones = T_(g, es, name + "ones", [128, 128], F32)
        ones3 = T_(g, es, name + "ones3", [4, 128], BF16)
        onesb = Buf()
        s.op("pool", lambda e: e.memset(ones[:], 1.0), writes=[onesb])
        s.op("pool", lambda e: e.memset(ones3[:], 1.0), reads=[onesb], writes=[onesb])
        nmask = 5 if kind.startswith("mla") else 4
        msk = T_(g, es, name + "msk", [128, nmask, 512], BF16)
        mskb = Buf()
        s.dma("sp", msk[:], g.I["mask_mla" if kind.startswith("mla") else "mask_fox"].rearrange("m p c -> p m c"), writes=[mskb])
        if fox is not None:
            fq = [T_(g, es, name + "fq%d" % k, [4, nq], BF16) for k in range(2)]
            nfk = [T_(g, es, name + "nfk%d" % k, [128, nkt], F32) for k in range(2)]
            fqb, nfkb = [Buf(), Buf()], [Buf(), Buf()]
            for k in range(2):
                s.op("pool", lambda e, k=k: e.memset(nfk[k][:], 0.0), writes=[nfkb[k]])
        for k in range(2):
            s.op("pool", lambda e, k=k: e.memset(Qh[k][64:128, :], 0.0), writes=[Qb[k]])
            s.op("pool", lambda e, k=k: e.memset(Kh[k][64:128, :], 1.0 if fox is not None else 0.0), writes=[Kb[k]])
        si = 0
        oi = 0
        deferred = []
        nfull = nk // 128

        def head_loads(h):
            if h >= H:
                return
            hb = h % 2
            Q_, K_, V_ = Qh[hb], Kh[hb], Vh[hb]
            s.dma("sp", Q_[0:dq, :], QT[h, :, q0:q0 + nq], reads=[g.scrb], writes=[Qb[hb]])
            s.dma("sp", K_[0:dq, :], KT[h, :, 0:nk], reads=[g.scrb], writes=[Kb[hb]])
            if nfull:
                s.dma("sp", V_[:, 0:nfull, :], VA[0:nfull * 128, h * 128:(h + 1) * 128].rearrange("(t p) c -> p t c", p=128),
                      reads=[g.scrb], writes=[Vb[hb]])
            if nk % 128:
                s.dma("sp", V_[0:nk % 128, nfull, :], VA[nfull * 128:nk, h * 128:(h + 1) * 128], reads=[g.scrb], writes=[Vb[hb]])
            if fox is not None:
                nfk_ = nfk[hb]
                s.dma("sp", Q_[64:67, :], fox["FT3"][h, :, fox["qoff"]:fox["qoff"] + nq], reads=[g.scrb], writes=[Qb[hb]])
                if nfull:
                    s.dma("sp", nfk_[:, 0:nfull], fox["FT"][h, 0:nfull * 128].rearrange("(t p) -> p t", p=128),
                          reads=[g.scrb], writes=[nfkb[hb]], allow_slow_non_contiguous=True)
                if nk % 128:
                    s.dma("sp", nfk_[0:nk % 128, nfull:nfull + 1], fox["FT"][h, nfull * 128:nk].rearrange("(p o) -> p o", o=1),
                          reads=[g.scrb], writes=[nfkb[hb]], allow_slow_non_contiguous=True)
                s.op("dve", lambda e, nfk_=nfk_: e.tensor_scalar(out=nfk_[:, :], in0=nfk_[:, :], scalar1=-1.0, scalar2=None, op0=ALU.mult),
                     reads=[nfkb[hb]], writes=[nfkb[hb]])
        head_loads(0)
        for h in range(H):
            hb = h % 2
            Q_, K_, V_ = Qh[hb], Kh[hb], Vh[hb]
            if fox is not None:
                nfk_ = nfk[hb]
            head_loads(h + 1)
            flat = []
            for (qo, nqb) in chunks(nq, 512):
                j = qo // 512
                vis = []
                for i in range(nkt):
                    kn = min(128, nk - i * 128)
                    if kind in ("mla_p", "fox_p"):
                        d = i - 4 * j
                        if d < 0:
                            vis.append((i, kn, None))
                        elif d < nmask:
                            vis.append((i, kn, d))
                    elif kind == "mla_s":
                        vis.append((i, kn, None))
                    else:
                        vis.append((i, kn, 0 if i * 128 >= fox["qoff"] else None))
                for vi, (i, kn, d) in enumerate(vis):
                    flat.append((qo, nqb, vi, len(vis), i, kn, d))
            base = si

            def issue_S(idx):
                qo, nqb, vi, nv, i, kn, d = flat[idx]
                ps_, psb_ = g.ps[(base + idx) % 4], g.psb[(base + idx) % 4]

                def mm_s(e):
                    last = e.matmul(ps_[0:kn, 0:nqb], lhsT=K_[:, i * 128:i * 128 + kn], rhs=Q_[:, qo:qo + nqb], start=True, stop=(d is None))
                    if d is not None:
                        last = e.matmul(ps_[0:kn, 0:nqb], lhsT=g.ident[0:kn, 0:kn], rhs=msk[0:kn, d, 0:nqb], start=False, stop=True)
                    return last
                rd = [Qb[hb], Kb[hb], mskb, g.identb]
                s.op("pe", mm_s, reads=rd, writes=[psb_])

            LOOK = 3
            for idx in range(min(LOOK, len(flat))):
                issue_S(idx)
            for idx, (qo, nqb, vi, nv, i, kn, d) in enumerate(flat):
                ps_, psb_ = g.ps[(base + idx) % 4], g.psb[(base + idx) % 4]
                pt_, ptb_ = pT[(base + idx) % NPT], pTb[(base + idx) % NPT]
                if vi == 0:
                    po, pob = g.ps[4 + oi % 2], g.psb[4 + oi % 2]
                    o_, ob_ = osb[oi % 2], osbb[oi % 2]
                    f_, fb_ = obf[oi % 2], obfb[oi % 2]
                    oi += 1
                if fox is None:
                    s.op("act", lambda e, pt_=pt_, ps_=ps_, kn=kn, nqb=nqb: e.activation(
                        out=pt_[0:kn, 0:nqb], in_=ps_[0:kn, 0:nqb], func=AF.Exp, scale=scale), reads=[psb_], writes=[ptb_])
                else:
                    s.op("act", lambda e, pt_=pt_, ps_=ps_, kn=kn, nqb=nqb, i=i: e.activation(
                        out=pt_[0:kn, 0:nqb], in_=ps_[0:kn, 0:nqb], func=AF.Exp, scale=1.0, bias=nfk_[0:kn, i:i + 1]),
                        reads=[psb_, nfkb[hb]], writes=[ptb_])
                if idx + LOOK < len(flat):
                    issue_S(idx + LOOK)
                for fn in deferred:
                    fn()
                deferred = []
                s.op("pe", lambda e, po=po, pt_=pt_, i=i, kn=kn, nqb=nqb, vi=vi, nv=nv: e.matmul(
                    po[:, 0:nqb], lhsT=V_[0:kn, i, :], rhs=pt_[0:kn, 0:nqb], start=(vi == 0), stop=(vi == nv - 1)),
                    reads=[Vb[hb], ptb_], writes=[pob])
                if vi == nv - 1:
                    s.op("act", lambda e, o_=o_, po=po, nqb=nqb: e.copy(out=o_[0:65, 0:nqb], in_=po[0:65, 0:nqb]),
                         reads=[pob], writes=[ob_])
                    rc_, rcb_ = rcp[(oi - 1) % 2], rcpb[(oi - 1) % 2]

                    def fin(po=po, pob=pob, o_=o_, ob_=ob_, f_=f_, fb_=fb_, nqb=nqb, qo=qo, h=h, rc_=rc_, rcb_=rcb_):
                        s.op("pe", lambda e: e.matmul(po[0:64, 0:nqb], lhsT=ones[64:65, 0:64], rhs=o_[64:65, 0:nqb], start=True, stop=True),
                             reads=[ob_, onesb], writes=[pob])
                        s.op("dve", lambda e: e.reciprocal(out=rc_[0:64, 0:nqb], in_=po[0:64, 0:nqb]), reads=[pob], writes=[rcb_])
                        s.op("dve", lambda e: e.tensor_tensor(out=f_[0:64, 0:nqb], in0=o_[0:64, 0:nqb], in1=rc_[0:64, 0:nqb], op=ALU.mult),
                             reads=[ob_, rcb_], writes=[fb_])
                        fr = out_base + 64 * h
                        s.dma("sp", g.MIXT[fr // 128, fr % 128:fr % 128 + 64, q0 + qo:q0 + qo + nqb], f_[0:64, 0:nqb],
                              reads=[fb_], writes=[g.scrb])
                    deferred.append(fin)
            si = base + len(flat)
        for fn in deferred:
            fn()


@staged
def out_proj_ln(g, tag, wsrc, l, eps):
    nc, s, R = g.nc, g.s, g.R
    with ExitStack() as es:
        wo = T_(g, es, "wo" + tag, [128, 8, D], BF16)
        wob = Buf()
        ws = wsrc.rearrange("(kt p) d -> p kt d", p=128)
        for kt in range(8):
            s.dma("pool", wo[:, kt, :], ws[:, kt, :], writes=[wob])
        ln = alloc_ln(g, es, l, 1, tag)
        NBUF = 4
        mt = [T_(g, es, "mt%d" % k + tag, [128, 8, 128], BF16) for k in range(NBUF)]
        xf = [T_(g, es, "xo%d" % k + tag, [128, D], F32) for k in range(NBUF)]
        y = [T_(g, es, "yo%d" % k + tag, [128, D], F32) for k in range(NBUF)]
        mtb, xfb, yb = [Buf() for _ in range(NBUF)], [Buf() for _ in range(NBUF)], [Buf() for _ in range(NBUF)]
        tiles = chunks(R, 128)

        def loads(ti):
            if ti >= len(tiles):
                return
            r0, n = tiles[ti]
            k = ti % NBUF
            s.dma("sp", mt[k][:, :, 0:n], g.MIXT[:, :, r0:r0 + n].rearrange("kt p c -> p kt c"), reads=[g.scrb], writes=[mtb[k]])
            s.dma("sp", xf[k][0:n, :], g.X[r0:r0 + n, :], reads=xbufs(g, r0, n), writes=[xfb[k]])
        loads(0)
        loads(1)
        for ti, (r0, n) in enumerate(tiles):
            k = ti % NBUF
            loads(ti + 2)
            for half in range(2):
                po, pob = next_ps(g)

                def mm(e, po=po, k=k, n=n, half=half):
                    last = None
                    for kt in range(8):
                        last = e.matmul(po[0:n, :], lhsT=mt[k][:, kt, 0:n], rhs=wo[:, kt, half * 512:(half + 1) * 512],
                                        start=(kt == 0), stop=(kt == 7))
                    return last
                s.op("pe", mm, reads=[mtb[k], wob], writes=[pob])
                s.op("dve", lambda e, po=po, k=k, n=n, half=half: e.scalar_tensor_tensor(
                    out=y[k][0:n, half * 512:(half + 1) * 512], in0=xf[k][0:n, half * 512:(half + 1) * 512], scalar=ALPHA,
                    in1=po[0:n, :], op0=ALU.mult, op1=ALU.add), reads=[pob, xfb[k]], writes=[yb[k]])
            layer_norm_rows(g, y[k], yb[k], n, ln["g"], ln["bt"], eps, y[k], yb[k], ln)
            s.dma("sp", g.X[r0:r0 + n, :], y[k][0:n, :], reads=[yb[k]], writes=xbufs(g, r0, n))


def even_stage(g, e):
    import os
    lim = int(os.environ.get("EV_STOP", "9"))
    even_proj(g, e)
    if lim <= 1:
        return
    s5_stage(g, e)
    if lim <= 2:
        return
    T, R, PAST = g.T, g.R, g.PAST
    attention(g, "ap%d" % e, 8, 96, g.QT, 0, T, g.KTp, g.VAp, T, "mla_p", SCALE_MLA, 512)
    attention(g, "as%d" % e, 8, 96, g.QT, T, DSEQ, g.KTs, g.VAs, PAST + DSEQ, "mla_s", SCALE_MLA, 512)
    out_proj_ln(g, "eo%d" % e, g.I["ewout"][e], 2 * e, EPS)


@staged
def even_proj(g, e):
    nc, s, R, T, PAST = g.nc, g.s, g.R, g.T, g.PAST
    NB = 512
    I, O = g.I, g.O
    with ExitStack() as es:
        tag = "ep%d" % e
        win = T_(g, es, "win" + tag, [128, 8, 928], BF16)
        wuq = T_(g, es, "wuq" + tag, [128, 2, 768], BF16)
        wukv = T_(g, es, "wukv" + tag, [128, 1024], BF16)
        wb = Buf()
        wins = I["ewin"][e].rearrange("(kt p) f -> p kt f", p=128)
        for kt in range(8):
            s.dma("pool", win[:, kt, :], wins[:, kt, :], writes=[wb])
        s.dma("pool", wuq[:], I["mla_w_uq"][e].rearrange("(kt p) f -> p kt f", p=128), writes=[wb])
        s.dma("pool", wukv[:], I["mla_w_ukv"][e], writes=[wb])
        gq = T_(g, es, "gq" + tag, [128, 256], F32)
        gkv = T_(g, es, "gkv" + tag, [128, 128], F32)
        s.dma("sp", gq[:], I["mla_q_norm"][e:e + 1, :].broadcast_to([128, 256]), writes=[wb])
        s.dma("sp", gkv[:], I["mla_kv_norm"][e:e + 1, :].broadcast_to([128, 128]), writes=[wb])
        xf = T_(g, es, "xf" + tag, [128, 4, D], F32)
        xb = T_(g, es, "xb" + tag, [128, 2, D], BF16)
        xT = T_(g, es, "xT" + tag, [128, 8, NB], BF16)
        xfb, xbb, xTb = [Buf() for _ in range(4)], [Buf(), Buf()], Buf()
        uT = T_(g, es, "uT" + tag, [128, 4, NB], BF16)
        uTb = Buf()
        cT = T_(g, es, "cT" + tag, [128, 3, NB], BF16)
        kpT = T_(g, es, "kpT" + tag, [128, NB], BF16)
        cTb = Buf()
        sq = T_(g, es, "sq" + tag, [128, 256], F32)
        ss = T_(g, es, "ss" + tag, [128, 1], F32)
        tmp = {"sq": sq, "ss": ss}
        tb = Buf()
        qn = T_(g, es, "qn" + tag, [128, 256], BF16)
        ckf = T_(g, es, "ckf" + tag, [128, 128], F32)
        ckb = T_(g, es, "ckb" + tag, [128, 128], BF16)
        kpf = T_(g, es, "kpf" + tag, [128, 1, 32], F32)
        kq = T_(g, es, "kq" + tag, [128, 96], BF16)
        rwb = Buf()
        s.op("pool", lambda e_: e_.memset(kq[:], 0.0), writes=[rwb])
        ra4 = T_(g, es, "ra" + tag, [128, 4, 4, 32], F32)
        rbt4 = T_(g, es, "rb" + tag, [128, 4, 4, 32], F32)
        rab4 = [Buf() for _ in range(4)]
        tA = T_(g, es, "tA" + tag, [128, 4, 32], F32)
        tB = T_(g, es, "tB" + tag, [128, 4, 32], F32)
        tAb = Buf()
        qb = T_(g, es, "qb" + tag, [128, 8, 96], BF16)
        qbb = Buf()
        QTs = T_(g, es, "QTs" + tag, [128, 8, NB], BF16)
        KTs_ = T_(g, es, "KTs" + tag, [128, 8, NB], BF16)
        VAs_ = T_(g, es, "VAs" + tag, [128, 4, 8, 128], BF16)
        QTb, KTb, VAb = Buf(), Buf(), Buf()
        s.op("pool", lambda e_: e_.memset(VAs_[:], 1.0), writes=[VAb])
        pcf = T_(g, es, "pcf" + tag, [128, 128], F32)
        pkf = T_(g, es, "pkf" + tag, [128, 32], F32)
        pcb = Buf()

        def kv_project(nb, dests):
            if "nokv" in dbg:
                return
            kv_project_(nb, dests)

        def kv_project_(nb, dests):
            for h in range(8):
                pk, pkb = next_ps(g)
                s.op("pe", lambda e_, pk=pk, h=h: e_.matmul(pk[0:64, 0:nb], lhsT=wukv[:, h * 128:h * 128 + 64], rhs=cT[:, 2, 0:nb],
                                                             start=True, stop=True), reads=[wb, cTb], writes=[pkb])
                s.op("act", lambda e_, pk=pk, h=h: e_.copy(out=KTs_[0:64, h, 0:nb], in_=pk[0:64, 0:nb]), reads=[pkb], writes=[KTb])
                s.op("pool", lambda e_, h=h: e_.tensor_copy(out=KTs_[64:96, h, 0:nb], in_=kpT[64:96, 0:nb]), reads=[cTb], writes=[KTb])
            for ti, (o, n) in enumerate(chunks(nb, 128)):
                pv, pvb = next_ps(g)
                s.op("pe", lambda e_, pv=pv, o=o, n=n: e_.matmul(
                    pv[0:n, :], lhsT=cT[:, 2, o:o + n], rhs=wukv[:, :].rearrange("p (h c) -> p h c", c=128)[:, :, 64:128],
                    start=True, stop=True), reads=[wb, cTb], writes=[pvb])
                s.op("dve", lambda e_, pv=pv, n=n, ti=ti: e_.tensor_copy(
                    out=VAs_[0:n, ti, :, 0:64], in_=pv[0:n, :].rearrange("p (h c) -> p h c", c=64)), reads=[pvb], writes=[VAb])
            for (lo, KTd, VAd, dr, cnt) in dests:
                s.dma("sp", KTd[:, :, dr:dr + cnt].rearrange("h p c -> p h c"), KTs_[0:96, :, lo:lo + cnt], reads=[KTb], writes=[g.scrb])
                a = lo
                while a < lo + cnt:
                    ti = a // 128
                    b = min(lo + cnt, (ti + 1) * 128)
                    s.dma("sp", VAd[dr + a - lo:dr + b - lo, :], VAs_[a - ti * 128:b - ti * 128, ti, :, :].rearrange("p h c -> p (h c)"),
                          reads=[VAb], writes=[g.scrb])
                    a = b

        import os
        dbg = os.environ.get("EP_DBG", "")
        pblks = chunks(PAST if "nopast" not in dbg else 0, NB)
        pcf4 = [T_(g, es, "pcf4_%d" % k + tag, [128, 4, 128], F32) for k in range(2)]
        pkf4 = [T_(g, es, "pkf4_%d" % k + tag, [128, 4, 32], F32) for k in range(2)]
        pcb4 = [[Buf() for _ in range(4)] for _ in range(2)]

        def past_loads(bi):
            if bi >= len(pblks):
                return
            k0_, nb_ = pblks[bi]
            for ti_, (o_, n_) in enumerate(chunks(nb_, 128)):
                s.dma("sp", pcf4[bi % 2][0:n_, ti_, :], I["c_ckv"][e, k0_ + o_:k0_ + o_ + n_, :], writes=[pcb4[bi % 2][ti_]])
                s.dma("sp", pkf4[bi % 2][0:n_, ti_, :], I["c_kpe"][e, k0_ + o_:k0_ + o_ + n_, :], writes=[pcb4[bi % 2][ti_]])
        past_loads(0)
        for bi, (k0, nb) in enumerate(pblks):
            past_loads(bi + 1)
            for ti, (o, n) in enumerate(chunks(nb, 128)):
                pcf, pkf, pcb = pcf4[bi % 2][:, ti, :], pkf4[bi % 2][:, ti, :], pcb4[bi % 2][ti]
                s.op("act", lambda e_, n=n, pcf=pcf: e_.copy(out=ckb[0:n, :], in_=pcf[0:n, :]), reads=[pcb], writes=[rwb])
                s.op("act", lambda e_, n=n, pkf=pkf: e_.copy(out=kq[0:n, 64:96], in_=pkf[0:n, :]), reads=[pcb], writes=[rwb])
                pt, ptb = next_pst(g)

                def tr(e_, pt=pt, n=n):
                    e_.transpose(out=pt[:, 0:n], in_=ckb[0:n, :], identity=g.ident[0:n, 0:n])
                    return e_.transpose(out=pt[0:96, 128:128 + n], in_=kq[0:n, :], identity=g.ident[0:n, 0:n])
                s.op("pe", tr, reads=[rwb, g.identb], writes=[ptb])
                s.op("dve", lambda e_, pt=pt, o=o, n=n: e_.tensor_copy(out=cT[:, 2, o:o + n], in_=pt[:, 0:n]), reads=[ptb], writes=[cTb])
                s.op("dve", lambda e_, pt=pt, o=o, n=n: e_.tensor_copy(out=kpT[64:96, o:o + n], in_=pt[64:96, 128:128 + n]),
                     reads=[ptb], writes=[cTb])
            kv_project(nb, [(0, g.KTs, g.VAs, k0, nb)])

        ublks = chunks(R, NB)
        for ti in range(4):
            x_load_tile(g, ublks[0], ti, xf, xfb)
        for bi, (r0, nb) in enumerate(ublks):
            load_xT(g, es, r0, nb, xf, xfb, xb, xbb, xT, xTb, preloaded=True)
            for ti in range(4):
                x_load_tile(g, ublks[bi + 1] if bi + 1 < len(ublks) else None, ti, xf, xfb)
            for ti, (o, n) in enumerate(chunks(nb, 128)):
                s.dma("sp", ra4[0:n, ti], I["ropeA16"][r0 + o:r0 + o + n, :, :], writes=[rab4[ti]])
                s.dma("sp", rbt4[0:n, ti], I["ropeB16"][r0 + o:r0 + o + n, :, :], writes=[rab4[ti]])
            for ft in range(4):
                pu, pub = next_ps(g)

                def mmu(e_, pu=pu, ft=ft):
                    last = None
                    for kt in range(8):
                        last = e_.matmul(pu[:, 0:nb], lhsT=win[:, kt, ft * 128:(ft + 1) * 128], rhs=xT[:, kt, 0:nb],
                                         start=(kt == 0), stop=(kt == 7))
                    return last
                s.op("pe", mmu, reads=[wb, xTb], writes=[pub])
                s.op("act", lambda e_, pu=pu, ft=ft: e_.copy(out=uT[:, ft, 0:nb], in_=pu[:, 0:nb]), reads=[pub], writes=[uTb])
            s.dma("sp", g.UT[:, :, r0:r0 + nb].rearrange("t p c -> p t c"), uT[:, :, 0:nb], reads=[uTb], writes=[g.scrb])
            def ph1a(ti, o, n):
                rr = r0 + o
                pm, pmb = next_ps(g)

                def mmt(e_, pm=pm, o=o, n=n):
                    last = None
                    for kt in range(8):
                        last = e_.matmul(pm[0:n, 0:416], lhsT=xT[:, kt, o:o + n], rhs=win[:, kt, 512:928],
                                         start=(kt == 0), stop=(kt == 7))
                    return last
                s.op("pe", mmt, reads=[wb, xTb], writes=[pmb])
                ra, rbt, rab = ra4[:, ti], rbt4[:, ti], rab4[ti]
                rms_rows(g, pm[0:n, 0:256], n, 256, gq, qn[0:n, :], tmp, tb, [pmb, wb], [rwb])
                rms_rows(g, pm[0:n, 256:384], n, 128, gkv, ckf[0:n, :], tmp, tb, [pmb, wb], [rwb])
                s.op("act", lambda e_, n=n: e_.copy(out=ckb[0:n, :], in_=ckf[0:n, :]), reads=[rwb], writes=[rwb])
                rope_rows(g, pm[0:n, 384:416].rearrange("p (h c) -> p h c", h=1), n, 1, 16, ra[:, 0:1, :], rbt[:, 0:1, :],
                          kpf[0:n], tA[:, 0:1, :], tB[:, 0:1, :], tAb, [pmb, rab], [rwb])
                s.op("act", lambda e_, n=n: e_.copy(out=kq[0:n, 64:96], in_=kpf[0:n, 0, :]), reads=[rwb], writes=[rwb])
                for (kd, lo, dr, cnt) in split_rows(g, rr, n):
                    pre = "p_" if kd == "p" else "s_"
                    s.dma("sp", O[pre + "ckv"][e, dr:dr + cnt, :], ckf[lo:lo + cnt, :], reads=[rwb], writes=[g.outb])
                    s.dma("sp", O[pre + "kpe"][e, dr:dr + cnt, :], kpf[lo:lo + cnt, 0, :], reads=[rwb], writes=[g.outb])
            def ph1b(ti, o, n):
                rr = r0 + o
                pt, ptb = next_pst(g)

                def tr2(e_, pt=pt, n=n):
                    e_.transpose(out=pt[:, 0:n], in_=qn[0:n, 0:128], identity=g.ident[0:n, 0:n])
                    e_.transpose(out=pt[:, 128:128 + n], in_=qn[0:n, 128:256], identity=g.ident[0:n, 0:n])
                    e_.transpose(out=pt[:, 256:256 + n], in_=ckb[0:n, :], identity=g.ident[0:n, 0:n])
                    return e_.transpose(out=pt[0:96, 384:384 + n], in_=kq[0:n, :], identity=g.ident[0:n, 0:n])
                s.op("pe", tr2, reads=[rwb, g.identb], writes=[ptb])
                s.op("dve", lambda e_, pt=pt, o=o, n=n: e_.tensor_copy(
                    out=cT[:, :, o:o + n], in_=pt[:, 0:384].rearrange("p (j c) -> p j c", j=3)[:, :, 0:n]), reads=[ptb], writes=[cTb])
                s.op("dve", lambda e_, pt=pt, o=o, n=n: e_.tensor_copy(out=kpT[64:96, o:o + n], in_=pt[64:96, 384:384 + n]),
                     reads=[ptb], writes=[cTb])
            def ph2a(ti, o, n):
                rr = r0 + o
                ra, rbt, rab = ra4[:, ti], rbt4[:, ti], rab4[ti]
                for hf in range(2):
                    pq, pqb = next_ps(g)

                    def mmq(e_, pq=pq, o=o, n=n, hf=hf):
                        e_.matmul(pq[0:n, 0:384], lhsT=cT[:, 0, o:o + n], rhs=wuq[:, 0, hf * 384:(hf + 1) * 384], start=True, stop=False)
                        return e_.matmul(pq[0:n, 0:384], lhsT=cT[:, 1, o:o + n], rhs=wuq[:, 1, hf * 384:(hf + 1) * 384],
                                         start=False, stop=True)
                    s.op("pe", mmq, reads=[cTb, wb], writes=[pqb])
                    pq3 = pq[0:n, 0:384].rearrange("p (h c) -> p h c", c=96)
                    s.op("act", lambda e_, pq3=pq3, n=n, hf=hf: e_.copy(out=qb[0:n, hf * 4:hf * 4 + 4, 0:64], in_=pq3[:, :, 0:64]),
                         reads=[pqb], writes=[qbb])
                    rope_rows(g, pq3[:, :, 64:96], n, 4, 16, ra, rbt, qb[0:n, hf * 4:hf * 4 + 4, 64:96], tA, tB, tAb,
                              [pqb, rab], [qbb])
            def ph2b(ti, o, n):
                pt, ptb = next_pst(g)

                def tr3(e_, pt=pt, n=n):
                    last = None
                    for h in range(8):
                        last = e_.transpose(out=pt[0:96, h * 128:h * 128 + n], in_=qb[0:n, h, :], identity=g.ident[0:n, 0:n])
                    return last
                s.op("pe", tr3, reads=[qbb, g.identb], writes=[ptb])
                s.op("dve", lambda e_, pt=pt, o=o, n=n: e_.tensor_copy(
                    out=QTs[0:96, :, o:o + n], in_=pt[0:96, :].rearrange("p (h c) -> p h c", h=8)[:, :, 0:n]), reads=[ptb], writes=[QTb])
            tl = chunks(nb, 128)
            ph1a(0, *tl[0])
            ph1b(0, *tl[0])
            for idx in range(len(tl)):
                if idx + 1 < len(tl):
                    ph1a(idx + 1, *tl[idx + 1])
                ph2a(idx, *tl[idx])
                if idx + 1 < len(tl):
                    ph1b(idx + 1, *tl[idx + 1])
                ph2b(idx, *tl[idx])
            s.dma("sp", g.QT[:, :, r0:r0 + nb].rearrange("h p c -> p h c"), QTs[0:96, :, 0:nb], reads=[QTb], writes=[g.scrb])
            dests = []
            for (kd, lo, dr, cnt) in split_rows(g, r0, nb):
                if kd == "p":
                    dests.append((lo, g.KTp, g.VAp, dr, cnt))
                else:
                    dests.append((lo, g.KTs, g.VAs, PAST + dr, cnt))
            kv_project(nb, dests)


@staged
def s5_stage(g, e):
    nc, s, R, T = g.nc, g.s, g.R, g.T
    I, O = g.I, g.O
    TC = 512
    with ExitStack() as es:
        tag = "s5%d" % e
        sm = lambda nm, w=16: T_(g, es, nm + tag, [128, w], F32)
        are, aim, ldt = sm("are"), sm("aim"), sm("ldt")
        ard, aid, mag, cs, sn, t0, t1 = sm("ard"), sm("aid"), sm("mag"), sm("cs"), sm("sn"), sm("t0"), sm("t1")
        abr, abi, nr, ni, den, fr, fi = sm("abr"), sm("abi"), sm("nr"), sm("ni"), sm("den"), sm("fr"), sm("fi")
        pb = Buf()
        vw = lambda ap: ap.rearrange("(pr g2) n -> (g2 n) pr", g2=2)
        for hp in range(2):
            s.dma("sp", are[:, hp * 8:hp * 8 + 8], vw(I["s5_a_re"][e])[:, hp * 8:hp * 8 + 8], writes=[pb], allow_slow_non_contiguous=True)
            s.dma("sp", aim[:, hp * 8:hp * 8 + 8], vw(I["s5_a_im"][e])[:, hp * 8:hp * 8 + 8], writes=[pb], allow_slow_non_contiguous=True)
        for g2 in range(2):
            s.dma("sp", ldt[g2 * 64:(g2 + 1) * 64, :],
                  I["s5_log_dt"][e:e + 1, :].rearrange("o (pr g2) -> o pr g2", g2=2)[:, :, g2].broadcast_to([64, 16]),
                  writes=[pb], allow_slow_non_contiguous=True)
        P = lambda eng, fn: s.op(eng, fn, reads=[pb], writes=[pb])
        P("act", lambda e_: e_.activation(out=ldt[:], in_=ldt[:], func=AF.Exp))
        P("dve", lambda e_: e_.tensor_tensor(out=ard[:], in0=are[:], in1=ldt[:], op=ALU.mult))
        P("dve", lambda e_: e_.tensor_tensor(out=aid[:], in0=aim[:], in1=ldt[:], op=ALU.mult))
        P("act", lambda e_: e_.activation(out=mag[:], in_=ard[:], func=AF.Exp))
        range_reduce(s, "dve", t0[:], aid[:], pb)
        P("act", lambda e_: e_.activation(out=sn[:], in_=t0[:], func=AF.Sin))
        P("dve", lambda e_: e_.tensor_scalar(out=t1[:], in0=aid[:], scalar1=math.pi / 2, scalar2=None, op0=ALU.add))
        range_reduce(s, "dve", t0[:], t1[:], pb)
        P("act", lambda e_: e_.activation(out=cs[:], in_=t0[:], func=AF.Sin))
        P("dve", lambda e_: e_.tensor_tensor(out=abr[:], in0=mag[:], in1=cs[:], op=ALU.mult))
        P("dve", lambda e_: e_.tensor_tensor(out=abi[:], in0=mag[:], in1=sn[:], op=ALU.mult))
        P("dve", lambda e_: e_.tensor_scalar(out=t0[:], in0=abr[:], scalar1=-1.0, scalar2=None, op0=ALU.add))
        P("dve", lambda e_: e_.tensor_tensor(out=nr[:], in0=t0[:], in1=are[:], op=ALU.mult))
        P("dve", lambda e_: e_.tensor_tensor(out=t1[:], in0=abi[:], in1=aim[:], op=ALU.mult))
        P("dve", lambda e_: e_.tensor_tensor(out=nr[:], in0=nr[:], in1=t1[:], op=ALU.add))
        P("dve", lambda e_: e_.tensor_tensor(out=ni[:], in0=abi[:], in1=are[:], op=ALU.mult))
        P("dve", lambda e_: e_.tensor_tensor(out=t1[:], in0=t0[:], in1=aim[:], op=ALU.mult))
        P("dve", lambda e_: e_.tensor_tensor(out=ni[:], in0=ni[:], in1=t1[:], op=ALU.subtract))
        P("dve", lambda e_: e_.tensor_tensor(out=den[:], in0=are[:], in1=are[:], op=ALU.mult))
        P("dve", lambda e_: e_.tensor_tensor(out=t1[:], in0=aim[:], in1=aim[:], op=ALU.mult))
        P("dve", lambda e_: e_.tensor_tensor(out=den[:], in0=den[:], in1=t1[:], op=ALU.add))
        P("dve", lambda e_: e_.reciprocal(out=den[:], in_=den[:]))
        P("dve", lambda e_: e_.tensor_tensor(out=fr[:], in0=nr[:], in1=den[:], op=ALU.mult))
        P("dve", lambda e_: e_.tensor_tensor(out=fi[:], in0=ni[:], in1=den[:], op=ALU.mult))
        Br = T_(g, es, "Br" + tag, [128, 16, 16], F32)
        Bi = T_(g, es, "Bi" + tag, [128, 16, 16], F32)
        Bbr = T_(g, es, "Bbr" + tag, [128, 16, 16], F32)
        Bbi = T_(g, es, "Bbi" + tag, [128, 16, 16], F32)
        Bt = T_(g, es, "Bt" + tag, [128, 16, 16], F32)
        vb = lambda ap: ap.rearrange("(pr g2) n c -> (g2 n) pr c", g2=2)
        s.dma("sp", Br[:], vb(I["s5_b_re"][e]), writes=[pb])
        s.dma("sp", Bi[:], vb(I["s5_b_im"][e]), writes=[pb])
        frb = fr[:, :].unsqueeze(2).to_broadcast([128, 16, 16])
        fib = fi[:, :].unsqueeze(2).to_broadcast([128, 16, 16])
        P("dve", lambda e_: e_.tensor_tensor(out=Bbr[:], in0=Br[:], in1=frb, op=ALU.mult))
        P("dve", lambda e_: e_.tensor_tensor(out=Bt[:], in0=Bi[:], in1=fib, op=ALU.mult))
        P("dve", lambda e_: e_.tensor_tensor(out=Bbr[:], in0=Bbr[:], in1=Bt[:], op=ALU.subtract))
        P("dve", lambda e_: e_.tensor_tensor(out=Bbi[:], in0=Bi[:], in1=frb, op=ALU.mult))
        P("dve", lambda e_: e_.tensor_tensor(out=Bt[:], in0=Br[:], in1=fib, op=ALU.mult))
        P("dve", lambda e_: e_.tensor_tensor(out=Bbi[:], in0=Bbi[:], in1=Bt[:], op=ALU.add))
        BP = [T_(g, es, "BP%d" % k + tag, [128, 16, 128], F32) for k in range(2)]
        LB = [T_(g, es, "LB%d" % k + tag, [128, 16, 128], BF16) for k in range(2)]
        identf = T_(g, es, "idf" + tag, [128, 128], F32)
        s.dma("sp", identf[:], I["ident_f"], writes=[pb])
        for k, Bb in enumerate((Bbr, Bbi)):
            P("pool", lambda e_, k=k: e_.memset(BP[k][:], 0.0))
            for g2 in range(2):
                for r in range(4):
                    c0 = 32 * r + 16 * g2
                    P("pool", lambda e_, k=k, Bb=Bb, g2=g2, r=r, c0=c0: e_.tensor_copy(
                        out=BP[k][g2 * 64:(g2 + 1) * 64, r::4, c0:c0 + 16], in_=Bb[g2 * 64:(g2 + 1) * 64, r::4, :]))
            for q4 in range(4):
                pp, ppb = next_ps(g)

                def trb(e_, pp=pp, k=k, q4=q4):
                    last = None
                    for jj in range(4):
                        last = e_.transpose(out=pp[:, jj * 128:(jj + 1) * 128], in_=BP[k][:, q4 * 4 + jj, :], identity=identf[:])
                    return last
                s.op("pe", trb, reads=[pb], writes=[ppb])
                s.op("act", lambda e_, pp=pp, k=k, q4=q4: e_.copy(
                    out=LB[k][:, q4 * 4:q4 * 4 + 4, :], in_=pp[:, :].rearrange("p (j c) -> p j c", j=4)), reads=[ppb], writes=[pb])
        CPf = [T_(g, es, "CPf%d" % k + tag, [128, 16, 128], F32) for k in range(2)]
        CP = [T_(g, es, "CP%d" % k + tag, [128, 16, 128], BF16) for k in range(2)]
        for k, nm in enumerate(("s5_c_re", "s5_c_im")):
            P("pool", lambda e_, k=k: e_.memset(CPf[k][:], 0.0))
            src = I[nm][e].rearrange("(pr g2) c n -> g2 n pr c", g2=2)
            for g2 in range(2):
                for r in range(4):
                    c0 = 32 * r + 16 * g2
                    for q in range(4):
                        s.dma("sp", CPf[k][g2 * 64:(g2 + 1) * 64, r + 4 * q, c0:c0 + 16], src[g2][:, r + 4 * q, :], reads=[pb],
                              writes=[pb], allow_slow_non_contiguous=True)
        P("act", lambda e_: e_.copy(out=CP[0][:], in_=CPf[0][:]))
        P("act", lambda e_: e_.mul(out=CP[1][:], in_=CPf[1][:], mul=-1.0))
        CP.append(T_(g, es, "CP2" + tag, [128, 16, 128], BF16))
        P("act", lambda e_: e_.mul(out=CP[2][:], in_=CPf[0][:], mul=-1.0))
        dsk = sm("dsk", 4)
        bgl = sm("bgl", 4)
        s.dma("sp", dsk[:], I["s5_d"][e].rearrange("(t p) -> p t", p=128), writes=[pb], allow_slow_non_contiguous=True)
        s.dma("sp", bgl[:], I["s5_b_glu"][e].rearrange("(t p) -> p t", p=128), writes=[pb], allow_slow_non_contiguous=True)
        wgl = T_(g, es, "wgl" + tag, [128, 4, 512], BF16)
        s.dma("pool", wgl[:], I["s5_w_glu"][e].rearrange("(kt p) f -> p kt f", p=128), writes=[pb])
        tau = T_(g, es, "tau" + tag, [128, TC], F32)
        s.dma("sp", tau[:], I["tau"], writes=[pb])
        cosT = T_(g, es, "cosT" + tag, [128, 16, TC], F32)
        sinT = T_(g, es, "sinT" + tag, [128, 16, TC], F32)
        ang = T_(g, es, "ang" + tag, [128, TC], F32)
        ang2 = T_(g, es, "ang2" + tag, [128, TC], F32)
        angb = Buf()
        tabb = Buf()
        for pr in range(16):
            s.op("dve", lambda e_, pr=pr: e_.tensor_scalar(out=ang[:], in0=tau[:], scalar1=aid[:, pr:pr + 1], scalar2=None, op0=ALU.mult),
                 reads=[pb], writes=[angb])
            range_reduce(s, "dve", ang2[:], ang[:], angb)
            s.op("act", lambda e_, pr=pr: e_.activation(out=sinT[:, pr, :], in_=ang2[:], func=AF.Sin), reads=[angb], writes=[tabb])
            s.op("dve", lambda e_: e_.tensor_scalar(out=ang[:], in0=ang[:], scalar1=math.pi / 2, scalar2=None, op0=ALU.add),
                 reads=[angb], writes=[angb])
            range_reduce(s, "dve", ang2[:], ang[:], angb)
            s.op("act", lambda e_, pr=pr: e_.activation(out=cosT[:, pr, :], in_=ang2[:], func=AF.Sin), reads=[angb], writes=[tabb])
        h0r, h0i = sm("h0r"), sm("h0i")
        hb = Buf()
        uT = T_(g, es, "uTs" + tag, [128, 4, TC], BF16)
        uTb = Buf()
        Wset = [[T_(g, es, "w%d_%d" % (ss_, k) + tag, [128, TC], F32) for k in range(6)] for ss_ in range(2)]
        Wbset = [[Buf() for _ in range(6)] for _ in range(2)]
        hb4s = [T_(g, es, "hb4_%d" % k + tag, [128, 4, 4, TC], BF16) for k in range(2)]
        hbfbs = [Buf(), Buf()]
        glr, gli, ht = sm("glr"), sm("gli"), sm("ht")
        glb = Buf()
        ysb = T_(g, es, "ysb" + tag, [128, TC], F32)
        y2 = T_(g, es, "y2" + tag, [128, TC], F32)
        ysbb = Buf()
        gT = T_(g, es, "gT" + tag, [128, 4, TC], BF16)
        gTb = Buf()
        sg = T_(g, es, "sgl" + tag, [128, TC], F32)
        oT = T_(g, es, "oT" + tag, [128, 4, TC], BF16)
        oTb = Buf()
        vs = lambda ap: ap.rearrange("(pr g2) n -> (g2 n) pr", g2=2)
        for seg, (c0, c1) in enumerate(((0, T), (T, R))):
            if seg == 0:
                s.op("pool", lambda e_: e_.memset(h0r[:], 0.0), reads=[hb], writes=[hb])
                s.op("pool", lambda e_: e_.memset(h0i[:], 0.0), reads=[hb], writes=[hb])
            else:
                for hp in range(2):
                    s.dma("sp", h0r[:, hp * 8:hp * 8 + 8], vs(I["st_re"][e])[:, hp * 8:hp * 8 + 8], reads=[hb], writes=[hb], allow_slow_non_contiguous=True)
                    s.dma("sp", h0i[:, hp * 8:hp * 8 + 8], vs(I["st_im"][e])[:, hp * 8:hp * 8 + 8], reads=[hb], writes=[hb], allow_slow_non_contiguous=True)
            for (co, tc) in chunks(c1 - c0, TC):
                col = c0 + co
                s.dma("sp", uT[:, :, 0:tc], g.UT[:, :, col:col + tc].rearrange("t p c -> p t c"), reads=[g.scrb], writes=[uTb])
                TT = lambda eng, o, a, b, op, rd, wr: s.op(eng, lambda e_: e_.tensor_tensor(out=o, in0=a, in1=b, op=op), reads=rd, writes=wr)

                def stageA(pr):
                    ft = pr // 4
                    W, Wb = Wset[pr % 2], Wbset[pr % 2]
                    pbr, pbrb = g.ps[2 * (pr % 2)], g.psb[2 * (pr % 2)]
                    pbi, pbib = g.ps[2 * (pr % 2) + 1], g.psb[2 * (pr % 2) + 1]
                    s.op("pe", lambda e_: e_.matmul(pbr[:, 0:tc], lhsT=LB[0][:, pr, :], rhs=uT[:, ft, 0:tc], start=True, stop=True),
                         reads=[pb, uTb], writes=[pbrb])
                    s.op("pe", lambda e_: e_.matmul(pbi[:, 0:tc], lhsT=LB[1][:, pr, :], rhs=uT[:, ft, 0:tc], start=True, stop=True),
                         reads=[pb, uTb], writes=[pbib])
                    c_, s_ = cosT[:, pr, 0:tc], sinT[:, pr, 0:tc]
                    TT("dve", W[0][:, 0:tc], pbr[:, 0:tc], c_, ALU.mult, [pbrb, tabb], [Wb[0]])
                    TT("dve", W[1][:, 0:tc], pbi[:, 0:tc], s_, ALU.mult, [pbib, tabb], [Wb[1]])
                    TT("pool", W[0][:, 0:tc], W[0][:, 0:tc], W[1][:, 0:tc], ALU.add, [Wb[0], Wb[1]], [Wb[0]])
                    TT("dve", W[2][:, 0:tc], pbi[:, 0:tc], c_, ALU.mult, [pbib, tabb], [Wb[2]])
                    TT("dve", W[5][:, 0:tc], pbr[:, 0:tc], s_, ALU.mult, [pbrb, tabb, Wb[5]], [Wb[5]])
                    TT("pool", W[2][:, 0:tc], W[2][:, 0:tc], W[5][:, 0:tc], ALU.subtract, [Wb[2], Wb[5]], [Wb[2]])

                def stageB(pr):
                    ft, p4 = pr // 4, pr % 4
                    W, Wb = Wset[pr % 2], Wbset[pr % 2]
                    hb4, hbfb = hb4s[ft % 2], hbfbs[ft % 2]
                    c_, s_ = cosT[:, pr, 0:tc], sinT[:, pr, 0:tc]
                    dec = mag[:, pr:pr + 1].to_broadcast([128, tc])
                    s.op("dve", lambda e_: e_.tensor_tensor_scan(
                        out=W[3][:, 0:tc], data0=dec, data1=W[0][:, 0:tc], initial=h0r[:, pr:pr + 1], op0=ALU.mult, op1=ALU.add),
                        reads=[Wb[0], hb, pb], writes=[Wb[3]])
                    s.op("dve", lambda e_: e_.tensor_tensor_scan(
                        out=W[4][:, 0:tc], data0=dec, data1=W[2][:, 0:tc], initial=h0i[:, pr:pr + 1], op0=ALU.mult, op1=ALU.add),
                        reads=[Wb[2], hb, pb], writes=[Wb[4]])
                    TT("dve", hb4[:, p4, 0, 0:tc], W[3][:, 0:tc], c_, ALU.mult, [Wb[3], tabb, hbfb], [hbfb])
                    TT("pool", hb4[:, p4, 1, 0:tc], W[4][:, 0:tc], s_, ALU.mult, [Wb[4], tabb, hbfb], [hbfb])
                    TT("dve", hb4[:, p4, 2, 0:tc], W[3][:, 0:tc], s_, ALU.mult, [Wb[3], tabb, hbfb], [hbfb])
                    TT("pool", hb4[:, p4, 3, 0:tc], W[4][:, 0:tc], c_, ALU.mult, [Wb[4], tabb, hbfb], [hbfb])
                    s.op("act", lambda e_: e_.copy(out=glr[:, pr:pr + 1], in_=W[3][:, tc - 1:tc]), reads=[Wb[3], glb], writes=[glb])
                    s.op("act", lambda e_: e_.copy(out=gli[:, pr:pr + 1], in_=W[4][:, tc - 1:tc]), reads=[Wb[4], glb], writes=[glb])

                def stageY(ft):
                    hb4, hbfb = hb4s[ft % 2], hbfbs[ft % 2]
                    py, pyb = g.ps[4 + ft % 2], g.psb[4 + ft % 2]

                    def mmy(e_):
                        last = None
                        for p4 in range(4):
                            for q_, ci in enumerate((0, 2, 1, 1)):
                                last = e_.matmul(py[:, 0:tc], lhsT=CP[ci][:, ft * 4 + p4, :], rhs=hb4[:, p4, q_, 0:tc],
                                                 start=(p4 == 0 and q_ == 0), stop=(p4 == 3 and q_ == 3))
                        return last
                    s.op("pe", mmy, reads=[pb, hbfb], writes=[pyb])
                    s.op("dve", lambda e_: e_.scalar_tensor_tensor(
                        out=ysb[:, 0:tc], in0=uT[:, ft, 0:tc], scalar=dsk[:, ft:ft + 1], in1=py[:, 0:tc], op0=ALU.mult, op1=ALU.add),
                        reads=[pyb, uTb, pb], writes=[ysbb])
                    s.op("pool", lambda e_: e_.tensor_tensor(out=y2[:, 0:tc], in0=ysb[:, 0:tc], in1=ysb[:, 0:tc], op=ALU.mult), reads=[ysbb], writes=[ysbb])
                    s.op("pool", lambda e_: e_.tensor_scalar(out=y2[:, 0:tc], in0=y2[:, 0:tc], scalar1=0.044715, scalar2=1.0,
                                                             op0=ALU.mult, op1=ALU.add), reads=[ysbb], writes=[ysbb])
                    s.op("pool", lambda e_: e_.tensor_tensor(out=y2[:, 0:tc], in0=y2[:, 0:tc], in1=ysb[:, 0:tc], op=ALU.mult), reads=[ysbb], writes=[ysbb])
                    s.op("act", lambda e_: e_.activation(out=y2[:, 0:tc], in_=y2[:, 0:tc], func=AF.Sigmoid, scale=1.5957691216), reads=[ysbb], writes=[ysbb])
                    s.op("dve", lambda e_: e_.tensor_tensor(out=gT[:, ft, 0:tc], in0=ysb[:, 0:tc], in1=y2[:, 0:tc], op=ALU.mult),
                         reads=[ysbb], writes=[gTb])

                stageA(0)
                for pr in range(16):
                    if pr + 1 < 16:
                        stageA(pr + 1)
                    stageB(pr)
                    if pr % 4 == 3:
                        stageY(pr // 4)
                cl, sl = cosT[:, :, tc - 1], sinT[:, :, tc - 1]
                H_ = lambda o_, a_, b_, op: s.op("dve", lambda e_: e_.tensor_tensor(out=o_, in0=a_, in1=b_, op=op),
                                                 reads=[glb, hb, tabb], writes=[hb])
                H_(h0r[:], glr[:], cl, ALU.mult)
                H_(ht[:], gli[:], sl, ALU.mult)
                H_(h0r[:], h0r[:], ht[:], ALU.subtract)
                H_(h0i[:], glr[:], sl, ALU.mult)
                H_(ht[:], gli[:], cl, ALU.mult)
                H_(h0i[:], h0i[:], ht[:], ALU.add)
                for ot in range(4):
                    pz, pzb = g.ps[4 + ot % 2], g.psb[4 + ot % 2]

                    def mmz(e_, pz=pz, ot=ot):
                        last = None
                        for kt in range(4):
                            last = e_.matmul(pz[:, 0:tc], lhsT=wgl[:, kt, ot * 128:(ot + 1) * 128], rhs=gT[:, kt, 0:tc], start=(kt == 0), stop=(kt == 3))
                        return last
                    s.op("pe", mmz, reads=[pb, gTb], writes=[pzb])
                    s.op("act", lambda e_, pz=pz, ot=ot: e_.activation(out=sg[:, 0:tc], in_=pz[:, 0:tc], func=AF.Sigmoid, bias=bgl[:, ot:ot + 1]),
                         reads=[pzb, pb, ysbb], writes=[ysbb])
                    s.op("dve", lambda e_, ot=ot: e_.tensor_tensor(out=oT[:, ot, 0:tc], in0=gT[:, ot, 0:tc], in1=sg[:, 0:tc], op=ALU.mult),
                         reads=[ysbb, gTb], writes=[oTb])
                s.dma("sp", g.MIXT[0:4, :, col:col + tc].rearrange("t p c -> p t c"), oT[:, :, 0:tc], reads=[oTb], writes=[g.scrb])
            pre = "p_" if seg == 0 else "s_"
            for hp in range(2):
                s.dma("sp", vs(O[pre + "s5re"][e])[:, hp * 8:hp * 8 + 8], h0r[:, hp * 8:hp * 8 + 8], reads=[hb], writes=[g.outb], allow_slow_non_contiguous=True)
                s.dma("sp", vs(O[pre + "s5im"][e])[:, hp * 8:hp * 8 + 8], h0i[:, hp * 8:hp * 8 + 8], reads=[hb], writes=[g.outb], allow_slow_non_contiguous=True)


def odd_stage(g, o):
    odd_proj(g, o)
    T, R, PAST = g.T, g.R, g.PAST
    v8 = lambda ap: ap.rearrange("t (hh p) c -> (t hh) p c", hh=2)
    attention(g, "fp%d" % o, 8, 64, v8(g.FQT), 0, T, v8(g.FKTp), g.VAp, T, "fox_p", 1.0, 0, fox=dict(FT=g.FTp, FT3=g.FT3p, qoff=0))
    attention(g, "fs%d" % o, 8, 64, v8(g.FQT), T, DSEQ, v8(g.FKTs), g.VAs, PAST + DSEQ, "fox_s", 1.0, 0,
              fox=dict(FT=g.FTs, FT3=g.FT3s, qoff=PAST))
    retention(g, o)
    out_proj_ln(g, "oo%d" % o, g.I["owout"][o], 2 * o + 1, EPS)


@staged
def odd_proj(g, o):
    nc, s, R, T, PAST = g.nc, g.s, g.R, g.T, g.PAST
    NB = 512
    I, O = g.I, g.O
    with ExitStack() as es:
        tag = "op%d" % o
        win = T_(g, es, "win" + tag, [128, 8, 3080], BF16)
        wb = Buf()
        wins = I["owin"][o].rearrange("(kt p) f -> p kt f", p=128)
        for kt in range(8):
            s.dma("pool", win[:, kt, :], wins[:, kt, :], writes=[wb])
        nbf = T_(g, es, "nbf" + tag, [8, 1], F32)
        s.dma("sp", nbf[:], I["fox_b_f"][o].rearrange("(p o) -> p o", o=1), writes=[wb], allow_slow_non_contiguous=True)
        s.op("dve", lambda e: e.tensor_scalar(out=nbf[:], in0=nbf[:], scalar1=-1.0, scalar2=None, op0=ALU.mult), reads=[wb], writes=[wb])
        xf = T_(g, es, "xf" + tag, [128, 4, D], F32)
        xb = T_(g, es, "xb" + tag, [128, 2, D], BF16)
        xT = T_(g, es, "xT" + tag, [128, 8, NB], BF16)
        xfb, xbb, xTb = [Buf() for _ in range(4)], [Buf(), Buf()], Buf()
        FQs = T_(g, es, "FQs" + tag, [128, 4, NB], BF16)
        FKs = T_(g, es, "FKs" + tag, [128, 4, NB], BF16)
        FQb, FKb = Buf(), Buf()
        lf = T_(g, es, "lf" + tag, [8, NB], F32)
        fc = T_(g, es, "fc" + tag, [8, NB], F32)
        one8 = T_(g, es, "one8" + tag, [8, NB], F32)
        car = T_(g, es, "car" + tag, [8, 1], F32)
        lfb, carb = Buf(), Buf()
        s.op("pool", lambda e: e.memset(one8[:], 1.0), writes=[wb])
        kf = [T_(g, es, "kf%d" % k + tag, [128, 512], F32) for k in range(2)]
        kfb = [Buf(), Buf()]
        VAs_ = T_(g, es, "VAo" + tag, [128, 8, 128], BF16)
        VAb = Buf()
        s.op("pool", lambda e: e.memset(VAs_[:], 1.0), writes=[VAb])
        ra4 = T_(g, es, "ra" + tag, [128, 4, 8, 64], F32)
        rbt4 = T_(g, es, "rb" + tag, [128, 4, 8, 64], F32)
        rab4 = [Buf() for _ in range(4)]
        tA = T_(g, es, "tA" + tag, [128, 8, 64], F32)
        tB = T_(g, es, "tB" + tag, [128, 8, 64], F32)
        tAb = Buf()
        qk = T_(g, es, "qk" + tag, [128, 8, 64], BF16)
        qkb = Buf()
        qkT = T_(g, es, "qkT" + tag, [128, 4, 128], BF16)
        qkTb = Buf()
        rvb = T_(g, es, "rvb" + tag, [128, 512], BF16)
        rgf = T_(g, es, "rgf" + tag, [128, 512], F32)
        rvbb = Buf()
        pkf = T_(g, es, "pkf" + tag, [128, 512], F32)
        pkb_ = T_(g, es, "pkb" + tag, [128, 512], BF16)
        pcb, pcb2 = Buf(), Buf()

        f3 = T_(g, es, "f3" + tag, [8, 3, NB], BF16)
        f3t = T_(g, es, "f3t" + tag, [8, NB], F32)
        f3b = Buf()

        def split3(lo, cnt, dst, d0):
            sl = slice(lo, lo + cnt)
            s.op("act", lambda e: e.copy(out=f3[:, 0, sl], in_=fc[:, sl]), reads=[lfb, f3b], writes=[f3b])
            s.op("dve", lambda e: e.tensor_tensor(out=f3t[:, sl], in0=fc[:, sl], in1=f3[:, 0, sl], op=ALU.subtract), reads=[lfb, f3b], writes=[f3b])
            s.op("act", lambda e: e.copy(out=f3[:, 1, sl], in_=f3t[:, sl]), reads=[f3b], writes=[f3b])
            s.op("dve", lambda e: e.tensor_tensor(out=f3t[:, sl], in0=f3t[:, sl], in1=f3[:, 1, sl], op=ALU.subtract), reads=[f3b], writes=[f3b])
            s.op("act", lambda e: e.copy(out=f3[:, 2, sl], in_=f3t[:, sl]), reads=[f3b], writes=[f3b])
            s.dma("sp", dst[:, :, d0:d0 + cnt], f3[:, :, sl], reads=[f3b], writes=[g.scrb])

        s.op("pool", lambda e: e.memset(car[:], 0.0), writes=[carb])
        pblks = chunks(PAST, NB)
        pk4 = [T_(g, es, "pk4_%d" % k + tag, [128, 4, 512], F32) for k in range(2)]
        pv4 = [T_(g, es, "pv4_%d" % k + tag, [128, 4, 512], F32) for k in range(2)]
        pb4 = [[Buf() for _ in range(4)] for _ in range(2)]

        def past_loads(bi):
            if bi >= len(pblks):
                return
            k0_, nb_ = pblks[bi]
            for ti_, (o_, n_) in enumerate(chunks(nb_, 128)):
                s.dma("sp", pk4[bi % 2][0:n_, ti_, :], I["c_fk"][o, k0_ + o_:k0_ + o_ + n_, :], writes=[pb4[bi % 2][ti_]])
                s.dma("sp", pv4[bi % 2][0:n_, ti_, :], I["c_fv"][o, k0_ + o_:k0_ + o_ + n_, :], writes=[pb4[bi % 2][ti_]])
        past_loads(0)
        for bi, (k0, nb) in enumerate(pblks):
            past_loads(bi + 1)
            for ti, (oo, n) in enumerate(chunks(nb, 128)):
                kk = k0 + oo
                pkf, pvf, pcb = pk4[bi % 2][:, ti, :], pv4[bi % 2][:, ti, :], pb4[bi % 2][ti]
                s.op("act", lambda e, n=n, pkf=pkf: e.copy(out=pkb_[0:n, :], in_=pkf[0:n, :]), reads=[pcb], writes=[pcb2])
                pt, ptb = next_pst(g)

                def tr(e, pt=pt, n=n):
                    last = None
                    for j in range(4):
                        last = e.transpose(out=pt[:, j * 128:j * 128 + n], in_=pkb_[0:n, j * 128:(j + 1) * 128], identity=g.ident[0:n, 0:n])
                    return last
                s.op("pe", tr, reads=[pcb2, g.identb], writes=[ptb])
                s.op("dve", lambda e, pt=pt, oo=oo, n=n: e.tensor_copy(
                    out=FKs[:, :, oo:oo + n], in_=pt[:, 0:512].rearrange("p (j c) -> p j c", j=4)[:, :, 0:n]), reads=[ptb], writes=[FKb])
                s.op("act", lambda e, n=n, pvf=pvf: e.copy(out=VAs_[0:n, :, 0:64], in_=pvf[0:n, :].rearrange("p (h c) -> p h c", c=64)),
                     reads=[pcb], writes=[VAb])
                s.dma("sp", g.VAs[kk:kk + n, :], VAs_[0:n, :, :].rearrange("p h c -> p (h c)"), reads=[VAb], writes=[g.scrb])
            s.dma("sp", g.FKTs[:, :, k0:k0 + nb].rearrange("t p c -> p t c"), FKs[:, :, 0:nb], reads=[FKb], writes=[g.scrb])
            s.dma("sp", lf[:, 0:nb], I["c_flf"][o, k0:k0 + nb, :].rearrange("k h -> h k"), reads=[lfb], writes=[lfb],
                  allow_slow_non_contiguous=True)
            s.op("dve", lambda e, nb=nb: e.tensor_tensor_scan(out=fc[:, 0:nb], data0=one8[:, 0:nb], data1=lf[:, 0:nb], initial=car[:, 0:1],
                                                               op0=ALU.mult, op1=ALU.add), reads=[lfb, carb, wb], writes=[lfb])
            s.op("act", lambda e, nb=nb: e.copy(out=car[:, 0:1], in_=fc[:, nb - 1:nb]), reads=[lfb, carb], writes=[carb])
            s.dma("sp", g.FTs[:, k0:k0 + nb], fc[:, 0:nb], reads=[lfb], writes=[g.scrb])
            split3(0, nb, g.FT3s, k0)
        carp = T_(g, es, "carp" + tag, [8, 1], F32)
        carpb = Buf()
        s.op("pool", lambda e: e.memset(carp[:], 0.0), writes=[carpb])

        ublks = chunks(R, NB)
        for ti in range(4):
            x_load_tile(g, ublks[0], ti, xf, xfb)
        for bi, (r0, nb) in enumerate(ublks):
            load_xT(g, es, r0, nb, xf, xfb, xb, xbb, xT, xTb, preloaded=True)
            for ti in range(4):
                x_load_tile(g, ublks[bi + 1] if bi + 1 < len(ublks) else None, ti, xf, xfb)
            for ti, (oo, n) in enumerate(chunks(nb, 128)):
                s.dma("sp", ra4[0:n, ti], I["ropeA32"][r0 + oo:r0 + oo + n], writes=[rab4[ti]])
                s.dma("sp", rbt4[0:n, ti], I["ropeB32"][r0 + oo:r0 + oo + n], writes=[rab4[ti]])
            for which, (dst, dstb, c0, sc) in enumerate(((FQs, FQb, 0, 0.125), (FKs, FKb, 512, 1.0))):
                for ft in range(4):
                    pu, pub = next_ps(g)

                    def mmu(e, pu=pu, ft=ft, c0=c0):
                        last = None
                        for kt in range(8):
                            last = e.matmul(pu[:, 0:nb], lhsT=win[:, kt, c0 + ft * 128:c0 + (ft + 1) * 128], rhs=xT[:, kt, 0:nb],
                                            start=(kt == 0), stop=(kt == 7))
                        return last
                    s.op("pe", mmu, reads=[wb, xTb], writes=[pub])
                    s.op("act", lambda e, pu=pu, ft=ft, dst=dst, sc=sc: e.mul(out=dst[:, ft, 0:nb], in_=pu[:, 0:nb], mul=sc),
                         reads=[pub], writes=[dstb])
            s.dma("sp", g.FQT[:, :, r0:r0 + nb].rearrange("t p c -> p t c"), FQs[:, :, 0:nb], reads=[FQb], writes=[g.scrb])
            for (kd, lo, dr, cnt) in split_rows(g, r0, nb):
                dstT = g.FKTp if kd == "p" else g.FKTs
                d0 = dr if kd == "p" else PAST + dr
                s.dma("sp", dstT[:, :, d0:d0 + cnt].rearrange("t p c -> p t c"), FKs[:, :, lo:lo + cnt], reads=[FKb], writes=[g.scrb])
            pl, plb = next_ps(g)

            def mml(e, pl=pl):
                last = None
                for kt in range(8):
                    last = e.matmul(pl[0:8, 0:nb], lhsT=win[:, kt, 1536:1544], rhs=xT[:, kt, 0:nb], start=(kt == 0), stop=(kt == 7))
                return last
            s.op("pe", mml, reads=[wb, xTb], writes=[plb])
            s.op("act", lambda e, pl=pl: e.activation(out=lf[:, 0:nb], in_=pl[0:8, 0:nb], func=AF.Exp, scale=-1.0, bias=nbf[:, 0:1]),
                 reads=[plb, wb, lfb], writes=[lfb])
            s.op("act", lambda e: e.activation(out=lf[:, 0:nb], in_=lf[:, 0:nb], func=AF.Ln, bias=1.0), reads=[lfb], writes=[lfb])
            s.op("dve", lambda e: e.tensor_scalar(out=lf[:, 0:nb], in0=lf[:, 0:nb], scalar1=-1.0, scalar2=None, op0=ALU.mult),
                 reads=[lfb], writes=[lfb])
            for (kd, lo, dr, cnt) in split_rows(g, r0, nb):
                cr, crb = (carp, carpb) if kd == "p" else (car, carb)
                pre = "p_" if kd == "p" else "s_"
                s.dma("sp", O[pre + "flf"][o, dr:dr + cnt, :].rearrange("k h -> h k"), lf[:, lo:lo + cnt], reads=[lfb], writes=[g.outb],
                      allow_slow_non_contiguous=True)
                s.op("dve", lambda e, lo=lo, cnt=cnt, cr=cr: e.tensor_tensor_scan(
                    out=fc[:, lo:lo + cnt], data0=one8[:, lo:lo + cnt], data1=lf[:, lo:lo + cnt], initial=cr[:, 0:1],
                    op0=ALU.mult, op1=ALU.add), reads=[lfb, crb, wb], writes=[lfb])
                s.op("act", lambda e, lo=lo, cnt=cnt, cr=cr: e.copy(out=cr[:, 0:1], in_=fc[:, lo + cnt - 1:lo + cnt]), reads=[lfb, crb], writes=[crb])
                dF = g.FTp if kd == "p" else g.FTs
                dF3 = g.FT3p if kd == "p" else g.FT3s
                d0 = dr if kd == "p" else PAST + dr
                s.dma("sp", dF[:, d0:d0 + cnt], fc[:, lo:lo + cnt], reads=[lfb], writes=[g.scrb])
                split3(lo, cnt, dF3, d0)
            for ti, (oo, n) in enumerate(chunks(nb, 128)):
                rr = r0 + oo
                pieces = split_rows(g, rr, n)

                def mmt(e, pm, c0, oo=oo, n=n):
                    last = None
                    for kt in range(8):
                        last = e.matmul(pm[0:n, :], lhsT=xT[:, kt, oo:oo + n], rhs=win[:, kt, c0:c0 + 512], start=(kt == 0), stop=(kt == 7))
                    return last
                pm, pmb = next_ps(g)
                s.op("pe", lambda e, pm=pm: mmt(e, pm, 512), reads=[wb, xTb], writes=[pmb])
                s.op("act", lambda e, pm=pm, n=n: e.copy(out=kf[0][0:n, :], in_=pm[0:n, :]), reads=[pmb], writes=[kfb[0]])
                for (kd, lo, dr, cnt) in pieces:
                    s.dma("sp", O[("p_" if kd == "p" else "s_") + "fk"][o, dr:dr + cnt, :], kf[0][lo:lo + cnt, :], reads=[kfb[0]], writes=[g.outb])
                pm, pmb = next_ps(g)
                s.op("pe", lambda e, pm=pm: mmt(e, pm, 1024), reads=[wb, xTb], writes=[pmb])
                s.op("act", lambda e, pm=pm, n=n: e.copy(out=kf[1][0:n, :], in_=pm[0:n, :]), reads=[pmb], writes=[kfb[1]])
                s.op("dve", lambda e, n=n: e.tensor_copy(out=VAs_[0:n, :, 0:64], in_=kf[1][0:n, :].rearrange("p (h c) -> p h c", c=64)),
                     reads=[kfb[1]], writes=[VAb])
                for (kd, lo, dr, cnt) in pieces:
                    s.dma("sp", O[("p_" if kd == "p" else "s_") + "fv"][o, dr:dr + cnt, :], kf[1][lo:lo + cnt, :], reads=[kfb[1]], writes=[g.outb])
                    dV = g.VAp if kd == "p" else g.VAs
                    d0 = dr if kd == "p" else PAST + dr
                    s.dma("sp", dV[d0:d0 + cnt, :], VAs_[lo:lo + cnt, :, :].rearrange("p h c -> p (h c)"), reads=[VAb], writes=[g.scrb])
                pm, pmb = next_ps(g)
                s.op("pe", lambda e, pm=pm: mmt(e, pm, 1544), reads=[wb, xTb], writes=[pmb])
                ra, rbt, rab = ra4[:, ti], rbt4[:, ti], rab4[ti]
                rope_rows(g, pm[0:n, :].rearrange("p (h c) -> p h c", c=64), n, 8, 32, ra, rbt, qk[0:n], tA, tB, tAb, [pmb, rab], [qkb])
                s.dma("sp", g.RK[rr:rr + n, :], qk[0:n, 4:8, :].rearrange("p h c -> p (h c)"), reads=[qkb], writes=[g.scrb])
                pm, pmb = next_ps(g)
                s.op("pe", lambda e, pm=pm: mmt(e, pm, 2056), reads=[wb, xTb], writes=[pmb])
                s.op("act", lambda e, pm=pm, n=n: e.copy(out=rvb[0:n, :], in_=pm[0:n, :]), reads=[pmb], writes=[rvbb])
                s.dma("sp", g.RV[rr:rr + n, :], rvb[0:n, :], reads=[rvbb], writes=[g.scrb])
                pm, pmb = next_ps(g)
                s.op("pe", lambda e, pm=pm: mmt(e, pm, 2568), reads=[wb, xTb], writes=[pmb])
                s.op("act", lambda e, pm=pm, n=n: e.activation(out=rgf[0:n, :], in_=pm[0:n, :], func=AF.Silu), reads=[pmb], writes=[rvbb])
                s.dma("sp", g.RG[rr:rr + n, :], rgf[0:n, :], reads=[rvbb], writes=[g.scrb])
                pt, ptb = next_pst(g)

                def tr4(e, pt=pt, n=n):
                    last = None
                    for j in range(4):
                        last = e.transpose(out=pt[:, j * 128:j * 128 + n], in_=qk[0:n, 2 * j:2 * j + 2, :].rearrange("p h c -> p (h c)"),
                                           identity=g.ident[0:n, 0:n])
                    return last
                s.op("pe", tr4, reads=[qkb, g.identb], writes=[ptb])
                s.op("dve", lambda e, pt=pt, n=n: e.tensor_copy(out=qkT[:, :, 0:n], in_=pt[:, 0:512].rearrange("p (j c) -> p j c", j=4)[:, :, 0:n]),
                     reads=[ptb], writes=[qkTb])
                s.dma("sp", g.RQKT[:, :, rr:rr + n].rearrange("t p c -> p t c"), qkT[:, :, 0:n], reads=[qkTb], writes=[g.scrb])


@staged
def retention(g, o):
    nc, s, R, T = g.nc, g.s, g.R, g.T
    I, O = g.I, g.O
    with ExitStack() as es:
        tag = "rt%d" % o
        decT = T_(g, es, "decT" + tag, [128, 4, 128], F32)
        gq = T_(g, es, "gqd" + tag, [128, 4, 128], F32)
        ginv = T_(g, es, "ginv" + tag, [128, 4], F32)
        cb = Buf()
        s.dma("sp", decT[:], I["ret_decT"].rearrange("h j i -> j h i"), writes=[cb])
        s.dma("sp", gq[:], I["ret_gq"].rearrange("h d i -> d h i"), writes=[cb])
        s.dma("sp", ginv[:], I["ret_ginv"], writes=[cb])
        S = [T_(g, es, "S%d" % h + tag, [64, 128], F32) for h in range(4)]
        Sb = [T_(g, es, "Sb%d" % h + tag, [64, 128], BF16) for h in range(4)]
        Sbuf = [Buf() for _ in range(4)]
        qT2 = [T_(g, es, "qT%d" % k + tag, [64, 8, 128], BF16) for k in range(2)]
        kt2 = [T_(g, es, "kt%d" % k + tag, [128, 256], BF16) for k in range(2)]
        v2 = [T_(g, es, "v%d" % k + tag, [128, 512], BF16) for k in range(2)]
        gg2 = [T_(g, es, "gg%d" % k + tag, [128, 512], F32) for k in range(2)]
        lb2 = [Buf(), Buf()]
        ci = 0
        PT = [T_(g, es, "PT%d" % k + tag, [128, 128], BF16) for k in range(4)]
        qd = [T_(g, es, "qd%d" % k + tag, [128, 128], BF16) for k in range(4)]
        kd_ = [T_(g, es, "kd%d" % k + tag, [128, 64], BF16) for k in range(4)]
        wbf = [Buf() for _ in range(4)]
        st4 = [T_(g, es, "st%d" % h + tag, [128, 6], F32) for h in range(4)]
        mv4 = [T_(g, es, "mv%d" % h + tag, [128, 2], F32) for h in range(4)]
        rs4 = [T_(g, es, "rs%d" % h + tag, [128, 1], F32) for h in range(4)]
        nm4 = [T_(g, es, "nm%d" % h + tag, [128, 1], F32) for h in range(4)]
        stb4 = [Buf() for _ in range(4)]
        on4 = [T_(g, es, "on%d" % h + tag, [128, 128], F32) for h in range(4)]
        ro = T_(g, es, "ro" + tag, [128, 4, 128], BF16)
        rob = Buf()
        roT = T_(g, es, "roT" + tag, [128, 4, 128], BF16)
        roTb = Buf()
        gam = [1.0 - 2.0 ** (-5.0 - h) for h in range(4)]
        it = 0
        for seg, (c0, c1) in enumerate(((0, T), (T, R))):
            for h in range(4):
                if seg == 0:
                    s.op("pool", lambda e, h=h: e.memset(S[h][:], 0.0), reads=[Sbuf[h]], writes=[Sbuf[h]])
                else:
                    s.dma("sp", S[h][:], I["st_ret"][o, h], reads=[Sbuf[h]], writes=[Sbuf[h]])
                s.op("act", lambda e, h=h: e.copy(out=Sb[h][:], in_=S[h][:]), reads=[Sbuf[h]], writes=[Sbuf[h]])
            cks = chunks(c1 - c0, 128)

            def ch_loads(j, cj):
                if j >= len(cks):
                    return
                co_, n_ = cks[j]
                rr_ = c0 + co_
                s.dma("sp", qT2[cj % 2][:, :, 0:n_], g.RQKT[:, :, rr_:rr_ + n_].rearrange("t (hh p) c -> p (t hh) c", hh=2),
                      reads=[g.scrb], writes=[lb2[cj % 2]])
                s.dma("sp", kt2[cj % 2][0:n_, :], g.RK[rr_:rr_ + n_, :], reads=[g.scrb], writes=[lb2[cj % 2]])
                s.dma("sp", v2[cj % 2][0:n_, :], g.RV[rr_:rr_ + n_, :], reads=[g.scrb], writes=[lb2[cj % 2]])
                s.dma("sp", gg2[cj % 2][0:n_, :], g.RG[rr_:rr_ + n_, :], reads=[g.scrb], writes=[lb2[cj % 2]])
            ch_loads(0, ci)
            for j, (co, n) in enumerate(cks):
                rr = c0 + co
                qT, kt_, v, gg, lb = qT2[ci % 2], kt2[ci % 2], v2[ci % 2], gg2[ci % 2], lb2[ci % 2]
                ci += 1
                ch_loads(j + 1, ci)
                backs = []
                for h in range(4):
                    k = h
                    it += 1
                    st, mv, rs, nm, stb, on = st4[h], mv4[h], rs4[h], nm4[h], stb4[h], on4[h]
                    p0 = 0
                    qh = qT[0:64, h, 0:n]
                    kh = qT[0:64, 4 + h, 0:n]
                    psc, pscb = next_ps(g)
                    s.op("pe", lambda e, psc=psc, kh=kh, qh=qh: e.matmul(psc[0:n, 0:n], lhsT=kh, rhs=qh, start=True, stop=True),
                         reads=[lb], writes=[pscb])
                    s.op("dve", lambda e, psc=psc, k=k, h=h: e.tensor_tensor(out=PT[k][0:n, 0:n], in0=psc[0:n, 0:n], in1=decT[0:n, h, 0:n],
                                                                              op=ALU.mult), reads=[pscb, cb], writes=[wbf[k]])
                    s.op("pool", lambda e, k=k, h=h, qh=qh, p0=p0: e.tensor_tensor(out=qd[k][p0:p0 + 64, 0:n], in0=qh, in1=gq[p0:p0 + 64, h, 0:n],
                                                                                   op=ALU.mult), reads=[lb, cb], writes=[wbf[k]])
                    s.op("dve", lambda e, k=k, h=h, kt_=kt_: e.tensor_scalar(out=kd_[k][0:n, :], in0=kt_[0:n, h * 64:(h + 1) * 64],
                                                                    scalar1=ginv[0:n, h:h + 1], scalar2=float(gam[h] ** (n - 1)),
                                                                    op0=ALU.mult, op1=ALU.mult), reads=[lb, cb], writes=[wbf[k]])
                    po, pob = next_ps(g)

                    def mmo(e, po=po, k=k, h=h, p0=p0, v=v):
                        e.matmul(po[0:n, 0:128], lhsT=PT[k][0:n, 0:n], rhs=v[0:n, h * 128:(h + 1) * 128], start=True, stop=False)
                        return e.matmul(po[0:n, 0:128], lhsT=qd[k][p0:p0 + 64, 0:n], rhs=Sb[h][:, :], start=False, stop=True)
                    s.op("pe", mmo, reads=[wbf[k], lb, Sbuf[h]], writes=[pob])
                    pst_, pstb_ = next_ps(g)
                    s.op("pe", lambda e, pst_=pst_, k=k, h=h, v=v: e.matmul(pst_[0:64, 0:128], lhsT=kd_[k][0:n, :], rhs=v[0:n, h * 128:(h + 1) * 128],
                                                                       start=True, stop=True), reads=[wbf[k], lb], writes=[pstb_])
                    def back(h=h, po=po, pob=pob, pst_=pst_, pstb_=pstb_, st=st, mv=mv, rs=rs, nm=nm, stb=stb, on=on, gg=gg, lb=lb, n=n):
                        s.op("dve", lambda e, pst_=pst_, h=h: e.scalar_tensor_tensor(out=S[h][:], in0=S[h][:], scalar=float(gam[h] ** n),
                                                                                     in1=pst_[0:64, 0:128], op0=ALU.mult, op1=ALU.add),
                             reads=[pstb_, Sbuf[h]], writes=[Sbuf[h]])
                        s.op("act", lambda e, h=h: e.copy(out=Sb[h][:], in_=S[h][:]), reads=[Sbuf[h]], writes=[Sbuf[h]])
                        s.op("dve", lambda e, po=po, st=st: e.bn_stats(out=st[0:n, :], in_=po[0:n, 0:128]), reads=[pob], writes=[stb])
                        s.op("dve", lambda e, mv=mv, st=st: e.bn_aggr(out=mv[0:n, :], in_=st[0:n, :]), reads=[stb], writes=[stb])
                        s.op("dve", lambda e, rs=rs, mv=mv: e.tensor_scalar(out=rs[0:n, :], in0=mv[0:n, 1:2], scalar1=EPS, scalar2=None, op0=ALU.add),
                             reads=[stb], writes=[stb])
                        s.op("act", lambda e, rs=rs: e.sqrt(out=rs[0:n, :], in_=rs[0:n, :]), reads=[stb], writes=[stb])
                        s.op("dve", lambda e, rs=rs: e.reciprocal(out=rs[0:n, :], in_=rs[0:n, :]), reads=[stb], writes=[stb])
                        s.op("dve", lambda e, nm=nm, mv=mv, rs=rs: e.scalar_tensor_tensor(out=nm[0:n, :], in0=mv[0:n, 0:1], scalar=-1.0, in1=rs[0:n, :],
                                                                     op0=ALU.mult, op1=ALU.mult), reads=[stb], writes=[stb])
                        s.op("act", lambda e, po=po, on=on, nm=nm, rs=rs: e.activation(out=on[0:n, :], in_=po[0:n, 0:128], func=AF.Identity, bias=nm[0:n, 0:1],
                                                                  scale=rs[0:n, 0:1]), reads=[stb, pob], writes=[stb])
                        s.op("pool", lambda e, h=h, on=on, gg=gg: e.tensor_tensor(out=ro[0:n, h, :], in0=on[0:n, :], in1=gg[0:n, h * 128:(h + 1) * 128], op=ALU.mult),
                             reads=[stb, lb], writes=[rob])
                    backs.append(back)
                    if h % 2 == 1:
                        for bk in backs:
                            bk()
                        backs = []
                pt, ptb = next_pst(g)

                def tr5(e, pt=pt):
                    last = None
                    for h in range(4):
                        last = e.transpose(out=pt[:, h * 128:h * 128 + n], in_=ro[0:n, h, :], identity=g.ident[0:n, 0:n])
                    return last
                s.op("pe", tr5, reads=[rob, g.identb], writes=[ptb])
                s.op("dve", lambda e, pt=pt: e.tensor_copy(out=roT[:, :, 0:n], in_=pt[:, 0:512].rearrange("p (j c) -> p j c", j=4)[:, :, 0:n]),
                     reads=[ptb], writes=[roTb])
                s.dma("sp", g.MIXT[4:8, :, rr:rr + n].rearrange("t p c -> p t c"), roT[:, :, 0:n], reads=[roTb], writes=[g.scrb])
            pre = "p_" if seg == 0 else "s_"
            for h in range(4):
                s.dma("sp", O[pre + "ret"][o, h], S[h][:], reads=[Sbuf[h]], writes=[g.outb])


def make_consts(SEQ, PAST):
    T = NMETA + SEQ
    R = T + DSEQ
    bf = ml_dtypes.bfloat16
    c = {"ident_bf": np.eye(128, dtype=np.float32).astype(bf), "ident_f": np.eye(128, dtype=np.float32)}
    kk = np.arange(128)[:, None]
    qq = np.arange(512)[None, :]
    lim = NMETA + 64 * (np.floor_divide(qq - NMETA, 64) + 1)
    NEGM = -30000.0
    c["mask_mla"] = np.stack([np.where((128 * d + kk) < lim, 0.0, NEGM) for d in range(5)]).astype(np.float32).astype(bf)
    c["mask_fox"] = np.stack([np.where((128 * d + kk) <= qq, 0.0, NEGM) for d in range(4)]).astype(np.float32).astype(bf)
    pos = np.concatenate([np.arange(T), NMETA + PAST + np.arange(DSEQ)]).astype(np.float32)
    inv = (10000.0 ** (-np.arange(16, dtype=np.float32) / 16)).astype(np.float32)
    ang = (pos[:, None] * inv[None, :]).astype(np.float32)
    cs, sn = np.cos(ang).astype(np.float32), np.sin(ang).astype(np.float32)
    A = np.concatenate([cs, cs], -1)
    B = np.concatenate([-sn, sn], -1)
    c["ropeA16"] = np.ascontiguousarray(np.broadcast_to(A[:, None, :], (R, 4, 32))).astype(np.float32)
    c["ropeB16"] = np.ascontiguousarray(np.broadcast_to(B[:, None, :], (R, 4, 32))).astype(np.float32)
    inv2 = (10000.0 ** (-np.arange(32, dtype=np.float32) / 32)).astype(np.float32)
    ang2 = (pos[:, None] * inv2[None, :]).astype(np.float32)
    c2, s2 = np.cos(ang2).astype(np.float32), np.sin(ang2).astype(np.float32)
    A2 = np.concatenate([c2, c2], -1)
    B2 = np.concatenate([-s2, s2], -1)
    scl = np.array([1.0] * 4 + [0.125] * 4, np.float32)[None, :, None]
    c["ropeA32"] = np.ascontiguousarray(A2[:, None, :] * scl).astype(np.float32)
    c["ropeB32"] = np.ascontiguousarray(B2[:, None, :] * scl).astype(np.float32)
    gam = np.array([1.0 - 2.0 ** (-5.0 - h) for h in range(4)], np.float64)
    jj = np.arange(128)
    dif = jj[None, :] - jj[:, None]
    c["ret_decT"] = np.stack([np.where(dif >= 0, gam[h] ** np.maximum(dif, 0), 0.0) for h in range(4)]).astype(np.float32)
    c["ret_gq"] = np.stack([np.broadcast_to((gam[h] ** (jj + 1.0))[None, :], (128, 128)) for h in range(4)]).astype(np.float32)
    c["ret_ginv"] = np.stack([gam[h] ** (-jj.astype(np.float64)) for h in range(4)], axis=1).astype(np.float32)
    c["tau"] = np.ascontiguousarray(np.broadcast_to(np.arange(1, 513, dtype=np.float32)[None, :], (128, 512)))
    return c


def make_in_maps(inp, n_cores=8):
    f = lambda a: np.ascontiguousarray(np.asarray(a, dtype=np.float32))
    consts = make_consts(inp["x_prompt"].shape[1], inp["cache_mla_ckv"].shape[2])
    nb = inp["x_prompt"].shape[0]
    maps = []
    for c in range(n_cores):
        bp = PROMPT_OF_CORE.get(c) if n_cores == 8 else c % nb
        xp = f(inp["x_prompt"][bp]) if bp is not None else np.zeros(inp["x_prompt"].shape[1:], np.float32)
        meta = f(inp["meta_tokens"]) if bp is not None else np.zeros(inp["meta_tokens"].shape, np.float32)
        m = {
            "xp": xp, "xs": f(inp["x_sample"][c]), "meta": meta,
            "c_ckv": f(inp["cache_mla_ckv"][:, c]), "c_kpe": f(inp["cache_mla_kpe"][:, c]),
            "c_fk": f(np.asarray(inp["cache_fox_k"])[:, c].reshape(2, -1, 512)),
            "c_fv": f(np.asarray(inp["cache_fox_v"])[:, c].reshape(2, -1, 512)),
            "c_flf": f(inp["cache_fox_logf"][:, c]),
            "st_re": f(inp["state_s5_re"][:, c]), "st_im": f(inp["state_s5_im"][:, c]), "st_ret": f(inp["state_ret"][:, c]),
            "ln_g": f(inp["ln_g"]), "ln_b": f(inp["ln_b"]),
            "wg": f(inp["ffn_w_gate"]), "wu": f(inp["ffn_w_up"]), "wd": f(inp["ffn_w_down"]),
            "ewin": f(inp["even_w_in"]), "ewout": f(inp["even_w_out"]),
            "owin": f(inp["odd_w_in"]), "owout": f(inp["odd_w_out"]), "fox_b_f": f(inp["fox_b_f"]),
        }
        for k in ("s5_a_re", "s5_a_im", "s5_b_re", "s5_b_im", "s5_c_re", "s5_c_im", "s5_d", "s5_log_dt", "s5_w_glu",
                  "s5_b_glu", "mla_q_norm", "mla_kv_norm", "mla_w_uq", "mla_w_ukv"):
            m[k] = f(inp[k])
        m.update(consts)
        maps.append(m)
    return maps


PROMPT_OF_CORE = {0: 0, 1: 1, 4: 2, 5: 3}
CORE_OF_PROMPT = {b: c for c, b in PROMPT_OF_CORE.items()}


def gather(res, SEQ, nb_p=4, n_cores=8):
    T = NMETA + SEQ
    r = res.results
    P = lambda k: np.stack([r[CORE_OF_PROMPT[b] if n_cores == 8 else b][k] for b in range(nb_p)])
    S = lambda k: np.stack([r[c][k] for c in range(n_cores)])
    sw = lambda a: np.swapaxes(a, 0, 1)
    outs = [P("y_p"), S("y_s"),
            sw(P("p_ckv")), sw(P("p_kpe")), sw(P("p_fk")).reshape(2, nb_p, T, 8, 64), sw(P("p_fv")).reshape(2, nb_p, T, 8, 64),
            sw(P("p_flf")), sw(P("p_s5re")), sw(P("p_s5im")), sw(P("p_ret")),
            sw(S("s_ckv")), sw(S("s_kpe")), sw(S("s_fk")).reshape(2, n_cores, DSEQ, 8, 64),
            sw(S("s_fv")).reshape(2, n_cores, DSEQ, 8, 64), sw(S("s_flf")), sw(S("s_s5re")), sw(S("s_s5im")), sw(S("s_ret"))]
    return tuple(np.ascontiguousarray(o.astype(np.float32)) for o in outs)


def kernel(**inputs):
    SEQ = inputs["x_prompt"].shape[1]
    PAST = inputs["cache_mla_ckv"].shape[2]
    nc = build(SEQ, PAST)
    in_maps = make_in_maps(inputs)
    res = run_bass_kernel_spmd(nc, in_maps, core_ids=list(range(8)))
    return gather(res, SEQ)
```
